# Optimizing a Trainium2 kernel written in Bass

```python
import math
import jax
import jax.numpy as jnp
from jax import lax
import numpy as np

D_MODEL = 1024
BATCH = 16
SEQ = 2048
DEPTH = 4

N_MIXERS = 3

SSD_EXPAND = 2
SSD_D_INNER = SSD_EXPAND * D_MODEL
SSD_HEADDIM = 64
SSD_N_HEADS = SSD_D_INNER // SSD_HEADDIM
SSD_N_GROUPS = 4
SSD_HEADS_PER_GROUP = SSD_N_HEADS // SSD_N_GROUPS
SSD_D_STATE = 128
SSD_CONV_WIDTH = 4
SSD_CHUNK = 128
SSD_CONV_DIM = SSD_D_INNER + 2 * SSD_N_GROUPS * SSD_D_STATE
SSD_IN_DIM = SSD_D_INNER + SSD_CONV_DIM + SSD_N_HEADS

MOBA_HEAD_DIM = 64
MOBA_N_HEADS = D_MODEL // MOBA_HEAD_DIM
MOBA_BLOCK = 256
MOBA_TOPK = 3
MOBA_Q_BLOCK = 128

CONV_KERNEL = 31

PEER_N_KEYS = 128
PEER_N_EXPERTS = PEER_N_KEYS * PEER_N_KEYS
PEER_HEADS = 8
PEER_TOPK = 16
PEER_QUERY_DIM = 256
PEER_HALF = PEER_QUERY_DIM // 2
PEER_TOKEN_BLOCK = 128

PLE_DIM = 256

LN_EPS = 1e-5
DEEPNORM_ALPHA = (2 * DEPTH) ** 0.25
DEEPNORM_BETA = (8 * DEPTH) ** -0.25

kernel_name = 'hybrid_ssd_moba_conformer_peer_deepnorm'


def n_layers_of_kind(kind):
    return len(range(kind, DEPTH, N_MIXERS))


def layer_norm(x, g, b):
    xf = x.astype(jnp.float32)
    mu = jnp.mean(xf, axis=-1, keepdims=True)
    var = jnp.mean(jnp.square(xf - mu), axis=-1, keepdims=True)
    return ((xf - mu) * lax.rsqrt(var + LN_EPS)).astype(x.dtype) * g + b


def causal_depthwise_conv(x, w, b):
    k_width, chans = w.shape
    y = lax.conv_general_dilated(
        x, w[:, None, :].astype(x.dtype), window_strides=(1,), padding=[(k_width - 1, 0)],
        dimension_numbers=('NWC', 'WIO', 'NWC'), feature_group_count=chans)
    return y + b


def alibi_slopes(n_heads):
    return 2.0 ** (-8.0 * jnp.arange(1, n_heads + 1, dtype=jnp.float32) / n_heads)


def ssd_mixer(x, w_in, conv_w, conv_b, dt_bias, a_log, d_skip, norm_g, w_out):
    f32 = jnp.float32
    bsz, seq, _ = x.shape
    G, R, P, N, L = SSD_N_GROUPS, SSD_HEADS_PER_GROUP, SSD_HEADDIM, SSD_D_STATE, SSD_CHUNK
    nc = seq // L
    zxbcdt = x @ w_in
    z = zxbcdt[..., :SSD_D_INNER]
    xbc = zxbcdt[..., SSD_D_INNER:SSD_D_INNER + SSD_CONV_DIM]
    dt = zxbcdt[..., SSD_D_INNER + SSD_CONV_DIM:]
    xbc = jax.nn.silu(causal_depthwise_conv(xbc, conv_w, conv_b)).astype(f32)
    xs = xbc[..., :SSD_D_INNER]
    b_in = xbc[..., SSD_D_INNER:SSD_D_INNER + G * N].reshape(bsz, nc, L, G, N)
    c_in = xbc[..., SSD_D_INNER + G * N:].reshape(bsz, nc, L, G, N)
    dt = jax.nn.softplus(dt.astype(f32) + dt_bias.astype(f32)).reshape(bsz, nc, L, G, R)
    a = -jnp.exp(a_log.astype(f32)).reshape(G, R)
    x_dt = xs.reshape(bsz, nc, L, G, R, P) * dt[..., None]
    a_cum = jnp.cumsum(dt * a, axis=2)
    causal = jnp.tril(jnp.ones((L, L), dtype=bool))
    seg = a_cum[:, :, :, None] - a_cum[:, :, None, :]
    decay_ls = jnp.exp(jnp.where(causal[None, None, :, :, None, None], seg, -jnp.inf))
    cb = jnp.einsum('bclgn,bcsgn->bclsg', c_in, b_in)
    y_diag = jnp.einsum('bclsgr,bcsgrp->bclgrp', cb[..., None] * decay_ls, x_dt)
    decay_to_end = jnp.exp(a_cum[:, :, -1:] - a_cum)
    states = jnp.einsum('bclgn,bclgr,bclgrp->bcgrpn', b_in, decay_to_end, x_dt)
    chunk_decay = jnp.exp(a_cum[:, :, -1])

    def step(h, inp):
        st, dec = inp
        return dec[..., None, None] * h + st, h

    h0 = jnp.zeros((bsz, G, R, P, N), f32)
    _, prev = lax.scan(step, h0, (jnp.moveaxis(states, 1, 0), jnp.moveaxis(chunk_decay, 1, 0)))
    y_off = jnp.einsum('bclgn,cbgrpn,bclgr->bclgrp', c_in, prev, jnp.exp(a_cum))
    y = (y_diag + y_off).reshape(bsz, seq, SSD_N_HEADS, P)
    y = y + d_skip.astype(f32)[:, None] * xs.reshape(bsz, seq, SSD_N_HEADS, P)
    gsz = SSD_D_INNER // G
    gated = y.reshape(bsz, seq, G, gsz) * jax.nn.silu(z.astype(f32)).reshape(bsz, seq, G, gsz)
    gated = gated * lax.rsqrt(jnp.mean(jnp.square(gated), axis=-1, keepdims=True) + LN_EPS)
    y = (gated.reshape(bsz, seq, SSD_D_INNER) * norm_g.astype(f32)).astype(x.dtype)
    return y @ w_out


def moba_attention(x, w_qkv, w_out):
    f32 = jnp.float32
    bsz, seq, _ = x.shape
    H, Dh = MOBA_N_HEADS, MOBA_HEAD_DIM
    n_blk = -(-seq // MOBA_BLOCK)
    pad = n_blk * MOBA_BLOCK - seq
    k_sel = min(MOBA_TOPK, n_blk)
    n_qblk = seq // MOBA_Q_BLOCK
    qkv = (x @ w_qkv).reshape(bsz, seq, 3, H, Dh)
    q = qkv[:, :, 0].transpose(0, 2, 1, 3)
    padw = ((0, 0), (0, 0), (0, pad), (0, 0))
    k = jnp.pad(qkv[:, :, 1].transpose(0, 2, 1, 3), padw)
    v = jnp.pad(qkv[:, :, 2].transpose(0, 2, 1, 3), padw)
    k_blocks = k.reshape(bsz, H, n_blk, MOBA_BLOCK, Dh)
    v_blocks = v.reshape(bsz, H, n_blk, MOBA_BLOCK, Dh)
    k_mean = jnp.mean(k_blocks, axis=3)
    slopes = alibi_slopes(H)
    scale = Dh ** -0.5
    offs = jnp.arange(MOBA_BLOCK)
    blk_ids = jnp.arange(n_blk)

    def attend_one_sequence(args):
        q_s, kb, vb, km = args

        def attend_query_block(qi):
            q0 = qi * MOBA_Q_BLOCK
            qb = lax.dynamic_slice_in_dim(q_s, q0, MOBA_Q_BLOCK, axis=1)
            t = q0 + jnp.arange(MOBA_Q_BLOCK)
            own = q0 // MOBA_BLOCK
            gate = jnp.einsum('hqd,hnd->hqn', qb, km).astype(f32)
            gate = jnp.where(blk_ids < own, gate, -jnp.inf)
            _, sel = lax.top_k(gate, k_sel)
            valid = sel < own
            k_g = jax.vmap(lambda kbh, ih: kbh[ih])(kb, sel)
            v_g = jax.vmap(lambda vbh, ih: vbh[ih])(vb, sel)
            s_pos = sel[..., None] * MOBA_BLOCK + offs
            logit_sel = (jnp.einsum('hqd,hqjkd->hqjk', qb, k_g).astype(f32) * scale
                         - slopes[:, None, None, None] * (t[None, :, None, None] - s_pos).astype(f32))
            logit_sel = jnp.where(valid[..., None], logit_sel, -jnp.inf)
            k_own = lax.dynamic_index_in_dim(kb, own, axis=1, keepdims=False)
            v_own = lax.dynamic_index_in_dim(vb, own, axis=1, keepdims=False)
            o_pos = own * MOBA_BLOCK + offs
            dist = (t[:, None] - o_pos[None, :]).astype(f32)
            logit_own = (jnp.einsum('hqd,hkd->hqk', qb, k_own).astype(f32) * scale
                         - slopes[:, None, None] * dist[None])
            logit_own = jnp.where((o_pos[None, :] <= t[:, None])[None], logit_own, -jnp.inf)
            n_sel = k_sel * MOBA_BLOCK
            probs = jax.nn.softmax(
                jnp.concatenate([logit_sel.reshape(H, MOBA_Q_BLOCK, n_sel), logit_own], axis=-1), axis=-1)
            p_sel = probs[..., :n_sel].reshape(H, MOBA_Q_BLOCK, k_sel, MOBA_BLOCK).astype(v_g.dtype)
            p_own = probs[..., n_sel:].astype(v_own.dtype)
            return (jnp.einsum('hqjk,hqjkd->hqd', p_sel, v_g)
                    + jnp.einsum('hqk,hkd->hqd', p_own, v_own))

        out = lax.map(attend_query_block, jnp.arange(n_qblk))
        return out.transpose(1, 0, 2, 3).reshape(H, seq, Dh)

    out = lax.map(attend_one_sequence, (q, k_blocks, v_blocks, k_mean))
    out = out.transpose(0, 2, 1, 3).reshape(bsz, seq, D_MODEL).astype(x.dtype)
    return out @ w_out


def conformer_conv_module(x, w_pw1, b_pw1, w_dw, b_dw, ln_g, ln_b, w_pw2):
    h = x @ w_pw1 + b_pw1
    h = h[..., :D_MODEL] * jax.nn.sigmoid(h[..., D_MODEL:])
    h = causal_depthwise_conv(h, w_dw, b_dw)
    h = jax.nn.silu(layer_norm(h, ln_g, ln_b))
    return h @ w_pw2


def peer_ffn(x, w_q, sub_keys, u, v):
    f32 = jnp.float32
    bsz, seq, d = x.shape
    n_tok = bsz * seq
    xt = x.reshape(n_tok // PEER_TOKEN_BLOCK, PEER_TOKEN_BLOCK, d)

    def block(xb):
        q = (xb @ w_q).reshape(PEER_TOKEN_BLOCK, PEER_HEADS, 2, PEER_HALF)
        s = jnp.einsum('thcd,hcnd->thcn', q, sub_keys).astype(f32)
        v1, i1 = lax.top_k(s[:, :, 0], PEER_TOPK)
        v2, i2 = lax.top_k(s[:, :, 1], PEER_TOPK)
        cand = (v1[..., :, None] + v2[..., None, :]).reshape(PEER_TOKEN_BLOCK, PEER_HEADS, PEER_TOPK * PEER_TOPK)
        cidx = (i1[..., :, None] * PEER_N_KEYS + i2[..., None, :]).reshape(PEER_TOKEN_BLOCK, PEER_HEADS, PEER_TOPK * PEER_TOPK)
        best, pos = lax.top_k(cand, PEER_TOPK)
        expert = jnp.take_along_axis(cidx, pos, axis=-1)
        gate = jax.nn.softmax(best, axis=-1)
        u_g = u[expert]
        v_g = v[expert]
        act = jax.nn.gelu(jnp.einsum('td,thkd->thk', xb, u_g).astype(f32))
        return jnp.einsum('thk,thkd->td', (gate * act).astype(v_g.dtype), v_g)

    return lax.map(block, xt).reshape(bsz, seq, d)


def setup_inputs(seed: int = 0) -> dict:
    key = jax.random.key(seed)
    ks = iter(jax.random.split(key, 40))
    f32 = jnp.float32

    def nrm(shape, std):
        return jax.random.normal(next(ks), shape, f32) * std

    n_a, n_b, n_c = n_layers_of_kind(0), n_layers_of_kind(1), n_layers_of_kind(2)
    beta = DEEPNORM_BETA
    dt0 = jnp.exp(jax.random.uniform(next(ks), (n_a, SSD_N_HEADS), f32)
                  * (math.log(0.1) - math.log(0.001)) + math.log(0.001))
    dt0 = jnp.maximum(dt0, 1e-4)
    inp = {}
    inp['x'] = nrm((BATCH, SEQ, D_MODEL), 1.0)
    inp['p'] = nrm((DEPTH, BATCH, SEQ, PLE_DIM), 1.0)
    inp['ssd_w_in'] = nrm((n_a, D_MODEL, SSD_IN_DIM), D_MODEL ** -0.5)
    inp['ssd_conv_w'] = nrm((n_a, SSD_CONV_WIDTH, SSD_CONV_DIM), SSD_CONV_WIDTH ** -0.5)
    inp['ssd_conv_b'] = nrm((n_a, SSD_CONV_DIM), 0.02)
    inp['ssd_dt_bias'] = dt0 + jnp.log(-jnp.expm1(-dt0))
    inp['ssd_a_log'] = jnp.log(jax.random.uniform(next(ks), (n_a, SSD_N_HEADS), f32, 1.0, 16.0))
    inp['ssd_d'] = 1.0 + nrm((n_a, SSD_N_HEADS), 0.02)
    inp['ssd_norm_g'] = 1.0 + nrm((n_a, SSD_D_INNER), 0.02)
    inp['ssd_w_out'] = nrm((n_a, SSD_D_INNER, D_MODEL), beta * SSD_D_INNER ** -0.5)
    inp['moba_w_qkv'] = nrm((n_b, D_MODEL, 3 * D_MODEL), D_MODEL ** -0.5)
    inp['moba_w_out'] = nrm((n_b, D_MODEL, D_MODEL), beta * D_MODEL ** -0.5)
    inp['conv_w_pw1'] = nrm((n_c, D_MODEL, 2 * D_MODEL), D_MODEL ** -0.5)
    inp['conv_b_pw1'] = nrm((n_c, 2 * D_MODEL), 0.02)
    inp['conv_w_dw'] = nrm((n_c, CONV_KERNEL, D_MODEL), CONV_KERNEL ** -0.5)
    inp['conv_b_dw'] = nrm((n_c, D_MODEL), 0.02)
    inp['conv_ln_g'] = 1.0 + nrm((n_c, D_MODEL), 0.02)
    inp['conv_ln_b'] = nrm((n_c, D_MODEL), 0.02)
    inp['conv_w_pw2'] = nrm((n_c, D_MODEL, D_MODEL), beta * D_MODEL ** -0.5)
    inp['peer_w_q'] = nrm((DEPTH, D_MODEL, PEER_HEADS * PEER_QUERY_DIM), D_MODEL ** -0.5)
    inp['peer_sub_keys'] = nrm((DEPTH, PEER_HEADS, 2, PEER_N_KEYS, PEER_HALF), PEER_HALF ** -0.5)
    inp['peer_u'] = nrm((DEPTH, PEER_N_EXPERTS, D_MODEL), D_MODEL ** -0.5)
    inp['peer_v'] = nrm((DEPTH, PEER_N_EXPERTS, D_MODEL), beta * PEER_HEADS ** -0.5)
    inp['ln_mix_g'] = 1.0 + nrm((DEPTH, D_MODEL), 0.02)
    inp['ln_mix_b'] = nrm((DEPTH, D_MODEL), 0.02)
    inp['ln_ffn_g'] = 1.0 + nrm((DEPTH, D_MODEL), 0.02)
    inp['ln_ffn_b'] = nrm((DEPTH, D_MODEL), 0.02)
    inp['ple_w_gate'] = nrm((DEPTH, D_MODEL, D_MODEL), D_MODEL ** -0.5)
    inp['ple_w_proj'] = nrm((DEPTH, PLE_DIM, D_MODEL), PLE_DIM ** -0.5)
    return inp


def reference(x, p, ssd_w_in, ssd_conv_w, ssd_conv_b, ssd_dt_bias, ssd_a_log, ssd_d, ssd_norm_g,
              ssd_w_out, moba_w_qkv, moba_w_out, conv_w_pw1, conv_b_pw1, conv_w_dw, conv_b_dw,
              conv_ln_g, conv_ln_b, conv_w_pw2, peer_w_q, peer_sub_keys, peer_u, peer_v,
              ln_mix_g, ln_mix_b, ln_ffn_g, ln_ffn_b, ple_w_gate, ple_w_proj):
    alpha = DEEPNORM_ALPHA
    for i in range(DEPTH):
        kind, j = i % N_MIXERS, i // N_MIXERS
        if kind == 0:
            mix = ssd_mixer(x, ssd_w_in[j], ssd_conv_w[j], ssd_conv_b[j], ssd_dt_bias[j], ssd_a_log[j],
                            ssd_d[j], ssd_norm_g[j], ssd_w_out[j])
        elif kind == 1:
            mix = moba_attention(x, moba_w_qkv[j], moba_w_out[j])
        else:
            mix = conformer_conv_module(x, conv_w_pw1[j], conv_b_pw1[j], conv_w_dw[j], conv_b_dw[j],
                                        conv_ln_g[j], conv_ln_b[j], conv_w_pw2[j])
        x = layer_norm(alpha * x + mix, ln_mix_g[i], ln_mix_b[i])
        ffn = peer_ffn(x, peer_w_q[i], peer_sub_keys[i], peer_u[i], peer_v[i])
        x = layer_norm(alpha * x + ffn, ln_ffn_g[i], ln_ffn_b[i])
        gate = jax.nn.sigmoid((x @ ple_w_gate[i]).astype(jnp.float32)).astype(x.dtype)
        x = x + gate * (p[i] @ ple_w_proj[i])
    return x
```

```python
import numpy as np
from contextlib import ExitStack, contextmanager
import concourse.bass as bass
import concourse.mybir as mybir
from concourse.bass_utils import run_bass_kernel_spmd

F32 = mybir.dt.float32
BF16 = mybir.dt.bfloat16
I32 = mybir.dt.int32
U32 = mybir.dt.uint32
AF = mybir.ActivationFunctionType
ALU = mybir.AluOpType
AX = mybir.AxisListType

D = 1024
ALPHA = 8.0 ** 0.25
EPS = 1e-5
NEG = -1.0e30
KD = 8


class Buf:
    __slots__ = ("t", "w", "r", "name")

    def __init__(self, t, name=""):
        self.t = t
        self.w = None
        self.r = {}
        self.name = name

    def __getitem__(self, idx):
        return self.t[idx]


class Alias:
    def __init__(self, parent, t):
        self.__dict__["parent"] = parent
        self.__dict__["t"] = t

    def __getitem__(self, idx):
        return self.t[idx]

    def __getattr__(self, k):
        return getattr(self.__dict__["parent"], k)

    def __setattr__(self, k, v):
        setattr(self.__dict__["parent"], k, v)


class KB:
    def __init__(self, nc):
        self.nc = nc
        self.root = ExitStack()
        self.stacks = [self.root]
        self.eng = {}
        for name, h in (("pe", nc.tensor), ("act", nc.scalar), ("dve", nc.vector), ("pool", nc.gpsimd), ("sp", nc.sync)):
            sem = self.root.enter_context(nc.semaphore("s_" + name))
            self.eng[name] = dict(h=h, sem=sem, cnt=0, seen={}, name=name)
        self.dq = {}
        for q in ("sp", "pool", "act"):
            sems = [self.root.enter_context(nc.semaphore("d_%s%d" % (q, i))) for i in range(KD)]
            self.dq[q] = dict(sems=sems, n=0, cnt=[0] * KD)
        self.uid = 0
        self.ninst = 0

    @contextmanager
    def scope(self):
        st = ExitStack()
        self.stacks.append(st)
        try:
            yield
        finally:
            self.barrier()
            self.stacks.pop()
            st.close()

    def _nm(self, name):
        self.uid += 1
        return "%s_%d" % (name, self.uid)

    def sb(self, name, shape, dt):
        t = self.stacks[-1].enter_context(self.nc.sbuf_tensor(self._nm(name), list(shape), dt))
        return Buf(t, name)

    def ps(self, name, shape, dt):
        t = self.stacks[-1].enter_context(self.nc.psum_tensor(self._nm(name), list(shape), dt))
        return Buf(t, name)

    def dram(self, name, shape, dt, kind="Internal"):
        t = self.nc.dram_tensor(name, list(shape), dt, kind=kind)
        return Buf(t.ap(), name)

    def _wait(self, e, deps):
        best = {}
        for sem, val in deps:
            k = id(sem)
            if k not in best or best[k][1] < val:
                best[k] = (sem, val)
        for k, (sem, val) in best.items():
            if e["seen"].get(k, 0) >= val:
                continue
            e["h"].wait_ge(sem, val)
            e["seen"][k] = val

    def _deps(self, e, r, w, skip_self, is_dma=False):
        deps = []
        me = None if is_dma else id(e["sem"])
        for b in r:
            if b.w is not None:
                deps.append(b.w)
        for b in w:
            if b.w is not None:
                deps.append(b.w)
            for k, tok in b.r.items():
                deps.append(tok)
        if skip_self:
            deps = [d for d in deps if id(d[0]) != me]
        return deps

    def _mark(self, tok, r, w):
        for b in w:
            b.w = tok
            b.r = {}
        k = id(tok[0])
        wroots = [getattr(x, "parent", x) for x in w]
        for b in r:
            if not any(getattr(b, "parent", b) is x for x in wroots):
                b.r[k] = tok

    def op(self, en, fn, r=(), w=()):
        e = self.eng[en]
        self._wait(e, self._deps(e, r, w, en == "pe"))
        inst = fn(e["h"])
        e["cnt"] += 1
        self.ninst += 1
        inst.then_inc(e["sem"], 1)
        tok = (e["sem"], e["cnt"])
        self._mark(tok, r, w)
        return tok

    def V(self, fn, r=(), w=()):
        return self.op("dve", fn, r, w)

    def A(self, fn, r=(), w=()):
        return self.op("act", fn, r, w)

    def G(self, fn, r=(), w=()):
        return self.op("pool", fn, r, w)

    def T(self, fn, r=(), w=()):
        return self.op("pe", fn, r, w)

    def dma(self, q, out, in_, r=(), w=()):
        e = self.eng[q]
        d = self.dq[q]
        i = d["n"] % KD
        d["n"] += 1
        sem = d["sems"][i]
        deps = self._deps(e, r, w, False, True)
        if d["cnt"][i] > 0:
            deps.append((sem, d["cnt"][i] * 16))
        self._wait(e, deps)
        e["h"].dma_start(out=out, in_=in_).then_inc(sem, 16)
        self.ninst += 1
        d["cnt"][i] += 1
        tok = (sem, d["cnt"][i] * 16)
        self._mark(tok, r, w)
        return tok

    def barrier(self):
        toks = []
        for e in self.eng.values():
            if e["cnt"] > 0:
                toks.append((e["sem"], e["cnt"]))
        for d in self.dq.values():
            for i in range(KD):
                if d["cnt"][i] > 0:
                    toks.append((d["sems"][i], d["cnt"][i] * 16))
        for e in self.eng.values():
            self._wait(e, toks)


class Consts:
    pass


CONST = None


def load_consts(kb, cdram):
    c = Consts()
    cf = kb.sb("cf", [128, CW], F32)
    kb.dma("sp", cf[:, :], cdram[:, :], r=[cdram], w=[cf])
    c.cf = cf
    c.identf = lambda: cf[:, 0:128]
    c.ones = lambda: cf[:, 128:256]
    c.triu = lambda: cf[:, 256:384]
    c.negmask = lambda: cf[:, 384:512]
    c.tri_q = lambda: cf[:, 512:640]
    c.iota128 = lambda: cf[:, 640:768]
    c.negones = lambda: cf[:, 768:896]
    ib = kb.sb("identb", [128, 128], BF16)
    kb.V(lambda e: e.tensor_copy(out=ib[:, :], in_=cf[:, 0:128]), r=[cf], w=[ib])
    c.identb = ib
    io = kb.sb("iotab", [128, 128], BF16)
    kb.V(lambda e: e.tensor_copy(out=io[:, :], in_=cf[:, 640:768]), r=[cf], w=[io])
    c.iotab = io
    c.eps = kb.sb("epsc", [128, 1], F32)
    kb.V(lambda e: e.memset(c.eps[:, :], EPS), w=[c.eps])
    global CONST
    CONST = c
    return c


CW = 896
NRELW = 2432


def host_consts():
    c = np.zeros((128, CW), np.float32)
    i = np.arange(128)
    c[:, 0:128] = np.eye(128, dtype=np.float32)
    c[:, 128:256] = 1.0
    c[:, 256:384] = (i[:, None] <= i[None, :]).astype(np.float32)
    c[:, 384:512] = np.where(i[:, None] <= i[None, :], 0.0, NEG)
    c[:, 512:640] = np.where(i[None, :] <= i[:, None], 0.0, NEG)
    c[:, 640:768] = i[None, :].astype(np.float32)
    c[:, 768:896] = -1.0
    return c


def host_nrel():
    return np.ascontiguousarray(np.broadcast_to((np.arange(NRELW) - 2304)[None, :].astype(np.float32), (128, NRELW)))


def bcast_row(kb, name, src_ap, n, q="sp", rbuf=None):
    t = kb.sb(name, [128, n], F32)
    kb.dma(q, t[:, :], src_ap.partition_broadcast(128), r=[rbuf] if rbuf else [], w=[t])
    return t


def load_w_bf(kb, dst, dst_ap_fn, src_ap, rows_k, cols, stage, cast_engs=("act", "pool")):
    step = stage[0].t.shape[1]
    n = 0
    for k in range(rows_k):
        for c0 in range(0, cols, step):
            cw = min(step, cols - c0)
            st = stage[n % len(stage)]
            kb.dma("sp" if n % 2 == 0 else "pool", st[:, 0:cw], src_ap[k * 128:(k + 1) * 128, c0:c0 + cw], w=[st])
            en = cast_engs[n % len(cast_engs)]
            if en == "act":
                kb.A(lambda e, st=st, k=k, c0=c0, cw=cw: e.activation(out=dst_ap_fn(k, c0, cw), in_=st[:, 0:cw], func=AF.Copy), r=[st], w=[dst])
            else:
                kb.op(en, lambda e, st=st, k=k, c0=c0, cw=cw: e.tensor_copy(out=dst_ap_fn(k, c0, cw), in_=st[:, 0:cw]), r=[st], w=[dst])
            n += 1


def transpose_to(kb, c, src_bf, ncol_chunks, pst, dst, dst_ap, evac="dve"):
    for c0 in range(0, ncol_chunks, 8):
        nn = min(8, ncol_chunks - c0)
        for j in range(nn):
            kb.T(lambda e, j=j, c0=c0: e.transpose(out=pst[:, j * 128:(j + 1) * 128], in_=src_bf[:, (c0 + j) * 128:(c0 + j + 1) * 128], identity=c.identb[:, :]),
                 r=[src_bf, c.identb], w=[pst])
        o = dst_ap(c0, nn)
        i = pst[:, 0:nn * 128].rearrange("p (a b) -> p a b", b=128)
        if evac == "act":
            kb.A(lambda e, o=o, i=i: e.activation(out=o, in_=i, func=AF.Copy), r=[pst], w=[dst])
        else:
            kb.V(lambda e, o=o, i=i: e.tensor_copy(out=o, in_=i), r=[pst], w=[dst])


def layer_norm(kb, y, g_bc, b_bc, out, stats, mv, rstd):
    kb.V(lambda e: e.bn_stats(out=stats[:, 0:6], in_=y[:, 0:512]), r=[y], w=[stats])
    kb.V(lambda e: e.bn_stats(out=stats[:, 6:12], in_=y[:, 512:1024]), r=[y], w=[stats])
    kb.V(lambda e: e.bn_aggr(out=mv[:, 0:2], in_=stats[:, 0:12]), r=[stats], w=[mv])
    kb.A(lambda e: e.activation(out=rstd[:, 0:1], in_=mv[:, 1:2], func=AF.Ln, bias=CONST.eps[:, 0:1]), r=[mv, CONST.eps], w=[rstd])
    kb.A(lambda e: e.activation(out=rstd[:, 0:1], in_=rstd[:, 0:1], func=AF.Exp, scale=-0.5), r=[rstd], w=[rstd])
    kb.V(lambda e: e.tensor_scalar(out=out[:, :], in0=y[:, :], scalar1=mv[:, 0:1], scalar2=rstd[:, 0:1], op0=ALU.subtract, op1=ALU.mult), r=[y, mv, rstd], w=[out])
    kb.G(lambda e: e.tensor_tensor(out=out[:, :], in0=out[:, :], in1=g_bc[:, :], op=ALU.mult), r=[out, g_bc], w=[out])
    kb.G(lambda e: e.tensor_tensor(out=out[:, :], in0=out[:, :], in1=b_bc[:, :], op=ALU.add), r=[out, b_bc], w=[out])


def resid_ln_store(kb, xt, mixps, g_bc, b_bc, ybuf, tmp, dst_dram, dst_ap, q="sp"):
    for h in range(2):
        kb.V(lambda e, h=h: e.scalar_tensor_tensor(out=ybuf[:, h * 512:(h + 1) * 512], in0=xt[:, h * 512:(h + 1) * 512], scalar=ALPHA,
                                                  in1=mixps[h][:, 0:512], op0=ALU.mult, op1=ALU.add), r=[xt, mixps[h]], w=[ybuf])
    layer_norm(kb, ybuf, g_bc, b_bc, ybuf, tmp["stats"], tmp["mv"], tmp["rstd"])
    if dst_ap is not None:
        kb.dma(q, dst_ap, ybuf[:, :], r=[ybuf], w=[dst_dram])


def ln_tmp(kb):
    return dict(stats=kb.sb("stats", [128, 12], F32), mv=kb.sb("mv", [128, 2], F32), rstd=kb.sb("rstd", [128, 1], F32))


GELU_MODE = "af"


def peer_prepass(kb, c, u_ap, v_ap, UVs, NCH=128):
    with kb.scope():
        st = [kb.sb("pp_st", [128, 1024], F32) for _ in range(4)]
        ub = [kb.sb("pp_ub", [128, 1024], BF16) for _ in range(2)]
        ut = [kb.sb("pp_ut", [128, 1024], BF16) for _ in range(2)]
        vb = [kb.sb("pp_vb", [128, 1024], BF16) for _ in range(2)]
        pst = [kb.ps("pp_ps", [128, 1024], BF16) for _ in range(2)]
        for ch in range(NCH):
            su, sv = st[(2 * ch) % 4], st[(2 * ch + 1) % 4]
            kb.dma("sp", su[:, :], u_ap[ch * 128:(ch + 1) * 128, :], w=[su])
            kb.dma("pool", sv[:, :], v_ap[ch * 128:(ch + 1) * 128, :], w=[sv])
            b = ub[ch % 2]
            kb.A(lambda e, b=b, su=su: e.activation(out=b[:, :], in_=su[:, :], func=AF.Copy), r=[su], w=[b])
            p = pst[ch % 2]
            for k in range(8):
                kb.T(lambda e, k=k, b=b, p=p: e.transpose(out=p[:, k * 128:(k + 1) * 128], in_=b[:, k * 128:(k + 1) * 128], identity=c.identb[:, :]),
                     r=[b, c.identb], w=[p])
            t = ut[ch % 2]
            kb.V(lambda e, t=t, p=p: e.tensor_copy(out=t[:, :], in_=p[:, :]), r=[p], w=[t])
            kb.dma("sp", UVs[ch][:, 0:1024], t[:, :], r=[t], w=[UVs])
            vv = vb[ch % 2]
            kb.G(lambda e, vv=vv, sv=sv: e.tensor_copy(out=vv[:, :], in_=sv[:, :]), r=[sv], w=[vv])
            kb.dma("pool", UVs[ch][:, 1024:2048], vv[:, :], r=[vv], w=[UVs])


def peer_phase(kb, c, T, Y1, OUT, p_ap, W, UVs, WQs, TG=256, NCH=128):
    NT = TG // 128
    NG = T // TG
    TB = 8
    with kb.scope():
        wg = kb.sb("wg", [128, 8, 1024], BF16)
        wp = kb.sb("wp", [128, 2, 1024], BF16)
        skT = kb.sb("skT", [128, 16, 128], BF16)
        with kb.scope():
            stage = [kb.sb("stg", [128, 2048], F32) for _ in range(2)]
            wq0 = kb.sb("wq0", [128, 8, 2048], BF16)
            load_w_bf(kb, wq0, lambda k, c0, cw: wq0[:, k, c0:c0 + cw], W["peer_w_q"], 8, 2048, stage)
            kb.dma("sp", WQs[:, :], wq0[:, :, :].rearrange("p k c -> p (k c)"), r=[wq0], w=[WQs])
            load_w_bf(kb, wg, lambda k, c0, cw: wg[:, k, c0:c0 + cw], W["ple_w_gate"], 8, 1024, stage)
            load_w_bf(kb, wp, lambda k, c0, cw: wp[:, k, c0:c0 + cw], W["ple_w_proj"], 2, 1024, stage)
            pskt = kb.ps("pskt", [128, 512], F32)
            sk = W["peer_sub_keys"].rearrange("h c n d -> (h c) n d")
            for hc in range(16):
                st = stage[hc % 2]
                kb.dma("sp", st[:, 0:128], sk[hc], w=[st])
                kb.T(lambda e, st=st: e.transpose(out=pskt[:, 0:128], in_=st[:, 0:128], identity=c.identf()), r=[st, c.cf], w=[pskt])
                kb.V(lambda e, hc=hc: e.tensor_copy(out=skT[:, hc, :], in_=pskt[:, 0:128]), r=[pskt], w=[skT])
        g_bc = bcast_row(kb, "lnf_g", W["ln_ffn_g"], 1024)
        b_bc = bcast_row(kb, "lnf_b", W["ln_ffn_b"], 1024, q="pool")
        iotaC = kb.sb("iotaC", [128, TB, 128], BF16)
        kb.V(lambda e: e.tensor_copy(out=iotaC[:, :, :], in_=c.iota128().unsqueeze(1).broadcast_to([128, TB, 128])), r=[c.cf], w=[iotaC])
        iota16 = c.iota128()[:, 0:16]

        xt = [kb.sb("xt", [128, 1024], F32) for _ in range(NT)]
        xbf = kb.sb("xbf", [128, 1024], BF16)
        xT = kb.sb("xT", [128, 8, TG], BF16)
        qT = kb.sb("qT", [128, 16, TG], BF16)
        S = kb.sb("S", [128, 16, 128], F32)
        S2 = kb.sb("S2", [128, 256], F32)
        V16 = kb.sb("V16", [128, 16, 16], F32)
        I16u = kb.sb("I16u", [128, 16, 16], U32)
        I16f = kb.sb("I16f", [128, 16, 16], F32)
        cand = kb.sb("cand", [128, 8, 256], F32)
        B16 = kb.sb("B16", [128, 8, 16], F32)
        P16u = kb.sb("P16u", [128, 8, 16], U32)
        ABu = kb.sb("ABu", [128, 2, 128], U32)
        ABf = kb.sb("ABf", [128, 2, 128], F32)
        eq = kb.sb("eq", [128, 8, 16, 16], F32)
        J = kb.sb("J", [128, 3, 128], F32)
        e16 = kb.sb("e16", [128, 8, 16], F32)
        ssum = kb.sb("ssum", [128, 8], F32)
        T3 = kb.sb("T3", [128, 3, TG], F32)
        OH1 = [kb.sb("OH1", [128, TB, 128], BF16) for _ in range(2)]
        OH2 = [kb.sb("OH2", [128, TB, 128], BF16) for _ in range(2)]
        OH2g = [kb.sb("OH2g", [128, TB, 128], BF16) for _ in range(2)]
        GT = kb.sb("GT", [128, TG, 128], BF16)
        wq = Alias(GT, GT[:, 0:128, :].rearrange("p t i -> p (t i)").rearrange("p (k c) -> p k c", c=2048))
        NB = 4
        UVb = [kb.sb("UVb", [128, 2048], BF16) for _ in range(NB)]
        Aact = [kb.sb("Aact", [128, TG], F32) for _ in range(2)]
        Wtb = [kb.sb("Wtb", [128, TG], BF16) for _ in range(3)]
        if GELU_MODE == "comp":
            gsq = [kb.sb("gsq", [128, TG], F32) for _ in range(2)]
            gu = [kb.sb("gu", [128, TG], F32) for _ in range(2)]
            gsg = [kb.sb("gsg", [128, TG], F32) for _ in range(2)]
        ybuf = kb.sb("ybuf", [128, 1024], F32)
        y2bf = kb.sb("y2bf", [128, 1024], BF16)
        y2T = kb.sb("y2T", [128, 8, 128], BF16)
        pt = kb.sb("pt", [128, 256], F32)
        pbf = kb.sb("pbf", [128, 256], BF16)
        pT = kb.sb("pT", [128, 2, 128], BF16)
        Sflat = S[:, :, :].rearrange("p a b -> p (a b)")
        sg = Alias(S, Sflat[:, 0:1024])
        ob = Alias(S, Sflat[:, 1024:2048])
        tmp = ln_tmp(kb)
        acc = [[kb.ps("acc", [128, 512], F32) for _ in range(2)] for _ in range(NT)]
        stp = [kb.ps("stp", [128, 512], F32) for _ in range(2)]
        mps = kb.ps("mps", [128, 512], F32)
        gps = [mps] if NT > 1 else [mps, kb.ps("gps", [128, 512], F32)]
        mpsb = kb.ps("mpsb", [128, 1024], BF16)
        gbanks = list(gps) + [b for pair in acc for b in pair]

        for g in range(NG):
            t0 = g * TG
            kb.dma("pool", wq[:, :, :].rearrange("p k c -> p (k c)"), WQs[:, :], r=[WQs], w=[wq])
            for ti in range(NT):
                kb.dma("sp", xt[ti][:, :], Y1[t0 + ti * 128:t0 + (ti + 1) * 128, :], r=[Y1], w=[xt[ti]])
                kb.A(lambda e, ti=ti: e.activation(out=xbf[:, :], in_=xt[ti][:, :], func=AF.Copy), r=[xt[ti]], w=[xbf])
                transpose_to(kb, c, xbf, 8, mpsb, xT, lambda c0, nn, ti=ti: xT[:, c0:c0 + nn, ti * 128:(ti + 1) * 128])
            for hc in range(16):
                qb = gbanks[hc % len(gbanks)]
                for k in range(8):
                    kb.T(lambda e, hc=hc, k=k, qb=qb: e.matmul(qb[:, 0:TG], lhsT=wq[:, k, hc * 128:(hc + 1) * 128], rhs=xT[:, k, :], start=(k == 0), stop=(k == 7)),
                         r=[wq, xT], w=[qb])
                if hc % 2 == 0:
                    kb.A(lambda e, hc=hc, qb=qb: e.activation(out=qT[:, hc, :], in_=qb[:, 0:TG], func=AF.Copy), r=[qb], w=[qT])
                else:
                    kb.V(lambda e, hc=hc, qb=qb: e.tensor_copy(out=qT[:, hc, :], in_=qb[:, 0:TG]), r=[qb], w=[qT])
            for ti in range(NT):
                tsl = slice(ti * 128, (ti + 1) * 128)
                for h4 in range(4):
                    for j in range(4):
                        hc = h4 * 4 + j
                        kb.T(lambda e, hc=hc, j=j: e.matmul(mps[:, j * 128:(j + 1) * 128], lhsT=qT[:, hc, tsl], rhs=skT[:, hc, :], start=True, stop=True),
                             r=[qT, skT], w=[mps])
                    kb.V(lambda e, h4=h4: e.tensor_copy(out=S[:, h4 * 4:(h4 + 1) * 4, :], in_=mps[:, :].rearrange("p (a b) -> p a b", b=128)), r=[mps], w=[S])
                for hc in range(16):
                    kb.V(lambda e, hc=hc: e.max(out=V16[:, hc, 0:8], in_=S[:, hc, :]), r=[S], w=[V16])
                    kb.V(lambda e, hc=hc: e.max_index(out=I16u[:, hc, 0:8], in_max=V16[:, hc, 0:8], in_values=S[:, hc, :]), r=[S, V16], w=[I16u])
                    kb.V(lambda e, hc=hc: e.match_replace(out=S2[:, 0:128], in_to_replace=V16[:, hc, 0:8], in_values=S[:, hc, :], imm_value=NEG), r=[S, V16], w=[S2])
                    kb.V(lambda e, hc=hc: e.max(out=V16[:, hc, 8:16], in_=S2[:, 0:128]), r=[S2], w=[V16])
                    kb.V(lambda e, hc=hc: e.max_index(out=I16u[:, hc, 8:16], in_max=V16[:, hc, 8:16], in_values=S2[:, 0:128]), r=[S2, V16], w=[I16u])
                kb.V(lambda e: e.tensor_copy(out=I16f[:, :, :], in_=I16u[:, :, :]), r=[I16u], w=[I16f])
                V4 = V16[:, :, :].rearrange("p (h c) k -> p h c k", c=2)
                I4 = I16f[:, :, :].rearrange("p (h c) k -> p h c k", c=2)
                cand4 = cand[:, :, :].rearrange("p h (a b) -> p h a b", b=16)
                kb.V(lambda e: e.tensor_tensor(out=cand4, in0=V4[:, :, 0, :].unsqueeze(3).broadcast_to([128, 8, 16, 16]),
                                               in1=V4[:, :, 1, :].unsqueeze(2).broadcast_to([128, 8, 16, 16]), op=ALU.add), r=[V16], w=[cand])
                for h in range(8):
                    kb.V(lambda e, h=h: e.max(out=B16[:, h, 0:8], in_=cand[:, h, :]), r=[cand], w=[B16])
                    kb.V(lambda e, h=h: e.max_index(out=P16u[:, h, 0:8], in_max=B16[:, h, 0:8], in_values=cand[:, h, :]), r=[cand, B16], w=[P16u])
                    kb.V(lambda e, h=h: e.match_replace(out=S2[:, :], in_to_replace=B16[:, h, 0:8], in_values=cand[:, h, :], imm_value=NEG), r=[cand, B16], w=[S2])
                    kb.V(lambda e, h=h: e.max(out=B16[:, h, 8:16], in_=S2[:, :]), r=[S2], w=[B16])
                    kb.V(lambda e, h=h: e.max_index(out=P16u[:, h, 8:16], in_max=B16[:, h, 8:16], in_values=S2[:, :]), r=[S2, B16], w=[P16u])
                Pfl = P16u[:, :, :].rearrange("p h k -> p (h k)")
                kb.V(lambda e: e.tensor_single_scalar(out=ABu[:, 0, :], in_=Pfl, scalar=4, op=ALU.logical_shift_right), r=[P16u], w=[ABu])
                kb.V(lambda e: e.tensor_single_scalar(out=ABu[:, 1, :], in_=Pfl, scalar=15, op=ALU.bitwise_and), r=[P16u], w=[ABu])
                kb.V(lambda e: e.tensor_copy(out=ABf[:, :, :], in_=ABu[:, :, :]), r=[ABu], w=[ABf])
                for ci in range(2):
                    ab4 = ABf[:, ci, :].rearrange("p (h k) -> p h k", k=16).unsqueeze(3).broadcast_to([128, 8, 16, 16])
                    kb.V(lambda e, ab4=ab4: e.tensor_tensor(out=eq[:, :, :, :], in0=ab4, in1=iota16.unsqueeze(1).unsqueeze(1).broadcast_to([128, 8, 16, 16]), op=ALU.is_equal),
                         r=[ABf, c.cf], w=[eq])
                    kb.V(lambda e, ci=ci: e.tensor_tensor(out=eq[:, :, :, :], in0=eq[:, :, :, :], in1=I4[:, :, ci, :].unsqueeze(2).broadcast_to([128, 8, 16, 16]), op=ALU.mult),
                         r=[eq, I16f], w=[eq])
                    kb.V(lambda e, ci=ci: e.tensor_reduce(out=J[:, ci, :].rearrange("p (h k) -> p h k", k=16), in_=eq[:, :, :, :], axis=AX.X, op=ALU.add), r=[eq], w=[J])
                kb.V(lambda e: e.tensor_tensor(out=e16[:, :, :], in0=B16[:, :, :], in1=B16[:, :, 0:1].broadcast_to([128, 8, 16]), op=ALU.subtract), r=[B16], w=[e16])
                kb.A(lambda e: e.activation(out=e16[:, :, :], in_=e16[:, :, :], func=AF.Exp), r=[e16], w=[e16])
                kb.V(lambda e: e.tensor_reduce(out=ssum[:, :], in_=e16[:, :, :], axis=AX.X, op=ALU.add), r=[e16], w=[ssum])
                kb.V(lambda e: e.reciprocal(out=ssum[:, :], in_=ssum[:, :]), r=[ssum], w=[ssum])
                kb.V(lambda e: e.tensor_tensor(out=J[:, 2, :].rearrange("p (h k) -> p h k", k=16), in0=e16[:, :, :], in1=ssum[:, :].unsqueeze(2).broadcast_to([128, 8, 16]), op=ALU.mult),
                     r=[e16, ssum], w=[J])
                for q3 in range(3):
                    kb.T(lambda e, q3=q3: e.transpose(out=mps[:, q3 * 128:(q3 + 1) * 128], in_=J[:, q3, :], identity=c.identf()), r=[J, c.cf], w=[mps])
                kb.V(lambda e: e.tensor_copy(out=T3[:, :, tsl], in_=mps[:, 0:384].rearrange("p (a b) -> p a b", b=128)), r=[mps], w=[T3])
            nsb = 0
            for s0 in range(0, TG, TB):
                o1, o2, o2g = OH1[nsb % 2], OH2[nsb % 2], OH2g[nsb % 2]
                nsb += 1
                kb.V(lambda e, o1=o1, s0=s0: e.tensor_tensor(out=o1[:, :, :], in0=iotaC[:, :, :], in1=T3[:, 0, s0:s0 + TB].unsqueeze(2).broadcast_to([128, TB, 128]), op=ALU.is_equal),
                     r=[iotaC, T3], w=[o1])
                kb.V(lambda e, o2=o2, s0=s0: e.tensor_tensor(out=o2[:, :, :], in0=iotaC[:, :, :], in1=T3[:, 1, s0:s0 + TB].unsqueeze(2).broadcast_to([128, TB, 128]), op=ALU.is_equal),
                     r=[iotaC, T3], w=[o2])
                kb.G(lambda e, o2=o2, o2g=o2g, s0=s0: e.tensor_tensor(out=o2g[:, :, :], in0=o2[:, :, :], in1=T3[:, 2, s0:s0 + TB].unsqueeze(2).broadcast_to([128, TB, 128]), op=ALU.mult),
                     r=[o2, T3], w=[o2g])
                for tb in range(TB):
                    gp = gbanks[((s0 + tb) // 4) % len(gbanks)]
                    kb.T(lambda e, gp=gp, tb=tb, o1=o1, o2g=o2g: e.matmul(gp[:, (tb % 4) * 128:(tb % 4 + 1) * 128], lhsT=o2g[:, tb, :], rhs=o1[:, tb, :], start=True, stop=True),
                         r=[o1, o2g], w=[gp])
                    if tb % 4 == 3:
                        tt = s0 + tb - 3
                        kb.A(lambda e, gp=gp, tt=tt: e.activation(out=GT[:, tt:tt + 4, :], in_=gp[:, :].rearrange("p (a b) -> p a b", b=128), func=AF.Copy), r=[gp], w=[GT])
            wts = {}

            def emit_u(ch):
                uvb = UVb[ch % NB]
                kb.dma("sp" if ch % 2 == 0 else "pool", uvb[:, :], UVs[ch], r=[UVs], w=[uvb])
                ub = Alias(uvb, uvb[:, 0:1024])
                sp_ = stp[ch % len(stp)]
                so = 0
                for k in range(8):
                    kb.T(lambda e, k=k, ub=ub, sp_=sp_, so=so: e.matmul(sp_[:, so:so + TG], lhsT=ub[:, k * 128:(k + 1) * 128], rhs=xT[:, k, :], start=(k == 0), stop=(k == 7)),
                         r=[ub, xT], w=[sp_])
                a_, w_ = Aact[ch % 2], Wtb[ch % 3]
                sv = sp_[:, so:so + TG]
                kb.A(lambda e, a_=a_, sv=sv: e.activation(out=a_[:, :], in_=sv, func=AF.Gelu_apprx_tanh), r=[sp_], w=[a_])
                kb.V(lambda e, a_=a_, w_=w_, ch=ch: e.tensor_tensor(out=w_[:, :], in0=a_[:, :], in1=GT[:, :, ch], op=ALU.mult), r=[a_, GT], w=[w_])

            def emit_v(ch):
                uvb = UVb[ch % NB]
                vb2 = Alias(uvb, uvb[:, 1024:2048])
                w_ = Wtb[ch % 3]
                for ti in range(NT):
                    for hf in range(2):
                        kb.T(lambda e, ti=ti, hf=hf, w_=w_, vb2=vb2, ch=ch: e.matmul(acc[ti][hf][:, 0:512], lhsT=w_[:, ti * 128:(ti + 1) * 128], rhs=vb2[:, hf * 512:(hf + 1) * 512],
                                                                                  start=(ch == 0), stop=(ch == NCH - 1)), r=[w_, vb2], w=[acc[ti][hf]])

            for ch in range(NCH):
                emit_u(ch)
                if ch >= 1:
                    emit_v(ch - 1)
            emit_v(NCH - 1)
            for ti in range(NT):
                tok0 = t0 + ti * 128
                resid_ln_store(kb, xt[ti], acc[ti], g_bc, b_bc, ybuf, tmp, None, None)
                kb.A(lambda e: e.activation(out=y2bf[:, :], in_=ybuf[:, :], func=AF.Copy), r=[ybuf], w=[y2bf])
                transpose_to(kb, c, y2bf, 8, mpsb, y2T, lambda c0, nn: y2T[:, c0:c0 + nn, :])
                kb.dma("pool", pt[:, :], p_ap[tok0:tok0 + 128, :], w=[pt])
                kb.A(lambda e: e.activation(out=pbf[:, :], in_=pt[:, :], func=AF.Copy), r=[pt], w=[pbf])
                transpose_to(kb, c, pbf, 2, mpsb, pT, lambda c0, nn: pT[:, c0:c0 + nn, :])
                for hf in range(2):
                    hs = slice(hf * 512, (hf + 1) * 512)
                    for k in range(8):
                        kb.T(lambda e, k=k, hs=hs: e.matmul(mps[:, :], lhsT=y2T[:, k, :], rhs=wg[:, k, hs], start=(k == 0), stop=(k == 7)), r=[y2T, wg], w=[mps])
                    kb.A(lambda e, hs=hs: e.activation(out=sg[:, hs], in_=mps[:, :], func=AF.Sigmoid), r=[mps], w=[sg])
                    for k in range(2):
                        kb.T(lambda e, k=k, hs=hs: e.matmul(mps[:, :], lhsT=pT[:, k, :], rhs=wp[:, k, hs], start=(k == 0), stop=(k == 1)), r=[pT, wp], w=[mps])
                    kb.V(lambda e, hs=hs: e.tensor_tensor(out=ob[:, hs], in0=sg[:, hs], in1=mps[:, :], op=ALU.mult), r=[sg, mps], w=[ob])
                kb.G(lambda e: e.tensor_tensor(out=ob[:, :], in0=ob[:, :], in1=ybuf[:, :], op=ALU.add), r=[ob, ybuf], w=[ob])
                kb.dma("sp", OUT[tok0:tok0 + 128, :], ob[:, :], r=[ob], w=[OUT])


def load_cols(kb, c, dst, dst_ap_fn, src_rows_ap, R, stage, pst, nblk=1, blk_stride=0):
    for b in range(nblk):
        kb.dma("sp", stage[0:R, 0:128], src_rows_ap(b), w=[stage])
        kb.T(lambda e: e.transpose(out=pst[:, 0:R], in_=stage[0:R, 0:128], identity=c.cf[0:R, 0:R]), r=[stage, c.cf], w=[pst])
        kb.V(lambda e, b=b: e.tensor_copy(out=dst_ap_fn(b), in_=pst[:, 0:R]), r=[pst], w=[dst])


def conf_phase(kb, c, T, S, XIN, Y1, W):
    GS = 512
    NG = T // GS
    GPS = S // GS
    KW = 31
    with kb.scope():
        w1 = kb.sb("w1", [128, 8, 2048], BF16)
        w2 = kb.sb("w2", [128, 8, 1024], BF16)
        b1 = kb.sb("b1", [128, 16], F32)
        wdw = kb.sb("wdw", [128, 8, KW], F32)
        vecs = kb.sb("vecs", [128, 3, 8], F32)
        with kb.scope():
            stage = [kb.sb("stg", [128, 2048], F32) for _ in range(2)]
            load_w_bf(kb, w1, lambda k, c0, cw: w1[:, k, c0:c0 + cw], W["conv_w_pw1"], 8, 2048, stage)
            load_w_bf(kb, w2, lambda k, c0, cw: w2[:, k, c0:c0 + cw], W["conv_w_pw2"], 8, 1024, stage)
            pst = kb.ps("pst", [128, 512], F32)
            load_cols(kb, c, b1, lambda b: b1[:, :], lambda b: W["conv_b_pw1"].rearrange("(c p) -> c p", p=128), 16, stage[0], pst)
            load_cols(kb, c, wdw, lambda b: wdw[:, b, :], lambda b: W["conv_w_dw"][:, b * 128:(b + 1) * 128], KW, stage[1], pst, nblk=8)
            for i, nm in enumerate(("conv_b_dw", "conv_ln_g", "conv_ln_b")):
                load_cols(kb, c, vecs, lambda b, i=i: vecs[:, i, :], lambda b, nm=nm: W[nm].rearrange("(c p) -> c p", p=128), 8, stage[i % 2], pst)
        g_bc = bcast_row(kb, "lnm_g", W["ln_mix_g"], 1024)
        b_bc = bcast_row(kb, "lnm_b", W["ln_mix_b"], 1024, q="pool")
        xt = [kb.sb("xt", [128, 1024], F32) for _ in range(4)]
        xbf = kb.sb("xbf", [128, 1024], BF16)
        xT = kb.sb("xT", [128, 8, GS], BF16)
        gluH = kb.sb("gluH", [128, 8, KW - 1 + GS], F32)
        hc = kb.sb("hc", [128, 8, GS], F32)
        hsq = kb.sb("hsq", [128, 8, GS], F32)
        zT = kb.sb("zT", [128, 8, GS], BF16)
        sgb = [kb.sb("sgb", [128, GS], F32) for _ in range(2)]
        mean = kb.sb("mean", [128, GS], F32)
        msq = kb.sb("msq", [128, GS], F32)
        rstd = kb.sb("rstd2", [128, GS], F32)
        tn = [kb.sb("tn", [128, GS], F32) for _ in range(2)]
        ybuf = kb.sb("ybuf", [128, 1024], F32)
        tmp = ln_tmp(kb)
        pa = kb.ps("pa", [128, 512], F32)
        pg = kb.ps("pg", [128, 512], F32)
        s1 = kb.ps("s1", [128, 512], F32)
        s2 = kb.ps("s2", [128, 512], F32)
        po = [kb.ps("po", [128, 512], F32) for _ in range(2)]
        ptr = kb.ps("ptr", [128, 1024], BF16)
        H = KW - 1
        for g in range(NG):
            t0 = g * GS
            for ti in range(4):
                kb.dma("sp" if ti % 2 == 0 else "pool", xt[ti][:, :], XIN[t0 + ti * 128:t0 + (ti + 1) * 128, :], r=[XIN], w=[xt[ti]])
                kb.A(lambda e, ti=ti: e.activation(out=xbf[:, :], in_=xt[ti][:, :], func=AF.Copy), r=[xt[ti]], w=[xbf])
                transpose_to(kb, c, xbf, 8, ptr, xT, lambda c0, nn, ti=ti: xT[:, c0:c0 + nn, ti * 128:(ti + 1) * 128])
            if g % GPS == 0:
                kb.G(lambda e: e.memset(gluH[:, :, 0:H], 0.0), w=[gluH])
            for cc in range(8):
                for k in range(8):
                    kb.T(lambda e, k=k, cc=cc: e.matmul(pa[:, :], lhsT=w1[:, k, cc * 128:(cc + 1) * 128], rhs=xT[:, k, :], start=(k == 0), stop=(k == 7)), r=[w1, xT], w=[pa])
                for k in range(8):
                    kb.T(lambda e, k=k, cc=cc: e.matmul(pg[:, :], lhsT=w1[:, k, 1024 + cc * 128:1024 + (cc + 1) * 128], rhs=xT[:, k, :], start=(k == 0), stop=(k == 7)), r=[w1, xT], w=[pg])
                sg_ = sgb[cc % 2]
                kb.A(lambda e, sg_=sg_, cc=cc: e.activation(out=sg_[:, :], in_=pg[:, :], func=AF.Sigmoid, bias=b1[:, 8 + cc:9 + cc]), r=[pg, b1], w=[sg_])
                kb.V(lambda e, sg_=sg_, cc=cc: e.scalar_tensor_tensor(out=gluH[:, cc, H:H + GS], in0=pa[:, :], scalar=b1[:, cc:cc + 1], in1=sg_[:, :], op0=ALU.add, op1=ALU.mult),
                     r=[pa, b1, sg_], w=[gluH])
                kb.V(lambda e, cc=cc: e.tensor_scalar(out=hc[:, cc, :], in0=gluH[:, cc, H:H + GS], scalar1=wdw[:, cc, H:H + 1], scalar2=vecs[:, 0, cc:cc + 1], op0=ALU.mult, op1=ALU.add),
                     r=[gluH, wdw, vecs], w=[hc])
                for k in range(H):
                    kb.V(lambda e, cc=cc, k=k: e.scalar_tensor_tensor(out=hc[:, cc, :], in0=gluH[:, cc, k:k + GS], scalar=wdw[:, cc, k:k + 1], in1=hc[:, cc, :], op0=ALU.mult, op1=ALU.add),
                         r=[gluH, wdw, hc], w=[hc])
            kb.G(lambda e: e.tensor_copy(out=gluH[:, :, 0:H], in_=gluH[:, :, GS:GS + H]), r=[gluH], w=[gluH])
            kb.A(lambda e: e.activation(out=hsq[:, :, :], in_=hc[:, :, :], func=AF.Square), r=[hc], w=[hsq])
            for cc in range(8):
                kb.T(lambda e, cc=cc: e.matmul(s1[:, :], lhsT=c.ones(), rhs=hc[:, cc, :], start=(cc == 0), stop=(cc == 7)), r=[c.cf, hc], w=[s1])
            for cc in range(8):
                kb.T(lambda e, cc=cc: e.matmul(s2[:, :], lhsT=c.ones(), rhs=hsq[:, cc, :], start=(cc == 0), stop=(cc == 7)), r=[c.cf, hsq], w=[s2])
            kb.V(lambda e: e.tensor_scalar(out=mean[:, :], in0=s1[:, :], scalar1=1.0 / 1024, scalar2=None, op0=ALU.mult), r=[s1], w=[mean])
            kb.V(lambda e: e.tensor_tensor(out=msq[:, :], in0=mean[:, :], in1=mean[:, :], op=ALU.mult), r=[mean], w=[msq])
            kb.V(lambda e: e.scalar_tensor_tensor(out=msq[:, :], in0=s2[:, :], scalar=1.0 / 1024, in1=msq[:, :], op0=ALU.mult, op1=ALU.subtract), r=[s2, msq], w=[msq])
            kb.A(lambda e: e.activation(out=rstd[:, :], in_=msq[:, :], func=AF.Ln, bias=CONST.eps[:, 0:1]), r=[msq, CONST.eps], w=[rstd])
            kb.A(lambda e: e.activation(out=rstd[:, :], in_=rstd[:, :], func=AF.Exp, scale=-0.5), r=[rstd], w=[rstd])
            for cc in range(8):
                t_ = tn[cc % 2]
                kb.G(lambda e, cc=cc, t_=t_: e.tensor_tensor(out=t_[:, :], in0=hc[:, cc, :], in1=mean[:, :], op=ALU.subtract), r=[hc, mean], w=[t_])
                kb.V(lambda e, t_=t_: e.tensor_tensor(out=t_[:, :], in0=t_[:, :], in1=rstd[:, :], op=ALU.mult), r=[t_, rstd], w=[t_])
                kb.V(lambda e, cc=cc, t_=t_: e.tensor_scalar(out=t_[:, :], in0=t_[:, :], scalar1=vecs[:, 1, cc:cc + 1], scalar2=vecs[:, 2, cc:cc + 1], op0=ALU.mult, op1=ALU.add), r=[t_, vecs], w=[t_])
                kb.A(lambda e, cc=cc, t_=t_: e.activation(out=zT[:, cc, :], in_=t_[:, :], func=AF.Silu), r=[t_], w=[zT])
            for ti in range(4):
                for hf in range(2):
                    for cc in range(8):
                        kb.T(lambda e, ti=ti, hf=hf, cc=cc: e.matmul(po[hf][:, :], lhsT=zT[:, cc, ti * 128:(ti + 1) * 128], rhs=w2[:, cc, hf * 512:(hf + 1) * 512], start=(cc == 0), stop=(cc == 7)),
                             r=[zT, w2], w=[po[hf]])
                resid_ln_store(kb, xt[ti], po, g_bc, b_bc, ybuf, tmp, Y1, Y1[t0 + ti * 128:t0 + (ti + 1) * 128, :], q="sp" if ti % 2 == 0 else "pool")


C2W = NRELW + 72


def host_c2():
    c2 = np.zeros((128, C2W), np.float32)
    c2[:, 0:NRELW] = (np.arange(NRELW) - 2304)[None, :]
    for own in range(9):
        c2[:, NRELW + own * 8:NRELW + own * 8 + 8] = np.where(np.arange(8) < own, 0.0, NEG)[None, :]
    return c2


def moba_phase(kb, c, T, S, XIN, Y1, W, c2dram):
    NSEQ = T // S
    NQ = S // 128
    NBLK = S // 256
    GS = 512
    with kb.scope():
        qT = kb.sb("qT_all", [128, 8, S], BF16)
        kT = kb.sb("kT_all", [128, 8, S], BF16)
        va = kb.sb("v_all", [128, NQ, 1024], BF16)
        kmf = kb.sb("kmf", [128, 8, 8], F32)
        kmT = kb.sb("kmT", [128, 8, 8], BF16)
        for sq in range(NSEQ):
            base = sq * S
            with kb.scope():
                wqkv = kb.sb("wqkv", [128, 8, 3072], BF16)
                stage = [kb.sb("stg", [128, 1024], F32) for _ in range(2)]
                load_w_bf(kb, wqkv, lambda k, c0, cw: wqkv[:, k, c0:c0 + cw], W["moba_w_qkv"], 8, 3072, stage)
                xt = [kb.sb("xt", [128, 1024], F32) for _ in range(2)]
                xbf = kb.sb("xbf", [128, 1024], BF16)
                xT = kb.sb("xT", [128, 8, GS], BF16)
                pp = [kb.ps("pp", [128, 512], F32) for _ in range(2)]
                ptr = kb.ps("ptr", [128, 1024], BF16)
                n = 0
                for g in range(S // GS):
                    t0 = base + g * GS
                    for ti in range(4):
                        x_ = xt[ti % 2]
                        kb.dma("sp" if ti % 2 == 0 else "pool", x_[:, :], XIN[t0 + ti * 128:t0 + (ti + 1) * 128, :], r=[XIN], w=[x_])
                        kb.A(lambda e, x_=x_: e.activation(out=xbf[:, :], in_=x_[:, :], func=AF.Copy), r=[x_], w=[xbf])
                        transpose_to(kb, c, xbf, 8, ptr, xT, lambda c0, nn, ti=ti: xT[:, c0:c0 + nn, ti * 128:(ti + 1) * 128])
                    gsl = slice(g * GS, (g + 1) * GS)
                    for pr in range(16):
                        p_ = pp[n % 2]
                        n += 1
                        for k in range(8):
                            kb.T(lambda e, k=k, pr=pr, p_=p_: e.matmul(p_[:, :], lhsT=wqkv[:, k, pr * 128:(pr + 1) * 128], rhs=xT[:, k, :], start=(k == 0), stop=(k == 7)), r=[wqkv, xT], w=[p_])
                        if pr < 8:
                            kb.A(lambda e, pr=pr, p_=p_: e.activation(out=qT[:, pr, gsl], in_=p_[:, :], func=AF.Copy, scale=0.125), r=[p_], w=[qT])
                        else:
                            kb.V(lambda e, pr=pr, p_=p_: e.tensor_copy(out=kT[:, pr - 8, gsl], in_=p_[:, :]), r=[p_], w=[kT])
                    for ti in range(4):
                        for hf in range(2):
                            p_ = pp[n % 2]
                            n += 1
                            for k in range(8):
                                kb.T(lambda e, k=k, ti=ti, hf=hf, p_=p_: e.matmul(p_[:, :], lhsT=xT[:, k, ti * 128:(ti + 1) * 128], rhs=wqkv[:, k, 2048 + hf * 512:2048 + (hf + 1) * 512], start=(k == 0), stop=(k == 7)),
                                     r=[wqkv, xT], w=[p_])
                            if hf == 0:
                                kb.A(lambda e, ti=ti, g=g, p_=p_: e.activation(out=va[:, g * 4 + ti, 0:512], in_=p_[:, :], func=AF.Copy), r=[p_], w=[va])
                            else:
                                kb.V(lambda e, ti=ti, g=g, p_=p_: e.tensor_copy(out=va[:, g * 4 + ti, 512:1024], in_=p_[:, :]), r=[p_], w=[va])
                kb.V(lambda e: e.tensor_reduce(out=kmf[:, :, 0:NBLK], in_=kT[:, :, :].rearrange("p a (b j) -> p a b j", j=256), axis=AX.X, op=ALU.add), r=[kT], w=[kmf])
                kb.A(lambda e: e.activation(out=kmT[:, :, 0:NBLK], in_=kmf[:, :, 0:NBLK], func=AF.Copy, scale=1.0 / 256), r=[kmf], w=[kmT])
            with kb.scope():
                wo = kb.sb("wo", [128, 8, 1024], BF16)
                with kb.scope():
                    stage = [kb.sb("stg", [128, 1024], F32) for _ in range(2)]
                    load_w_bf(kb, wo, lambda k, c0, cw: wo[:, k, c0:c0 + cw], W["moba_w_out"], 8, 1024, stage)
                c2 = kb.sb("c2", [128, C2W], F32)
                kb.dma("sp", c2[:, :], c2dram[:, :], r=[c2dram], w=[c2])
                g_bc = bcast_row(kb, "lnm_g", W["ln_mix_g"], 1024)
                b_bc = bcast_row(kb, "lnm_b", W["ln_mix_b"], 1024, q="pool")
                xt = kb.sb("xt", [128, 1024], F32)
                L = kb.sb("L", [128, S], F32)
                Pb = kb.sb("Pb", [128, S], BF16)
                PT = kb.sb("PT", [128, NQ, 128], BF16)
                gm = kb.sb("gm", [128, 16, 8], F32)
                m8 = kb.sb("m8", [128, 16, 8], F32)
                selb = kb.sb("selb", [128, 16, 8], F32)
                rmax = kb.sb("rmax", [128, 1], F32)
                rsum = kb.sb("rsum", [128, 1], F32)
                attn = kb.sb("attn", [128, 1024], BF16)
                attnT = kb.sb("attnT", [128, 8, 128], BF16)
                ybuf = kb.sb("ybuf", [128, 1024], F32)
                tmp = ln_tmp(kb)
                pl = [kb.ps("pl", [128, 512], F32) for _ in range(4)]
                ptp = kb.ps("ptp", [128, 1024], BF16)
                pv = kb.ps("pv", [128, 512], F32)
                po = [kb.ps("po", [128, 512], F32) for _ in range(2)]
                for qi in range(NQ):
                    q0 = qi * 128
                    own = qi // 2
                    nk = q0 + 128
                    qs = slice(q0, q0 + 128)
                    kb.dma("pool", xt[:, :], XIN[base + q0:base + q0 + 128, :], r=[XIN], w=[xt])
                    gated = own >= 4
                    if gated:
                        pgt = po[1]
                        for h in range(16):
                            pr, r0 = h // 2, (h % 2) * 64
                            kb.T(lambda e, h=h, pr=pr, r0=r0: e.matmul(pgt[:, h * 8:(h + 1) * 8], lhsT=qT[r0:r0 + 64, pr, qs], rhs=kmT[r0:r0 + 64, pr, 0:8], start=True, stop=True), r=[qT, kmT], w=[pgt])
                        kb.V(lambda e: e.tensor_tensor(out=gm[:, :, :], in0=pgt[:, 0:128].rearrange("p (h n) -> p h n", n=8),
                                                       in1=c2[:, NRELW + own * 8:NRELW + own * 8 + 8].unsqueeze(1).broadcast_to([128, 16, 8]), op=ALU.add), r=[pgt, c2], w=[gm])
                        for h in range(16):
                            kb.V(lambda e, h=h: e.max(out=m8[:, h, :], in_=gm[:, h, :]), r=[gm], w=[m8])
                        kb.V(lambda e: e.tensor_tensor(out=selb[:, :, :], in0=gm[:, :, :], in1=m8[:, :, 2:3].broadcast_to([128, 16, 8]), op=ALU.is_ge), r=[gm, m8], w=[selb])
                        kb.V(lambda e: e.tensor_scalar(out=selb[:, :, :], in0=selb[:, :, :], scalar1=1.0, scalar2=1.0e30, op0=ALU.subtract, op1=ALU.mult), r=[selb], w=[selb])
                    for h in range(16):
                        pr, r0 = h // 2, (h % 2) * 64
                        slope = 2.0 ** (-(h + 1) / 2.0)
                        off = 2177 - q0
                        for j in range((nk + 511) // 512):
                            c0, c1 = j * 512, min(nk, (j + 1) * 512)
                            kb.T(lambda e, j=j, c0=c0, c1=c1, pr=pr, r0=r0: e.matmul(pl[j][:, 0:c1 - c0], lhsT=qT[r0:r0 + 64, pr, qs], rhs=kT[r0:r0 + 64, pr, c0:c1], start=True, stop=True), r=[qT, kT], w=[pl[j]])
                            kb.V(lambda e, j=j, c0=c0, c1=c1: e.scalar_tensor_tensor(out=L[:, c0:c1], in0=c2[:, off + c0:off + c1], scalar=slope, in1=pl[j][:, 0:c1 - c0], op0=ALU.mult, op1=ALU.add),
                                 r=[c2, pl[j]], w=[L])
                        if gated:
                            kb.V(lambda e, h=h: e.tensor_tensor(out=L[:, 0:own * 256].rearrange("p (b j) -> p b j", j=256), in0=L[:, 0:own * 256].rearrange("p (b j) -> p b j", j=256),
                                                                in1=selb[:, h, 0:own].unsqueeze(2).broadcast_to([128, own, 256]), op=ALU.add), r=[L, selb], w=[L])
                        kb.V(lambda e: e.tensor_tensor(out=L[:, nk - 128:nk], in0=L[:, nk - 128:nk], in1=c.tri_q(), op=ALU.add), r=[L, c.cf], w=[L])
                        kb.V(lambda e: e.tensor_reduce(out=rmax[:, :], in_=L[:, 0:nk], axis=AX.X, op=ALU.max), r=[L], w=[rmax])
                        kb.V(lambda e: e.tensor_scalar(out=rmax[:, :], in0=rmax[:, :], scalar1=-1.0, scalar2=None, op0=ALU.mult), r=[rmax], w=[rmax])
                        kb.V(lambda e: e.memset(rsum[:, :], 0.0), w=[rsum])
                        kb.A(lambda e: e.activation(out=Pb[:, 0:nk], in_=L[:, 0:nk], func=AF.Exp, bias=rmax[:, 0:1], accum_out=rsum[:, 0:1]), r=[L, rmax, rsum], w=[Pb, rsum])
                        nj = nk // 128
                        for j in range(nj):
                            kb.T(lambda e, j=j: e.transpose(out=ptp[:, (j % 8) * 128:(j % 8 + 1) * 128], in_=Pb[:, j * 128:(j + 1) * 128], identity=c.identb[:, :]), r=[Pb, c.identb], w=[ptp])
                            if j % 8 == 7 or j == nj - 1:
                                j0 = (j // 8) * 8
                                nn = j - j0 + 1
                                o = PT[:, j0:j0 + nn, :]
                                i_ = ptp[:, 0:nn * 128].rearrange("p (a b) -> p a b", b=128)
                                if (j // 8) % 2 == 0:
                                    kb.V(lambda e, o=o, i_=i_: e.tensor_copy(out=o, in_=i_), r=[ptp], w=[PT])
                                else:
                                    kb.A(lambda e, o=o, i_=i_: e.activation(out=o, in_=i_, func=AF.Copy), r=[ptp], w=[PT])
                        for j in range(nj):
                            kb.T(lambda e, j=j, h=h: e.matmul(pv[:, 0:64], lhsT=PT[:, j, :], rhs=va[:, j, h * 64:(h + 1) * 64], start=(j == 0), stop=(j == nj - 1)), r=[PT, va], w=[pv])
                        kb.V(lambda e: e.reciprocal(out=rsum[:, :], in_=rsum[:, :]), r=[rsum], w=[rsum])
                        kb.V(lambda e, h=h: e.tensor_scalar(out=attn[:, h * 64:(h + 1) * 64], in0=pv[:, 0:64], scalar1=rsum[:, 0:1], scalar2=None, op0=ALU.mult), r=[pv, rsum], w=[attn])
                    transpose_to(kb, c, attn, 8, ptp, attnT, lambda c0, nn: attnT[:, c0:c0 + nn, :])
                    for hf in range(2):
                        for k in range(8):
                            kb.T(lambda e, k=k, hf=hf: e.matmul(po[hf][:, :], lhsT=attnT[:, k, :], rhs=wo[:, k, hf * 512:(hf + 1) * 512], start=(k == 0), stop=(k == 7)), r=[attnT, wo], w=[po[hf]])
                    resid_ln_store(kb, xt, po, g_bc, b_bc, ybuf, tmp, Y1, Y1[base + q0:base + q0 + 128, :])


def ssd_phase_a(kb, c, T, S, XIN, W, XS, BTM, BCT, ZS, DT):
    GS = 512
    NG = T // GS
    GPS = S // GS
    with kb.scope():
        win = kb.sb("win", [128, 8, 5152], BF16)
        cw = kb.sb("cw", [128, 24, 4], F32)
        cb = kb.sb("cb", [128, 24], F32)
        with kb.scope():
            stage = [kb.sb("stg", [128, 2048], F32) for _ in range(2)]
            load_w_bf(kb, win, lambda k, c0, cw_: win[:, k, c0:c0 + cw_], W["ssd_w_in"], 8, 5152, stage)
            pst = kb.ps("pst", [128, 512], F32)
            load_cols(kb, c, cw, lambda b: cw[:, b, :], lambda b: W["ssd_conv_w"][:, b * 128:(b + 1) * 128], 4, stage[0], pst, nblk=24)
            load_cols(kb, c, cb, lambda b: cb[:, :], lambda b: W["ssd_conv_b"].rearrange("(c p) -> c p", p=128), 24, stage[1], pst)
        dtb = bcast_row(kb, "dtb", W["ssd_dt_bias"], 32)
        one1 = kb.sb("one1", [128, 1], F32)
        kb.V(lambda e: e.memset(one1[:, :], 1.0), w=[one1])
        xt = [kb.sb("xt", [128, 1024], F32) for _ in range(2)]
        xbf = kb.sb("xbf", [128, 1024], BF16)
        xT = kb.sb("xT", [128, 8, GS], BF16)
        rawH = [kb.sb("rawH", [128, 3 + GS], F32) for _ in range(2)]
        hal = kb.sb("hal", [128, 24, 3], F32)
        cacc = [kb.sb("cacc", [128, GS], F32) for _ in range(2)]
        xbcT = kb.sb("xbcT", [128, 24, GS], BF16)
        xs_sb = [kb.sb("xs_sb", [128, 2048], BF16) for _ in range(2)]
        b_sb = [kb.sb("b_sb", [128, 512], BF16) for _ in range(2)]
        zs_sb = [kb.sb("zs_sb", [128, 2048], BF16) for _ in range(2)]
        dtr = kb.sb("dtr", [128, 32], F32)
        dab = kb.sb("dab", [128, 32], F32)
        dmx = kb.sb("dmx", [128, 32], F32)
        dt_sb = [kb.sb("dt_sb", [128, 32], F32) for _ in range(2)]
        pa = [kb.ps("pa", [128, 512], F32) for _ in range(2)]
        pz = [kb.ps("pz", [128, 512], F32) for _ in range(2)]
        pd = kb.ps("pd", [128, 512], F32)
        ptr = [kb.ps("ptr", [128, 1024], BF16) for _ in range(2)]
        n = 0
        for g in range(NG):
            t0 = g * GS
            for ti in range(4):
                x_ = xt[ti % 2]
                kb.dma("sp" if ti % 2 == 0 else "pool", x_[:, :], XIN[t0 + ti * 128:t0 + (ti + 1) * 128, :], r=[XIN], w=[x_])
                kb.A(lambda e, x_=x_: e.activation(out=xbf[:, :], in_=x_[:, :], func=AF.Copy), r=[x_], w=[xbf])
                transpose_to(kb, c, xbf, 8, ptr[0], xT, lambda c0, nn, ti=ti: xT[:, c0:c0 + nn, ti * 128:(ti + 1) * 128])
            if g % GPS == 0:
                kb.G(lambda e: e.memset(hal[:, :, :], 0.0), w=[hal])
            for fc in range(24):
                p_, rh, ac = pa[fc % 2], rawH[fc % 2], cacc[fc % 2]
                col0 = 2048 + fc * 128
                for k in range(8):
                    kb.T(lambda e, k=k, col0=col0, p_=p_: e.matmul(p_[:, :], lhsT=win[:, k, col0:col0 + 128], rhs=xT[:, k, :], start=(k == 0), stop=(k == 7)), r=[win, xT], w=[p_])
                kb.A(lambda e, p_=p_, rh=rh: e.activation(out=rh[:, 3:3 + GS], in_=p_[:, :], func=AF.Copy), r=[p_], w=[rh])
                kb.G(lambda e, rh=rh, fc=fc: e.tensor_copy(out=rh[:, 0:3], in_=hal[:, fc, :]), r=[hal], w=[rh])
                kb.V(lambda e, rh=rh, ac=ac, fc=fc: e.tensor_scalar(out=ac[:, :], in0=rh[:, 3:3 + GS], scalar1=cw[:, fc, 3:4], scalar2=cb[:, fc:fc + 1], op0=ALU.mult, op1=ALU.add), r=[rh, cw, cb], w=[ac])
                for k in range(3):
                    kb.V(lambda e, rh=rh, ac=ac, fc=fc, k=k: e.scalar_tensor_tensor(out=ac[:, :], in0=rh[:, k:k + GS], scalar=cw[:, fc, k:k + 1], in1=ac[:, :], op0=ALU.mult, op1=ALU.add), r=[rh, cw, ac], w=[ac])
                kb.G(lambda e, rh=rh, fc=fc: e.tensor_copy(out=hal[:, fc, :], in_=rh[:, GS:GS + 3]), r=[rh], w=[hal])
                kb.A(lambda e, ac=ac, fc=fc: e.activation(out=xbcT[:, fc, :], in_=ac[:, :], func=AF.Silu), r=[ac], w=[xbcT])
            for j in range(8):
                kb.dma("sp" if j % 2 == 0 else "pool", BCT[j][:, t0:t0 + GS], xbcT[:, 16 + j, :], r=[xbcT], w=[BCT])
            for ti in range(4):
                tsl = slice(ti * 128, (ti + 1) * 128)
                rows = slice(t0 + ti * 128, t0 + (ti + 1) * 128)
                xs_, b_, zs_, dt_ = xs_sb[ti % 2], b_sb[ti % 2], zs_sb[ti % 2], dt_sb[ti % 2]
                for half in range(2):
                    pt_ = ptr[half]
                    for j in range(8):
                        kb.T(lambda e, j=j, half=half, pt_=pt_: e.transpose(out=pt_[:, j * 128:(j + 1) * 128], in_=xbcT[:, half * 8 + j, tsl], identity=c.identb[:, :]), r=[xbcT, c.identb], w=[pt_])
                    if half == 0:
                        kb.V(lambda e, pt_=pt_, xs_=xs_: e.tensor_copy(out=xs_[:, 0:1024], in_=pt_[:, :]), r=[pt_], w=[xs_])
                    else:
                        kb.A(lambda e, pt_=pt_, xs_=xs_: e.activation(out=xs_[:, 1024:2048], in_=pt_[:, :], func=AF.Copy), r=[pt_], w=[xs_])
                kb.dma("sp", XS[rows, :], xs_[:, :], r=[xs_], w=[XS])
                for j in range(4):
                    kb.T(lambda e, j=j: e.transpose(out=ptr[0][:, j * 128:(j + 1) * 128], in_=xbcT[:, 16 + j, tsl], identity=c.identb[:, :]), r=[xbcT, c.identb], w=[ptr[0]])
                kb.V(lambda e, b_=b_: e.tensor_copy(out=b_[:, :], in_=ptr[0][:, 0:512]), r=[ptr[0]], w=[b_])
                kb.dma("pool", BTM[rows, :], b_[:, :], r=[b_], w=[BTM])
                for sl in range(4):
                    p_ = pz[n % 2]
                    n += 1
                    for k in range(8):
                        kb.T(lambda e, k=k, sl=sl, p_=p_: e.matmul(p_[:, :], lhsT=xT[:, k, tsl], rhs=win[:, k, sl * 512:(sl + 1) * 512], start=(k == 0), stop=(k == 7)), r=[xT, win], w=[p_])
                    kb.A(lambda e, sl=sl, p_=p_, zs_=zs_: e.activation(out=zs_[:, sl * 512:(sl + 1) * 512], in_=p_[:, :], func=AF.Silu), r=[p_], w=[zs_])
                kb.dma("sp", ZS[rows, :], zs_[:, :], r=[zs_], w=[ZS])
                for k in range(8):
                    kb.T(lambda e, k=k: e.matmul(pd[:, 0:32], lhsT=xT[:, k, tsl], rhs=win[:, k, 5120:5152], start=(k == 0), stop=(k == 7)), r=[xT, win], w=[pd])
                kb.V(lambda e: e.tensor_tensor(out=dtr[:, :], in0=pd[:, 0:32], in1=dtb[:, :], op=ALU.add), r=[pd, dtb], w=[dtr])
                kb.A(lambda e: e.activation(out=dab[:, :], in_=dtr[:, :], func=AF.Abs), r=[dtr], w=[dab])
                kb.A(lambda e: e.activation(out=dab[:, :], in_=dab[:, :], func=AF.Exp, scale=-1.0), r=[dab], w=[dab])
                kb.A(lambda e: e.activation(out=dab[:, :], in_=dab[:, :], func=AF.Ln, bias=one1[:, 0:1]), r=[dab, one1], w=[dab])
                kb.V(lambda e: e.tensor_single_scalar(out=dmx[:, :], in_=dtr[:, :], scalar=0.0, op=ALU.max), r=[dtr], w=[dmx])
                kb.V(lambda e, dt_=dt_: e.tensor_tensor(out=dt_[:, :], in0=dmx[:, :], in1=dab[:, :], op=ALU.add), r=[dmx, dab], w=[dt_])
                kb.dma("pool", DT[rows, :], dt_[:, :], r=[dt_], w=[DT])


def ssd_phase_b(kb, c, T, S, XIN, Y1, W, XS, BTM, BCT, ZS, DT):
    NC_ = T // 128
    CPS = S // 128
    with kb.scope():
        wout = kb.sb("wout", [128, 16, 1024], BF16)
        with kb.scope():
            stage = [kb.sb("stg", [128, 1024], F32) for _ in range(2)]
            load_w_bf(kb, wout, lambda k, c0, cw_: wout[:, k, c0:c0 + cw_], W["ssd_w_out"], 16, 1024, stage)
        g_bc = bcast_row(kb, "lnm_g", W["ln_mix_g"], 1024)
        b_bc = bcast_row(kb, "lnm_b", W["ln_mix_b"], 1024, q="pool")
        ng_bc = bcast_row(kb, "ng_bc", W["ssd_norm_g"], 2048)
        aneg = bcast_row(kb, "aneg", W["ssd_a_log"], 32, q="pool")
        kb.A(lambda e: e.activation(out=aneg[:, :], in_=aneg[:, :], func=AF.Exp), r=[aneg], w=[aneg])
        kb.V(lambda e: e.tensor_scalar(out=aneg[:, :], in0=aneg[:, :], scalar1=-1.0, scalar2=None, op0=ALU.mult), r=[aneg], w=[aneg])
        dsk = bcast_row(kb, "dsk", W["ssd_d"], 32)
        xt = kb.sb("xt", [128, 1024], F32)
        xs = kb.sb("xs", [128, 2048], BF16)
        bt = kb.sb("bt", [128, 512], BF16)
        zs = kb.sb("zs", [128, 2048], BF16)
        dt = kb.sb("dt", [128, 32], F32)
        bct = kb.sb("bct", [128, 8, 128], BF16)
        dtA = kb.sb("dtA", [128, 32], F32)
        acs = kb.sb("acs", [128, 64], F32)
        ea = kb.sb("ea", [128, 32], F32)
        dte = kb.sb("dte", [128, 32], F32)
        cd = kb.sb("cd", [128, 32], F32)
        xdt = kb.sb("xdt", [128, 2048], BF16)
        xe = kb.sb("xe", [128, 2048], BF16)
        Mh = kb.sb("Mh", [128, 32, 128], F32)
        cbt = kb.sb("cbt", [128, 4, 128], F32)
        Dm = [kb.sb("Dm", [128, 4, 128], F32) for _ in range(2)]
        Wd = kb.sb("Wd", [128, 32, 128], BF16)
        yoff = kb.sb("yoff", [128, 2048], F32)
        y = kb.sb("y", [128, 2048], F32)
        t2 = kb.sb("t2", [128, 2048], F32)
        ss = kb.sb("ss", [128, 4], F32)
        gnb = kb.sb("gnb", [128, 2048], BF16)
        gnT = kb.sb("gnT", [128, 16, 128], BF16)
        H = kb.sb("H", [128, 2048], F32)
        Hbf = kb.sb("Hbf", [128, 2048], BF16)
        ybuf = kb.sb("ybuf", [128, 1024], F32)
        tmp = ln_tmp(kb)
        py = [kb.ps("py", [128, 512], F32) for _ in range(4)]
        pd = [kb.ps("pd", [128, 512], F32) for _ in range(2)]
        pm = kb.ps("pm", [128, 512], F32)
        ptr = kb.ps("ptr", [128, 1024], BF16)
        v3 = lambda ap: ap.rearrange("p (h d) -> p h d", d=64)
        for ci in range(NC_):
            rows = slice(ci * 128, (ci + 1) * 128)
            kb.dma("sp", xs[:, :], XS[rows, :], r=[XS], w=[xs])
            kb.dma("pool", zs[:, :], ZS[rows, :], r=[ZS], w=[zs])
            kb.dma("sp", bt[:, :], BTM[rows, :], r=[BTM], w=[bt])
            kb.dma("pool", dt[:, :], DT[rows, :], r=[DT], w=[dt])
            kb.dma("sp", bct[:, :, :], BCT.t.rearrange("j p t -> p j t")[:, :, rows], r=[BCT], w=[bct])
            kb.dma("pool", xt[:, :], XIN[rows, :], r=[XIN], w=[xt])
            if ci % CPS == 0:
                kb.G(lambda e: e.memset(H[:, :], 0.0), w=[H])
                kb.G(lambda e: e.memset(Hbf[:, :], 0.0), w=[Hbf])
            kb.V(lambda e: e.tensor_tensor(out=dtA[:, :], in0=dt[:, :], in1=aneg[:, :], op=ALU.mult), r=[dt, aneg], w=[dtA])
            kb.T(lambda e: e.matmul(pm[:, 0:32], lhsT=c.triu(), rhs=dtA[:, :], start=True, stop=True), r=[c.cf, dtA], w=[pm])
            kb.T(lambda e: e.matmul(pm[:, 32:64], lhsT=c.ones(), rhs=dtA[:, :], start=True, stop=True), r=[c.cf, dtA], w=[pm])
            kb.V(lambda e: e.tensor_copy(out=acs[:, :], in_=pm[:, 0:64]), r=[pm], w=[acs])
            kb.A(lambda e: e.activation(out=ea[:, :], in_=acs[:, 0:32], func=AF.Exp), r=[acs], w=[ea])
            kb.V(lambda e: e.tensor_tensor(out=dte[:, :], in0=acs[:, 32:64], in1=acs[:, 0:32], op=ALU.subtract), r=[acs], w=[dte])
            kb.A(lambda e: e.activation(out=dte[:, :], in_=dte[:, :], func=AF.Exp), r=[dte], w=[dte])
            kb.V(lambda e: e.tensor_tensor(out=dte[:, :], in0=dte[:, :], in1=dt[:, :], op=ALU.mult), r=[dte, dt], w=[dte])
            kb.A(lambda e: e.activation(out=cd[:, :], in_=acs[:, 32:64], func=AF.Exp), r=[acs], w=[cd])
            kb.V(lambda e: e.tensor_tensor(out=v3(xdt[:, :]), in0=v3(xs[:, :]), in1=dt[:, :].unsqueeze(2).broadcast_to([128, 32, 64]), op=ALU.mult), r=[xs, dt], w=[xdt])
            kb.G(lambda e: e.tensor_tensor(out=v3(xe[:, :]), in0=v3(xs[:, :]), in1=dte[:, :].unsqueeze(2).broadcast_to([128, 32, 64]), op=ALU.mult), r=[xs, dte], w=[xe])
            kb.V(lambda e: e.tensor_tensor(out=Mh[:, :, :], in0=c.triu().unsqueeze(1).broadcast_to([128, 32, 128]), in1=dtA[:, :].unsqueeze(2).broadcast_to([128, 32, 128]), op=ALU.mult), r=[c.cf, dtA], w=[Mh])
            for g in range(4):
                kb.T(lambda e, g=g: e.matmul(pm[:, g * 128:(g + 1) * 128], lhsT=bct[:, g, :], rhs=bct[:, 4 + g, :], start=True, stop=True), r=[bct], w=[pm])
            kb.V(lambda e: e.tensor_copy(out=cbt[:, :, :], in_=pm[:, :].rearrange("p (a b) -> p a b", b=128)), r=[pm], w=[cbt])
            for g in range(4):
                gs = slice(g * 512, (g + 1) * 512)
                kb.T(lambda e, g=g, gs=gs: e.matmul(py[g][:, :], lhsT=bct[:, 4 + g, :], rhs=Hbf[:, gs], start=True, stop=True), r=[bct, Hbf], w=[py[g]])
                kb.V(lambda e, g=g, gs=gs: e.tensor_tensor(out=v3(yoff[:, gs]), in0=v3(py[g][:, :]), in1=ea[:, g * 8:(g + 1) * 8].unsqueeze(2).broadcast_to([128, 8, 64]), op=ALU.mult), r=[py[g], ea], w=[yoff])
            for hb in range(8):
                p_, d_ = pd[hb % 2], Dm[hb % 2]
                for i in range(4):
                    h = hb * 4 + i
                    kb.T(lambda e, i=i, h=h, p_=p_: e.matmul(p_[:, i * 128:(i + 1) * 128], lhsT=c.ones(), rhs=Mh[:, h, :], start=True, stop=False), r=[c.cf, Mh], w=[p_])
                    kb.T(lambda e, i=i, h=h, p_=p_: e.matmul(p_[:, i * 128:(i + 1) * 128], lhsT=Mh[:, h, :], rhs=c.negones(), start=False, stop=True), r=[c.cf, Mh], w=[p_])
                kb.V(lambda e, p_=p_, d_=d_: e.tensor_tensor(out=d_[:, :, :], in0=p_[:, :].rearrange("p (a b) -> p a b", b=128), in1=c.negmask().unsqueeze(1).broadcast_to([128, 4, 128]), op=ALU.add), r=[p_, c.cf], w=[d_])
                kb.A(lambda e, d_=d_: e.activation(out=d_[:, :, :], in_=d_[:, :, :], func=AF.Exp), r=[d_], w=[d_])
                kb.G(lambda e, d_=d_, hb=hb: e.tensor_tensor(out=Wd[:, hb * 4:(hb + 1) * 4, :], in0=d_[:, :, :], in1=cbt[:, hb // 2, :].unsqueeze(1).broadcast_to([128, 4, 128]), op=ALU.mult), r=[d_, cbt], w=[Wd])
            for h in range(32):
                g = h // 8
                kb.T(lambda e, h=h, g=g: e.matmul(py[g][:, (h % 8) * 64:(h % 8 + 1) * 64], lhsT=Wd[:, h, :], rhs=xdt[:, h * 64:(h + 1) * 64], start=True, stop=True), r=[Wd, xdt], w=[py[g]])
            for g in range(4):
                gs = slice(g * 512, (g + 1) * 512)
                kb.V(lambda e, g=g, gs=gs: e.tensor_tensor(out=y[:, gs], in0=py[g][:, :], in1=yoff[:, gs], op=ALU.add), r=[py[g], yoff], w=[y])
            kb.G(lambda e: e.tensor_tensor(out=v3(t2[:, :]), in0=v3(xs[:, :]), in1=dsk[:, :].unsqueeze(2).broadcast_to([128, 32, 64]), op=ALU.mult), r=[xs, dsk], w=[t2])
            kb.V(lambda e: e.tensor_tensor(out=y[:, :], in0=y[:, :], in1=t2[:, :], op=ALU.add), r=[y, t2], w=[y])
            kb.V(lambda e: e.tensor_tensor(out=y[:, :], in0=y[:, :], in1=zs[:, :], op=ALU.mult), r=[y, zs], w=[y])
            kb.V(lambda e: e.memset(ss[:, :], 0.0), w=[ss])
            for g in range(4):
                gs = slice(g * 512, (g + 1) * 512)
                kb.A(lambda e, g=g, gs=gs: e.activation(out=t2[:, gs], in_=y[:, gs], func=AF.Square, accum_out=ss[:, g:g + 1]), r=[y, ss], w=[t2, ss])
            kb.A(lambda e: e.activation(out=ss[:, :], in_=ss[:, :], func=AF.Ln, scale=1.0 / 512, bias=CONST.eps[:, 0:1]), r=[ss, CONST.eps], w=[ss])
            kb.A(lambda e: e.activation(out=ss[:, :], in_=ss[:, :], func=AF.Exp, scale=-0.5), r=[ss], w=[ss])
            kb.V(lambda e: e.tensor_tensor(out=y[:, :].rearrange("p (g d) -> p g d", d=512), in0=y[:, :].rearrange("p (g d) -> p g d", d=512), in1=ss[:, :].unsqueeze(2).broadcast_to([128, 4, 512]), op=ALU.mult), r=[y, ss], w=[y])
            kb.G(lambda e: e.tensor_tensor(out=gnb[:, :], in0=y[:, :], in1=ng_bc[:, :], op=ALU.mult), r=[y, ng_bc], w=[gnb])
            transpose_to(kb, c, gnb, 16, ptr, gnT, lambda c0, nn: gnT[:, c0:c0 + nn, :])
            po = pd
            for hf in range(2):
                for k in range(16):
                    kb.T(lambda e, k=k, hf=hf: e.matmul(po[hf][:, :], lhsT=gnT[:, k, :], rhs=wout[:, k, hf * 512:(hf + 1) * 512], start=(k == 0), stop=(k == 15)), r=[gnT, wout], w=[po[hf]])
            resid_ln_store(kb, xt, po, g_bc, b_bc, ybuf, tmp, Y1, Y1[rows, :])
            for g in range(4):
                gs = slice(g * 512, (g + 1) * 512)
                kb.T(lambda e, g=g, gs=gs: e.matmul(py[g][:, :], lhsT=bt[:, g * 128:(g + 1) * 128], rhs=xe[:, gs], start=True, stop=True), r=[bt, xe], w=[py[g]])
            kb.V(lambda e: e.tensor_tensor(out=v3(H[:, :]), in0=v3(H[:, :]), in1=cd[:, :].unsqueeze(2).broadcast_to([128, 32, 64]), op=ALU.mult), r=[H, cd], w=[H])
            for g in range(4):
                gs = slice(g * 512, (g + 1) * 512)
                kb.V(lambda e, g=g, gs=gs: e.tensor_tensor(out=H[:, gs], in0=H[:, gs], in1=py[g][:, :], op=ALU.add), r=[H, py[g]], w=[H])
            kb.A(lambda e: e.activation(out=Hbf[:, :], in_=H[:, :], func=AF.Copy), r=[H], w=[Hbf])


W_SHAPES = {
    "ssd_w_in": (2, 1024, 5152), "ssd_conv_w": (2, 4, 3072), "ssd_conv_b": (2, 3072), "ssd_dt_bias": (2, 32),
    "ssd_a_log": (2, 32), "ssd_d": (2, 32), "ssd_norm_g": (2, 2048), "ssd_w_out": (2, 2048, 1024),
    "moba_w_qkv": (1, 1024, 3072), "moba_w_out": (1, 1024, 1024),
    "conv_w_pw1": (1, 1024, 2048), "conv_b_pw1": (1, 2048), "conv_w_dw": (1, 31, 1024), "conv_b_dw": (1, 1024),
    "conv_ln_g": (1, 1024), "conv_ln_b": (1, 1024), "conv_w_pw2": (1, 1024, 1024),
    "peer_w_q": (4, 1024, 2048), "peer_sub_keys": (4, 8, 2, 128, 128), "peer_u": (4, 16384, 1024), "peer_v": (4, 16384, 1024),
    "ln_mix_g": (4, 1024), "ln_mix_b": (4, 1024), "ln_ffn_g": (4, 1024), "ln_ffn_b": (4, 1024),
    "ple_w_gate": (4, 1024, 1024), "ple_w_proj": (4, 256, 1024),
}
DEPTH = 4
PER_LAYER = ("peer_w_q", "peer_sub_keys", "peer_u", "peer_v", "ln_mix_g", "ln_mix_b", "ln_ffn_g", "ln_ffn_b", "ple_w_gate", "ple_w_proj")


def build_full(T, S, depth=DEPTH, TG=256):
    nc = bass.Bass("TRN2", target_bir_lowering=False)
    kb = KB(nc)
    with nc.allow_low_precision("bf16 matmul operands with fp32 accumulation"):
        cd = kb.dram("consts", [128, CW], F32, kind="ExternalInput")
        c2d = kb.dram("c2", [128, C2W], F32, kind="ExternalInput")
        X = kb.dram("x", [T, 1024], F32, kind="ExternalInput")
        P = kb.dram("p", [DEPTH, T, 256], F32, kind="ExternalInput")
        OUT = kb.dram("out", [T, 1024], F32, kind="ExternalOutput")
        Wd = {k: kb.dram(k, list(shp), F32, kind="ExternalInput").t for k, shp in W_SHAPES.items()}
        XA = [kb.dram("xa%d" % i, [T, 1024], F32) for i in range(2)]
        Y1 = kb.dram("y1", [T, 1024], F32)
        UVs = kb.dram("UVs", [128, 128, 2048], BF16)
        WQs = kb.dram("WQs", [128, 16384], BF16)
        XS = kb.dram("XS", [T, 2048], BF16)
        BTM = kb.dram("BTM", [T, 512], BF16)
        BCT = kb.dram("BCT", [8, 128, T], BF16)
        ZS = kb.dram("ZS", [T, 2048], BF16)
        DT = kb.dram("DT", [T, 32], F32)
        c = load_consts(kb, cd)
        xin = X
        for i in range(depth):
            kind, j = i % 3, i // 3
            W = {}
            for k in W_SHAPES:
                if k in PER_LAYER:
                    W[k] = Wd[k][i]
                elif k.startswith(("ssd_", "moba_", "conv_")):
                    n = W_SHAPES[k][0]
                    W[k] = Wd[k][min(j, n - 1)]
            if kind == 0:
                ssd_phase_a(kb, c, T, S, xin, W, XS, BTM, BCT, ZS, DT)
                ssd_phase_b(kb, c, T, S, xin, Y1, W, XS, BTM, BCT, ZS, DT)
            elif kind == 1:
                moba_phase(kb, c, T, S, xin, Y1, W, c2d)
            else:
                conf_phase(kb, c, T, S, xin, Y1, W)
            peer_prepass(kb, c, W["peer_u"], W["peer_v"], UVs)
            xout = OUT if i == depth - 1 else XA[i % 2]
            peer_phase(kb, c, T, Y1, xout, P.t[i], W, UVs, WQs, TG=TG)
            xin = xout
        kb.barrier()
    return nc, kb


_CACHE = {}


def kernel(**inputs):
    NCORE = 8
    B, S = inputs["x"].shape[0], inputs["x"].shape[1]
    per = B // NCORE
    T = per * S
    key = (T, S)
    if key not in _CACHE:
        _CACHE[key] = build_full(T, S)[0]
    nc = _CACHE[key]
    consts, c2 = host_consts(), host_c2()
    shared = {k: np.ascontiguousarray(np.asarray(inputs[k], dtype=np.float32)) for k in W_SHAPES}
    x = np.asarray(inputs["x"], dtype=np.float32)
    p = np.asarray(inputs["p"], dtype=np.float32)
    in_maps = []
    for ci in range(NCORE):
        m = dict(shared)
        m["consts"] = consts
        m["c2"] = c2
        m["x"] = np.ascontiguousarray(x[ci * per:(ci + 1) * per].reshape(T, 1024))
        m["p"] = np.ascontiguousarray(p[:, ci * per:(ci + 1) * per].reshape(DEPTH, T, 256))
        in_maps.append(m)
    res = run_bass_kernel_spmd(nc, in_maps, core_ids=list(range(NCORE)))
    outs = [np.asarray(r["out"]).reshape(per, S, 1024) for r in res.results]
    return np.concatenate(outs, axis=0).astype(np.float32)
```

```python
import numpy as np
from contextlib import ExitStack, contextmanager
import concourse.bass as bass
import concourse.mybir as mybir
from concourse.bass_utils import run_bass_kernel_spmd

F32 = mybir.dt.float32
BF16 = mybir.dt.bfloat16
I32 = mybir.dt.int32
U32 = mybir.dt.uint32
AF = mybir.ActivationFunctionType
ALU = mybir.AluOpType
AX = mybir.AxisListType

D = 1024
ALPHA = 8.0 ** 0.25
EPS = 1e-5
NEG = -1.0e30
KD = 8


class Buf:
    __slots__ = ("t", "w", "r", "name")

    def __init__(self, t, name=""):
        self.t = t
        self.w = None
        self.r = {}
        self.name = name

    def __getitem__(self, idx):
        return self.t[idx]


class Alias:
    def __init__(self, parent, t):
        self.__dict__["parent"] = parent
        self.__dict__["t"] = t

    def __getitem__(self, idx):
        return self.t[idx]

    def __getattr__(self, k):
        return getattr(self.__dict__["parent"], k)

    def __setattr__(self, k, v):
        setattr(self.__dict__["parent"], k, v)


class KB:
    def __init__(self, nc):
        self.nc = nc
        self.root = ExitStack()
        self.stacks = [self.root]
        self.eng = {}
        for name, h in (("pe", nc.tensor), ("act", nc.scalar), ("dve", nc.vector), ("pool", nc.gpsimd), ("sp", nc.sync)):
            sem = self.root.enter_context(nc.semaphore("s_" + name))
            self.eng[name] = dict(h=h, sem=sem, cnt=0, seen={}, name=name)
        self.dq = {}
        for q in ("sp", "pool", "act"):
            sems = [self.root.enter_context(nc.semaphore("d_%s%d" % (q, i))) for i in range(KD)]
            self.dq[q] = dict(sems=sems, n=0, cnt=[0] * KD)
        self.uid = 0
        self.ninst = 0

    @contextmanager
    def scope(self):
        st = ExitStack()
        self.stacks.append(st)
        try:
            yield
        finally:
            self.barrier()
            self.stacks.pop()
            st.close()

    def _nm(self, name):
        self.uid += 1
        return "%s_%d" % (name, self.uid)

    def sb(self, name, shape, dt):
        t = self.stacks[-1].enter_context(self.nc.sbuf_tensor(self._nm(name), list(shape), dt))
        return Buf(t, name)

    def ps(self, name, shape, dt):
        t = self.stacks[-1].enter_context(self.nc.psum_tensor(self._nm(name), list(shape), dt))
        return Buf(t, name)

    def dram(self, name, shape, dt, kind="Internal"):
        t = self.nc.dram_tensor(name, list(shape), dt, kind=kind)
        return Buf(t.ap(), name)

    def _wait(self, e, deps):
        best = {}
        for sem, val in deps:
            k = id(sem)
            if k not in best or best[k][1] < val:
                best[k] = (sem, val)
        for k, (sem, val) in best.items():
            if e["seen"].get(k, 0) >= val:
                continue
            e["h"].wait_ge(sem, val)
            e["seen"][k] = val

    def _deps(self, e, r, w, skip_self, is_dma=False):
        deps = []
        me = None if is_dma else id(e["sem"])
        for b in r:
            if b.w is not None:
                deps.append(b.w)
        for b in w:
            if b.w is not None:
                deps.append(b.w)
            for k, tok in b.r.items():
                deps.append(tok)
        if skip_self:
            deps = [d for d in deps if id(d[0]) != me]
        return deps

    def _mark(self, tok, r, w):
        for b in w:
            b.w = tok
            b.r = {}
        k = id(tok[0])
        wroots = [getattr(x, "parent", x) for x in w]
        for b in r:
            if not any(getattr(b, "parent", b) is x for x in wroots):
                b.r[k] = tok

    def op(self, en, fn, r=(), w=()):
        e = self.eng[en]
        self._wait(e, self._deps(e, r, w, en == "pe"))
        inst = fn(e["h"])
        e["cnt"] += 1
        self.ninst += 1
        inst.then_inc(e["sem"], 1)
        tok = (e["sem"], e["cnt"])
        self._mark(tok, r, w)
        return tok

    def V(self, fn, r=(), w=()):
        return self.op("dve", fn, r, w)

    def A(self, fn, r=(), w=()):
        return self.op("act", fn, r, w)

    def G(self, fn, r=(), w=()):
        return self.op("pool", fn, r, w)

    def T(self, fn, r=(), w=()):
        return self.op("pe", fn, r, w)

    def dma(self, q, out, in_, r=(), w=()):
        e = self.eng[q]
        d = self.dq[q]
        i = d["n"] % KD
        d["n"] += 1
        sem = d["sems"][i]
        deps = self._deps(e, r, w, False, True)
        if d["cnt"][i] > 0:
            deps.append((sem, d["cnt"][i] * 16))
        self._wait(e, deps)
        e["h"].dma_start(out=out, in_=in_).then_inc(sem, 16)
        self.ninst += 1
        d["cnt"][i] += 1
        tok = (sem, d["cnt"][i] * 16)
        self._mark(tok, r, w)
        return tok

    def barrier(self):
        toks = []
        for e in self.eng.values():
            if e["cnt"] > 0:
                toks.append((e["sem"], e["cnt"]))
        for d in self.dq.values():
            for i in range(KD):
                if d["cnt"][i] > 0:
                    toks.append((d["sems"][i], d["cnt"][i] * 16))
        for e in self.eng.values():
            self._wait(e, toks)


class Consts:
    pass


CONST = None


def load_consts(kb, cdram):
    c = Consts()
    cf = kb.sb("cf", [128, CW], F32)
    kb.dma("sp", cf[:, :], cdram[:, :], r=[cdram], w=[cf])
    c.cf = cf
    c.identf = lambda: cf[:, 0:128]
    c.ones = lambda: cf[:, 128:256]
    c.triu = lambda: cf[:, 256:384]
    c.negmask = lambda: cf[:, 384:512]
    c.tri_q = lambda: cf[:, 512:640]
    c.iota128 = lambda: cf[:, 640:768]
    c.negones = lambda: cf[:, 768:896]
    ib = kb.sb("identb", [128, 128], BF16)
    kb.V(lambda e: e.tensor_copy(out=ib[:, :], in_=cf[:, 0:128]), r=[cf], w=[ib])
    c.identb = ib
    io = kb.sb("iotab", [128, 128], BF16)
    kb.V(lambda e: e.tensor_copy(out=io[:, :], in_=cf[:, 640:768]), r=[cf], w=[io])
    c.iotab = io
    c.eps = kb.sb("epsc", [128, 1], F32)
    kb.V(lambda e: e.memset(c.eps[:, :], EPS), w=[c.eps])
    global CONST
    CONST = c
    return c


CW = 896
NRELW = 2432


def host_consts():
    c = np.zeros((128, CW), np.float32)
    i = np.arange(128)
    c[:, 0:128] = np.eye(128, dtype=np.float32)
    c[:, 128:256] = 1.0
    c[:, 256:384] = (i[:, None] <= i[None, :]).astype(np.float32)
    c[:, 384:512] = np.where(i[:, None] <= i[None, :], 0.0, NEG)
    c[:, 512:640] = np.where(i[None, :] <= i[:, None], 0.0, NEG)
    c[:, 640:768] = i[None, :].astype(np.float32)
    c[:, 768:896] = -1.0
    return c


def host_nrel():
    return np.ascontiguousarray(np.broadcast_to((np.arange(NRELW) - 2304)[None, :].astype(np.float32), (128, NRELW)))


def bcast_row(kb, name, src_ap, n, q="sp", rbuf=None):
    t = kb.sb(name, [128, n], F32)
    kb.dma(q, t[:, :], src_ap.partition_broadcast(128), r=[rbuf] if rbuf else [], w=[t])
    return t


def load_w_bf(kb, dst, dst_ap_fn, src_ap, rows_k, cols, stage, cast_engs=("act", "pool")):
    step = stage[0].t.shape[1]
    n = 0
    for k in range(rows_k):
        for c0 in range(0, cols, step):
            cw = min(step, cols - c0)
            st = stage[n % len(stage)]
            kb.dma("sp" if n % 2 == 0 else "pool", st[:, 0:cw], src_ap[k * 128:(k + 1) * 128, c0:c0 + cw], w=[st])
            en = cast_engs[n % len(cast_engs)]
            if en == "act":
                kb.A(lambda e, st=st, k=k, c0=c0, cw=cw: e.activation(out=dst_ap_fn(k, c0, cw), in_=st[:, 0:cw], func=AF.Copy), r=[st], w=[dst])
            else:
                kb.op(en, lambda e, st=st, k=k, c0=c0, cw=cw: e.tensor_copy(out=dst_ap_fn(k, c0, cw), in_=st[:, 0:cw]), r=[st], w=[dst])
            n += 1


def transpose_to(kb, c, src_bf, ncol_chunks, pst, dst, dst_ap, evac="dve"):
    for c0 in range(0, ncol_chunks, 8):
        nn = min(8, ncol_chunks - c0)
        for j in range(nn):
            kb.T(lambda e, j=j, c0=c0: e.transpose(out=pst[:, j * 128:(j + 1) * 128], in_=src_bf[:, (c0 + j) * 128:(c0 + j + 1) * 128], identity=c.identb[:, :]),
                 r=[src_bf, c.identb], w=[pst])
        o = dst_ap(c0, nn)
        i = pst[:, 0:nn * 128].rearrange("p (a b) -> p a b", b=128)
        if evac == "act":
            kb.A(lambda e, o=o, i=i: e.activation(out=o, in_=i, func=AF.Copy), r=[pst], w=[dst])
        else:
            kb.V(lambda e, o=o, i=i: e.tensor_copy(out=o, in_=i), r=[pst], w=[dst])


def layer_norm(kb, y, g_bc, b_bc, out, stats, mv, rstd):
    kb.V(lambda e: e.bn_stats(out=stats[:, 0:6], in_=y[:, 0:512]), r=[y], w=[stats])
    kb.V(lambda e: e.bn_stats(out=stats[:, 6:12], in_=y[:, 512:1024]), r=[y], w=[stats])
    kb.V(lambda e: e.bn_aggr(out=mv[:, 0:2], in_=stats[:, 0:12]), r=[stats], w=[mv])
    kb.A(lambda e: e.activation(out=rstd[:, 0:1], in_=mv[:, 1:2], func=AF.Ln, bias=CONST.eps[:, 0:1]), r=[mv, CONST.eps], w=[rstd])
    kb.A(lambda e: e.activation(out=rstd[:, 0:1], in_=rstd[:, 0:1], func=AF.Exp, scale=-0.5), r=[rstd], w=[rstd])
    kb.V(lambda e: e.tensor_scalar(out=out[:, :], in0=y[:, :], scalar1=mv[:, 0:1], scalar2=rstd[:, 0:1], op0=ALU.subtract, op1=ALU.mult), r=[y, mv, rstd], w=[out])
    kb.G(lambda e: e.tensor_tensor(out=out[:, :], in0=out[:, :], in1=g_bc[:, :], op=ALU.mult), r=[out, g_bc], w=[out])
    kb.G(lambda e: e.tensor_tensor(out=out[:, :], in0=out[:, :], in1=b_bc[:, :], op=ALU.add), r=[out, b_bc], w=[out])


def resid_ln_store(kb, xt, mixps, g_bc, b_bc, ybuf, tmp, dst_dram, dst_ap, q="sp"):
    for h in range(2):
        kb.V(lambda e, h=h: e.scalar_tensor_tensor(out=ybuf[:, h * 512:(h + 1) * 512], in0=xt[:, h * 512:(h + 1) * 512], scalar=ALPHA,
                                                  in1=mixps[h][:, 0:512], op0=ALU.mult, op1=ALU.add), r=[xt, mixps[h]], w=[ybuf])
    layer_norm(kb, ybuf, g_bc, b_bc, ybuf, tmp["stats"], tmp["mv"], tmp["rstd"])
    if dst_ap is not None:
        kb.dma(q, dst_ap, ybuf[:, :], r=[ybuf], w=[dst_dram])


def ln_tmp(kb):
    return dict(stats=kb.sb("stats", [128, 12], F32), mv=kb.sb("mv", [128, 2], F32), rstd=kb.sb("rstd", [128, 1], F32))


GELU_MODE = "af"


def peer_prepass(kb, c, u_ap, v_ap, UVs, NCH=128):
    with kb.scope():
        st = [kb.sb("pp_st", [128, 1024], F32) for _ in range(4)]
        ub = [kb.sb("pp_ub", [128, 1024], BF16) for _ in range(2)]
        ut = [kb.sb("pp_ut", [128, 1024], BF16) for _ in range(2)]
        vb = [kb.sb("pp_vb", [128, 1024], BF16) for _ in range(2)]
        pst = [kb.ps("pp_ps", [128, 1024], BF16) for _ in range(2)]
        for ch in range(NCH):
            su, sv = st[(2 * ch) % 4], st[(2 * ch + 1) % 4]
            kb.dma("sp", su[:, :], u_ap[ch * 128:(ch + 1) * 128, :], w=[su])
            kb.dma("pool", sv[:, :], v_ap[ch * 128:(ch + 1) * 128, :], w=[sv])
            b = ub[ch % 2]
            kb.A(lambda e, b=b, su=su: e.activation(out=b[:, :], in_=su[:, :], func=AF.Copy), r=[su], w=[b])
            p = pst[ch % 2]
            for k in range(8):
                kb.T(lambda e, k=k, b=b, p=p: e.transpose(out=p[:, k * 128:(k + 1) * 128], in_=b[:, k * 128:(k + 1) * 128], identity=c.identb[:, :]),
                     r=[b, c.identb], w=[p])
            t = ut[ch % 2]
            kb.V(lambda e, t=t, p=p: e.tensor_copy(out=t[:, :], in_=p[:, :]), r=[p], w=[t])
            kb.dma("sp", UVs[ch][:, 0:1024], t[:, :], r=[t], w=[UVs])
            vv = vb[ch % 2]
            kb.G(lambda e, vv=vv, sv=sv: e.tensor_copy(out=vv[:, :], in_=sv[:, :]), r=[sv], w=[vv])
            kb.dma("pool", UVs[ch][:, 1024:2048], vv[:, :], r=[vv], w=[UVs])


def peer_q_phase(kb, c, T, Y1, W, XTd, QTd):
    GS = 512
    with kb.scope():
        wq = kb.sb("wq", [128, 8, 2048], BF16)
        with kb.scope():
            stage = [kb.sb("stg", [128, 2048], F32) for _ in range(2)]
            load_w_bf(kb, wq, lambda k, c0, cw: wq[:, k, c0:c0 + cw], W["peer_w_q"], 8, 2048, stage)
        xt = [kb.sb("xt", [128, 1024], F32) for _ in range(2)]
        xbf = [kb.sb("xbf", [128, 1024], BF16) for _ in range(2)]
        xT = [kb.sb("xT5", [128, 8, GS], BF16) for _ in range(2)]
        qT = [kb.sb("qT5", [128, 16, GS], BF16) for _ in range(2)]
        pq = [kb.ps("pq", [128, 512], F32) for _ in range(4)]
        ptr = [kb.ps("ptr", [128, 1024], BF16) for _ in range(2)]
        XTv = XTd.t.rearrange("k p t -> p k t")
        QTv = QTd.t.rearrange("k p t -> p k t")
        n = 0
        for b in range(T // GS):
            t0 = b * GS
            x5, q5 = xT[b % 2], qT[b % 2]
            for ti in range(4):
                x_, xb_ = xt[ti % 2], xbf[ti % 2]
                kb.dma("sp" if ti % 2 == 0 else "pool", x_[:, :], Y1[t0 + ti * 128:t0 + (ti + 1) * 128, :], r=[Y1], w=[x_])
                kb.A(lambda e, x_=x_, xb_=xb_: e.activation(out=xb_[:, :], in_=x_[:, :], func=AF.Copy), r=[x_], w=[xb_])
                transpose_to(kb, c, xb_, 8, ptr[ti % 2], x5, lambda c0, nn, ti=ti, x5=x5: x5[:, c0:c0 + nn, ti * 128:(ti + 1) * 128])
            kb.dma("sp", XTv[:, :, t0:t0 + GS], x5[:, :, :], r=[x5], w=[XTd])
            for hc in range(16):
                p_ = pq[n % 4]
                n += 1
                for k in range(8):
                    kb.T(lambda e, hc=hc, k=k, p_=p_, x5=x5: e.matmul(p_[:, :], lhsT=wq[:, k, hc * 128:(hc + 1) * 128], rhs=x5[:, k, :], start=(k == 0), stop=(k == 7)), r=[wq, x5], w=[p_])
                if hc % 2 == 0:
                    kb.A(lambda e, hc=hc, p_=p_, q5=q5: e.activation(out=q5[:, hc, :], in_=p_[:, :], func=AF.Copy), r=[p_], w=[q5])
                else:
                    kb.V(lambda e, hc=hc, p_=p_, q5=q5: e.tensor_copy(out=q5[:, hc, :], in_=p_[:, :]), r=[p_], w=[q5])
            kb.dma("pool", QTv[:, :, t0:t0 + GS], q5[:, :, :], r=[q5], w=[QTd])


def peer_phase(kb, c, T, Y1, OUT, p_ap, W, UVs, XTd, QTd, TG=256, NCH=128):
    NT = TG // 128
    NG = T // TG
    TB = 8
    with kb.scope():
        wg = kb.sb("wg", [128, 8, 1024], BF16)
        wp = kb.sb("wp", [128, 2, 1024], BF16)
        skT = kb.sb("skT", [128, 16, 128], BF16)
        with kb.scope():
            stage = [kb.sb("stg", [128, 2048], F32) for _ in range(2)]
            load_w_bf(kb, wg, lambda k, c0, cw: wg[:, k, c0:c0 + cw], W["ple_w_gate"], 8, 1024, stage)
            load_w_bf(kb, wp, lambda k, c0, cw: wp[:, k, c0:c0 + cw], W["ple_w_proj"], 2, 1024, stage)
            pskt = kb.ps("pskt", [128, 512], F32)
            sk = W["peer_sub_keys"].rearrange("h c n d -> (h c) n d")
            for hc in range(16):
                st = stage[hc % 2]
                kb.dma("sp", st[:, 0:128], sk[hc], w=[st])
                kb.T(lambda e, st=st: e.transpose(out=pskt[:, 0:128], in_=st[:, 0:128], identity=c.identf()), r=[st, c.cf], w=[pskt])
                kb.V(lambda e, hc=hc: e.tensor_copy(out=skT[:, hc, :], in_=pskt[:, 0:128]), r=[pskt], w=[skT])
        g_bc = bcast_row(kb, "lnf_g", W["ln_ffn_g"], 1024)
        b_bc = bcast_row(kb, "lnf_b", W["ln_ffn_b"], 1024, q="pool")
        iotaC = kb.sb("iotaC", [128, TB, 128], BF16)
        kb.V(lambda e: e.tensor_copy(out=iotaC[:, :, :], in_=c.iota128().unsqueeze(1).broadcast_to([128, TB, 128])), r=[c.cf], w=[iotaC])
        iota16 = c.iota128()[:, 0:16]

        xT2 = [kb.sb("xT", [128, 8, TG], BF16) for _ in range(2)]
        qT2 = [kb.sb("qT", [128, 16, TG], BF16) for _ in range(2)]
        T32 = [kb.sb("T3", [128, 3, TG], BF16) for _ in range(2)]
        xt = kb.sb("xt", [128, 1024], F32)
        S = kb.sb("S", [128, 16, 128], F32)
        S2 = kb.sb("S2", [128, 256], F32)
        V16 = kb.sb("V16", [128, 16, 16], F32)
        I16u = kb.sb("I16u", [128, 16, 16], U32)
        I16f = kb.sb("I16f", [128, 16, 16], F32)
        cand = kb.sb("cand", [128, 8, 256], F32)
        B16 = kb.sb("B16", [128, 8, 16], F32)
        P16u = kb.sb("P16u", [128, 8, 16], U32)
        ABu = kb.sb("ABu", [128, 2, 128], U32)
        ABf = kb.sb("ABf", [128, 2, 128], F32)
        eq = kb.sb("eq", [128, 8, 16, 16], F32)
        J = kb.sb("J", [128, 3, 128], F32)
        e16 = kb.sb("e16", [128, 8, 16], F32)
        ssum = kb.sb("ssum", [128, 8], F32)
        OH1 = [kb.sb("OH1", [128, TB, 128], BF16) for _ in range(2)]
        OH2 = [kb.sb("OH2", [128, TB, 128], BF16) for _ in range(2)]
        OH2g = [kb.sb("OH2g", [128, TB, 128], BF16) for _ in range(2)]
        GT = kb.sb("GT", [128, TG, 128], BF16)
        NB = 4
        UVb = [kb.sb("UVb", [128, 2048], BF16) for _ in range(NB)]
        Aact = [kb.sb("Aact", [128, TG], F32) for _ in range(2)]
        Wtb = [kb.sb("Wtb", [128, TG], BF16) for _ in range(3)]
        ybuf = kb.sb("ybuf", [128, 1024], F32)
        y2bf = kb.sb("y2bf", [128, 1024], BF16)
        y2T = kb.sb("y2T", [128, 8, 128], BF16)
        pt = kb.sb("pt", [128, 256], F32)
        pbf = kb.sb("pbf", [128, 256], BF16)
        pT = kb.sb("pT", [128, 2, 128], BF16)
        Sflat = S[:, :, :].rearrange("p a b -> p (a b)")
        sg = Alias(S, Sflat[:, 0:1024])
        ob = Alias(S, Sflat[:, 1024:2048])
        tmp = ln_tmp(kb)
        acc = [[kb.ps("acc", [128, 512], F32) for _ in range(2)] for _ in range(NT)]
        stp = [kb.ps("stp", [128, 512], F32) for _ in range(2)]
        mps = kb.ps("mps", [128, 512], F32)
        mpsb = kb.ps("mpsb", [128, 1024], BF16)
        if NT == 1:
            gps2 = kb.ps("gps", [128, 512], F32)
        gbanks = [mps] + [b for pair in acc for b in pair] if NT > 1 else [mps, gps2] + [b for pair in acc for b in pair]
        XTv = XTd.t.rearrange("k p t -> p k t")
        QTv = QTd.t.rearrange("k p t -> p k t")

        def stage1(g):
            t0 = g * TG
            xT, qT, T3 = xT2[g % 2], qT2[g % 2], T32[g % 2]
            kb.dma("sp", xT[:, :, :], XTv[:, :, t0:t0 + TG], r=[XTd], w=[xT])
            kb.dma("pool", qT[:, :, :], QTv[:, :, t0:t0 + TG], r=[QTd], w=[qT])
            yield
            for ti in range(NT):
                tsl = slice(ti * 128, (ti + 1) * 128)
                for h4 in range(4):
                    for j in range(4):
                        hc = h4 * 4 + j
                        kb.T(lambda e, hc=hc, j=j: e.matmul(mps[:, j * 128:(j + 1) * 128], lhsT=qT[:, hc, tsl], rhs=skT[:, hc, :], start=True, stop=True),
                             r=[qT, skT], w=[mps])
                    kb.V(lambda e, h4=h4: e.tensor_copy(out=S[:, h4 * 4:(h4 + 1) * 4, :], in_=mps[:, :].rearrange("p (a b) -> p a b", b=128)), r=[mps], w=[S])
                    yield
                for hc in range(16):
                    kb.V(lambda e, hc=hc: e.max(out=V16[:, hc, 0:8], in_=S[:, hc, :]), r=[S], w=[V16])
                    kb.V(lambda e, hc=hc: e.max_index(out=I16u[:, hc, 0:8], in_max=V16[:, hc, 0:8], in_values=S[:, hc, :]), r=[S, V16], w=[I16u])
                    kb.V(lambda e, hc=hc: e.match_replace(out=S2[:, 0:128], in_to_replace=V16[:, hc, 0:8], in_values=S[:, hc, :], imm_value=NEG), r=[S, V16], w=[S2])
                    yield
                    kb.V(lambda e, hc=hc: e.max(out=V16[:, hc, 8:16], in_=S2[:, 0:128]), r=[S2], w=[V16])
                    kb.V(lambda e, hc=hc: e.max_index(out=I16u[:, hc, 8:16], in_max=V16[:, hc, 8:16], in_values=S2[:, 0:128]), r=[S2, V16], w=[I16u])
                    yield
                kb.V(lambda e: e.tensor_copy(out=I16f[:, :, :], in_=I16u[:, :, :]), r=[I16u], w=[I16f])
                V4 = V16[:, :, :].rearrange("p (h c) k -> p h c k", c=2)
                I4 = I16f[:, :, :].rearrange("p (h c) k -> p h c k", c=2)
                cand4 = cand[:, :, :].rearrange("p h (a b) -> p h a b", b=16)
                kb.V(lambda e: e.tensor_tensor(out=cand4, in0=V4[:, :, 0, :].unsqueeze(3).broadcast_to([128, 8, 16, 16]),
                                               in1=V4[:, :, 1, :].unsqueeze(2).broadcast_to([128, 8, 16, 16]), op=ALU.add), r=[V16], w=[cand])
                yield
                for h in range(8):
                    kb.V(lambda e, h=h: e.max(out=B16[:, h, 0:8], in_=cand[:, h, :]), r=[cand], w=[B16])
                    kb.V(lambda e, h=h: e.max_index(out=P16u[:, h, 0:8], in_max=B16[:, h, 0:8], in_values=cand[:, h, :]), r=[cand, B16], w=[P16u])
                    kb.V(lambda e, h=h: e.match_replace(out=S2[:, :], in_to_replace=B16[:, h, 0:8], in_values=cand[:, h, :], imm_value=NEG), r=[cand, B16], w=[S2])
                    yield
                    kb.V(lambda e, h=h: e.max(out=B16[:, h, 8:16], in_=S2[:, :]), r=[S2], w=[B16])
                    kb.V(lambda e, h=h: e.max_index(out=P16u[:, h, 8:16], in_max=B16[:, h, 8:16], in_values=S2[:, :]), r=[S2, B16], w=[P16u])
                    yield
                Pfl = P16u[:, :, :].rearrange("p h k -> p (h k)")
                kb.V(lambda e: e.tensor_single_scalar(out=ABu[:, 0, :], in_=Pfl, scalar=4, op=ALU.logical_shift_right), r=[P16u], w=[ABu])
                kb.V(lambda e: e.tensor_single_scalar(out=ABu[:, 1, :], in_=Pfl, scalar=15, op=ALU.bitwise_and), r=[P16u], w=[ABu])
                kb.V(lambda e: e.tensor_copy(out=ABf[:, :, :], in_=ABu[:, :, :]), r=[ABu], w=[ABf])
                yield
                for ci in range(2):
                    ab4 = ABf[:, ci, :].rearrange("p (h k) -> p h k", k=16).unsqueeze(3).broadcast_to([128, 8, 16, 16])
                    kb.V(lambda e, ab4=ab4: e.tensor_tensor(out=eq[:, :, :, :], in0=ab4, in1=iota16.unsqueeze(1).unsqueeze(1).broadcast_to([128, 8, 16, 16]), op=ALU.is_equal),
                         r=[ABf, c.cf], w=[eq])
                    yield
                    kb.V(lambda e, ci=ci: e.tensor_tensor(out=eq[:, :, :, :], in0=eq[:, :, :, :], in1=I4[:, :, ci, :].unsqueeze(2).broadcast_to([128, 8, 16, 16]), op=ALU.mult),
                         r=[eq, I16f], w=[eq])
                    yield
                    kb.V(lambda e, ci=ci: e.tensor_reduce(out=J[:, ci, :].rearrange("p (h k) -> p h k", k=16), in_=eq[:, :, :, :], axis=AX.X, op=ALU.add), r=[eq], w=[J])
                    yield
                kb.V(lambda e: e.tensor_tensor(out=e16[:, :, :], in0=B16[:, :, :], in1=B16[:, :, 0:1].broadcast_to([128, 8, 16]), op=ALU.subtract), r=[B16], w=[e16])
                kb.A(lambda e: e.activation(out=e16[:, :, :], in_=e16[:, :, :], func=AF.Exp), r=[e16], w=[e16])
                kb.V(lambda e: e.tensor_reduce(out=ssum[:, :], in_=e16[:, :, :], axis=AX.X, op=ALU.add), r=[e16], w=[ssum])
                yield
                kb.V(lambda e: e.reciprocal(out=ssum[:, :], in_=ssum[:, :]), r=[ssum], w=[ssum])
                kb.V(lambda e: e.tensor_tensor(out=J[:, 2, :].rearrange("p (h k) -> p h k", k=16), in0=e16[:, :, :], in1=ssum[:, :].unsqueeze(2).broadcast_to([128, 8, 16]), op=ALU.mult),
                     r=[e16, ssum], w=[J])
                yield
                for q3 in range(3):
                    kb.T(lambda e, q3=q3: e.transpose(out=mps[:, q3 * 128:(q3 + 1) * 128], in_=J[:, q3, :], identity=c.identf()), r=[J, c.cf], w=[mps])
                kb.V(lambda e: e.tensor_copy(out=T3[:, :, tsl], in_=mps[:, 0:384].rearrange("p (a b) -> p a b", b=128)), r=[mps], w=[T3])
                yield

        def stage2(g):
            T3 = T32[g % 2]
            nsb = 0
            for s0 in range(0, TG, TB):
                o1, o2, o2g = OH1[nsb % 2], OH2[nsb % 2], OH2g[nsb % 2]
                nsb += 1
                kb.V(lambda e, o1=o1, s0=s0: e.tensor_tensor(out=o1[:, :, :], in0=iotaC[:, :, :], in1=T3[:, 0, s0:s0 + TB].unsqueeze(2).broadcast_to([128, TB, 128]), op=ALU.is_equal),
                     r=[iotaC, T3], w=[o1])
                kb.V(lambda e, o2=o2, s0=s0: e.tensor_tensor(out=o2[:, :, :], in0=iotaC[:, :, :], in1=T3[:, 1, s0:s0 + TB].unsqueeze(2).broadcast_to([128, TB, 128]), op=ALU.is_equal),
                     r=[iotaC, T3], w=[o2])
                kb.G(lambda e, o2=o2, o2g=o2g, s0=s0: e.tensor_tensor(out=o2g[:, :, :], in0=o2[:, :, :], in1=T3[:, 2, s0:s0 + TB].unsqueeze(2).broadcast_to([128, TB, 128]), op=ALU.mult),
                     r=[o2, T3], w=[o2g])
                for tb in range(TB):
                    gp = gbanks[((s0 + tb) // 4) % len(gbanks)]
                    kb.T(lambda e, gp=gp, tb=tb, o1=o1, o2g=o2g: e.matmul(gp[:, (tb % 4) * 128:(tb % 4 + 1) * 128], lhsT=o2g[:, tb, :], rhs=o1[:, tb, :], start=True, stop=True),
                         r=[o1, o2g], w=[gp])
                    if tb % 4 == 3:
                        tt = s0 + tb - 3
                        kb.A(lambda e, gp=gp, tt=tt: e.activation(out=GT[:, tt:tt + 4, :], in_=gp[:, :].rearrange("p (a b) -> p a b", b=128), func=AF.Copy), r=[gp], w=[GT])

        def chunk_loop(g, nxt):
            xT = xT2[g % 2]

            def emit_u(ch):
                uvb = UVb[ch % NB]
                kb.dma("sp" if ch % 2 == 0 else "pool", uvb[:, :], UVs[ch], r=[UVs], w=[uvb])
                ub = Alias(uvb, uvb[:, 0:1024])
                sp_ = stp[ch % 2]
                for k in range(8):
                    kb.T(lambda e, k=k, ub=ub, sp_=sp_: e.matmul(sp_[:, 0:TG], lhsT=ub[:, k * 128:(k + 1) * 128], rhs=xT[:, k, :], start=(k == 0), stop=(k == 7)),
                         r=[ub, xT], w=[sp_])
                a_, w_ = Aact[ch % 2], Wtb[ch % 3]
                kb.A(lambda e, a_=a_, sp_=sp_: e.activation(out=a_[:, :], in_=sp_[:, 0:TG], func=AF.Gelu_apprx_tanh), r=[sp_], w=[a_])
                kb.V(lambda e, a_=a_, w_=w_, ch=ch: e.tensor_tensor(out=w_[:, :], in0=a_[:, :], in1=GT[:, :, ch], op=ALU.mult), r=[a_, GT], w=[w_])

            def emit_v(ch):
                uvb = UVb[ch % NB]
                vb2 = Alias(uvb, uvb[:, 1024:2048])
                w_ = Wtb[ch % 3]
                for ti in range(NT):
                    for hf in range(2):
                        kb.T(lambda e, ti=ti, hf=hf, w_=w_, vb2=vb2, ch=ch: e.matmul(acc[ti][hf][:, 0:512], lhsT=w_[:, ti * 128:(ti + 1) * 128], rhs=vb2[:, hf * 512:(hf + 1) * 512],
                                                                                  start=(ch == 0), stop=(ch == NCH - 1)), r=[w_, vb2], w=[acc[ti][hf]])

            for ch in range(NCH):
                emit_u(ch)
                if ch >= 1:
                    emit_v(ch - 1)
                if nxt is not None and ch >= 2:
                    next(nxt, None)
            emit_v(NCH - 1)
            if nxt is not None:
                for _ in nxt:
                    pass

        def epilogue(g):
            t0 = g * TG
            for ti in range(NT):
                tok0 = t0 + ti * 128
                kb.dma("sp", xt[:, :], Y1[tok0:tok0 + 128, :], r=[Y1], w=[xt])
                kb.dma("pool", pt[:, :], p_ap[tok0:tok0 + 128, :], w=[pt])
                resid_ln_store(kb, xt, acc[ti], g_bc, b_bc, ybuf, tmp, None, None)
                kb.A(lambda e: e.activation(out=y2bf[:, :], in_=ybuf[:, :], func=AF.Copy), r=[ybuf], w=[y2bf])
                transpose_to(kb, c, y2bf, 8, mpsb, y2T, lambda c0, nn: y2T[:, c0:c0 + nn, :])
                kb.A(lambda e: e.activation(out=pbf[:, :], in_=pt[:, :], func=AF.Copy), r=[pt], w=[pbf])
                transpose_to(kb, c, pbf, 2, mpsb, pT, lambda c0, nn: pT[:, c0:c0 + nn, :])
                for hf in range(2):
                    hs = slice(hf * 512, (hf + 1) * 512)
                    for k in range(8):
                        kb.T(lambda e, k=k, hs=hs: e.matmul(mps[:, :], lhsT=y2T[:, k, :], rhs=wg[:, k, hs], start=(k == 0), stop=(k == 7)), r=[y2T, wg], w=[mps])
                    kb.A(lambda e, hs=hs: e.activation(out=sg[:, hs], in_=mps[:, :], func=AF.Sigmoid), r=[mps], w=[sg])
                    for k in range(2):
                        kb.T(lambda e, k=k, hs=hs: e.matmul(mps[:, :], lhsT=pT[:, k, :], rhs=wp[:, k, hs], start=(k == 0), stop=(k == 1)), r=[pT, wp], w=[mps])
                    kb.V(lambda e, hs=hs: e.tensor_tensor(out=ob[:, hs], in0=sg[:, hs], in1=mps[:, :], op=ALU.mult), r=[sg, mps], w=[ob])
                kb.G(lambda e: e.tensor_tensor(out=ob[:, :], in0=ob[:, :], in1=ybuf[:, :], op=ALU.add), r=[ob, ybuf], w=[ob])
                kb.dma("sp", OUT[tok0:tok0 + 128, :], ob[:, :], r=[ob], w=[OUT])

        for _ in stage1(0):
            pass
        for g in range(NG):
            nxt = stage1(g + 1) if g + 1 < NG else None
            stage2(g)
            chunk_loop(g, nxt)
            epilogue(g)


def load_cols(kb, c, dst, dst_ap_fn, src_rows_ap, R, stage, pst, nblk=1, blk_stride=0):
    for b in range(nblk):
        kb.dma("sp", stage[0:R, 0:128], src_rows_ap(b), w=[stage])
        kb.T(lambda e: e.transpose(out=pst[:, 0:R], in_=stage[0:R, 0:128], identity=c.cf[0:R, 0:R]), r=[stage, c.cf], w=[pst])
        kb.V(lambda e, b=b: e.tensor_copy(out=dst_ap_fn(b), in_=pst[:, 0:R]), r=[pst], w=[dst])


def conf_phase(kb, c, T, S, XIN, Y1, W):
    GS = 512
    NG = T // GS
    GPS = S // GS
    KW = 31
    with kb.scope():
        w1 = kb.sb("w1", [128, 8, 2048], BF16)
        w2 = kb.sb("w2", [128, 8, 1024], BF16)
        b1 = kb.sb("b1", [128, 16], F32)
        wdw = kb.sb("wdw", [128, 8, KW], F32)
        vecs = kb.sb("vecs", [128, 3, 8], F32)
        with kb.scope():
            stage = [kb.sb("stg", [128, 2048], F32) for _ in range(2)]
            load_w_bf(kb, w1, lambda k, c0, cw: w1[:, k, c0:c0 + cw], W["conv_w_pw1"], 8, 2048, stage)
            load_w_bf(kb, w2, lambda k, c0, cw: w2[:, k, c0:c0 + cw], W["conv_w_pw2"], 8, 1024, stage)
            pst = kb.ps("pst", [128, 512], F32)
            load_cols(kb, c, b1, lambda b: b1[:, :], lambda b: W["conv_b_pw1"].rearrange("(c p) -> c p", p=128), 16, stage[0], pst)
            load_cols(kb, c, wdw, lambda b: wdw[:, b, :], lambda b: W["conv_w_dw"][:, b * 128:(b + 1) * 128], KW, stage[1], pst, nblk=8)
            for i, nm in enumerate(("conv_b_dw", "conv_ln_g", "conv_ln_b")):
                load_cols(kb, c, vecs, lambda b, i=i: vecs[:, i, :], lambda b, nm=nm: W[nm].rearrange("(c p) -> c p", p=128), 8, stage[i % 2], pst)
        g_bc = bcast_row(kb, "lnm_g", W["ln_mix_g"], 1024)
        b_bc = bcast_row(kb, "lnm_b", W["ln_mix_b"], 1024, q="pool")
        xt = [kb.sb("xt", [128, 1024], F32) for _ in range(4)]
        xbf = kb.sb("xbf", [128, 1024], BF16)
        xT = kb.sb("xT", [128, 8, GS], BF16)
        gluH = kb.sb("gluH", [128, 8, KW - 1 + GS], F32)
        hc = kb.sb("hc", [128, 8, GS], F32)
        hsq = kb.sb("hsq", [128, 8, GS], F32)
        zT = kb.sb("zT", [128, 8, GS], BF16)
        sgb = [kb.sb("sgb", [128, GS], F32) for _ in range(2)]
        mean = kb.sb("mean", [128, GS], F32)
        msq = kb.sb("msq", [128, GS], F32)
        rstd = kb.sb("rstd2", [128, GS], F32)
        tn = [kb.sb("tn", [128, GS], F32) for _ in range(2)]
        ybuf = kb.sb("ybuf", [128, 1024], F32)
        tmp = ln_tmp(kb)
        pa = kb.ps("pa", [128, 512], F32)
        pg = kb.ps("pg", [128, 512], F32)
        s1 = kb.ps("s1", [128, 512], F32)
        s2 = kb.ps("s2", [128, 512], F32)
        po = [kb.ps("po", [128, 512], F32) for _ in range(2)]
        ptr = kb.ps("ptr", [128, 1024], BF16)
        H = KW - 1
        for g in range(NG):
            t0 = g * GS
            for ti in range(4):
                kb.dma("sp" if ti % 2 == 0 else "pool", xt[ti][:, :], XIN[t0 + ti * 128:t0 + (ti + 1) * 128, :], r=[XIN], w=[xt[ti]])
                kb.A(lambda e, ti=ti: e.activation(out=xbf[:, :], in_=xt[ti][:, :], func=AF.Copy), r=[xt[ti]], w=[xbf])
                transpose_to(kb, c, xbf, 8, ptr, xT, lambda c0, nn, ti=ti: xT[:, c0:c0 + nn, ti * 128:(ti + 1) * 128])
            if g % GPS == 0:
                kb.G(lambda e: e.memset(gluH[:, :, 0:H], 0.0), w=[gluH])
            for cc in range(8):
                for k in range(8):
                    kb.T(lambda e, k=k, cc=cc: e.matmul(pa[:, :], lhsT=w1[:, k, cc * 128:(cc + 1) * 128], rhs=xT[:, k, :], start=(k == 0), stop=(k == 7)), r=[w1, xT], w=[pa])
                for k in range(8):
                    kb.T(lambda e, k=k, cc=cc: e.matmul(pg[:, :], lhsT=w1[:, k, 1024 + cc * 128:1024 + (cc + 1) * 128], rhs=xT[:, k, :], start=(k == 0), stop=(k == 7)), r=[w1, xT], w=[pg])
                sg_ = sgb[cc % 2]
                kb.A(lambda e, sg_=sg_, cc=cc: e.activation(out=sg_[:, :], in_=pg[:, :], func=AF.Sigmoid, bias=b1[:, 8 + cc:9 + cc]), r=[pg, b1], w=[sg_])
                kb.V(lambda e, sg_=sg_, cc=cc: e.scalar_tensor_tensor(out=gluH[:, cc, H:H + GS], in0=pa[:, :], scalar=b1[:, cc:cc + 1], in1=sg_[:, :], op0=ALU.add, op1=ALU.mult),
                     r=[pa, b1, sg_], w=[gluH])
                kb.V(lambda e, cc=cc: e.tensor_scalar(out=hc[:, cc, :], in0=gluH[:, cc, H:H + GS], scalar1=wdw[:, cc, H:H + 1], scalar2=vecs[:, 0, cc:cc + 1], op0=ALU.mult, op1=ALU.add),
                     r=[gluH, wdw, vecs], w=[hc])
                for k in range(H):
                    kb.V(lambda e, cc=cc, k=k: e.scalar_tensor_tensor(out=hc[:, cc, :], in0=gluH[:, cc, k:k + GS], scalar=wdw[:, cc, k:k + 1], in1=hc[:, cc, :], op0=ALU.mult, op1=ALU.add),
                         r=[gluH, wdw, hc], w=[hc])
            kb.G(lambda e: e.tensor_copy(out=gluH[:, :, 0:H], in_=gluH[:, :, GS:GS + H]), r=[gluH], w=[gluH])
            kb.A(lambda e: e.activation(out=hsq[:, :, :], in_=hc[:, :, :], func=AF.Square), r=[hc], w=[hsq])
            for cc in range(8):
                kb.T(lambda e, cc=cc: e.matmul(s1[:, :], lhsT=c.ones(), rhs=hc[:, cc, :], start=(cc == 0), stop=(cc == 7)), r=[c.cf, hc], w=[s1])
            for cc in range(8):
                kb.T(lambda e, cc=cc: e.matmul(s2[:, :], lhsT=c.ones(), rhs=hsq[:, cc, :], start=(cc == 0), stop=(cc == 7)), r=[c.cf, hsq], w=[s2])
            kb.V(lambda e: e.tensor_scalar(out=mean[:, :], in0=s1[:, :], scalar1=1.0 / 1024, scalar2=None, op0=ALU.mult), r=[s1], w=[mean])
            kb.V(lambda e: e.tensor_tensor(out=msq[:, :], in0=mean[:, :], in1=mean[:, :], op=ALU.mult), r=[mean], w=[msq])
            kb.V(lambda e: e.scalar_tensor_tensor(out=msq[:, :], in0=s2[:, :], scalar=1.0 / 1024, in1=msq[:, :], op0=ALU.mult, op1=ALU.subtract), r=[s2, msq], w=[msq])
            kb.A(lambda e: e.activation(out=rstd[:, :], in_=msq[:, :], func=AF.Ln, bias=CONST.eps[:, 0:1]), r=[msq, CONST.eps], w=[rstd])
            kb.A(lambda e: e.activation(out=rstd[:, :], in_=rstd[:, :], func=AF.Exp, scale=-0.5), r=[rstd], w=[rstd])
            for cc in range(8):
                t_ = tn[cc % 2]
                kb.G(lambda e, cc=cc, t_=t_: e.tensor_tensor(out=t_[:, :], in0=hc[:, cc, :], in1=mean[:, :], op=ALU.subtract), r=[hc, mean], w=[t_])
                kb.V(lambda e, t_=t_: e.tensor_tensor(out=t_[:, :], in0=t_[:, :], in1=rstd[:, :], op=ALU.mult), r=[t_, rstd], w=[t_])
                kb.V(lambda e, cc=cc, t_=t_: e.tensor_scalar(out=t_[:, :], in0=t_[:, :], scalar1=vecs[:, 1, cc:cc + 1], scalar2=vecs[:, 2, cc:cc + 1], op0=ALU.mult, op1=ALU.add), r=[t_, vecs], w=[t_])
                kb.A(lambda e, cc=cc, t_=t_: e.activation(out=zT[:, cc, :], in_=t_[:, :], func=AF.Silu), r=[t_], w=[zT])
            for ti in range(4):
                for hf in range(2):
                    for cc in range(8):
                        kb.T(lambda e, ti=ti, hf=hf, cc=cc: e.matmul(po[hf][:, :], lhsT=zT[:, cc, ti * 128:(ti + 1) * 128], rhs=w2[:, cc, hf * 512:(hf + 1) * 512], start=(cc == 0), stop=(cc == 7)),
                             r=[zT, w2], w=[po[hf]])
                resid_ln_store(kb, xt[ti], po, g_bc, b_bc, ybuf, tmp, Y1, Y1[t0 + ti * 128:t0 + (ti + 1) * 128, :], q="sp" if ti % 2 == 0 else "pool")


C2W = NRELW + 72


def host_c2():
    c2 = np.zeros((128, C2W), np.float32)
    c2[:, 0:NRELW] = (np.arange(NRELW) - 2304)[None, :]
    for own in range(9):
        c2[:, NRELW + own * 8:NRELW + own * 8 + 8] = np.where(np.arange(8) < own, 0.0, NEG)[None, :]
    return c2


def moba_phase(kb, c, T, S, XIN, Y1, W, c2dram):
    NSEQ = T // S
    NQ = S // 128
    NBLK = S // 256
    GS = 512
    with kb.scope():
        qT = kb.sb("qT_all", [128, 8, S], BF16)
        kT = kb.sb("kT_all", [128, 8, S], BF16)
        va = kb.sb("v_all", [128, NQ, 1024], BF16)
        kmf = kb.sb("kmf", [128, 8, 8], F32)
        kmT = kb.sb("kmT", [128, 8, 8], BF16)
        for sq in range(NSEQ):
            base = sq * S
            with kb.scope():
                wqkv = kb.sb("wqkv", [128, 8, 3072], BF16)
                stage = [kb.sb("stg", [128, 1024], F32) for _ in range(2)]
                load_w_bf(kb, wqkv, lambda k, c0, cw: wqkv[:, k, c0:c0 + cw], W["moba_w_qkv"], 8, 3072, stage)
                xt = [kb.sb("xt", [128, 1024], F32) for _ in range(2)]
                xbf = kb.sb("xbf", [128, 1024], BF16)
                xT = kb.sb("xT", [128, 8, GS], BF16)
                pp = [kb.ps("pp", [128, 512], F32) for _ in range(2)]
                ptr = kb.ps("ptr", [128, 1024], BF16)
                n = 0
                for g in range(S // GS):
                    t0 = base + g * GS
                    for ti in range(4):
                        x_ = xt[ti % 2]
                        kb.dma("sp" if ti % 2 == 0 else "pool", x_[:, :], XIN[t0 + ti * 128:t0 + (ti + 1) * 128, :], r=[XIN], w=[x_])
                        kb.A(lambda e, x_=x_: e.activation(out=xbf[:, :], in_=x_[:, :], func=AF.Copy), r=[x_], w=[xbf])
                        transpose_to(kb, c, xbf, 8, ptr, xT, lambda c0, nn, ti=ti: xT[:, c0:c0 + nn, ti * 128:(ti + 1) * 128])
                    gsl = slice(g * GS, (g + 1) * GS)
                    for pr in range(16):
                        p_ = pp[n % 2]
                        n += 1
                        for k in range(8):
                            kb.T(lambda e, k=k, pr=pr, p_=p_: e.matmul(p_[:, :], lhsT=wqkv[:, k, pr * 128:(pr + 1) * 128], rhs=xT[:, k, :], start=(k == 0), stop=(k == 7)), r=[wqkv, xT], w=[p_])
                        if pr < 8:
                            kb.A(lambda e, pr=pr, p_=p_: e.activation(out=qT[:, pr, gsl], in_=p_[:, :], func=AF.Copy, scale=0.125), r=[p_], w=[qT])
                        else:
                            kb.V(lambda e, pr=pr, p_=p_: e.tensor_copy(out=kT[:, pr - 8, gsl], in_=p_[:, :]), r=[p_], w=[kT])
                    for ti in range(4):
                        for hf in range(2):
                            p_ = pp[n % 2]
                            n += 1
                            for k in range(8):
                                kb.T(lambda e, k=k, ti=ti, hf=hf, p_=p_: e.matmul(p_[:, :], lhsT=xT[:, k, ti * 128:(ti + 1) * 128], rhs=wqkv[:, k, 2048 + hf * 512:2048 + (hf + 1) * 512], start=(k == 0), stop=(k == 7)),
                                     r=[wqkv, xT], w=[p_])
                            if hf == 0:
                                kb.A(lambda e, ti=ti, g=g, p_=p_: e.activation(out=va[:, g * 4 + ti, 0:512], in_=p_[:, :], func=AF.Copy), r=[p_], w=[va])
                            else:
                                kb.V(lambda e, ti=ti, g=g, p_=p_: e.tensor_copy(out=va[:, g * 4 + ti, 512:1024], in_=p_[:, :]), r=[p_], w=[va])
                kb.V(lambda e: e.tensor_reduce(out=kmf[:, :, 0:NBLK], in_=kT[:, :, :].rearrange("p a (b j) -> p a b j", j=256), axis=AX.X, op=ALU.add), r=[kT], w=[kmf])
                kb.A(lambda e: e.activation(out=kmT[:, :, 0:NBLK], in_=kmf[:, :, 0:NBLK], func=AF.Copy, scale=1.0 / 256), r=[kmf], w=[kmT])
            with kb.scope():
                wo = kb.sb("wo", [128, 8, 1024], BF16)
                with kb.scope():
                    stage = [kb.sb("stg", [128, 1024], F32) for _ in range(2)]
                    load_w_bf(kb, wo, lambda k, c0, cw: wo[:, k, c0:c0 + cw], W["moba_w_out"], 8, 1024, stage)
                c2 = kb.sb("c2", [128, C2W], F32)
                kb.dma("sp", c2[:, :], c2dram[:, :], r=[c2dram], w=[c2])
                g_bc = bcast_row(kb, "lnm_g", W["ln_mix_g"], 1024)
                b_bc = bcast_row(kb, "lnm_b", W["ln_mix_b"], 1024, q="pool")
                xt = kb.sb("xt", [128, 1024], F32)
                L = kb.sb("L", [128, S], F32)
                Pb = kb.sb("Pb", [128, S], BF16)
                PT = kb.sb("PT", [128, NQ, 128], BF16)
                gm = kb.sb("gm", [128, 16, 8], F32)
                m8 = kb.sb("m8", [128, 16, 8], F32)
                selb = kb.sb("selb", [128, 16, 8], F32)
                rmax = kb.sb("rmax", [128, 1], F32)
                rsum = kb.sb("rsum", [128, 1], F32)
                attn = kb.sb("attn", [128, 1024], BF16)
                attnT = kb.sb("attnT", [128, 8, 128], BF16)
                ybuf = kb.sb("ybuf", [128, 1024], F32)
                tmp = ln_tmp(kb)
                pl = [kb.ps("pl", [128, 512], F32) for _ in range(4)]
                ptp = kb.ps("ptp", [128, 1024], BF16)
                pv = kb.ps("pv", [128, 512], F32)
                po = [kb.ps("po", [128, 512], F32) for _ in range(2)]
                for qi in range(NQ):
                    q0 = qi * 128
                    own = qi // 2
                    nk = q0 + 128
                    qs = slice(q0, q0 + 128)
                    kb.dma("pool", xt[:, :], XIN[base + q0:base + q0 + 128, :], r=[XIN], w=[xt])
                    gated = own >= 4
                    if gated:
                        pgt = po[1]
                        for h in range(16):
                            pr, r0 = h // 2, (h % 2) * 64
                            kb.T(lambda e, h=h, pr=pr, r0=r0: e.matmul(pgt[:, h * 8:(h + 1) * 8], lhsT=qT[r0:r0 + 64, pr, qs], rhs=kmT[r0:r0 + 64, pr, 0:8], start=True, stop=True), r=[qT, kmT], w=[pgt])
                        kb.V(lambda e: e.tensor_tensor(out=gm[:, :, :], in0=pgt[:, 0:128].rearrange("p (h n) -> p h n", n=8),
                                                       in1=c2[:, NRELW + own * 8:NRELW + own * 8 + 8].unsqueeze(1).broadcast_to([128, 16, 8]), op=ALU.add), r=[pgt, c2], w=[gm])
                        for h in range(16):
                            kb.V(lambda e, h=h: e.max(out=m8[:, h, :], in_=gm[:, h, :]), r=[gm], w=[m8])
                        kb.V(lambda e: e.tensor_tensor(out=selb[:, :, :], in0=gm[:, :, :], in1=m8[:, :, 2:3].broadcast_to([128, 16, 8]), op=ALU.is_ge), r=[gm, m8], w=[selb])
                        kb.V(lambda e: e.tensor_scalar(out=selb[:, :, :], in0=selb[:, :, :], scalar1=1.0, scalar2=1.0e30, op0=ALU.subtract, op1=ALU.mult), r=[selb], w=[selb])
                    for h in range(16):
                        pr, r0 = h // 2, (h % 2) * 64
                        slope = 2.0 ** (-(h + 1) / 2.0)
                        off = 2177 - q0
                        for j in range((nk + 511) // 512):
                            c0, c1 = j * 512, min(nk, (j + 1) * 512)
                            kb.T(lambda e, j=j, c0=c0, c1=c1, pr=pr, r0=r0: e.matmul(pl[j][:, 0:c1 - c0], lhsT=qT[r0:r0 + 64, pr, qs], rhs=kT[r0:r0 + 64, pr, c0:c1], start=True, stop=True), r=[qT, kT], w=[pl[j]])
                            kb.V(lambda e, j=j, c0=c0, c1=c1: e.scalar_tensor_tensor(out=L[:, c0:c1], in0=c2[:, off + c0:off + c1], scalar=slope, in1=pl[j][:, 0:c1 - c0], op0=ALU.mult, op1=ALU.add),
                                 r=[c2, pl[j]], w=[L])
                        if gated:
                            kb.V(lambda e, h=h: e.tensor_tensor(out=L[:, 0:own * 256].rearrange("p (b j) -> p b j", j=256), in0=L[:, 0:own * 256].rearrange("p (b j) -> p b j", j=256),
                                                                in1=selb[:, h, 0:own].unsqueeze(2).broadcast_to([128, own, 256]), op=ALU.add), r=[L, selb], w=[L])
                        kb.V(lambda e: e.tensor_tensor(out=L[:, nk - 128:nk], in0=L[:, nk - 128:nk], in1=c.tri_q(), op=ALU.add), r=[L, c.cf], w=[L])
                        kb.V(lambda e: e.tensor_reduce(out=rmax[:, :], in_=L[:, 0:nk], axis=AX.X, op=ALU.max), r=[L], w=[rmax])
                        kb.V(lambda e: e.tensor_scalar(out=rmax[:, :], in0=rmax[:, :], scalar1=-1.0, scalar2=None, op0=ALU.mult), r=[rmax], w=[rmax])
                        kb.V(lambda e: e.memset(rsum[:, :], 0.0), w=[rsum])
                        kb.A(lambda e: e.activation(out=Pb[:, 0:nk], in_=L[:, 0:nk], func=AF.Exp, bias=rmax[:, 0:1], accum_out=rsum[:, 0:1]), r=[L, rmax, rsum], w=[Pb, rsum])
                        nj = nk // 128
                        for j in range(nj):
                            kb.T(lambda e, j=j: e.transpose(out=ptp[:, (j % 8) * 128:(j % 8 + 1) * 128], in_=Pb[:, j * 128:(j + 1) * 128], identity=c.identb[:, :]), r=[Pb, c.identb], w=[ptp])
                            if j % 8 == 7 or j == nj - 1:
                                j0 = (j // 8) * 8
                                nn = j - j0 + 1
                                o = PT[:, j0:j0 + nn, :]
                                i_ = ptp[:, 0:nn * 128].rearrange("p (a b) -> p a b", b=128)
                                if (j // 8) % 2 == 0:
                                    kb.V(lambda e, o=o, i_=i_: e.tensor_copy(out=o, in_=i_), r=[ptp], w=[PT])
                                else:
                                    kb.A(lambda e, o=o, i_=i_: e.activation(out=o, in_=i_, func=AF.Copy), r=[ptp], w=[PT])
                        for j in range(nj):
                            kb.T(lambda e, j=j, h=h: e.matmul(pv[:, 0:64], lhsT=PT[:, j, :], rhs=va[:, j, h * 64:(h + 1) * 64], start=(j == 0), stop=(j == nj - 1)), r=[PT, va], w=[pv])
                        kb.V(lambda e: e.reciprocal(out=rsum[:, :], in_=rsum[:, :]), r=[rsum], w=[rsum])
                        kb.V(lambda e, h=h: e.tensor_scalar(out=attn[:, h * 64:(h + 1) * 64], in0=pv[:, 0:64], scalar1=rsum[:, 0:1], scalar2=None, op0=ALU.mult), r=[pv, rsum], w=[attn])
                    transpose_to(kb, c, attn, 8, ptp, attnT, lambda c0, nn: attnT[:, c0:c0 + nn, :])
                    for hf in range(2):
                        for k in range(8):
                            kb.T(lambda e, k=k, hf=hf: e.matmul(po[hf][:, :], lhsT=attnT[:, k, :], rhs=wo[:, k, hf * 512:(hf + 1) * 512], start=(k == 0), stop=(k == 7)), r=[attnT, wo], w=[po[hf]])
                    resid_ln_store(kb, xt, po, g_bc, b_bc, ybuf, tmp, Y1, Y1[base + q0:base + q0 + 128, :])


def ssd_phase_a(kb, c, T, S, XIN, W, XS, BTM, BCT, ZS, DT):
    GS = 512
    NG = T // GS
    GPS = S // GS
    with kb.scope():
        win = kb.sb("win", [128, 8, 5152], BF16)
        cw = kb.sb("cw", [128, 24, 4], F32)
        cb = kb.sb("cb", [128, 24], F32)
        with kb.scope():
            stage = [kb.sb("stg", [128, 2048], F32) for _ in range(2)]
            load_w_bf(kb, win, lambda k, c0, cw_: win[:, k, c0:c0 + cw_], W["ssd_w_in"], 8, 5152, stage)
            pst = kb.ps("pst", [128, 512], F32)
            load_cols(kb, c, cw, lambda b: cw[:, b, :], lambda b: W["ssd_conv_w"][:, b * 128:(b + 1) * 128], 4, stage[0], pst, nblk=24)
            load_cols(kb, c, cb, lambda b: cb[:, :], lambda b: W["ssd_conv_b"].rearrange("(c p) -> c p", p=128), 24, stage[1], pst)
        dtb = bcast_row(kb, "dtb", W["ssd_dt_bias"], 32)
        one1 = kb.sb("one1", [128, 1], F32)
        kb.V(lambda e: e.memset(one1[:, :], 1.0), w=[one1])
        xt = [kb.sb("xt", [128, 1024], F32) for _ in range(2)]
        xbf = kb.sb("xbf", [128, 1024], BF16)
        xT = kb.sb("xT", [128, 8, GS], BF16)
        rawH = [kb.sb("rawH", [128, 3 + GS], F32) for _ in range(2)]
        hal = kb.sb("hal", [128, 24, 3], F32)
        cacc = [kb.sb("cacc", [128, GS], F32) for _ in range(2)]
        xbcT = kb.sb("xbcT", [128, 24, GS], BF16)
        xs_sb = [kb.sb("xs_sb", [128, 2048], BF16) for _ in range(2)]
        b_sb = [kb.sb("b_sb", [128, 512], BF16) for _ in range(2)]
        zs_sb = [kb.sb("zs_sb", [128, 2048], BF16) for _ in range(2)]
        dtr = kb.sb("dtr", [128, 32], F32)
        dab = kb.sb("dab", [128, 32], F32)
        dmx = kb.sb("dmx", [128, 32], F32)
        dt_sb = [kb.sb("dt_sb", [128, 32], F32) for _ in range(2)]
        pa = [kb.ps("pa", [128, 512], F32) for _ in range(2)]
        pz = [kb.ps("pz", [128, 512], F32) for _ in range(2)]
        pd = kb.ps("pd", [128, 512], F32)
        ptr = [kb.ps("ptr", [128, 1024], BF16) for _ in range(2)]
        n = 0
        for g in range(NG):
            t0 = g * GS
            for ti in range(4):
                x_ = xt[ti % 2]
                kb.dma("sp" if ti % 2 == 0 else "pool", x_[:, :], XIN[t0 + ti * 128:t0 + (ti + 1) * 128, :], r=[XIN], w=[x_])
                kb.A(lambda e, x_=x_: e.activation(out=xbf[:, :], in_=x_[:, :], func=AF.Copy), r=[x_], w=[xbf])
                transpose_to(kb, c, xbf, 8, ptr[0], xT, lambda c0, nn, ti=ti: xT[:, c0:c0 + nn, ti * 128:(ti + 1) * 128])
            if g % GPS == 0:
                kb.G(lambda e: e.memset(hal[:, :, :], 0.0), w=[hal])
            for fc in range(24):
                p_, rh, ac = pa[fc % 2], rawH[fc % 2], cacc[fc % 2]
                col0 = 2048 + fc * 128
                for k in range(8):
                    kb.T(lambda e, k=k, col0=col0, p_=p_: e.matmul(p_[:, :], lhsT=win[:, k, col0:col0 + 128], rhs=xT[:, k, :], start=(k == 0), stop=(k == 7)), r=[win, xT], w=[p_])
                kb.A(lambda e, p_=p_, rh=rh: e.activation(out=rh[:, 3:3 + GS], in_=p_[:, :], func=AF.Copy), r=[p_], w=[rh])
                kb.G(lambda e, rh=rh, fc=fc: e.tensor_copy(out=rh[:, 0:3], in_=hal[:, fc, :]), r=[hal], w=[rh])
                kb.V(lambda e, rh=rh, ac=ac, fc=fc: e.tensor_scalar(out=ac[:, :], in0=rh[:, 3:3 + GS], scalar1=cw[:, fc, 3:4], scalar2=cb[:, fc:fc + 1], op0=ALU.mult, op1=ALU.add), r=[rh, cw, cb], w=[ac])
                for k in range(3):
                    kb.V(lambda e, rh=rh, ac=ac, fc=fc, k=k: e.scalar_tensor_tensor(out=ac[:, :], in0=rh[:, k:k + GS], scalar=cw[:, fc, k:k + 1], in1=ac[:, :], op0=ALU.mult, op1=ALU.add), r=[rh, cw, ac], w=[ac])
                kb.G(lambda e, rh=rh, fc=fc: e.tensor_copy(out=hal[:, fc, :], in_=rh[:, GS:GS + 3]), r=[rh], w=[hal])
                kb.A(lambda e, ac=ac, fc=fc: e.activation(out=xbcT[:, fc, :], in_=ac[:, :], func=AF.Silu), r=[ac], w=[xbcT])
            for j in range(8):
                kb.dma("sp" if j % 2 == 0 else "pool", BCT[j][:, t0:t0 + GS], xbcT[:, 16 + j, :], r=[xbcT], w=[BCT])
            for ti in range(4):
                tsl = slice(ti * 128, (ti + 1) * 128)
                rows = slice(t0 + ti * 128, t0 + (ti + 1) * 128)
                xs_, b_, zs_, dt_ = xs_sb[ti % 2], b_sb[ti % 2], zs_sb[ti % 2], dt_sb[ti % 2]
                for half in range(2):
                    pt_ = ptr[half]
                    for j in range(8):
                        kb.T(lambda e, j=j, half=half, pt_=pt_: e.transpose(out=pt_[:, j * 128:(j + 1) * 128], in_=xbcT[:, half * 8 + j, tsl], identity=c.identb[:, :]), r=[xbcT, c.identb], w=[pt_])
                    if half == 0:
                        kb.V(lambda e, pt_=pt_, xs_=xs_: e.tensor_copy(out=xs_[:, 0:1024], in_=pt_[:, :]), r=[pt_], w=[xs_])
                    else:
                        kb.A(lambda e, pt_=pt_, xs_=xs_: e.activation(out=xs_[:, 1024:2048], in_=pt_[:, :], func=AF.Copy), r=[pt_], w=[xs_])
                kb.dma("sp", XS[rows, :], xs_[:, :], r=[xs_], w=[XS])
                for j in range(4):
                    kb.T(lambda e, j=j: e.transpose(out=ptr[0][:, j * 128:(j + 1) * 128], in_=xbcT[:, 16 + j, tsl], identity=c.identb[:, :]), r=[xbcT, c.identb], w=[ptr[0]])
                kb.V(lambda e, b_=b_: e.tensor_copy(out=b_[:, :], in_=ptr[0][:, 0:512]), r=[ptr[0]], w=[b_])
                kb.dma("pool", BTM[rows, :], b_[:, :], r=[b_], w=[BTM])
                for sl in range(4):
                    p_ = pz[n % 2]
                    n += 1
                    for k in range(8):
                        kb.T(lambda e, k=k, sl=sl, p_=p_: e.matmul(p_[:, :], lhsT=xT[:, k, tsl], rhs=win[:, k, sl * 512:(sl + 1) * 512], start=(k == 0), stop=(k == 7)), r=[xT, win], w=[p_])
                    kb.A(lambda e, sl=sl, p_=p_, zs_=zs_: e.activation(out=zs_[:, sl * 512:(sl + 1) * 512], in_=p_[:, :], func=AF.Silu), r=[p_], w=[zs_])
                kb.dma("sp", ZS[rows, :], zs_[:, :], r=[zs_], w=[ZS])
                for k in range(8):
                    kb.T(lambda e, k=k: e.matmul(pd[:, 0:32], lhsT=xT[:, k, tsl], rhs=win[:, k, 5120:5152], start=(k == 0), stop=(k == 7)), r=[xT, win], w=[pd])
                kb.V(lambda e: e.tensor_tensor(out=dtr[:, :], in0=pd[:, 0:32], in1=dtb[:, :], op=ALU.add), r=[pd, dtb], w=[dtr])
                kb.A(lambda e: e.activation(out=dab[:, :], in_=dtr[:, :], func=AF.Abs), r=[dtr], w=[dab])
                kb.A(lambda e: e.activation(out=dab[:, :], in_=dab[:, :], func=AF.Exp, scale=-1.0), r=[dab], w=[dab])
                kb.A(lambda e: e.activation(out=dab[:, :], in_=dab[:, :], func=AF.Ln, bias=one1[:, 0:1]), r=[dab, one1], w=[dab])
                kb.V(lambda e: e.tensor_single_scalar(out=dmx[:, :], in_=dtr[:, :], scalar=0.0, op=ALU.max), r=[dtr], w=[dmx])
                kb.V(lambda e, dt_=dt_: e.tensor_tensor(out=dt_[:, :], in0=dmx[:, :], in1=dab[:, :], op=ALU.add), r=[dmx, dab], w=[dt_])
                kb.dma("pool", DT[rows, :], dt_[:, :], r=[dt_], w=[DT])


def ssd_phase_b(kb, c, T, S, XIN, Y1, W, XS, BTM, BCT, ZS, DT):
    NC_ = T // 128
    CPS = S // 128
    with kb.scope():
        wout = kb.sb("wout", [128, 16, 1024], BF16)
        with kb.scope():
            stage = [kb.sb("stg", [128, 1024], F32) for _ in range(2)]
            load_w_bf(kb, wout, lambda k, c0, cw_: wout[:, k, c0:c0 + cw_], W["ssd_w_out"], 16, 1024, stage)
        g_bc = bcast_row(kb, "lnm_g", W["ln_mix_g"], 1024)
        b_bc = bcast_row(kb, "lnm_b", W["ln_mix_b"], 1024, q="pool")
        ng_bc = bcast_row(kb, "ng_bc", W["ssd_norm_g"], 2048)
        aneg = bcast_row(kb, "aneg", W["ssd_a_log"], 32, q="pool")
        kb.A(lambda e: e.activation(out=aneg[:, :], in_=aneg[:, :], func=AF.Exp), r=[aneg], w=[aneg])
        kb.V(lambda e: e.tensor_scalar(out=aneg[:, :], in0=aneg[:, :], scalar1=-1.0, scalar2=None, op0=ALU.mult), r=[aneg], w=[aneg])
        dsk = bcast_row(kb, "dsk", W["ssd_d"], 32)
        xt = kb.sb("xt", [128, 1024], F32)
        xs = kb.sb("xs", [128, 2048], BF16)
        bt = kb.sb("bt", [128, 512], BF16)
        zs = kb.sb("zs", [128, 2048], BF16)
        dt = kb.sb("dt", [128, 32], F32)
        bct = kb.sb("bct", [128, 8, 128], BF16)
        dtA = kb.sb("dtA", [128, 32], F32)
        acs = kb.sb("acs", [128, 64], F32)
        ea = kb.sb("ea", [128, 32], F32)
        dte = kb.sb("dte", [128, 32], F32)
        cd = kb.sb("cd", [128, 32], F32)
        xdt = kb.sb("xdt", [128, 2048], BF16)
        xe = kb.sb("xe", [128, 2048], BF16)
        Mh = kb.sb("Mh", [128, 32, 128], F32)
        cbt = kb.sb("cbt", [128, 4, 128], F32)
        Dm = [kb.sb("Dm", [128, 4, 128], F32) for _ in range(2)]
        Wd = kb.sb("Wd", [128, 32, 128], BF16)
        yoff = kb.sb("yoff", [128, 2048], F32)
        y = kb.sb("y", [128, 2048], F32)
        t2 = kb.sb("t2", [128, 2048], F32)
        ss = kb.sb("ss", [128, 4], F32)
        gnb = kb.sb("gnb", [128, 2048], BF16)
        gnT = kb.sb("gnT", [128, 16, 128], BF16)
        H = kb.sb("H", [128, 2048], F32)
        Hbf = kb.sb("Hbf", [128, 2048], BF16)
        ybuf = kb.sb("ybuf", [128, 1024], F32)
        tmp = ln_tmp(kb)
        py = [kb.ps("py", [128, 512], F32) for _ in range(4)]
        pd = [kb.ps("pd", [128, 512], F32) for _ in range(2)]
        pm = kb.ps("pm", [128, 512], F32)
        ptr = kb.ps("ptr", [128, 1024], BF16)
        v3 = lambda ap: ap.rearrange("p (h d) -> p h d", d=64)
        for ci in range(NC_):
            rows = slice(ci * 128, (ci + 1) * 128)
            kb.dma("sp", xs[:, :], XS[rows, :], r=[XS], w=[xs])
            kb.dma("pool", zs[:, :], ZS[rows, :], r=[ZS], w=[zs])
            kb.dma("sp", bt[:, :], BTM[rows, :], r=[BTM], w=[bt])
            kb.dma("pool", dt[:, :], DT[rows, :], r=[DT], w=[dt])
            kb.dma("sp", bct[:, :, :], BCT.t.rearrange("j p t -> p j t")[:, :, rows], r=[BCT], w=[bct])
            kb.dma("pool", xt[:, :], XIN[rows, :], r=[XIN], w=[xt])
            if ci % CPS == 0:
                kb.G(lambda e: e.memset(H[:, :], 0.0), w=[H])
                kb.G(lambda e: e.memset(Hbf[:, :], 0.0), w=[Hbf])
            kb.V(lambda e: e.tensor_tensor(out=dtA[:, :], in0=dt[:, :], in1=aneg[:, :], op=ALU.mult), r=[dt, aneg], w=[dtA])
            kb.T(lambda e: e.matmul(pm[:, 0:32], lhsT=c.triu(), rhs=dtA[:, :], start=True, stop=True), r=[c.cf, dtA], w=[pm])
            kb.T(lambda e: e.matmul(pm[:, 32:64], lhsT=c.ones(), rhs=dtA[:, :], start=True, stop=True), r=[c.cf, dtA], w=[pm])
            kb.V(lambda e: e.tensor_copy(out=acs[:, :], in_=pm[:, 0:64]), r=[pm], w=[acs])
            kb.A(lambda e: e.activation(out=ea[:, :], in_=acs[:, 0:32], func=AF.Exp), r=[acs], w=[ea])
            kb.V(lambda e: e.tensor_tensor(out=dte[:, :], in0=acs[:, 32:64], in1=acs[:, 0:32], op=ALU.subtract), r=[acs], w=[dte])
            kb.A(lambda e: e.activation(out=dte[:, :], in_=dte[:, :], func=AF.Exp), r=[dte], w=[dte])
            kb.V(lambda e: e.tensor_tensor(out=dte[:, :], in0=dte[:, :], in1=dt[:, :], op=ALU.mult), r=[dte, dt], w=[dte])
            kb.A(lambda e: e.activation(out=cd[:, :], in_=acs[:, 32:64], func=AF.Exp), r=[acs], w=[cd])
            kb.V(lambda e: e.tensor_tensor(out=v3(xdt[:, :]), in0=v3(xs[:, :]), in1=dt[:, :].unsqueeze(2).broadcast_to([128, 32, 64]), op=ALU.mult), r=[xs, dt], w=[xdt])
            kb.G(lambda e: e.tensor_tensor(out=v3(xe[:, :]), in0=v3(xs[:, :]), in1=dte[:, :].unsqueeze(2).broadcast_to([128, 32, 64]), op=ALU.mult), r=[xs, dte], w=[xe])
            kb.V(lambda e: e.tensor_tensor(out=Mh[:, :, :], in0=c.triu().unsqueeze(1).broadcast_to([128, 32, 128]), in1=dtA[:, :].unsqueeze(2).broadcast_to([128, 32, 128]), op=ALU.mult), r=[c.cf, dtA], w=[Mh])
            for g in range(4):
                kb.T(lambda e, g=g: e.matmul(pm[:, g * 128:(g + 1) * 128], lhsT=bct[:, g, :], rhs=bct[:, 4 + g, :], start=True, stop=True), r=[bct], w=[pm])
            kb.V(lambda e: e.tensor_copy(out=cbt[:, :, :], in_=pm[:, :].rearrange("p (a b) -> p a b", b=128)), r=[pm], w=[cbt])
            for g in range(4):
                gs = slice(g * 512, (g + 1) * 512)
                kb.T(lambda e, g=g, gs=gs: e.matmul(py[g][:, :], lhsT=bct[:, 4 + g, :], rhs=Hbf[:, gs], start=True, stop=True), r=[bct, Hbf], w=[py[g]])
                kb.V(lambda e, g=g, gs=gs: e.tensor_tensor(out=v3(yoff[:, gs]), in0=v3(py[g][:, :]), in1=ea[:, g * 8:(g + 1) * 8].unsqueeze(2).broadcast_to([128, 8, 64]), op=ALU.mult), r=[py[g], ea], w=[yoff])
            for hb in range(8):
                p_, d_ = pd[hb % 2], Dm[hb % 2]
                for i in range(4):
                    h = hb * 4 + i
                    kb.T(lambda e, i=i, h=h, p_=p_: e.matmul(p_[:, i * 128:(i + 1) * 128], lhsT=c.ones(), rhs=Mh[:, h, :], start=True, stop=False), r=[c.cf, Mh], w=[p_])
                    kb.T(lambda e, i=i, h=h, p_=p_: e.matmul(p_[:, i * 128:(i + 1) * 128], lhsT=Mh[:, h, :], rhs=c.negones(), start=False, stop=True), r=[c.cf, Mh], w=[p_])
                kb.V(lambda e, p_=p_, d_=d_: e.tensor_tensor(out=d_[:, :, :], in0=p_[:, :].rearrange("p (a b) -> p a b", b=128), in1=c.negmask().unsqueeze(1).broadcast_to([128, 4, 128]), op=ALU.add), r=[p_, c.cf], w=[d_])
                kb.A(lambda e, d_=d_: e.activation(out=d_[:, :, :], in_=d_[:, :, :], func=AF.Exp), r=[d_], w=[d_])
                kb.G(lambda e, d_=d_, hb=hb: e.tensor_tensor(out=Wd[:, hb * 4:(hb + 1) * 4, :], in0=d_[:, :, :], in1=cbt[:, hb // 2, :].unsqueeze(1).broadcast_to([128, 4, 128]), op=ALU.mult), r=[d_, cbt], w=[Wd])
            for h in range(32):
                g = h // 8
                kb.T(lambda e, h=h, g=g: e.matmul(py[g][:, (h % 8) * 64:(h % 8 + 1) * 64], lhsT=Wd[:, h, :], rhs=xdt[:, h * 64:(h + 1) * 64], start=True, stop=True), r=[Wd, xdt], w=[py[g]])
            for g in range(4):
                gs = slice(g * 512, (g + 1) * 512)
                kb.V(lambda e, g=g, gs=gs: e.tensor_tensor(out=y[:, gs], in0=py[g][:, :], in1=yoff[:, gs], op=ALU.add), r=[py[g], yoff], w=[y])
            kb.G(lambda e: e.tensor_tensor(out=v3(t2[:, :]), in0=v3(xs[:, :]), in1=dsk[:, :].unsqueeze(2).broadcast_to([128, 32, 64]), op=ALU.mult), r=[xs, dsk], w=[t2])
            kb.V(lambda e: e.tensor_tensor(out=y[:, :], in0=y[:, :], in1=t2[:, :], op=ALU.add), r=[y, t2], w=[y])
            kb.V(lambda e: e.tensor_tensor(out=y[:, :], in0=y[:, :], in1=zs[:, :], op=ALU.mult), r=[y, zs], w=[y])
            kb.V(lambda e: e.memset(ss[:, :], 0.0), w=[ss])
            for g in range(4):
                gs = slice(g * 512, (g + 1) * 512)
                kb.A(lambda e, g=g, gs=gs: e.activation(out=t2[:, gs], in_=y[:, gs], func=AF.Square, accum_out=ss[:, g:g + 1]), r=[y, ss], w=[t2, ss])
            kb.A(lambda e: e.activation(out=ss[:, :], in_=ss[:, :], func=AF.Ln, scale=1.0 / 512, bias=CONST.eps[:, 0:1]), r=[ss, CONST.eps], w=[ss])
            kb.A(lambda e: e.activation(out=ss[:, :], in_=ss[:, :], func=AF.Exp, scale=-0.5), r=[ss], w=[ss])
            kb.V(lambda e: e.tensor_tensor(out=y[:, :].rearrange("p (g d) -> p g d", d=512), in0=y[:, :].rearrange("p (g d) -> p g d", d=512), in1=ss[:, :].unsqueeze(2).broadcast_to([128, 4, 512]), op=ALU.mult), r=[y, ss], w=[y])
            kb.G(lambda e: e.tensor_tensor(out=gnb[:, :], in0=y[:, :], in1=ng_bc[:, :], op=ALU.mult), r=[y, ng_bc], w=[gnb])
            transpose_to(kb, c, gnb, 16, ptr, gnT, lambda c0, nn: gnT[:, c0:c0 + nn, :])
            po = pd
            for hf in range(2):
                for k in range(16):
                    kb.T(lambda e, k=k, hf=hf: e.matmul(po[hf][:, :], lhsT=gnT[:, k, :], rhs=wout[:, k, hf * 512:(hf + 1) * 512], start=(k == 0), stop=(k == 15)), r=[gnT, wout], w=[po[hf]])
            resid_ln_store(kb, xt, po, g_bc, b_bc, ybuf, tmp, Y1, Y1[rows, :])
            for g in range(4):
                gs = slice(g * 512, (g + 1) * 512)
                kb.T(lambda e, g=g, gs=gs: e.matmul(py[g][:, :], lhsT=bt[:, g * 128:(g + 1) * 128], rhs=xe[:, gs], start=True, stop=True), r=[bt, xe], w=[py[g]])
            kb.V(lambda e: e.tensor_tensor(out=v3(H[:, :]), in0=v3(H[:, :]), in1=cd[:, :].unsqueeze(2).broadcast_to([128, 32, 64]), op=ALU.mult), r=[H, cd], w=[H])
            for g in range(4):
                gs = slice(g * 512, (g + 1) * 512)
                kb.V(lambda e, g=g, gs=gs: e.tensor_tensor(out=H[:, gs], in0=H[:, gs], in1=py[g][:, :], op=ALU.add), r=[H, py[g]], w=[H])
            kb.A(lambda e: e.activation(out=Hbf[:, :], in_=H[:, :], func=AF.Copy), r=[H], w=[Hbf])


W_SHAPES = {
    "ssd_w_in": (2, 1024, 5152), "ssd_conv_w": (2, 4, 3072), "ssd_conv_b": (2, 3072), "ssd_dt_bias": (2, 32),
    "ssd_a_log": (2, 32), "ssd_d": (2, 32), "ssd_norm_g": (2, 2048), "ssd_w_out": (2, 2048, 1024),
    "moba_w_qkv": (1, 1024, 3072), "moba_w_out": (1, 1024, 1024),
    "conv_w_pw1": (1, 1024, 2048), "conv_b_pw1": (1, 2048), "conv_w_dw": (1, 31, 1024), "conv_b_dw": (1, 1024),
    "conv_ln_g": (1, 1024), "conv_ln_b": (1, 1024), "conv_w_pw2": (1, 1024, 1024),
    "peer_w_q": (4, 1024, 2048), "peer_sub_keys": (4, 8, 2, 128, 128), "peer_u": (4, 16384, 1024), "peer_v": (4, 16384, 1024),
    "ln_mix_g": (4, 1024), "ln_mix_b": (4, 1024), "ln_ffn_g": (4, 1024), "ln_ffn_b": (4, 1024),
    "ple_w_gate": (4, 1024, 1024), "ple_w_proj": (4, 256, 1024),
}
DEPTH = 4
PER_LAYER = ("peer_w_q", "peer_sub_keys", "peer_u", "peer_v", "ln_mix_g", "ln_mix_b", "ln_ffn_g", "ln_ffn_b", "ple_w_gate", "ple_w_proj")


def build_full(T, S, depth=DEPTH, TG=256):
    nc = bass.Bass("TRN2", target_bir_lowering=False)
    kb = KB(nc)
    with nc.allow_low_precision("bf16 matmul operands with fp32 accumulation"):
        cd = kb.dram("consts", [128, CW], F32, kind="ExternalInput")
        c2d = kb.dram("c2", [128, C2W], F32, kind="ExternalInput")
        X = kb.dram("x", [T, 1024], F32, kind="ExternalInput")
        P = kb.dram("p", [DEPTH, T, 256], F32, kind="ExternalInput")
        OUT = kb.dram("out", [T, 1024], F32, kind="ExternalOutput")
        Wd = {k: kb.dram(k, list(shp), F32, kind="ExternalInput").t for k, shp in W_SHAPES.items()}
        XA = [kb.dram("xa%d" % i, [T, 1024], F32) for i in range(2)]
        Y1 = kb.dram("y1", [T, 1024], F32)
        UVs = kb.dram("UVs", [128, 128, 2048], BF16)
        XTd = kb.dram("XTd", [8, 128, T], BF16)
        QTd = kb.dram("QTd", [16, 128, T], BF16)
        XS = kb.dram("XS", [T, 2048], BF16)
        BTM = kb.dram("BTM", [T, 512], BF16)
        BCT = kb.dram("BCT", [8, 128, T], BF16)
        ZS = kb.dram("ZS", [T, 2048], BF16)
        DT = kb.dram("DT", [T, 32], F32)
        c = load_consts(kb, cd)
        xin = X
        for i in range(depth):
            kind, j = i % 3, i // 3
            W = {}
            for k in W_SHAPES:
                if k in PER_LAYER:
                    W[k] = Wd[k][i]
                elif k.startswith(("ssd_", "moba_", "conv_")):
                    n = W_SHAPES[k][0]
                    W[k] = Wd[k][min(j, n - 1)]
            if kind == 0:
                ssd_phase_a(kb, c, T, S, xin, W, XS, BTM, BCT, ZS, DT)
                ssd_phase_b(kb, c, T, S, xin, Y1, W, XS, BTM, BCT, ZS, DT)
            elif kind == 1:
                moba_phase(kb, c, T, S, xin, Y1, W, c2d)
            else:
                conf_phase(kb, c, T, S, xin, Y1, W)
            peer_prepass(kb, c, W["peer_u"], W["peer_v"], UVs)
            xout = OUT if i == depth - 1 else XA[i % 2]
            peer_q_phase(kb, c, T, Y1, W, XTd, QTd)
            peer_phase(kb, c, T, Y1, xout, P.t[i], W, UVs, XTd, QTd, TG=TG)
            xin = xout
        kb.barrier()
    return nc, kb


_CACHE = {}


def kernel(**inputs):
    NCORE = 8
    B, S = inputs["x"].shape[0], inputs["x"].shape[1]
    per = B // NCORE
    T = per * S
    key = (T, S)
    if key not in _CACHE:
        _CACHE[key] = build_full(T, S)[0]
    nc = _CACHE[key]
    consts, c2 = host_consts(), host_c2()
    shared = {k: np.ascontiguousarray(np.asarray(inputs[k], dtype=np.float32)) for k in W_SHAPES}
    x = np.asarray(inputs["x"], dtype=np.float32)
    p = np.asarray(inputs["p"], dtype=np.float32)
    in_maps = []
    for ci in range(NCORE):
        m = dict(shared)
        m["consts"] = consts
        m["c2"] = c2
        m["x"] = np.ascontiguousarray(x[ci * per:(ci + 1) * per].reshape(T, 1024))
        m["p"] = np.ascontiguousarray(p[:, ci * per:(ci + 1) * per].reshape(DEPTH, T, 256))
        in_maps.append(m)
    res = run_bass_kernel_spmd(nc, in_maps, core_ids=list(range(NCORE)))
    outs = [np.asarray(r["out"]).reshape(per, S, 1024) for r in res.results]
    return np.concatenate(outs, axis=0).astype(np.float32)
```

```python
import numpy as np
from contextlib import ExitStack, contextmanager
import concourse.bass as bass
import concourse.mybir as mybir
from concourse.bass_utils import run_bass_kernel_spmd

F32 = mybir.dt.float32
BF16 = mybir.dt.bfloat16
I32 = mybir.dt.int32
U32 = mybir.dt.uint32
AF = mybir.ActivationFunctionType
ALU = mybir.AluOpType
AX = mybir.AxisListType

D = 1024
ALPHA = 8.0 ** 0.25
EPS = 1e-5
NEG = -1.0e30
KD = 8


class Buf:
    __slots__ = ("t", "w", "r", "name")

    def __init__(self, t, name=""):
        self.t = t
        self.w = None
        self.r = {}
        self.name = name

    def __getitem__(self, idx):
        return self.t[idx]


class Alias:
    def __init__(self, parent, t):
        self.__dict__["parent"] = parent
        self.__dict__["t"] = t

    def __getitem__(self, idx):
        return self.t[idx]

    def __getattr__(self, k):
        return getattr(self.__dict__["parent"], k)

    def __setattr__(self, k, v):
        setattr(self.__dict__["parent"], k, v)


class KB:
    def __init__(self, nc):
        self.nc = nc
        self.root = ExitStack()
        self.stacks = [self.root]
        self.eng = {}
        for name, h in (("pe", nc.tensor), ("act", nc.scalar), ("dve", nc.vector), ("pool", nc.gpsimd), ("sp", nc.sync)):
            sem = self.root.enter_context(nc.semaphore("s_" + name))
            self.eng[name] = dict(h=h, sem=sem, cnt=0, seen={}, name=name)
        self.dq = {}
        for q in ("sp", "pool", "act"):
            sems = [self.root.enter_context(nc.semaphore("d_%s%d" % (q, i))) for i in range(KD)]
            self.dq[q] = dict(sems=sems, n=0, cnt=[0] * KD)
        self.uid = 0
        self.ninst = 0

    @contextmanager
    def scope(self):
        st = ExitStack()
        self.stacks.append(st)
        try:
            yield
        finally:
            self.barrier()
            self.stacks.pop()
            st.close()

    def _nm(self, name):
        self.uid += 1
        return "%s_%d" % (name, self.uid)

    def sb(self, name, shape, dt):
        t = self.stacks[-1].enter_context(self.nc.sbuf_tensor(self._nm(name), list(shape), dt))
        return Buf(t, name)

    def ps(self, name, shape, dt):
        t = self.stacks[-1].enter_context(self.nc.psum_tensor(self._nm(name), list(shape), dt))
        return Buf(t, name)

    def dram(self, name, shape, dt, kind="Internal"):
        t = self.nc.dram_tensor(name, list(shape), dt, kind=kind)
        return Buf(t.ap(), name)

    def _wait(self, e, deps):
        best = {}
        for sem, val in deps:
            k = id(sem)
            if k not in best or best[k][1] < val:
                best[k] = (sem, val)
        for k, (sem, val) in best.items():
            if e["seen"].get(k, 0) >= val:
                continue
            e["h"].wait_ge(sem, val)
            e["seen"][k] = val

    def _deps(self, e, r, w, skip_self, is_dma=False):
        deps = []
        me = None if is_dma else id(e["sem"])
        for b in r:
            if b.w is not None:
                deps.append(b.w)
        for b in w:
            if b.w is not None:
                deps.append(b.w)
            for k, tok in b.r.items():
                deps.append(tok)
        if skip_self:
            deps = [d for d in deps if id(d[0]) != me]
        return deps

    def _mark(self, tok, r, w):
        for b in w:
            b.w = tok
            b.r = {}
        k = id(tok[0])
        wroots = [getattr(x, "parent", x) for x in w]
        for b in r:
            if not any(getattr(b, "parent", b) is x for x in wroots):
                b.r[k] = tok

    def op(self, en, fn, r=(), w=()):
        e = self.eng[en]
        self._wait(e, self._deps(e, r, w, en == "pe"))
        inst = fn(e["h"])
        e["cnt"] += 1
        self.ninst += 1
        inst.then_inc(e["sem"], 1)
        tok = (e["sem"], e["cnt"])
        self._mark(tok, r, w)
        return tok

    def V(self, fn, r=(), w=()):
        return self.op("dve", fn, r, w)

    def A(self, fn, r=(), w=()):
        return self.op("act", fn, r, w)

    def G(self, fn, r=(), w=()):
        return self.op("pool", fn, r, w)

    def T(self, fn, r=(), w=()):
        return self.op("pe", fn, r, w)

    def dma(self, q, out, in_, r=(), w=()):
        e = self.eng[q]
        d = self.dq[q]
        i = d["n"] % KD
        d["n"] += 1
        sem = d["sems"][i]
        deps = self._deps(e, r, w, False, True)
        if d["cnt"][i] > 0:
            deps.append((sem, d["cnt"][i] * 16))
        self._wait(e, deps)
        e["h"].dma_start(out=out, in_=in_).then_inc(sem, 16)
        self.ninst += 1
        d["cnt"][i] += 1
        tok = (sem, d["cnt"][i] * 16)
        self._mark(tok, r, w)
        return tok

    def barrier(self):
        toks = []
        for e in self.eng.values():
            if e["cnt"] > 0:
                toks.append((e["sem"], e["cnt"]))
        for d in self.dq.values():
            for i in range(KD):
                if d["cnt"][i] > 0:
                    toks.append((d["sems"][i], d["cnt"][i] * 16))
        for e in self.eng.values():
            self._wait(e, toks)


class Consts:
    pass


CONST = None


def load_consts(kb, cdram):
    c = Consts()
    cf = kb.sb("cf", [128, CW], F32)
    kb.dma("sp", cf[:, :], cdram[:, :], r=[cdram], w=[cf])
    c.cf = cf
    c.identf = lambda: cf[:, 0:128]
    c.ones = lambda: cf[:, 128:256]
    c.triu = lambda: cf[:, 256:384]
    c.negmask = lambda: cf[:, 384:512]
    c.tri_q = lambda: cf[:, 512:640]
    c.iota128 = lambda: cf[:, 640:768]
    c.negones = lambda: cf[:, 768:896]
    ib = kb.sb("identb", [128, 128], BF16)
    kb.V(lambda e: e.tensor_copy(out=ib[:, :], in_=cf[:, 0:128]), r=[cf], w=[ib])
    c.identb = ib
    io = kb.sb("iotab", [128, 128], BF16)
    kb.V(lambda e: e.tensor_copy(out=io[:, :], in_=cf[:, 640:768]), r=[cf], w=[io])
    c.iotab = io
    c.eps = kb.sb("epsc", [128, 1], F32)
    kb.V(lambda e: e.memset(c.eps[:, :], EPS), w=[c.eps])
    global CONST
    CONST = c
    return c


CW = 896
NRELW = 2432


def host_consts():
    c = np.zeros((128, CW), np.float32)
    i = np.arange(128)
    c[:, 0:128] = np.eye(128, dtype=np.float32)
    c[:, 128:256] = 1.0
    c[:, 256:384] = (i[:, None] <= i[None, :]).astype(np.float32)
    c[:, 384:512] = np.where(i[:, None] <= i[None, :], 0.0, NEG)
    c[:, 512:640] = np.where(i[None, :] <= i[:, None], 0.0, NEG)
    c[:, 640:768] = i[None, :].astype(np.float32)
    c[:, 768:896] = -1.0
    return c


def host_nrel():
    return np.ascontiguousarray(np.broadcast_to((np.arange(NRELW) - 2304)[None, :].astype(np.float32), (128, NRELW)))


def bcast_row(kb, name, src_ap, n, q="sp", rbuf=None):
    t = kb.sb(name, [128, n], F32)
    kb.dma(q, t[:, :], src_ap.partition_broadcast(128), r=[rbuf] if rbuf else [], w=[t])
    return t


def load_w_bf(kb, dst, dst_ap_fn, src_ap, rows_k, cols, stage, cast_engs=("act", "pool")):
    step = stage[0].t.shape[1]
    n = 0
    for k in range(rows_k):
        for c0 in range(0, cols, step):
            cw = min(step, cols - c0)
            st = stage[n % len(stage)]
            kb.dma("sp" if n % 2 == 0 else "pool", st[:, 0:cw], src_ap[k * 128:(k + 1) * 128, c0:c0 + cw], w=[st])
            en = cast_engs[n % len(cast_engs)]
            if en == "act":
                kb.A(lambda e, st=st, k=k, c0=c0, cw=cw: e.activation(out=dst_ap_fn(k, c0, cw), in_=st[:, 0:cw], func=AF.Copy), r=[st], w=[dst])
            else:
                kb.op(en, lambda e, st=st, k=k, c0=c0, cw=cw: e.tensor_copy(out=dst_ap_fn(k, c0, cw), in_=st[:, 0:cw]), r=[st], w=[dst])
            n += 1


def transpose_to(kb, c, src_bf, ncol_chunks, pst, dst, dst_ap, evac="dve"):
    for c0 in range(0, ncol_chunks, 8):
        nn = min(8, ncol_chunks - c0)
        for j in range(nn):
            kb.T(lambda e, j=j, c0=c0: e.transpose(out=pst[:, j * 128:(j + 1) * 128], in_=src_bf[:, (c0 + j) * 128:(c0 + j + 1) * 128], identity=c.identb[:, :]),
                 r=[src_bf, c.identb], w=[pst])
        o = dst_ap(c0, nn)
        i = pst[:, 0:nn * 128].rearrange("p (a b) -> p a b", b=128)
        if evac == "act":
            kb.A(lambda e, o=o, i=i: e.activation(out=o, in_=i, func=AF.Copy), r=[pst], w=[dst])
        else:
            kb.V(lambda e, o=o, i=i: e.tensor_copy(out=o, in_=i), r=[pst], w=[dst])


def layer_norm(kb, y, g_bc, b_bc, out, stats, mv, rstd):
    kb.V(lambda e: e.bn_stats(out=stats[:, 0:6], in_=y[:, 0:512]), r=[y], w=[stats])
    kb.V(lambda e: e.bn_stats(out=stats[:, 6:12], in_=y[:, 512:1024]), r=[y], w=[stats])
    kb.V(lambda e: e.bn_aggr(out=mv[:, 0:2], in_=stats[:, 0:12]), r=[stats], w=[mv])
    kb.A(lambda e: e.activation(out=rstd[:, 0:1], in_=mv[:, 1:2], func=AF.Ln, bias=CONST.eps[:, 0:1]), r=[mv, CONST.eps], w=[rstd])
    kb.A(lambda e: e.activation(out=rstd[:, 0:1], in_=rstd[:, 0:1], func=AF.Exp, scale=-0.5), r=[rstd], w=[rstd])
    kb.V(lambda e: e.tensor_scalar(out=out[:, :], in0=y[:, :], scalar1=mv[:, 0:1], scalar2=rstd[:, 0:1], op0=ALU.subtract, op1=ALU.mult), r=[y, mv, rstd], w=[out])
    kb.G(lambda e: e.tensor_tensor(out=out[:, :], in0=out[:, :], in1=g_bc[:, :], op=ALU.mult), r=[out, g_bc], w=[out])
    kb.G(lambda e: e.tensor_tensor(out=out[:, :], in0=out[:, :], in1=b_bc[:, :], op=ALU.add), r=[out, b_bc], w=[out])


def resid_ln_store(kb, xt, mixps, g_bc, b_bc, ybuf, tmp, dst_dram, dst_ap, q="sp"):
    for h in range(2):
        kb.V(lambda e, h=h: e.scalar_tensor_tensor(out=ybuf[:, h * 512:(h + 1) * 512], in0=xt[:, h * 512:(h + 1) * 512], scalar=ALPHA,
                                                  in1=mixps[h][:, 0:512], op0=ALU.mult, op1=ALU.add), r=[xt, mixps[h]], w=[ybuf])
    layer_norm(kb, ybuf, g_bc, b_bc, ybuf, tmp["stats"], tmp["mv"], tmp["rstd"])
    if dst_ap is not None:
        kb.dma(q, dst_ap, ybuf[:, :], r=[ybuf], w=[dst_dram])


def ln_tmp(kb):
    return dict(stats=kb.sb("stats", [128, 12], F32), mv=kb.sb("mv", [128, 2], F32), rstd=kb.sb("rstd", [128, 1], F32))


GELU_MODE = "af"


def peer_prepass(kb, c, u_ap, v_ap, UVs, NCH=128):
    with kb.scope():
        st = [kb.sb("pp_st", [128, 1024], F32) for _ in range(4)]
        ub = [kb.sb("pp_ub", [128, 1024], BF16) for _ in range(2)]
        ut = [kb.sb("pp_ut", [128, 1024], BF16) for _ in range(2)]
        vb = [kb.sb("pp_vb", [128, 1024], BF16) for _ in range(2)]
        pst = [kb.ps("pp_ps", [128, 1024], BF16) for _ in range(2)]
        for ch in range(NCH):
            su, sv = st[(2 * ch) % 4], st[(2 * ch + 1) % 4]
            kb.dma("sp", su[:, :], u_ap[ch * 128:(ch + 1) * 128, :], w=[su])
            kb.dma("pool", sv[:, :], v_ap[ch * 128:(ch + 1) * 128, :], w=[sv])
            b = ub[ch % 2]
            kb.A(lambda e, b=b, su=su: e.activation(out=b[:, :], in_=su[:, :], func=AF.Copy), r=[su], w=[b])
            p = pst[ch % 2]
            for k in range(8):
                kb.T(lambda e, k=k, b=b, p=p: e.transpose(out=p[:, k * 128:(k + 1) * 128], in_=b[:, k * 128:(k + 1) * 128], identity=c.identb[:, :]),
                     r=[b, c.identb], w=[p])
            t = ut[ch % 2]
            kb.V(lambda e, t=t, p=p: e.tensor_copy(out=t[:, :], in_=p[:, :]), r=[p], w=[t])
            kb.dma("sp", UVs[ch][:, 0:1024], t[:, :], r=[t], w=[UVs])
            vv = vb[ch % 2]
            kb.G(lambda e, vv=vv, sv=sv: e.tensor_copy(out=vv[:, :], in_=sv[:, :]), r=[sv], w=[vv])
            kb.dma("pool", UVs[ch][:, 1024:2048], vv[:, :], r=[vv], w=[UVs])


def peer_q_phase(kb, c, T, Y1, W, XTd, QTd):
    GS = 512
    with kb.scope():
        wq = kb.sb("wq", [128, 8, 2048], BF16)
        with kb.scope():
            stage = [kb.sb("stg", [128, 2048], F32) for _ in range(2)]
            load_w_bf(kb, wq, lambda k, c0, cw: wq[:, k, c0:c0 + cw], W["peer_w_q"], 8, 2048, stage)
        xt = [kb.sb("xt", [128, 1024], F32) for _ in range(2)]
        xbf = [kb.sb("xbf", [128, 1024], BF16) for _ in range(2)]
        xT = [kb.sb("xT5", [128, 8, GS], BF16) for _ in range(2)]
        qT = [kb.sb("qT5", [128, 16, GS], BF16) for _ in range(2)]
        pq = [kb.ps("pq", [128, 512], F32) for _ in range(4)]
        ptr = [kb.ps("ptr", [128, 1024], BF16) for _ in range(2)]
        XTv = XTd.t.rearrange("k p t -> p k t")
        QTv = QTd.t.rearrange("k p t -> p k t")
        n = 0
        for b in range(T // GS):
            t0 = b * GS
            x5, q5 = xT[b % 2], qT[b % 2]
            for ti in range(4):
                x_, xb_ = xt[ti % 2], xbf[ti % 2]
                kb.dma("sp" if ti % 2 == 0 else "pool", x_[:, :], Y1[t0 + ti * 128:t0 + (ti + 1) * 128, :], r=[Y1], w=[x_])
                kb.A(lambda e, x_=x_, xb_=xb_: e.activation(out=xb_[:, :], in_=x_[:, :], func=AF.Copy), r=[x_], w=[xb_])
                transpose_to(kb, c, xb_, 8, ptr[ti % 2], x5, lambda c0, nn, ti=ti, x5=x5: x5[:, c0:c0 + nn, ti * 128:(ti + 1) * 128])
            kb.dma("sp", XTv[:, :, t0:t0 + GS], x5[:, :, :], r=[x5], w=[XTd])
            for hc in range(16):
                p_ = pq[n % 4]
                n += 1
                for k in range(8):
                    kb.T(lambda e, hc=hc, k=k, p_=p_, x5=x5: e.matmul(p_[:, :], lhsT=wq[:, k, hc * 128:(hc + 1) * 128], rhs=x5[:, k, :], start=(k == 0), stop=(k == 7)), r=[wq, x5], w=[p_])
                if hc % 2 == 0:
                    kb.A(lambda e, hc=hc, p_=p_, q5=q5: e.activation(out=q5[:, hc, :], in_=p_[:, :], func=AF.Copy), r=[p_], w=[q5])
                else:
                    kb.V(lambda e, hc=hc, p_=p_, q5=q5: e.tensor_copy(out=q5[:, hc, :], in_=p_[:, :]), r=[p_], w=[q5])
            kb.dma("pool", QTv[:, :, t0:t0 + GS], q5[:, :, :], r=[q5], w=[QTd])


def peer_phase(kb, c, T, Y1, OUT, p_ap, W, UVs, XTd, QTd, TG=256, NCH=128):
    NT = TG // 128
    NG = T // TG
    TB = 8
    with kb.scope():
        wg = kb.sb("wg", [128, 8, 1024], BF16)
        wp = kb.sb("wp", [128, 2, 1024], BF16)
        skT = kb.sb("skT", [128, 16, 128], BF16)
        with kb.scope():
            stage = [kb.sb("stg", [128, 2048], F32) for _ in range(2)]
            load_w_bf(kb, wg, lambda k, c0, cw: wg[:, k, c0:c0 + cw], W["ple_w_gate"], 8, 1024, stage)
            load_w_bf(kb, wp, lambda k, c0, cw: wp[:, k, c0:c0 + cw], W["ple_w_proj"], 2, 1024, stage)
            pskt = kb.ps("pskt", [128, 512], F32)
            sk = W["peer_sub_keys"].rearrange("h c n d -> (h c) n d")
            for hc in range(16):
                st = stage[hc % 2]
                kb.dma("sp", st[:, 0:128], sk[hc], w=[st])
                kb.T(lambda e, st=st: e.transpose(out=pskt[:, 0:128], in_=st[:, 0:128], identity=c.identf()), r=[st, c.cf], w=[pskt])
                kb.V(lambda e, hc=hc: e.tensor_copy(out=skT[:, hc, :], in_=pskt[:, 0:128]), r=[pskt], w=[skT])
        g_bc = bcast_row(kb, "lnf_g", W["ln_ffn_g"], 1024)
        b_bc = bcast_row(kb, "lnf_b", W["ln_ffn_b"], 1024, q="pool")
        iotaC = kb.sb("iotaC", [128, TB, 128], BF16)
        kb.V(lambda e: e.tensor_copy(out=iotaC[:, :, :], in_=c.iota128().unsqueeze(1).broadcast_to([128, TB, 128])), r=[c.cf], w=[iotaC])
        iota16 = c.iota128()[:, 0:16]

        xT2 = [kb.sb("xT", [128, 8, TG], BF16) for _ in range(2)]
        qT2 = [kb.sb("qT", [128, 16, TG], BF16) for _ in range(2)]
        T32 = [kb.sb("T3", [128, 3, TG], F32) for _ in range(2)]
        xt = kb.sb("xt", [128, 1024], F32)
        S = kb.sb("S", [128, 16, 128], F32)
        S2 = kb.sb("S2", [128, 256], F32)
        V16 = kb.sb("V16", [128, 16, 16], F32)
        I16u = kb.sb("I16u", [128, 16, 16], U32)
        I16f = kb.sb("I16f", [128, 16, 16], F32)
        cand = kb.sb("cand", [128, 8, 256], F32)
        B16 = kb.sb("B16", [128, 8, 16], F32)
        P16u = kb.sb("P16u", [128, 8, 16], U32)
        ABu = kb.sb("ABu", [128, 2, 128], U32)
        ABf = kb.sb("ABf", [128, 2, 128], F32)
        eq = kb.sb("eq", [128, 8, 16, 16], F32)
        J = kb.sb("J", [128, 3, 128], F32)
        e16 = kb.sb("e16", [128, 8, 16], F32)
        ssum = kb.sb("ssum", [128, 8], F32)
        OH1 = [kb.sb("OH1", [128, TB, 128], BF16) for _ in range(2)]
        OH2g = [kb.sb("OH2g", [128, TB, 128], BF16) for _ in range(2)]
        GT = kb.sb("GT", [128, TG, 128], BF16)
        NB = 4
        UVb = [kb.sb("UVb", [128, 2048], BF16) for _ in range(NB)]
        Aact = [kb.sb("Aact", [128, TG], F32) for _ in range(2)]
        Wtb = [kb.sb("Wtb", [128, TG], BF16) for _ in range(3)]
        ybuf = kb.sb("ybuf", [128, 1024], F32)
        y2bf = kb.sb("y2bf", [128, 1024], BF16)
        y2T = kb.sb("y2T", [128, 8, 128], BF16)
        pt = kb.sb("pt", [128, 256], F32)
        pbf = kb.sb("pbf", [128, 256], BF16)
        pT = kb.sb("pT", [128, 2, 128], BF16)
        Sflat = S[:, :, :].rearrange("p a b -> p (a b)")
        sg = Alias(S, Sflat[:, 0:1024])
        ob = Alias(S, Sflat[:, 1024:2048])
        tmp = ln_tmp(kb)
        acc = [[kb.ps("acc", [128, 512], F32) for _ in range(2)] for _ in range(NT)]
        stp = [kb.ps("stp", [128, 512], F32) for _ in range(2)]
        mps = kb.ps("mps", [128, 512], F32)
        mpsb = kb.ps("mpsb", [128, 1024], BF16)
        if NT == 1:
            gps2 = kb.ps("gps", [128, 512], F32)
        gbanks = [mps] + [b for pair in acc for b in pair] if NT > 1 else [mps, gps2] + [b for pair in acc for b in pair]
        XTv = XTd.t.rearrange("k p t -> p k t")
        QTv = QTd.t.rearrange("k p t -> p k t")

        def stage1(g):
            t0 = g * TG
            xT, qT, T3 = xT2[g % 2], qT2[g % 2], T32[g % 2]
            kb.dma("sp", xT[:, :, :], XTv[:, :, t0:t0 + TG], r=[XTd], w=[xT])
            kb.dma("pool", qT[:, :, :], QTv[:, :, t0:t0 + TG], r=[QTd], w=[qT])
            yield
            for ti in range(NT):
                tsl = slice(ti * 128, (ti + 1) * 128)
                for h4 in range(4):
                    for j in range(4):
                        hc = h4 * 4 + j
                        kb.T(lambda e, hc=hc, j=j: e.matmul(mps[:, j * 128:(j + 1) * 128], lhsT=qT[:, hc, tsl], rhs=skT[:, hc, :], start=True, stop=True),
                             r=[qT, skT], w=[mps])
                    kb.V(lambda e, h4=h4: e.tensor_copy(out=S[:, h4 * 4:(h4 + 1) * 4, :], in_=mps[:, :].rearrange("p (a b) -> p a b", b=128)), r=[mps], w=[S])
                    yield
                for hc in range(16):
                    kb.V(lambda e, hc=hc: e.max(out=V16[:, hc, 0:8], in_=S[:, hc, :]), r=[S], w=[V16])
                    kb.V(lambda e, hc=hc: e.max_index(out=I16u[:, hc, 0:8], in_max=V16[:, hc, 0:8], in_values=S[:, hc, :]), r=[S, V16], w=[I16u])
                    kb.V(lambda e, hc=hc: e.match_replace(out=S2[:, 0:128], in_to_replace=V16[:, hc, 0:8], in_values=S[:, hc, :], imm_value=NEG), r=[S, V16], w=[S2])
                    yield
                    kb.V(lambda e, hc=hc: e.max(out=V16[:, hc, 8:16], in_=S2[:, 0:128]), r=[S2], w=[V16])
                    kb.V(lambda e, hc=hc: e.max_index(out=I16u[:, hc, 8:16], in_max=V16[:, hc, 8:16], in_values=S2[:, 0:128]), r=[S2, V16], w=[I16u])
                    yield
                kb.V(lambda e: e.tensor_copy(out=I16f[:, :, :], in_=I16u[:, :, :]), r=[I16u], w=[I16f])
                V4 = V16[:, :, :].rearrange("p (h c) k -> p h c k", c=2)
                I4 = I16f[:, :, :].rearrange("p (h c) k -> p h c k", c=2)
                cand4 = cand[:, :, :].rearrange("p h (a b) -> p h a b", b=16)
                kb.V(lambda e: e.tensor_tensor(out=cand4, in0=V4[:, :, 0, :].unsqueeze(3).broadcast_to([128, 8, 16, 16]),
                                               in1=V4[:, :, 1, :].unsqueeze(2).broadcast_to([128, 8, 16, 16]), op=ALU.add), r=[V16], w=[cand])
                yield
                for h in range(8):
                    kb.V(lambda e, h=h: e.max(out=B16[:, h, 0:8], in_=cand[:, h, :]), r=[cand], w=[B16])
                    kb.V(lambda e, h=h: e.max_index(out=P16u[:, h, 0:8], in_max=B16[:, h, 0:8], in_values=cand[:, h, :]), r=[cand, B16], w=[P16u])
                    kb.V(lambda e, h=h: e.match_replace(out=S2[:, :], in_to_replace=B16[:, h, 0:8], in_values=cand[:, h, :], imm_value=NEG), r=[cand, B16], w=[S2])
                    yield
                    kb.V(lambda e, h=h: e.max(out=B16[:, h, 8:16], in_=S2[:, :]), r=[S2], w=[B16])
                    kb.V(lambda e, h=h: e.max_index(out=P16u[:, h, 8:16], in_max=B16[:, h, 8:16], in_values=S2[:, :]), r=[S2, B16], w=[P16u])
                    yield
                Pfl = P16u[:, :, :].rearrange("p h k -> p (h k)")
                kb.V(lambda e: e.tensor_single_scalar(out=ABu[:, 0, :], in_=Pfl, scalar=4, op=ALU.logical_shift_right), r=[P16u], w=[ABu])
                kb.V(lambda e: e.tensor_single_scalar(out=ABu[:, 1, :], in_=Pfl, scalar=15, op=ALU.bitwise_and), r=[P16u], w=[ABu])
                kb.V(lambda e: e.tensor_copy(out=ABf[:, :, :], in_=ABu[:, :, :]), r=[ABu], w=[ABf])
                yield
                for ci in range(2):
                    ab4 = ABf[:, ci, :].rearrange("p (h k) -> p h k", k=16).unsqueeze(3).broadcast_to([128, 8, 16, 16])
                    kb.V(lambda e, ab4=ab4: e.tensor_tensor(out=eq[:, :, :, :], in0=ab4, in1=iota16.unsqueeze(1).unsqueeze(1).broadcast_to([128, 8, 16, 16]), op=ALU.is_equal),
                         r=[ABf, c.cf], w=[eq])
                    yield
                    kb.V(lambda e, ci=ci: e.tensor_tensor(out=eq[:, :, :, :], in0=eq[:, :, :, :], in1=I4[:, :, ci, :].unsqueeze(2).broadcast_to([128, 8, 16, 16]), op=ALU.mult),
                         r=[eq, I16f], w=[eq])
                    yield
                    kb.V(lambda e, ci=ci: e.tensor_reduce(out=J[:, ci, :].rearrange("p (h k) -> p h k", k=16), in_=eq[:, :, :, :], axis=AX.X, op=ALU.add), r=[eq], w=[J])
                    yield
                kb.V(lambda e: e.tensor_tensor(out=e16[:, :, :], in0=B16[:, :, :], in1=B16[:, :, 0:1].broadcast_to([128, 8, 16]), op=ALU.subtract), r=[B16], w=[e16])
                kb.A(lambda e: e.activation(out=e16[:, :, :], in_=e16[:, :, :], func=AF.Exp), r=[e16], w=[e16])
                kb.V(lambda e: e.tensor_reduce(out=ssum[:, :], in_=e16[:, :, :], axis=AX.X, op=ALU.add), r=[e16], w=[ssum])
                yield
                kb.V(lambda e: e.reciprocal(out=ssum[:, :], in_=ssum[:, :]), r=[ssum], w=[ssum])
                kb.V(lambda e: e.tensor_tensor(out=J[:, 2, :].rearrange("p (h k) -> p h k", k=16), in0=e16[:, :, :], in1=ssum[:, :].unsqueeze(2).broadcast_to([128, 8, 16]), op=ALU.mult),
                     r=[e16, ssum], w=[J])
                yield
                for q3 in range(3):
                    kb.T(lambda e, q3=q3: e.transpose(out=mps[:, q3 * 128:(q3 + 1) * 128], in_=J[:, q3, :], identity=c.identf()), r=[J, c.cf], w=[mps])
                kb.V(lambda e: e.tensor_copy(out=T3[:, :, tsl], in_=mps[:, 0:384].rearrange("p (a b) -> p a b", b=128)), r=[mps], w=[T3])
                yield

        def stage2(g):
            T3f = T32[g % 2]
            nsb = 0
            for s0 in range(0, TG, TB):
                o1, o2g = OH1[nsb % 2], OH2g[nsb % 2]
                nsb += 1
                for tb in range(TB):
                    t_ = s0 + tb
                    kb.V(lambda e, o1=o1, tb=tb, t_=t_: e.tensor_scalar(out=o1[:, tb, :], in0=c.iotab[:, :], scalar1=T3f[:, 0, t_:t_ + 1], scalar2=None, op0=ALU.is_equal),
                         r=[c.iotab, T3f], w=[o1])
                    kb.V(lambda e, o2g=o2g, tb=tb, t_=t_: e.tensor_scalar(out=o2g[:, tb, :], in0=c.iotab[:, :], scalar1=T3f[:, 1, t_:t_ + 1], scalar2=T3f[:, 2, t_:t_ + 1], op0=ALU.is_equal, op1=ALU.mult),
                         r=[c.iotab, T3f], w=[o2g])
                for tb in range(TB):
                    gp = gbanks[((s0 + tb) // 4) % len(gbanks)]
                    kb.T(lambda e, gp=gp, tb=tb, o1=o1, o2g=o2g: e.matmul(gp[:, (tb % 4) * 128:(tb % 4 + 1) * 128], lhsT=o2g[:, tb, :], rhs=o1[:, tb, :], start=True, stop=True),
                         r=[o1, o2g], w=[gp])
                    if tb % 4 == 3:
                        tt = s0 + tb - 3
                        kb.A(lambda e, gp=gp, tt=tt: e.activation(out=GT[:, tt:tt + 4, :], in_=gp[:, :].rearrange("p (a b) -> p a b", b=128), func=AF.Copy), r=[gp], w=[GT])

        def chunk_loop(g, nxt):
            xT = xT2[g % 2]

            def emit_u(ch):
                uvb = UVb[ch % NB]
                kb.dma("sp" if ch % 2 == 0 else "pool", uvb[:, :], UVs[ch], r=[UVs], w=[uvb])
                ub = Alias(uvb, uvb[:, 0:1024])
                sp_ = stp[ch % 2]
                for k in range(8):
                    kb.T(lambda e, k=k, ub=ub, sp_=sp_: e.matmul(sp_[:, 0:TG], lhsT=ub[:, k * 128:(k + 1) * 128], rhs=xT[:, k, :], start=(k == 0), stop=(k == 7)),
                         r=[ub, xT], w=[sp_])
                a_, w_ = Aact[ch % 2], Wtb[ch % 3]
                kb.A(lambda e, a_=a_, sp_=sp_: e.activation(out=a_[:, :], in_=sp_[:, 0:TG], func=AF.Gelu_apprx_tanh), r=[sp_], w=[a_])
                kb.V(lambda e, a_=a_, w_=w_, ch=ch: e.tensor_tensor(out=w_[:, :], in0=a_[:, :], in1=GT[:, :, ch], op=ALU.mult), r=[a_, GT], w=[w_])

            def emit_v(ch):
                uvb = UVb[ch % NB]
                vb2 = Alias(uvb, uvb[:, 1024:2048])
                w_ = Wtb[ch % 3]
                for ti in range(NT):
                    for hf in range(2):
                        kb.T(lambda e, ti=ti, hf=hf, w_=w_, vb2=vb2, ch=ch: e.matmul(acc[ti][hf][:, 0:512], lhsT=w_[:, ti * 128:(ti + 1) * 128], rhs=vb2[:, hf * 512:(hf + 1) * 512],
                                                                                  start=(ch == 0), stop=(ch == NCH - 1)), r=[w_, vb2], w=[acc[ti][hf]])

            for ch in range(NCH):
                emit_u(ch)
                if ch >= 1:
                    emit_v(ch - 1)
                if nxt is not None and ch >= 2:
                    next(nxt, None)
            emit_v(NCH - 1)
            if nxt is not None:
                for _ in nxt:
                    pass

        def epilogue(g):
            t0 = g * TG
            for ti in range(NT):
                tok0 = t0 + ti * 128
                kb.dma("sp", xt[:, :], Y1[tok0:tok0 + 128, :], r=[Y1], w=[xt])
                kb.dma("pool", pt[:, :], p_ap[tok0:tok0 + 128, :], w=[pt])
                resid_ln_store(kb, xt, acc[ti], g_bc, b_bc, ybuf, tmp, None, None)
                kb.A(lambda e: e.activation(out=y2bf[:, :], in_=ybuf[:, :], func=AF.Copy), r=[ybuf], w=[y2bf])
                transpose_to(kb, c, y2bf, 8, mpsb, y2T, lambda c0, nn: y2T[:, c0:c0 + nn, :])
                kb.A(lambda e: e.activation(out=pbf[:, :], in_=pt[:, :], func=AF.Copy), r=[pt], w=[pbf])
                transpose_to(kb, c, pbf, 2, mpsb, pT, lambda c0, nn: pT[:, c0:c0 + nn, :])
                for hf in range(2):
                    hs = slice(hf * 512, (hf + 1) * 512)
                    for k in range(8):
                        kb.T(lambda e, k=k, hs=hs: e.matmul(mps[:, :], lhsT=y2T[:, k, :], rhs=wg[:, k, hs], start=(k == 0), stop=(k == 7)), r=[y2T, wg], w=[mps])
                    kb.A(lambda e, hs=hs: e.activation(out=sg[:, hs], in_=mps[:, :], func=AF.Sigmoid), r=[mps], w=[sg])
                    for k in range(2):
                        kb.T(lambda e, k=k, hs=hs: e.matmul(mps[:, :], lhsT=pT[:, k, :], rhs=wp[:, k, hs], start=(k == 0), stop=(k == 1)), r=[pT, wp], w=[mps])
                    kb.V(lambda e, hs=hs: e.tensor_tensor(out=ob[:, hs], in0=sg[:, hs], in1=mps[:, :], op=ALU.mult), r=[sg, mps], w=[ob])
                kb.G(lambda e: e.tensor_tensor(out=ob[:, :], in0=ob[:, :], in1=ybuf[:, :], op=ALU.add), r=[ob, ybuf], w=[ob])
                kb.dma("sp", OUT[tok0:tok0 + 128, :], ob[:, :], r=[ob], w=[OUT])

        for _ in stage1(0):
            pass
        for g in range(NG):
            nxt = stage1(g + 1) if g + 1 < NG else None
            stage2(g)
            chunk_loop(g, nxt)
            epilogue(g)


def load_cols(kb, c, dst, dst_ap_fn, src_rows_ap, R, stage, pst, nblk=1, blk_stride=0):
    for b in range(nblk):
        kb.dma("sp", stage[0:R, 0:128], src_rows_ap(b), w=[stage])
        kb.T(lambda e: e.transpose(out=pst[:, 0:R], in_=stage[0:R, 0:128], identity=c.cf[0:R, 0:R]), r=[stage, c.cf], w=[pst])
        kb.V(lambda e, b=b: e.tensor_copy(out=dst_ap_fn(b), in_=pst[:, 0:R]), r=[pst], w=[dst])


def conf_phase(kb, c, T, S, XIN, Y1, W):
    GS = 512
    NG = T // GS
    GPS = S // GS
    KW = 31
    with kb.scope():
        w1 = kb.sb("w1", [128, 8, 2048], BF16)
        w2 = kb.sb("w2", [128, 8, 1024], BF16)
        b1 = kb.sb("b1", [128, 16], F32)
        wdw = kb.sb("wdw", [128, 8, KW], F32)
        vecs = kb.sb("vecs", [128, 3, 8], F32)
        with kb.scope():
            stage = [kb.sb("stg", [128, 2048], F32) for _ in range(2)]
            load_w_bf(kb, w1, lambda k, c0, cw: w1[:, k, c0:c0 + cw], W["conv_w_pw1"], 8, 2048, stage)
            load_w_bf(kb, w2, lambda k, c0, cw: w2[:, k, c0:c0 + cw], W["conv_w_pw2"], 8, 1024, stage)
            pst = kb.ps("pst", [128, 512], F32)
            load_cols(kb, c, b1, lambda b: b1[:, :], lambda b: W["conv_b_pw1"].rearrange("(c p) -> c p", p=128), 16, stage[0], pst)
            load_cols(kb, c, wdw, lambda b: wdw[:, b, :], lambda b: W["conv_w_dw"][:, b * 128:(b + 1) * 128], KW, stage[1], pst, nblk=8)
            for i, nm in enumerate(("conv_b_dw", "conv_ln_g", "conv_ln_b")):
                load_cols(kb, c, vecs, lambda b, i=i: vecs[:, i, :], lambda b, nm=nm: W[nm].rearrange("(c p) -> c p", p=128), 8, stage[i % 2], pst)
        g_bc = bcast_row(kb, "lnm_g", W["ln_mix_g"], 1024)
        b_bc = bcast_row(kb, "lnm_b", W["ln_mix_b"], 1024, q="pool")
        xt = [kb.sb("xt", [128, 1024], F32) for _ in range(4)]
        xbf = kb.sb("xbf", [128, 1024], BF16)
        xT = kb.sb("xT", [128, 8, GS], BF16)
        gluH = kb.sb("gluH", [128, 8, KW - 1 + GS], F32)
        hc = kb.sb("hc", [128, 8, GS], F32)
        hsq = kb.sb("hsq", [128, 8, GS], F32)
        zT = kb.sb("zT", [128, 8, GS], BF16)
        sgb = [kb.sb("sgb", [128, GS], F32) for _ in range(2)]
        mean = kb.sb("mean", [128, GS], F32)
        msq = kb.sb("msq", [128, GS], F32)
        rstd = kb.sb("rstd2", [128, GS], F32)
        tn = [kb.sb("tn", [128, GS], F32) for _ in range(2)]
        ybuf = kb.sb("ybuf", [128, 1024], F32)
        tmp = ln_tmp(kb)
        pa = kb.ps("pa", [128, 512], F32)
        pg = kb.ps("pg", [128, 512], F32)
        s1 = kb.ps("s1", [128, 512], F32)
        s2 = kb.ps("s2", [128, 512], F32)
        po = [kb.ps("po", [128, 512], F32) for _ in range(2)]
        ptr = kb.ps("ptr", [128, 1024], BF16)
        H = KW - 1
        for g in range(NG):
            t0 = g * GS
            for ti in range(4):
                kb.dma("sp" if ti % 2 == 0 else "pool", xt[ti][:, :], XIN[t0 + ti * 128:t0 + (ti + 1) * 128, :], r=[XIN], w=[xt[ti]])
                kb.A(lambda e, ti=ti: e.activation(out=xbf[:, :], in_=xt[ti][:, :], func=AF.Copy), r=[xt[ti]], w=[xbf])
                transpose_to(kb, c, xbf, 8, ptr, xT, lambda c0, nn, ti=ti: xT[:, c0:c0 + nn, ti * 128:(ti + 1) * 128])
            if g % GPS == 0:
                kb.G(lambda e: e.memset(gluH[:, :, 0:H], 0.0), w=[gluH])
            for cc in range(8):
                for k in range(8):
                    kb.T(lambda e, k=k, cc=cc: e.matmul(pa[:, :], lhsT=w1[:, k, cc * 128:(cc + 1) * 128], rhs=xT[:, k, :], start=(k == 0), stop=(k == 7)), r=[w1, xT], w=[pa])
                for k in range(8):
                    kb.T(lambda e, k=k, cc=cc: e.matmul(pg[:, :], lhsT=w1[:, k, 1024 + cc * 128:1024 + (cc + 1) * 128], rhs=xT[:, k, :], start=(k == 0), stop=(k == 7)), r=[w1, xT], w=[pg])
                sg_ = sgb[cc % 2]
                kb.A(lambda e, sg_=sg_, cc=cc: e.activation(out=sg_[:, :], in_=pg[:, :], func=AF.Sigmoid, bias=b1[:, 8 + cc:9 + cc]), r=[pg, b1], w=[sg_])
                kb.V(lambda e, sg_=sg_, cc=cc: e.scalar_tensor_tensor(out=gluH[:, cc, H:H + GS], in0=pa[:, :], scalar=b1[:, cc:cc + 1], in1=sg_[:, :], op0=ALU.add, op1=ALU.mult),
                     r=[pa, b1, sg_], w=[gluH])
                kb.V(lambda e, cc=cc: e.tensor_scalar(out=hc[:, cc, :], in0=gluH[:, cc, H:H + GS], scalar1=wdw[:, cc, H:H + 1], scalar2=vecs[:, 0, cc:cc + 1], op0=ALU.mult, op1=ALU.add),
                     r=[gluH, wdw, vecs], w=[hc])
                for k in range(H):
                    kb.V(lambda e, cc=cc, k=k: e.scalar_tensor_tensor(out=hc[:, cc, :], in0=gluH[:, cc, k:k + GS], scalar=wdw[:, cc, k:k + 1], in1=hc[:, cc, :], op0=ALU.mult, op1=ALU.add),
                         r=[gluH, wdw, hc], w=[hc])
            kb.G(lambda e: e.tensor_copy(out=gluH[:, :, 0:H], in_=gluH[:, :, GS:GS + H]), r=[gluH], w=[gluH])
            kb.A(lambda e: e.activation(out=hsq[:, :, :], in_=hc[:, :, :], func=AF.Square), r=[hc], w=[hsq])
            for cc in range(8):
                kb.T(lambda e, cc=cc: e.matmul(s1[:, :], lhsT=c.ones(), rhs=hc[:, cc, :], start=(cc == 0), stop=(cc == 7)), r=[c.cf, hc], w=[s1])
            for cc in range(8):
                kb.T(lambda e, cc=cc: e.matmul(s2[:, :], lhsT=c.ones(), rhs=hsq[:, cc, :], start=(cc == 0), stop=(cc == 7)), r=[c.cf, hsq], w=[s2])
            kb.V(lambda e: e.tensor_scalar(out=mean[:, :], in0=s1[:, :], scalar1=1.0 / 1024, scalar2=None, op0=ALU.mult), r=[s1], w=[mean])
            kb.V(lambda e: e.tensor_tensor(out=msq[:, :], in0=mean[:, :], in1=mean[:, :], op=ALU.mult), r=[mean], w=[msq])
            kb.V(lambda e: e.scalar_tensor_tensor(out=msq[:, :], in0=s2[:, :], scalar=1.0 / 1024, in1=msq[:, :], op0=ALU.mult, op1=ALU.subtract), r=[s2, msq], w=[msq])
            kb.A(lambda e: e.activation(out=rstd[:, :], in_=msq[:, :], func=AF.Ln, bias=CONST.eps[:, 0:1]), r=[msq, CONST.eps], w=[rstd])
            kb.A(lambda e: e.activation(out=rstd[:, :], in_=rstd[:, :], func=AF.Exp, scale=-0.5), r=[rstd], w=[rstd])
            for cc in range(8):
                t_ = tn[cc % 2]
                kb.G(lambda e, cc=cc, t_=t_: e.tensor_tensor(out=t_[:, :], in0=hc[:, cc, :], in1=mean[:, :], op=ALU.subtract), r=[hc, mean], w=[t_])
                kb.V(lambda e, t_=t_: e.tensor_tensor(out=t_[:, :], in0=t_[:, :], in1=rstd[:, :], op=ALU.mult), r=[t_, rstd], w=[t_])
                kb.V(lambda e, cc=cc, t_=t_: e.tensor_scalar(out=t_[:, :], in0=t_[:, :], scalar1=vecs[:, 1, cc:cc + 1], scalar2=vecs[:, 2, cc:cc + 1], op0=ALU.mult, op1=ALU.add), r=[t_, vecs], w=[t_])
                kb.A(lambda e, cc=cc, t_=t_: e.activation(out=zT[:, cc, :], in_=t_[:, :], func=AF.Silu), r=[t_], w=[zT])
            for ti in range(4):
                for hf in range(2):
                    for cc in range(8):
                        kb.T(lambda e, ti=ti, hf=hf, cc=cc: e.matmul(po[hf][:, :], lhsT=zT[:, cc, ti * 128:(ti + 1) * 128], rhs=w2[:, cc, hf * 512:(hf + 1) * 512], start=(cc == 0), stop=(cc == 7)),
                             r=[zT, w2], w=[po[hf]])
                resid_ln_store(kb, xt[ti], po, g_bc, b_bc, ybuf, tmp, Y1, Y1[t0 + ti * 128:t0 + (ti + 1) * 128, :], q="sp" if ti % 2 == 0 else "pool")


C2W = NRELW + 72


def host_c2():
    c2 = np.zeros((128, C2W), np.float32)
    c2[:, 0:NRELW] = (np.arange(NRELW) - 2304)[None, :]
    for own in range(9):
        c2[:, NRELW + own * 8:NRELW + own * 8 + 8] = np.where(np.arange(8) < own, 0.0, NEG)[None, :]
    return c2


def moba_phase(kb, c, T, S, XIN, Y1, W, c2dram):
    NSEQ = T // S
    NQ = S // 128
    NBLK = S // 256
    GS = 512
    with kb.scope():
        qT = kb.sb("qT_all", [128, 8, S], BF16)
        kT = kb.sb("kT_all", [128, 8, S], BF16)
        va = kb.sb("v_all", [128, NQ, 1024], BF16)
        kmf = kb.sb("kmf", [128, 8, 8], F32)
        kmT = kb.sb("kmT", [128, 8, 8], BF16)
        for sq in range(NSEQ):
            base = sq * S
            with kb.scope():
                wqkv = kb.sb("wqkv", [128, 8, 3072], BF16)
                stage = [kb.sb("stg", [128, 1024], F32) for _ in range(2)]
                load_w_bf(kb, wqkv, lambda k, c0, cw: wqkv[:, k, c0:c0 + cw], W["moba_w_qkv"], 8, 3072, stage)
                xt = [kb.sb("xt", [128, 1024], F32) for _ in range(2)]
                xbf = kb.sb("xbf", [128, 1024], BF16)
                xT = kb.sb("xT", [128, 8, GS], BF16)
                pp = [kb.ps("pp", [128, 512], F32) for _ in range(2)]
                ptr = kb.ps("ptr", [128, 1024], BF16)
                n = 0
                for g in range(S // GS):
                    t0 = base + g * GS
                    for ti in range(4):
                        x_ = xt[ti % 2]
                        kb.dma("sp" if ti % 2 == 0 else "pool", x_[:, :], XIN[t0 + ti * 128:t0 + (ti + 1) * 128, :], r=[XIN], w=[x_])
                        kb.A(lambda e, x_=x_: e.activation(out=xbf[:, :], in_=x_[:, :], func=AF.Copy), r=[x_], w=[xbf])
                        transpose_to(kb, c, xbf, 8, ptr, xT, lambda c0, nn, ti=ti: xT[:, c0:c0 + nn, ti * 128:(ti + 1) * 128])
                    gsl = slice(g * GS, (g + 1) * GS)
                    for pr in range(16):
                        p_ = pp[n % 2]
                        n += 1
                        for k in range(8):
                            kb.T(lambda e, k=k, pr=pr, p_=p_: e.matmul(p_[:, :], lhsT=wqkv[:, k, pr * 128:(pr + 1) * 128], rhs=xT[:, k, :], start=(k == 0), stop=(k == 7)), r=[wqkv, xT], w=[p_])
                        if pr < 8:
                            kb.A(lambda e, pr=pr, p_=p_: e.activation(out=qT[:, pr, gsl], in_=p_[:, :], func=AF.Copy, scale=0.125), r=[p_], w=[qT])
                        else:
                            kb.V(lambda e, pr=pr, p_=p_: e.tensor_copy(out=kT[:, pr - 8, gsl], in_=p_[:, :]), r=[p_], w=[kT])
                    for ti in range(4):
                        for hf in range(2):
                            p_ = pp[n % 2]
                            n += 1
                            for k in range(8):
                                kb.T(lambda e, k=k, ti=ti, hf=hf, p_=p_: e.matmul(p_[:, :], lhsT=xT[:, k, ti * 128:(ti + 1) * 128], rhs=wqkv[:, k, 2048 + hf * 512:2048 + (hf + 1) * 512], start=(k == 0), stop=(k == 7)),
                                     r=[wqkv, xT], w=[p_])
                            if hf == 0:
                                kb.A(lambda e, ti=ti, g=g, p_=p_: e.activation(out=va[:, g * 4 + ti, 0:512], in_=p_[:, :], func=AF.Copy), r=[p_], w=[va])
                            else:
                                kb.V(lambda e, ti=ti, g=g, p_=p_: e.tensor_copy(out=va[:, g * 4 + ti, 512:1024], in_=p_[:, :]), r=[p_], w=[va])
                kb.V(lambda e: e.tensor_reduce(out=kmf[:, :, 0:NBLK], in_=kT[:, :, :].rearrange("p a (b j) -> p a b j", j=256), axis=AX.X, op=ALU.add), r=[kT], w=[kmf])
                kb.A(lambda e: e.activation(out=kmT[:, :, 0:NBLK], in_=kmf[:, :, 0:NBLK], func=AF.Copy, scale=1.0 / 256), r=[kmf], w=[kmT])
            with kb.scope():
                wo = kb.sb("wo", [128, 8, 1024], BF16)
                with kb.scope():
                    stage = [kb.sb("stg", [128, 1024], F32) for _ in range(2)]
                    load_w_bf(kb, wo, lambda k, c0, cw: wo[:, k, c0:c0 + cw], W["moba_w_out"], 8, 1024, stage)
                c2 = kb.sb("c2", [128, C2W], F32)
                kb.dma("sp", c2[:, :], c2dram[:, :], r=[c2dram], w=[c2])
                g_bc = bcast_row(kb, "lnm_g", W["ln_mix_g"], 1024)
                b_bc = bcast_row(kb, "lnm_b", W["ln_mix_b"], 1024, q="pool")
                xt = kb.sb("xt", [128, 1024], F32)
                L2 = [kb.sb("L", [128, S], F32) for _ in range(2)]
                Pb2 = [kb.sb("Pb", [128, S], BF16) for _ in range(2)]
                PT2 = [kb.sb("PT", [128, NQ, 128], BF16) for _ in range(2)]
                gm = kb.sb("gm", [128, 16, 8], F32)
                m8 = kb.sb("m8", [128, 16, 8], F32)
                selb = kb.sb("selb", [128, 16, 8], F32)
                rmax2 = [kb.sb("rmax", [128, 1], F32) for _ in range(2)]
                rsum2 = [kb.sb("rsum", [128, 1], F32) for _ in range(2)]
                attn = kb.sb("attn", [128, 1024], BF16)
                attnT = kb.sb("attnT", [128, 8, 128], BF16)
                ybuf = kb.sb("ybuf", [128, 1024], F32)
                tmp = ln_tmp(kb)
                pl = [kb.ps("pl", [128, 512], F32) for _ in range(4)]
                ptp = kb.ps("ptp", [128, 1024], BF16)
                pv = kb.ps("pv", [128, 512], F32)
                po = [kb.ps("po", [128, 512], F32) for _ in range(2)]
                for qi in range(NQ):
                    q0 = qi * 128
                    own = qi // 2
                    nk = q0 + 128
                    qs = slice(q0, q0 + 128)
                    kb.dma("pool", xt[:, :], XIN[base + q0:base + q0 + 128, :], r=[XIN], w=[xt])
                    gated = own >= 4
                    if gated:
                        pgt = po[1]
                        for h in range(16):
                            pr, r0 = h // 2, (h % 2) * 64
                            kb.T(lambda e, h=h, pr=pr, r0=r0: e.matmul(pgt[:, h * 8:(h + 1) * 8], lhsT=qT[r0:r0 + 64, pr, qs], rhs=kmT[r0:r0 + 64, pr, 0:8], start=True, stop=True), r=[qT, kmT], w=[pgt])
                        kb.V(lambda e: e.tensor_tensor(out=gm[:, :, :], in0=pgt[:, 0:128].rearrange("p (h n) -> p h n", n=8),
                                                       in1=c2[:, NRELW + own * 8:NRELW + own * 8 + 8].unsqueeze(1).broadcast_to([128, 16, 8]), op=ALU.add), r=[pgt, c2], w=[gm])
                        for h in range(16):
                            kb.V(lambda e, h=h: e.max(out=m8[:, h, :], in_=gm[:, h, :]), r=[gm], w=[m8])
                        kb.V(lambda e: e.tensor_tensor(out=selb[:, :, :], in0=gm[:, :, :], in1=m8[:, :, 2:3].broadcast_to([128, 16, 8]), op=ALU.is_ge), r=[gm, m8], w=[selb])
                        kb.V(lambda e: e.tensor_scalar(out=selb[:, :, :], in0=selb[:, :, :], scalar1=1.0, scalar2=1.0e30, op0=ALU.subtract, op1=ALU.mult), r=[selb], w=[selb])
                    for h in range(16):
                        pr, r0 = h // 2, (h % 2) * 64
                        slope = 2.0 ** (-(h + 1) / 2.0)
                        off = 2177 - q0
                        L, Pb, PT, rmax, rsum = L2[h % 2], Pb2[h % 2], PT2[h % 2], rmax2[h % 2], rsum2[h % 2]
                        for j0 in range((nk + 511) // 512):
                            c0, c1 = j0 * 512, min(nk, (j0 + 1) * 512)
                            j = (j0 + 2 * (h % 2)) % 4 if nk <= 1024 else j0
                            kb.T(lambda e, j=j, c0=c0, c1=c1, pr=pr, r0=r0: e.matmul(pl[j][:, 0:c1 - c0], lhsT=qT[r0:r0 + 64, pr, qs], rhs=kT[r0:r0 + 64, pr, c0:c1], start=True, stop=True), r=[qT, kT], w=[pl[j]])
                            kb.V(lambda e, j=j, c0=c0, c1=c1: e.scalar_tensor_tensor(out=L[:, c0:c1], in0=c2[:, off + c0:off + c1], scalar=slope, in1=pl[j][:, 0:c1 - c0], op0=ALU.mult, op1=ALU.add),
                                 r=[c2, pl[j]], w=[L])
                        if gated:
                            kb.V(lambda e, h=h: e.tensor_tensor(out=L[:, 0:own * 256].rearrange("p (b j) -> p b j", j=256), in0=L[:, 0:own * 256].rearrange("p (b j) -> p b j", j=256),
                                                                in1=selb[:, h, 0:own].unsqueeze(2).broadcast_to([128, own, 256]), op=ALU.add), r=[L, selb], w=[L])
                        kb.V(lambda e: e.tensor_tensor(out=L[:, nk - 128:nk], in0=L[:, nk - 128:nk], in1=c.tri_q(), op=ALU.add), r=[L, c.cf], w=[L])
                        kb.V(lambda e: e.tensor_reduce(out=rmax[:, :], in_=L[:, 0:nk], axis=AX.X, op=ALU.max), r=[L], w=[rmax])
                        kb.V(lambda e: e.tensor_scalar(out=rmax[:, :], in0=rmax[:, :], scalar1=-1.0, scalar2=None, op0=ALU.mult), r=[rmax], w=[rmax])
                        kb.V(lambda e: e.memset(rsum[:, :], 0.0), w=[rsum])
                        kb.A(lambda e: e.activation(out=Pb[:, 0:nk], in_=L[:, 0:nk], func=AF.Exp, bias=rmax[:, 0:1], accum_out=rsum[:, 0:1]), r=[L, rmax, rsum], w=[Pb, rsum])
                        nj = nk // 128
                        for j in range(nj):
                            kb.T(lambda e, j=j: e.transpose(out=ptp[:, (j % 8) * 128:(j % 8 + 1) * 128], in_=Pb[:, j * 128:(j + 1) * 128], identity=c.identb[:, :]), r=[Pb, c.identb], w=[ptp])
                            if j % 8 == 7 or j == nj - 1:
                                j0 = (j // 8) * 8
                                nn = j - j0 + 1
                                o = PT[:, j0:j0 + nn, :]
                                i_ = ptp[:, 0:nn * 128].rearrange("p (a b) -> p a b", b=128)
                                if (j // 8) % 2 == 0:
                                    kb.V(lambda e, o=o, i_=i_: e.tensor_copy(out=o, in_=i_), r=[ptp], w=[PT])
                                else:
                                    kb.A(lambda e, o=o, i_=i_: e.activation(out=o, in_=i_, func=AF.Copy), r=[ptp], w=[PT])
                        for j in range(nj):
                            kb.T(lambda e, j=j, h=h: e.matmul(pv[:, 0:64], lhsT=PT[:, j, :], rhs=va[:, j, h * 64:(h + 1) * 64], start=(j == 0), stop=(j == nj - 1)), r=[PT, va], w=[pv])
                        kb.V(lambda e: e.reciprocal(out=rsum[:, :], in_=rsum[:, :]), r=[rsum], w=[rsum])
                        kb.V(lambda e, h=h: e.tensor_scalar(out=attn[:, h * 64:(h + 1) * 64], in0=pv[:, 0:64], scalar1=rsum[:, 0:1], scalar2=None, op0=ALU.mult), r=[pv, rsum], w=[attn])
                    transpose_to(kb, c, attn, 8, ptp, attnT, lambda c0, nn: attnT[:, c0:c0 + nn, :])
                    for hf in range(2):
                        for k in range(8):
                            kb.T(lambda e, k=k, hf=hf: e.matmul(po[hf][:, :], lhsT=attnT[:, k, :], rhs=wo[:, k, hf * 512:(hf + 1) * 512], start=(k == 0), stop=(k == 7)), r=[attnT, wo], w=[po[hf]])
                    resid_ln_store(kb, xt, po, g_bc, b_bc, ybuf, tmp, Y1, Y1[base + q0:base + q0 + 128, :])


def ssd_phase_a(kb, c, T, S, XIN, W, XS, BTM, BCT, ZS, DT):
    GS = 512
    NG = T // GS
    GPS = S // GS
    with kb.scope():
        win = kb.sb("win", [128, 8, 5152], BF16)
        cw = kb.sb("cw", [128, 24, 4], F32)
        cb = kb.sb("cb", [128, 24], F32)
        with kb.scope():
            stage = [kb.sb("stg", [128, 2048], F32) for _ in range(2)]
            load_w_bf(kb, win, lambda k, c0, cw_: win[:, k, c0:c0 + cw_], W["ssd_w_in"], 8, 5152, stage)
            pst = kb.ps("pst", [128, 512], F32)
            load_cols(kb, c, cw, lambda b: cw[:, b, :], lambda b: W["ssd_conv_w"][:, b * 128:(b + 1) * 128], 4, stage[0], pst, nblk=24)
            load_cols(kb, c, cb, lambda b: cb[:, :], lambda b: W["ssd_conv_b"].rearrange("(c p) -> c p", p=128), 24, stage[1], pst)
        dtb = bcast_row(kb, "dtb", W["ssd_dt_bias"], 32)
        one1 = kb.sb("one1", [128, 1], F32)
        kb.V(lambda e: e.memset(one1[:, :], 1.0), w=[one1])
        xt = [kb.sb("xt", [128, 1024], F32) for _ in range(2)]
        xbf = kb.sb("xbf", [128, 1024], BF16)
        xT = kb.sb("xT", [128, 8, GS], BF16)
        rawH = [kb.sb("rawH", [128, 3 + GS], F32) for _ in range(2)]
        hal = kb.sb("hal", [128, 24, 3], F32)
        cacc = [kb.sb("cacc", [128, GS], F32) for _ in range(2)]
        xbcT = kb.sb("xbcT", [128, 24, GS], BF16)
        xs_sb = [kb.sb("xs_sb", [128, 2048], BF16) for _ in range(2)]
        b_sb = [kb.sb("b_sb", [128, 512], BF16) for _ in range(2)]
        zs_sb = [kb.sb("zs_sb", [128, 2048], BF16) for _ in range(2)]
        dtr = kb.sb("dtr", [128, 32], F32)
        dab = kb.sb("dab", [128, 32], F32)
        dmx = kb.sb("dmx", [128, 32], F32)
        dt_sb = [kb.sb("dt_sb", [128, 32], F32) for _ in range(2)]
        pa = [kb.ps("pa", [128, 512], F32) for _ in range(2)]
        pz = [kb.ps("pz", [128, 512], F32) for _ in range(2)]
        pd = kb.ps("pd", [128, 512], F32)
        ptr = [kb.ps("ptr", [128, 1024], BF16) for _ in range(2)]
        n = 0
        for g in range(NG):
            t0 = g * GS
            for ti in range(4):
                x_ = xt[ti % 2]
                kb.dma("sp" if ti % 2 == 0 else "pool", x_[:, :], XIN[t0 + ti * 128:t0 + (ti + 1) * 128, :], r=[XIN], w=[x_])
                kb.A(lambda e, x_=x_: e.activation(out=xbf[:, :], in_=x_[:, :], func=AF.Copy), r=[x_], w=[xbf])
                transpose_to(kb, c, xbf, 8, ptr[0], xT, lambda c0, nn, ti=ti: xT[:, c0:c0 + nn, ti * 128:(ti + 1) * 128])
            if g % GPS == 0:
                kb.G(lambda e: e.memset(hal[:, :, :], 0.0), w=[hal])
            for fc in range(24):
                p_, rh, ac = pa[fc % 2], rawH[fc % 2], cacc[fc % 2]
                col0 = 2048 + fc * 128
                for k in range(8):
                    kb.T(lambda e, k=k, col0=col0, p_=p_: e.matmul(p_[:, :], lhsT=win[:, k, col0:col0 + 128], rhs=xT[:, k, :], start=(k == 0), stop=(k == 7)), r=[win, xT], w=[p_])
                kb.A(lambda e, p_=p_, rh=rh: e.activation(out=rh[:, 3:3 + GS], in_=p_[:, :], func=AF.Copy), r=[p_], w=[rh])
                kb.G(lambda e, rh=rh, fc=fc: e.tensor_copy(out=rh[:, 0:3], in_=hal[:, fc, :]), r=[hal], w=[rh])
                kb.V(lambda e, rh=rh, ac=ac, fc=fc: e.tensor_scalar(out=ac[:, :], in0=rh[:, 3:3 + GS], scalar1=cw[:, fc, 3:4], scalar2=cb[:, fc:fc + 1], op0=ALU.mult, op1=ALU.add), r=[rh, cw, cb], w=[ac])
                for k in range(3):
                    kb.V(lambda e, rh=rh, ac=ac, fc=fc, k=k: e.scalar_tensor_tensor(out=ac[:, :], in0=rh[:, k:k + GS], scalar=cw[:, fc, k:k + 1], in1=ac[:, :], op0=ALU.mult, op1=ALU.add), r=[rh, cw, ac], w=[ac])
                kb.G(lambda e, rh=rh, fc=fc: e.tensor_copy(out=hal[:, fc, :], in_=rh[:, GS:GS + 3]), r=[rh], w=[hal])
                kb.A(lambda e, ac=ac, fc=fc: e.activation(out=xbcT[:, fc, :], in_=ac[:, :], func=AF.Silu), r=[ac], w=[xbcT])
            for j in range(8):
                kb.dma("sp" if j % 2 == 0 else "pool", BCT[j][:, t0:t0 + GS], xbcT[:, 16 + j, :], r=[xbcT], w=[BCT])
            for ti in range(4):
                tsl = slice(ti * 128, (ti + 1) * 128)
                rows = slice(t0 + ti * 128, t0 + (ti + 1) * 128)
                xs_, b_, zs_, dt_ = xs_sb[ti % 2], b_sb[ti % 2], zs_sb[ti % 2], dt_sb[ti % 2]
                for half in range(2):
                    pt_ = ptr[half]
                    for j in range(8):
                        kb.T(lambda e, j=j, half=half, pt_=pt_: e.transpose(out=pt_[:, j * 128:(j + 1) * 128], in_=xbcT[:, half * 8 + j, tsl], identity=c.identb[:, :]), r=[xbcT, c.identb], w=[pt_])
                    if half == 0:
                        kb.V(lambda e, pt_=pt_, xs_=xs_: e.tensor_copy(out=xs_[:, 0:1024], in_=pt_[:, :]), r=[pt_], w=[xs_])
                    else:
                        kb.A(lambda e, pt_=pt_, xs_=xs_: e.activation(out=xs_[:, 1024:2048], in_=pt_[:, :], func=AF.Copy), r=[pt_], w=[xs_])
                kb.dma("sp", XS[rows, :], xs_[:, :], r=[xs_], w=[XS])
                for j in range(4):
                    kb.T(lambda e, j=j: e.transpose(out=ptr[0][:, j * 128:(j + 1) * 128], in_=xbcT[:, 16 + j, tsl], identity=c.identb[:, :]), r=[xbcT, c.identb], w=[ptr[0]])
                kb.V(lambda e, b_=b_: e.tensor_copy(out=b_[:, :], in_=ptr[0][:, 0:512]), r=[ptr[0]], w=[b_])
                kb.dma("pool", BTM[rows, :], b_[:, :], r=[b_], w=[BTM])
                for sl in range(4):
                    p_ = pz[n % 2]
                    n += 1
                    for k in range(8):
                        kb.T(lambda e, k=k, sl=sl, p_=p_: e.matmul(p_[:, :], lhsT=xT[:, k, tsl], rhs=win[:, k, sl * 512:(sl + 1) * 512], start=(k == 0), stop=(k == 7)), r=[xT, win], w=[p_])
                    kb.A(lambda e, sl=sl, p_=p_, zs_=zs_: e.activation(out=zs_[:, sl * 512:(sl + 1) * 512], in_=p_[:, :], func=AF.Silu), r=[p_], w=[zs_])
                kb.dma("sp", ZS[rows, :], zs_[:, :], r=[zs_], w=[ZS])
                for k in range(8):
                    kb.T(lambda e, k=k: e.matmul(pd[:, 0:32], lhsT=xT[:, k, tsl], rhs=win[:, k, 5120:5152], start=(k == 0), stop=(k == 7)), r=[xT, win], w=[pd])
                kb.V(lambda e: e.tensor_tensor(out=dtr[:, :], in0=pd[:, 0:32], in1=dtb[:, :], op=ALU.add), r=[pd, dtb], w=[dtr])
                kb.A(lambda e: e.activation(out=dab[:, :], in_=dtr[:, :], func=AF.Abs), r=[dtr], w=[dab])
                kb.A(lambda e: e.activation(out=dab[:, :], in_=dab[:, :], func=AF.Exp, scale=-1.0), r=[dab], w=[dab])
                kb.A(lambda e: e.activation(out=dab[:, :], in_=dab[:, :], func=AF.Ln, bias=one1[:, 0:1]), r=[dab, one1], w=[dab])
                kb.V(lambda e: e.tensor_single_scalar(out=dmx[:, :], in_=dtr[:, :], scalar=0.0, op=ALU.max), r=[dtr], w=[dmx])
                kb.V(lambda e, dt_=dt_: e.tensor_tensor(out=dt_[:, :], in0=dmx[:, :], in1=dab[:, :], op=ALU.add), r=[dmx, dab], w=[dt_])
                kb.dma("pool", DT[rows, :], dt_[:, :], r=[dt_], w=[DT])


def ssd_phase_b(kb, c, T, S, XIN, Y1, W, XS, BTM, BCT, ZS, DT):
    NC_ = T // 128
    CPS = S // 128
    with kb.scope():
        wout = kb.sb("wout", [128, 16, 1024], BF16)
        with kb.scope():
            stage = [kb.sb("stg", [128, 1024], F32) for _ in range(2)]
            load_w_bf(kb, wout, lambda k, c0, cw_: wout[:, k, c0:c0 + cw_], W["ssd_w_out"], 16, 1024, stage)
        g_bc = bcast_row(kb, "lnm_g", W["ln_mix_g"], 1024)
        b_bc = bcast_row(kb, "lnm_b", W["ln_mix_b"], 1024, q="pool")
        ng_bc = bcast_row(kb, "ng_bc", W["ssd_norm_g"], 2048)
        aneg = bcast_row(kb, "aneg", W["ssd_a_log"], 32, q="pool")
        kb.A(lambda e: e.activation(out=aneg[:, :], in_=aneg[:, :], func=AF.Exp), r=[aneg], w=[aneg])
        kb.V(lambda e: e.tensor_scalar(out=aneg[:, :], in0=aneg[:, :], scalar1=-1.0, scalar2=None, op0=ALU.mult), r=[aneg], w=[aneg])
        dsk = bcast_row(kb, "dsk", W["ssd_d"], 32)
        xt = kb.sb("xt", [128, 1024], F32)
        xs = kb.sb("xs", [128, 2048], BF16)
        bt = kb.sb("bt", [128, 512], BF16)
        zs = kb.sb("zs", [128, 2048], BF16)
        dt = kb.sb("dt", [128, 32], F32)
        bct = kb.sb("bct", [128, 8, 128], BF16)
        dtA = kb.sb("dtA", [128, 32], F32)
        acs = kb.sb("acs", [128, 64], F32)
        ea = kb.sb("ea", [128, 32], F32)
        dte = kb.sb("dte", [128, 32], F32)
        cd = kb.sb("cd", [128, 32], F32)
        xdt = kb.sb("xdt", [128, 2048], BF16)
        xe = kb.sb("xe", [128, 2048], BF16)
        Mh = kb.sb("Mh", [128, 32, 128], F32)
        cbt = kb.sb("cbt", [128, 4, 128], F32)
        Dm = [kb.sb("Dm", [128, 4, 128], F32) for _ in range(2)]
        Wd = kb.sb("Wd", [128, 32, 128], BF16)
        yoff = kb.sb("yoff", [128, 2048], F32)
        y = kb.sb("y", [128, 2048], F32)
        t2 = kb.sb("t2", [128, 2048], F32)
        ss = kb.sb("ss", [128, 4], F32)
        gnb = kb.sb("gnb", [128, 2048], BF16)
        gnT = kb.sb("gnT", [128, 16, 128], BF16)
        H = kb.sb("H", [128, 2048], F32)
        Hbf = kb.sb("Hbf", [128, 2048], BF16)
        ybuf = kb.sb("ybuf", [128, 1024], F32)
        tmp = ln_tmp(kb)
        py = [kb.ps("py", [128, 512], F32) for _ in range(4)]
        pd = [kb.ps("pd", [128, 512], F32) for _ in range(2)]
        pm = kb.ps("pm", [128, 512], F32)
        ptr = kb.ps("ptr", [128, 1024], BF16)
        v3 = lambda ap: ap.rearrange("p (h d) -> p h d", d=64)
        for ci in range(NC_):
            rows = slice(ci * 128, (ci + 1) * 128)
            kb.dma("sp", xs[:, :], XS[rows, :], r=[XS], w=[xs])
            kb.dma("pool", zs[:, :], ZS[rows, :], r=[ZS], w=[zs])
            kb.dma("sp", bt[:, :], BTM[rows, :], r=[BTM], w=[bt])
            kb.dma("pool", dt[:, :], DT[rows, :], r=[DT], w=[dt])
            kb.dma("sp", bct[:, :, :], BCT.t.rearrange("j p t -> p j t")[:, :, rows], r=[BCT], w=[bct])
            kb.dma("pool", xt[:, :], XIN[rows, :], r=[XIN], w=[xt])
            if ci % CPS == 0:
                kb.G(lambda e: e.memset(H[:, :], 0.0), w=[H])
                kb.G(lambda e: e.memset(Hbf[:, :], 0.0), w=[Hbf])
            kb.V(lambda e: e.tensor_tensor(out=dtA[:, :], in0=dt[:, :], in1=aneg[:, :], op=ALU.mult), r=[dt, aneg], w=[dtA])
            kb.T(lambda e: e.matmul(pm[:, 0:32], lhsT=c.triu(), rhs=dtA[:, :], start=True, stop=True), r=[c.cf, dtA], w=[pm])
            kb.T(lambda e: e.matmul(pm[:, 32:64], lhsT=c.ones(), rhs=dtA[:, :], start=True, stop=True), r=[c.cf, dtA], w=[pm])
            kb.V(lambda e: e.tensor_copy(out=acs[:, :], in_=pm[:, 0:64]), r=[pm], w=[acs])
            kb.A(lambda e: e.activation(out=ea[:, :], in_=acs[:, 0:32], func=AF.Exp), r=[acs], w=[ea])
            kb.V(lambda e: e.tensor_tensor(out=dte[:, :], in0=acs[:, 32:64], in1=acs[:, 0:32], op=ALU.subtract), r=[acs], w=[dte])
            kb.A(lambda e: e.activation(out=dte[:, :], in_=dte[:, :], func=AF.Exp), r=[dte], w=[dte])
            kb.V(lambda e: e.tensor_tensor(out=dte[:, :], in0=dte[:, :], in1=dt[:, :], op=ALU.mult), r=[dte, dt], w=[dte])
            kb.A(lambda e: e.activation(out=cd[:, :], in_=acs[:, 32:64], func=AF.Exp), r=[acs], w=[cd])
            kb.V(lambda e: e.tensor_tensor(out=v3(xdt[:, :]), in0=v3(xs[:, :]), in1=dt[:, :].unsqueeze(2).broadcast_to([128, 32, 64]), op=ALU.mult), r=[xs, dt], w=[xdt])
            kb.G(lambda e: e.tensor_tensor(out=v3(xe[:, :]), in0=v3(xs[:, :]), in1=dte[:, :].unsqueeze(2).broadcast_to([128, 32, 64]), op=ALU.mult), r=[xs, dte], w=[xe])
            kb.V(lambda e: e.tensor_tensor(out=Mh[:, :, :], in0=c.triu().unsqueeze(1).broadcast_to([128, 32, 128]), in1=dtA[:, :].unsqueeze(2).broadcast_to([128, 32, 128]), op=ALU.mult), r=[c.cf, dtA], w=[Mh])
            for g in range(4):
                kb.T(lambda e, g=g: e.matmul(pm[:, g * 128:(g + 1) * 128], lhsT=bct[:, g, :], rhs=bct[:, 4 + g, :], start=True, stop=True), r=[bct], w=[pm])
            kb.V(lambda e: e.tensor_copy(out=cbt[:, :, :], in_=pm[:, :].rearrange("p (a b) -> p a b", b=128)), r=[pm], w=[cbt])
            for g in range(4):
                gs = slice(g * 512, (g + 1) * 512)
                kb.T(lambda e, g=g, gs=gs: e.matmul(py[g][:, :], lhsT=bct[:, 4 + g, :], rhs=Hbf[:, gs], start=True, stop=True), r=[bct, Hbf], w=[py[g]])
                kb.V(lambda e, g=g, gs=gs: e.tensor_tensor(out=v3(yoff[:, gs]), in0=v3(py[g][:, :]), in1=ea[:, g * 8:(g + 1) * 8].unsqueeze(2).broadcast_to([128, 8, 64]), op=ALU.mult), r=[py[g], ea], w=[yoff])
            for hb in range(8):
                p_, d_ = pd[hb % 2], Dm[hb % 2]
                for i in range(4):
                    h = hb * 4 + i
                    kb.T(lambda e, i=i, h=h, p_=p_: e.matmul(p_[:, i * 128:(i + 1) * 128], lhsT=c.ones(), rhs=Mh[:, h, :], start=True, stop=False), r=[c.cf, Mh], w=[p_])
                    kb.T(lambda e, i=i, h=h, p_=p_: e.matmul(p_[:, i * 128:(i + 1) * 128], lhsT=Mh[:, h, :], rhs=c.negones(), start=False, stop=True), r=[c.cf, Mh], w=[p_])
                kb.V(lambda e, p_=p_, d_=d_: e.tensor_tensor(out=d_[:, :, :], in0=p_[:, :].rearrange("p (a b) -> p a b", b=128), in1=c.negmask().unsqueeze(1).broadcast_to([128, 4, 128]), op=ALU.add), r=[p_, c.cf], w=[d_])
                kb.A(lambda e, d_=d_: e.activation(out=d_[:, :, :], in_=d_[:, :, :], func=AF.Exp), r=[d_], w=[d_])
                kb.G(lambda e, d_=d_, hb=hb: e.tensor_tensor(out=Wd[:, hb * 4:(hb + 1) * 4, :], in0=d_[:, :, :], in1=cbt[:, hb // 2, :].unsqueeze(1).broadcast_to([128, 4, 128]), op=ALU.mult), r=[d_, cbt], w=[Wd])
            for h in range(32):
                g = h // 8
                kb.T(lambda e, h=h, g=g: e.matmul(py[g][:, (h % 8) * 64:(h % 8 + 1) * 64], lhsT=Wd[:, h, :], rhs=xdt[:, h * 64:(h + 1) * 64], start=True, stop=True), r=[Wd, xdt], w=[py[g]])
            for g in range(4):
                gs = slice(g * 512, (g + 1) * 512)
                kb.V(lambda e, g=g, gs=gs: e.tensor_tensor(out=y[:, gs], in0=py[g][:, :], in1=yoff[:, gs], op=ALU.add), r=[py[g], yoff], w=[y])
            kb.G(lambda e: e.tensor_tensor(out=v3(t2[:, :]), in0=v3(xs[:, :]), in1=dsk[:, :].unsqueeze(2).broadcast_to([128, 32, 64]), op=ALU.mult), r=[xs, dsk], w=[t2])
            kb.V(lambda e: e.tensor_tensor(out=y[:, :], in0=y[:, :], in1=t2[:, :], op=ALU.add), r=[y, t2], w=[y])
            kb.V(lambda e: e.tensor_tensor(out=y[:, :], in0=y[:, :], in1=zs[:, :], op=ALU.mult), r=[y, zs], w=[y])
            kb.V(lambda e: e.memset(ss[:, :], 0.0), w=[ss])
            for g in range(4):
                gs = slice(g * 512, (g + 1) * 512)
                kb.A(lambda e, g=g, gs=gs: e.activation(out=t2[:, gs], in_=y[:, gs], func=AF.Square, accum_out=ss[:, g:g + 1]), r=[y, ss], w=[t2, ss])
            kb.A(lambda e: e.activation(out=ss[:, :], in_=ss[:, :], func=AF.Ln, scale=1.0 / 512, bias=CONST.eps[:, 0:1]), r=[ss, CONST.eps], w=[ss])
            kb.A(lambda e: e.activation(out=ss[:, :], in_=ss[:, :], func=AF.Exp, scale=-0.5), r=[ss], w=[ss])
            kb.V(lambda e: e.tensor_tensor(out=y[:, :].rearrange("p (g d) -> p g d", d=512), in0=y[:, :].rearrange("p (g d) -> p g d", d=512), in1=ss[:, :].unsqueeze(2).broadcast_to([128, 4, 512]), op=ALU.mult), r=[y, ss], w=[y])
            kb.G(lambda e: e.tensor_tensor(out=gnb[:, :], in0=y[:, :], in1=ng_bc[:, :], op=ALU.mult), r=[y, ng_bc], w=[gnb])
            transpose_to(kb, c, gnb, 16, ptr, gnT, lambda c0, nn: gnT[:, c0:c0 + nn, :])
            po = pd
            for hf in range(2):
                for k in range(16):
                    kb.T(lambda e, k=k, hf=hf: e.matmul(po[hf][:, :], lhsT=gnT[:, k, :], rhs=wout[:, k, hf * 512:(hf + 1) * 512], start=(k == 0), stop=(k == 15)), r=[gnT, wout], w=[po[hf]])
            resid_ln_store(kb, xt, po, g_bc, b_bc, ybuf, tmp, Y1, Y1[rows, :])
            for g in range(4):
                gs = slice(g * 512, (g + 1) * 512)
                kb.T(lambda e, g=g, gs=gs: e.matmul(py[g][:, :], lhsT=bt[:, g * 128:(g + 1) * 128], rhs=xe[:, gs], start=True, stop=True), r=[bt, xe], w=[py[g]])
            kb.V(lambda e: e.tensor_tensor(out=v3(H[:, :]), in0=v3(H[:, :]), in1=cd[:, :].unsqueeze(2).broadcast_to([128, 32, 64]), op=ALU.mult), r=[H, cd], w=[H])
            for g in range(4):
                gs = slice(g * 512, (g + 1) * 512)
                kb.V(lambda e, g=g, gs=gs: e.tensor_tensor(out=H[:, gs], in0=H[:, gs], in1=py[g][:, :], op=ALU.add), r=[H, py[g]], w=[H])
            kb.A(lambda e: e.activation(out=Hbf[:, :], in_=H[:, :], func=AF.Copy), r=[H], w=[Hbf])


W_SHAPES = {
    "ssd_w_in": (2, 1024, 5152), "ssd_conv_w": (2, 4, 3072), "ssd_conv_b": (2, 3072), "ssd_dt_bias": (2, 32),
    "ssd_a_log": (2, 32), "ssd_d": (2, 32), "ssd_norm_g": (2, 2048), "ssd_w_out": (2, 2048, 1024),
    "moba_w_qkv": (1, 1024, 3072), "moba_w_out": (1, 1024, 1024),
    "conv_w_pw1": (1, 1024, 2048), "conv_b_pw1": (1, 2048), "conv_w_dw": (1, 31, 1024), "conv_b_dw": (1, 1024),
    "conv_ln_g": (1, 1024), "conv_ln_b": (1, 1024), "conv_w_pw2": (1, 1024, 1024),
    "peer_w_q": (4, 1024, 2048), "peer_sub_keys": (4, 8, 2, 128, 128), "peer_u": (4, 16384, 1024), "peer_v": (4, 16384, 1024),
    "ln_mix_g": (4, 1024), "ln_mix_b": (4, 1024), "ln_ffn_g": (4, 1024), "ln_ffn_b": (4, 1024),
    "ple_w_gate": (4, 1024, 1024), "ple_w_proj": (4, 256, 1024),
}
DEPTH = 4
PER_LAYER = ("peer_w_q", "peer_sub_keys", "peer_u", "peer_v", "ln_mix_g", "ln_mix_b", "ln_ffn_g", "ln_ffn_b", "ple_w_gate", "ple_w_proj")


def build_full(T, S, depth=DEPTH, TG=256):
    nc = bass.Bass("TRN2", target_bir_lowering=False)
    kb = KB(nc)
    with nc.allow_low_precision("bf16 matmul operands with fp32 accumulation"):
        cd = kb.dram("consts", [128, CW], F32, kind="ExternalInput")
        c2d = kb.dram("c2", [128, C2W], F32, kind="ExternalInput")
        X = kb.dram("x", [T, 1024], F32, kind="ExternalInput")
        P = kb.dram("p", [DEPTH, T, 256], F32, kind="ExternalInput")
        OUT = kb.dram("out", [T, 1024], F32, kind="ExternalOutput")
        Wd = {k: kb.dram(k, list(shp), F32, kind="ExternalInput").t for k, shp in W_SHAPES.items()}
        XA = [kb.dram("xa%d" % i, [T, 1024], F32) for i in range(2)]
        Y1 = kb.dram("y1", [T, 1024], F32)
        UVs = kb.dram("UVs", [128, 128, 2048], BF16)
        XTd = kb.dram("XTd", [8, 128, T], BF16)
        QTd = kb.dram("QTd", [16, 128, T], BF16)
        XS = kb.dram("XS", [T, 2048], BF16)
        BTM = kb.dram("BTM", [T, 512], BF16)
        BCT = kb.dram("BCT", [8, 128, T], BF16)
        ZS = kb.dram("ZS", [T, 2048], BF16)
        DT = kb.dram("DT", [T, 32], F32)
        c = load_consts(kb, cd)
        xin = X
        for i in range(depth):
            kind, j = i % 3, i // 3
            W = {}
            for k in W_SHAPES:
                if k in PER_LAYER:
                    W[k] = Wd[k][i]
                elif k.startswith(("ssd_", "moba_", "conv_")):
                    n = W_SHAPES[k][0]
                    W[k] = Wd[k][min(j, n - 1)]
            if kind == 0:
                ssd_phase_a(kb, c, T, S, xin, W, XS, BTM, BCT, ZS, DT)
                ssd_phase_b(kb, c, T, S, xin, Y1, W, XS, BTM, BCT, ZS, DT)
            elif kind == 1:
                moba_phase(kb, c, T, S, xin, Y1, W, c2d)
            else:
                conf_phase(kb, c, T, S, xin, Y1, W)
            peer_prepass(kb, c, W["peer_u"], W["peer_v"], UVs)
            xout = OUT if i == depth - 1 else XA[i % 2]
            peer_q_phase(kb, c, T, Y1, W, XTd, QTd)
            peer_phase(kb, c, T, Y1, xout, P.t[i], W, UVs, XTd, QTd, TG=TG)
            xin = xout
        kb.barrier()
    return nc, kb


_CACHE = {}


def kernel(**inputs):
    NCORE = 8
    B, S = inputs["x"].shape[0], inputs["x"].shape[1]
    per = B // NCORE
    T = per * S
    key = (T, S)
    if key not in _CACHE:
        _CACHE[key] = build_full(T, S)[0]
    nc = _CACHE[key]
    consts, c2 = host_consts(), host_c2()
    shared = {k: np.ascontiguousarray(np.asarray(inputs[k], dtype=np.float32)) for k in W_SHAPES}
    x = np.asarray(inputs["x"], dtype=np.float32)
    p = np.asarray(inputs["p"], dtype=np.float32)
    in_maps = []
    for ci in range(NCORE):
        m = dict(shared)
        m["consts"] = consts
        m["c2"] = c2
        m["x"] = np.ascontiguousarray(x[ci * per:(ci + 1) * per].reshape(T, 1024))
        m["p"] = np.ascontiguousarray(p[:, ci * per:(ci + 1) * per].reshape(DEPTH, T, 256))
        in_maps.append(m)
    res = run_bass_kernel_spmd(nc, in_maps, core_ids=list(range(NCORE)))
    outs = [np.asarray(r["out"]).reshape(per, S, 1024) for r in res.results]
    return np.concatenate(outs, axis=0).astype(np.float32)
```

```python
import numpy as np
from contextlib import ExitStack, contextmanager
import concourse.bass as bass
import concourse.mybir as mybir
from concourse.bass_utils import run_bass_kernel_spmd

F32 = mybir.dt.float32
BF16 = mybir.dt.bfloat16
I32 = mybir.dt.int32
U32 = mybir.dt.uint32
AF = mybir.ActivationFunctionType
ALU = mybir.AluOpType
AX = mybir.AxisListType

D = 1024
ALPHA = 8.0 ** 0.25
EPS = 1e-5
NEG = -1.0e30
KD = 8


class Buf:
    __slots__ = ("t", "w", "r", "name")

    def __init__(self, t, name=""):
        self.t = t
        self.w = None
        self.r = {}
        self.name = name

    def __getitem__(self, idx):
        return self.t[idx]


class Alias:
    def __init__(self, parent, t):
        self.__dict__["parent"] = parent
        self.__dict__["t"] = t

    def __getitem__(self, idx):
        return self.t[idx]

    def __getattr__(self, k):
        return getattr(self.__dict__["parent"], k)

    def __setattr__(self, k, v):
        setattr(self.__dict__["parent"], k, v)


class KB:
    def __init__(self, nc):
        self.nc = nc
        self.root = ExitStack()
        self.stacks = [self.root]
        self.eng = {}
        for name, h in (("pe", nc.tensor), ("act", nc.scalar), ("dve", nc.vector), ("pool", nc.gpsimd), ("sp", nc.sync)):
            sem = self.root.enter_context(nc.semaphore("s_" + name))
            self.eng[name] = dict(h=h, sem=sem, cnt=0, seen={}, name=name)
        self.dq = {}
        for q in ("sp", "pool", "act"):
            sems = [self.root.enter_context(nc.semaphore("d_%s%d" % (q, i))) for i in range(KD)]
            self.dq[q] = dict(sems=sems, n=0, cnt=[0] * KD)
        self.uid = 0
        self.ninst = 0

    @contextmanager
    def scope(self):
        st = ExitStack()
        self.stacks.append(st)
        try:
            yield
        finally:
            self.barrier()
            self.stacks.pop()
            st.close()

    def _nm(self, name):
        self.uid += 1
        return "%s_%d" % (name, self.uid)

    def sb(self, name, shape, dt):
        t = self.stacks[-1].enter_context(self.nc.sbuf_tensor(self._nm(name), list(shape), dt))
        return Buf(t, name)

    def ps(self, name, shape, dt):
        t = self.stacks[-1].enter_context(self.nc.psum_tensor(self._nm(name), list(shape), dt))
        return Buf(t, name)

    def dram(self, name, shape, dt, kind="Internal"):
        t = self.nc.dram_tensor(name, list(shape), dt, kind=kind)
        return Buf(t.ap(), name)

    def _wait(self, e, deps):
        best = {}
        for sem, val in deps:
            k = id(sem)
            if k not in best or best[k][1] < val:
                best[k] = (sem, val)
        for k, (sem, val) in best.items():
            if e["seen"].get(k, 0) >= val:
                continue
            e["h"].wait_ge(sem, val)
            e["seen"][k] = val

    def _deps(self, e, r, w, skip_self, is_dma=False):
        deps = []
        me = None if is_dma else id(e["sem"])
        for b in r:
            if b.w is not None:
                deps.append(b.w)
        for b in w:
            if b.w is not None:
                deps.append(b.w)
            for k, tok in b.r.items():
                deps.append(tok)
        if skip_self:
            deps = [d for d in deps if id(d[0]) != me]
        return deps

    def _mark(self, tok, r, w):
        for b in w:
            b.w = tok
            b.r = {}
        k = id(tok[0])
        wroots = [getattr(x, "parent", x) for x in w]
        for b in r:
            if not any(getattr(b, "parent", b) is x for x in wroots):
                b.r[k] = tok

    def op(self, en, fn, r=(), w=()):
        e = self.eng[en]
        self._wait(e, self._deps(e, r, w, en == "pe"))
        inst = fn(e["h"])
        e["cnt"] += 1
        self.ninst += 1
        inst.then_inc(e["sem"], 1)
        tok = (e["sem"], e["cnt"])
        self._mark(tok, r, w)
        return tok

    def V(self, fn, r=(), w=()):
        return self.op("dve", fn, r, w)

    def A(self, fn, r=(), w=()):
        return self.op("act", fn, r, w)

    def G(self, fn, r=(), w=()):
        return self.op("pool", fn, r, w)

    def T(self, fn, r=(), w=()):
        return self.op("pe", fn, r, w)

    def dma(self, q, out, in_, r=(), w=()):
        e = self.eng[q]
        d = self.dq[q]
        i = d["n"] % KD
        d["n"] += 1
        sem = d["sems"][i]
        deps = self._deps(e, r, w, False, True)
        if d["cnt"][i] > 0:
            deps.append((sem, d["cnt"][i] * 16))
        self._wait(e, deps)
        e["h"].dma_start(out=out, in_=in_).then_inc(sem, 16)
        self.ninst += 1
        d["cnt"][i] += 1
        tok = (sem, d["cnt"][i] * 16)
        self._mark(tok, r, w)
        return tok

    def barrier(self):
        toks = []
        for e in self.eng.values():
            if e["cnt"] > 0:
                toks.append((e["sem"], e["cnt"]))
        for d in self.dq.values():
            for i in range(KD):
                if d["cnt"][i] > 0:
                    toks.append((d["sems"][i], d["cnt"][i] * 16))
        for e in self.eng.values():
            self._wait(e, toks)


class Consts:
    pass


CONST = None


def load_consts(kb, cdram):
    c = Consts()
    cf = kb.sb("cf", [128, CW], F32)
    kb.dma("sp", cf[:, :], cdram[:, :], r=[cdram], w=[cf])
    c.cf = cf
    c.identf = lambda: cf[:, 0:128]
    c.ones = lambda: cf[:, 128:256]
    c.triu = lambda: cf[:, 256:384]
    c.negmask = lambda: cf[:, 384:512]
    c.tri_q = lambda: cf[:, 512:640]
    c.iota128 = lambda: cf[:, 640:768]
    c.negones = lambda: cf[:, 768:896]
    ib = kb.sb("identb", [128, 128], BF16)
    kb.V(lambda e: e.tensor_copy(out=ib[:, :], in_=cf[:, 0:128]), r=[cf], w=[ib])
    c.identb = ib
    io = kb.sb("iotab", [128, 128], BF16)
    kb.V(lambda e: e.tensor_copy(out=io[:, :], in_=cf[:, 640:768]), r=[cf], w=[io])
    c.iotab = io
    c.eps = kb.sb("epsc", [128, 1], F32)
    kb.V(lambda e: e.memset(c.eps[:, :], EPS), w=[c.eps])
    global CONST
    CONST = c
    return c


CW = 896
NRELW = 2432


def host_consts():
    c = np.zeros((128, CW), np.float32)
    i = np.arange(128)
    c[:, 0:128] = np.eye(128, dtype=np.float32)
    c[:, 128:256] = 1.0
    c[:, 256:384] = (i[:, None] <= i[None, :]).astype(np.float32)
    c[:, 384:512] = np.where(i[:, None] <= i[None, :], 0.0, NEG)
    c[:, 512:640] = np.where(i[None, :] <= i[:, None], 0.0, NEG)
    c[:, 640:768] = i[None, :].astype(np.float32)
    c[:, 768:896] = -1.0
    return c


def host_nrel():
    return np.ascontiguousarray(np.broadcast_to((np.arange(NRELW) - 2304)[None, :].astype(np.float32), (128, NRELW)))


def bcast_row(kb, name, src_ap, n, q="sp", rbuf=None):
    t = kb.sb(name, [128, n], F32)
    kb.dma(q, t[:, :], src_ap.partition_broadcast(128), r=[rbuf] if rbuf else [], w=[t])
    return t


def load_w_bf(kb, dst, dst_ap_fn, src_ap, rows_k, cols, stage, cast_engs=("act", "dve")):
    step = stage[0].t.shape[1]
    n = 0
    for k in range(rows_k):
        for c0 in range(0, cols, step):
            cw = min(step, cols - c0)
            st = stage[n % len(stage)]
            kb.dma("sp" if n % 2 == 0 else "pool", st[:, 0:cw], src_ap[k * 128:(k + 1) * 128, c0:c0 + cw], w=[st])
            en = cast_engs[n % len(cast_engs)]
            if en == "act":
                kb.A(lambda e, st=st, k=k, c0=c0, cw=cw: e.activation(out=dst_ap_fn(k, c0, cw), in_=st[:, 0:cw], func=AF.Copy), r=[st], w=[dst])
            else:
                kb.op(en, lambda e, st=st, k=k, c0=c0, cw=cw: e.tensor_copy(out=dst_ap_fn(k, c0, cw), in_=st[:, 0:cw]), r=[st], w=[dst])
            n += 1


def transpose_to(kb, c, src_bf, ncol_chunks, pst, dst, dst_ap, evac="dve"):
    for c0 in range(0, ncol_chunks, 8):
        nn = min(8, ncol_chunks - c0)
        for j in range(nn):
            kb.T(lambda e, j=j, c0=c0: e.transpose(out=pst[:, j * 128:(j + 1) * 128], in_=src_bf[:, (c0 + j) * 128:(c0 + j + 1) * 128], identity=c.identb[:, :]),
                 r=[src_bf, c.identb], w=[pst])
        o = dst_ap(c0, nn)
        i = pst[:, 0:nn * 128].rearrange("p (a b) -> p a b", b=128)
        if evac == "act":
            kb.A(lambda e, o=o, i=i: e.activation(out=o, in_=i, func=AF.Copy), r=[pst], w=[dst])
        else:
            kb.V(lambda e, o=o, i=i: e.tensor_copy(out=o, in_=i), r=[pst], w=[dst])


def layer_norm(kb, y, g_bc, b_bc, out, stats, mv, rstd):
    kb.V(lambda e: e.bn_stats(out=stats[:, 0:6], in_=y[:, 0:512]), r=[y], w=[stats])
    kb.V(lambda e: e.bn_stats(out=stats[:, 6:12], in_=y[:, 512:1024]), r=[y], w=[stats])
    kb.V(lambda e: e.bn_aggr(out=mv[:, 0:2], in_=stats[:, 0:12]), r=[stats], w=[mv])
    kb.A(lambda e: e.activation(out=rstd[:, 0:1], in_=mv[:, 1:2], func=AF.Ln, bias=CONST.eps[:, 0:1]), r=[mv, CONST.eps], w=[rstd])
    kb.A(lambda e: e.activation(out=rstd[:, 0:1], in_=rstd[:, 0:1], func=AF.Exp, scale=-0.5), r=[rstd], w=[rstd])
    kb.V(lambda e: e.tensor_scalar(out=out[:, :], in0=y[:, :], scalar1=mv[:, 0:1], scalar2=rstd[:, 0:1], op0=ALU.subtract, op1=ALU.mult), r=[y, mv, rstd], w=[out])
    kb.V(lambda e: e.tensor_tensor(out=out[:, :], in0=out[:, :], in1=g_bc[:, :], op=ALU.mult), r=[out, g_bc], w=[out])
    kb.V(lambda e: e.tensor_tensor(out=out[:, :], in0=out[:, :], in1=b_bc[:, :], op=ALU.add), r=[out, b_bc], w=[out])


def resid_ln_store(kb, xt, mixps, g_bc, b_bc, ybuf, tmp, dst_dram, dst_ap, q="sp"):
    for h in range(2):
        kb.V(lambda e, h=h: e.scalar_tensor_tensor(out=ybuf[:, h * 512:(h + 1) * 512], in0=xt[:, h * 512:(h + 1) * 512], scalar=ALPHA,
                                                  in1=mixps[h][:, 0:512], op0=ALU.mult, op1=ALU.add), r=[xt, mixps[h]], w=[ybuf])
    layer_norm(kb, ybuf, g_bc, b_bc, ybuf, tmp["stats"], tmp["mv"], tmp["rstd"])
    if dst_ap is not None:
        kb.dma(q, dst_ap, ybuf[:, :], r=[ybuf], w=[dst_dram])


def ln_tmp(kb):
    return dict(stats=kb.sb("stats", [128, 12], F32), mv=kb.sb("mv", [128, 2], F32), rstd=kb.sb("rstd", [128, 1], F32))


GELU_MODE = "af"


def peer_prepass(kb, c, u_ap, v_ap, UVs, NCH=128):
    with kb.scope():
        st = [kb.sb("pp_st", [128, 1024], F32) for _ in range(4)]
        ub = [kb.sb("pp_ub", [128, 1024], BF16) for _ in range(2)]
        ut = [kb.sb("pp_ut", [128, 1024], BF16) for _ in range(2)]
        vb = [kb.sb("pp_vb", [128, 1024], BF16) for _ in range(2)]
        pst = [kb.ps("pp_ps", [128, 1024], BF16) for _ in range(2)]
        for ch in range(NCH):
            su, sv = st[(2 * ch) % 4], st[(2 * ch + 1) % 4]
            kb.dma("sp", su[:, :], u_ap[ch * 128:(ch + 1) * 128, :], w=[su])
            kb.dma("pool", sv[:, :], v_ap[ch * 128:(ch + 1) * 128, :], w=[sv])
            b = ub[ch % 2]
            kb.A(lambda e, b=b, su=su: e.activation(out=b[:, :], in_=su[:, :], func=AF.Copy), r=[su], w=[b])
            p = pst[ch % 2]
            for k in range(8):
                kb.T(lambda e, k=k, b=b, p=p: e.transpose(out=p[:, k * 128:(k + 1) * 128], in_=b[:, k * 128:(k + 1) * 128], identity=c.identb[:, :]),
                     r=[b, c.identb], w=[p])
            t = ut[ch % 2]
            kb.V(lambda e, t=t, p=p: e.tensor_copy(out=t[:, :], in_=p[:, :]), r=[p], w=[t])
            kb.dma("sp", UVs[ch][:, 0:1024], t[:, :], r=[t], w=[UVs])
            vv = vb[ch % 2]
            if ch % 2 == 0:
                kb.V(lambda e, vv=vv, sv=sv: e.tensor_copy(out=vv[:, :], in_=sv[:, :]), r=[sv], w=[vv])
            else:
                kb.A(lambda e, vv=vv, sv=sv: e.activation(out=vv[:, :], in_=sv[:, :], func=AF.Copy), r=[sv], w=[vv])
            kb.dma("pool", UVs[ch][:, 1024:2048], vv[:, :], r=[vv], w=[UVs])


def peer_q_phase(kb, c, T, Y1, W, XTd, QTd):
    GS = 512
    with kb.scope():
        wq = kb.sb("wq", [128, 8, 2048], BF16)
        with kb.scope():
            stage = [kb.sb("stg", [128, 2048], F32) for _ in range(2)]
            load_w_bf(kb, wq, lambda k, c0, cw: wq[:, k, c0:c0 + cw], W["peer_w_q"], 8, 2048, stage)
        xt = [kb.sb("xt", [128, 1024], F32) for _ in range(2)]
        xbf = [kb.sb("xbf", [128, 1024], BF16) for _ in range(2)]
        xT = [kb.sb("xT5", [128, 8, GS], BF16) for _ in range(2)]
        qT = [kb.sb("qT5", [128, 16, GS], BF16) for _ in range(2)]
        pq = [kb.ps("pq", [128, 512], F32) for _ in range(4)]
        ptr = [kb.ps("ptr", [128, 1024], BF16) for _ in range(2)]
        XTv = XTd.t.rearrange("k p t -> p k t")
        QTv = QTd.t.rearrange("k p t -> p k t")
        n = 0
        for b in range(T // GS):
            t0 = b * GS
            x5, q5 = xT[b % 2], qT[b % 2]
            for ti in range(4):
                x_, xb_ = xt[ti % 2], xbf[ti % 2]
                kb.dma("sp" if ti % 2 == 0 else "pool", x_[:, :], Y1[t0 + ti * 128:t0 + (ti + 1) * 128, :], r=[Y1], w=[x_])
                kb.A(lambda e, x_=x_, xb_=xb_: e.activation(out=xb_[:, :], in_=x_[:, :], func=AF.Copy), r=[x_], w=[xb_])
                transpose_to(kb, c, xb_, 8, ptr[ti % 2], x5, lambda c0, nn, ti=ti, x5=x5: x5[:, c0:c0 + nn, ti * 128:(ti + 1) * 128])
            kb.dma("sp", XTv[:, :, t0:t0 + GS], x5[:, :, :], r=[x5], w=[XTd])
            for hc in range(16):
                p_ = pq[n % 4]
                n += 1
                for k in range(8):
                    kb.T(lambda e, hc=hc, k=k, p_=p_, x5=x5: e.matmul(p_[:, :], lhsT=wq[:, k, hc * 128:(hc + 1) * 128], rhs=x5[:, k, :], start=(k == 0), stop=(k == 7)), r=[wq, x5], w=[p_])
                if hc % 2 == 0:
                    kb.A(lambda e, hc=hc, p_=p_, q5=q5: e.activation(out=q5[:, hc, :], in_=p_[:, :], func=AF.Copy), r=[p_], w=[q5])
                else:
                    kb.V(lambda e, hc=hc, p_=p_, q5=q5: e.tensor_copy(out=q5[:, hc, :], in_=p_[:, :]), r=[p_], w=[q5])
            kb.dma("pool", QTv[:, :, t0:t0 + GS], q5[:, :, :], r=[q5], w=[QTd])


def peer_phase(kb, c, T, Y1, OUT, p_ap, W, UVs, XTd, QTd, TG=256, NCH=128):
    NT = TG // 128
    NG = T // TG
    TB = 8
    with kb.scope():
        wg = kb.sb("wg", [128, 8, 1024], BF16)
        wp = kb.sb("wp", [128, 2, 1024], BF16)
        skT = kb.sb("skT", [128, 16, 128], BF16)
        with kb.scope():
            stage = [kb.sb("stg", [128, 2048], F32) for _ in range(2)]
            load_w_bf(kb, wg, lambda k, c0, cw: wg[:, k, c0:c0 + cw], W["ple_w_gate"], 8, 1024, stage)
            load_w_bf(kb, wp, lambda k, c0, cw: wp[:, k, c0:c0 + cw], W["ple_w_proj"], 2, 1024, stage)
            pskt = kb.ps("pskt", [128, 512], F32)
            sk = W["peer_sub_keys"].rearrange("h c n d -> (h c) n d")
            for hc in range(16):
                st = stage[hc % 2]
                kb.dma("sp", st[:, 0:128], sk[hc], w=[st])
                kb.T(lambda e, st=st: e.transpose(out=pskt[:, 0:128], in_=st[:, 0:128], identity=c.identf()), r=[st, c.cf], w=[pskt])
                kb.V(lambda e, hc=hc: e.tensor_copy(out=skT[:, hc, :], in_=pskt[:, 0:128]), r=[pskt], w=[skT])
        g_bc = bcast_row(kb, "lnf_g", W["ln_ffn_g"], 1024)
        b_bc = bcast_row(kb, "lnf_b", W["ln_ffn_b"], 1024, q="pool")
        iotaC = kb.sb("iotaC", [128, TB, 128], BF16)
        kb.V(lambda e: e.tensor_copy(out=iotaC[:, :, :], in_=c.iota128().unsqueeze(1).broadcast_to([128, TB, 128])), r=[c.cf], w=[iotaC])
        iota16 = c.iota128()[:, 0:16]

        xT2 = [kb.sb("xT", [128, 8, TG], BF16) for _ in range(2)]
        qT2 = [kb.sb("qT", [128, 16, TG], BF16) for _ in range(2)]
        T32 = [kb.sb("T3", [128, 3, TG], F32) for _ in range(2)]
        xt = kb.sb("xt", [128, 1024], F32)
        S = kb.sb("S", [128, 16, 128], F32)
        S2 = kb.sb("S2", [128, 256], F32)
        V16 = kb.sb("V16", [128, 16, 16], F32)
        I16u = kb.sb("I16u", [128, 16, 16], U32)
        I16f = kb.sb("I16f", [128, 16, 16], F32)
        cand = kb.sb("cand", [128, 8, 256], F32)
        B16 = kb.sb("B16", [128, 8, 16], F32)
        P16u = kb.sb("P16u", [128, 8, 16], U32)
        ABu = kb.sb("ABu", [128, 2, 128], U32)
        ABf = kb.sb("ABf", [128, 2, 128], F32)
        eq = kb.sb("eq", [128, 8, 16, 16], F32)
        J = kb.sb("J", [128, 3, 128], F32)
        e16 = kb.sb("e16", [128, 8, 16], F32)
        ssum = kb.sb("ssum", [128, 8], F32)
        OH1 = [kb.sb("OH1", [128, TB, 128], BF16) for _ in range(2)]
        OH2g = [kb.sb("OH2g", [128, TB, 128], BF16) for _ in range(2)]
        GT = kb.sb("GT", [128, TG, 128], BF16)
        NB = 4
        UVb = [kb.sb("UVb", [128, 2048], BF16) for _ in range(NB)]
        Aact = [kb.sb("Aact", [128, TG], F32) for _ in range(2)]
        Wtb = [kb.sb("Wtb", [128, TG], BF16) for _ in range(3)]
        ybuf = kb.sb("ybuf", [128, 1024], F32)
        y2bf = kb.sb("y2bf", [128, 1024], BF16)
        y2T = kb.sb("y2T", [128, 8, 128], BF16)
        pt = kb.sb("pt", [128, 256], F32)
        pbf = kb.sb("pbf", [128, 256], BF16)
        pT = kb.sb("pT", [128, 2, 128], BF16)
        Sflat = S[:, :, :].rearrange("p a b -> p (a b)")
        sg = Alias(S, Sflat[:, 0:1024])
        ob = Alias(S, Sflat[:, 1024:2048])
        tmp = ln_tmp(kb)
        acc = [[kb.ps("acc", [128, 512], F32) for _ in range(2)] for _ in range(NT)]
        stp = [kb.ps("stp", [128, 512], F32) for _ in range(2)]
        mps = kb.ps("mps", [128, 512], F32)
        mpsb = kb.ps("mpsb", [128, 1024], BF16)
        if NT == 1:
            gps2 = kb.ps("gps", [128, 512], F32)
        gbanks = [mps] + [b for pair in acc for b in pair] if NT > 1 else [mps, gps2] + [b for pair in acc for b in pair]
        XTv = XTd.t.rearrange("k p t -> p k t")
        QTv = QTd.t.rearrange("k p t -> p k t")

        def stage1(g):
            t0 = g * TG
            xT, qT, T3 = xT2[g % 2], qT2[g % 2], T32[g % 2]
            kb.dma("sp", xT[:, :, :], XTv[:, :, t0:t0 + TG], r=[XTd], w=[xT])
            kb.dma("pool", qT[:, :, :], QTv[:, :, t0:t0 + TG], r=[QTd], w=[qT])
            yield
            for ti in range(NT):
                tsl = slice(ti * 128, (ti + 1) * 128)
                for h4 in range(4):
                    for j in range(4):
                        hc = h4 * 4 + j
                        kb.T(lambda e, hc=hc, j=j: e.matmul(mps[:, j * 128:(j + 1) * 128], lhsT=qT[:, hc, tsl], rhs=skT[:, hc, :], start=True, stop=True),
                             r=[qT, skT], w=[mps])
                    kb.V(lambda e, h4=h4: e.tensor_copy(out=S[:, h4 * 4:(h4 + 1) * 4, :], in_=mps[:, :].rearrange("p (a b) -> p a b", b=128)), r=[mps], w=[S])
                    yield
                for hc in range(16):
                    kb.V(lambda e, hc=hc: e.max(out=V16[:, hc, 0:8], in_=S[:, hc, :]), r=[S], w=[V16])
                    kb.V(lambda e, hc=hc: e.max_index(out=I16u[:, hc, 0:8], in_max=V16[:, hc, 0:8], in_values=S[:, hc, :]), r=[S, V16], w=[I16u])
                    kb.V(lambda e, hc=hc: e.match_replace(out=S2[:, 0:128], in_to_replace=V16[:, hc, 0:8], in_values=S[:, hc, :], imm_value=NEG), r=[S, V16], w=[S2])
                    yield
                    kb.V(lambda e, hc=hc: e.max(out=V16[:, hc, 8:16], in_=S2[:, 0:128]), r=[S2], w=[V16])
                    kb.V(lambda e, hc=hc: e.max_index(out=I16u[:, hc, 8:16], in_max=V16[:, hc, 8:16], in_values=S2[:, 0:128]), r=[S2, V16], w=[I16u])
                    yield
                kb.V(lambda e: e.tensor_copy(out=I16f[:, :, :], in_=I16u[:, :, :]), r=[I16u], w=[I16f])
                V4 = V16[:, :, :].rearrange("p (h c) k -> p h c k", c=2)
                I4 = I16f[:, :, :].rearrange("p (h c) k -> p h c k", c=2)
                cand4 = cand[:, :, :].rearrange("p h (a b) -> p h a b", b=16)
                kb.V(lambda e: e.tensor_tensor(out=cand4, in0=V4[:, :, 0, :].unsqueeze(3).broadcast_to([128, 8, 16, 16]),
                                               in1=V4[:, :, 1, :].unsqueeze(2).broadcast_to([128, 8, 16, 16]), op=ALU.add), r=[V16], w=[cand])
                yield
                for h in range(8):
                    kb.V(lambda e, h=h: e.max(out=B16[:, h, 0:8], in_=cand[:, h, :]), r=[cand], w=[B16])
                    kb.V(lambda e, h=h: e.max_index(out=P16u[:, h, 0:8], in_max=B16[:, h, 0:8], in_values=cand[:, h, :]), r=[cand, B16], w=[P16u])
                    kb.V(lambda e, h=h: e.match_replace(out=S2[:, :], in_to_replace=B16[:, h, 0:8], in_values=cand[:, h, :], imm_value=NEG), r=[cand, B16], w=[S2])
                    yield
                    kb.V(lambda e, h=h: e.max(out=B16[:, h, 8:16], in_=S2[:, :]), r=[S2], w=[B16])
                    kb.V(lambda e, h=h: e.max_index(out=P16u[:, h, 8:16], in_max=B16[:, h, 8:16], in_values=S2[:, :]), r=[S2, B16], w=[P16u])
                    yield
                Pfl = P16u[:, :, :].rearrange("p h k -> p (h k)")
                kb.V(lambda e: e.tensor_single_scalar(out=ABu[:, 0, :], in_=Pfl, scalar=4, op=ALU.logical_shift_right), r=[P16u], w=[ABu])
                kb.V(lambda e: e.tensor_single_scalar(out=ABu[:, 1, :], in_=Pfl, scalar=15, op=ALU.bitwise_and), r=[P16u], w=[ABu])
                kb.V(lambda e: e.tensor_copy(out=ABf[:, :, :], in_=ABu[:, :, :]), r=[ABu], w=[ABf])
                yield
                for ci in range(2):
                    ab4 = ABf[:, ci, :].rearrange("p (h k) -> p h k", k=16).unsqueeze(3).broadcast_to([128, 8, 16, 16])
                    kb.V(lambda e, ab4=ab4: e.tensor_tensor(out=eq[:, :, :, :], in0=ab4, in1=iota16.unsqueeze(1).unsqueeze(1).broadcast_to([128, 8, 16, 16]), op=ALU.is_equal),
                         r=[ABf, c.cf], w=[eq])
                    yield
                    kb.V(lambda e, ci=ci: e.tensor_tensor(out=eq[:, :, :, :], in0=eq[:, :, :, :], in1=I4[:, :, ci, :].unsqueeze(2).broadcast_to([128, 8, 16, 16]), op=ALU.mult),
                         r=[eq, I16f], w=[eq])
                    yield
                    kb.V(lambda e, ci=ci: e.tensor_reduce(out=J[:, ci, :].rearrange("p (h k) -> p h k", k=16), in_=eq[:, :, :, :], axis=AX.X, op=ALU.add), r=[eq], w=[J])
                    yield
                kb.V(lambda e: e.tensor_tensor(out=e16[:, :, :], in0=B16[:, :, :], in1=B16[:, :, 0:1].broadcast_to([128, 8, 16]), op=ALU.subtract), r=[B16], w=[e16])
                kb.A(lambda e: e.activation(out=e16[:, :, :], in_=e16[:, :, :], func=AF.Exp), r=[e16], w=[e16])
                kb.V(lambda e: e.tensor_reduce(out=ssum[:, :], in_=e16[:, :, :], axis=AX.X, op=ALU.add), r=[e16], w=[ssum])
                yield
                kb.V(lambda e: e.reciprocal(out=ssum[:, :], in_=ssum[:, :]), r=[ssum], w=[ssum])
                kb.V(lambda e: e.tensor_tensor(out=J[:, 2, :].rearrange("p (h k) -> p h k", k=16), in0=e16[:, :, :], in1=ssum[:, :].unsqueeze(2).broadcast_to([128, 8, 16]), op=ALU.mult),
                     r=[e16, ssum], w=[J])
                yield
                for q3 in range(3):
                    kb.T(lambda e, q3=q3: e.transpose(out=mps[:, q3 * 128:(q3 + 1) * 128], in_=J[:, q3, :], identity=c.identf()), r=[J, c.cf], w=[mps])
                kb.V(lambda e: e.tensor_copy(out=T3[:, :, tsl], in_=mps[:, 0:384].rearrange("p (a b) -> p a b", b=128)), r=[mps], w=[T3])
                yield

        def stage2(g):
            T3f = T32[g % 2]
            nsb = 0
            for s0 in range(0, TG, TB):
                o1, o2g = OH1[nsb % 2], OH2g[nsb % 2]
                nsb += 1
                for tb in range(TB):
                    t_ = s0 + tb
                    kb.V(lambda e, o1=o1, tb=tb, t_=t_: e.tensor_scalar(out=o1[:, tb, :], in0=c.iotab[:, :], scalar1=T3f[:, 0, t_:t_ + 1], scalar2=None, op0=ALU.is_equal),
                         r=[c.iotab, T3f], w=[o1])
                    kb.V(lambda e, o2g=o2g, tb=tb, t_=t_: e.tensor_scalar(out=o2g[:, tb, :], in0=c.iotab[:, :], scalar1=T3f[:, 1, t_:t_ + 1], scalar2=T3f[:, 2, t_:t_ + 1], op0=ALU.is_equal, op1=ALU.mult),
                         r=[c.iotab, T3f], w=[o2g])
                for tb in range(TB):
                    gp = gbanks[((s0 + tb) // 4) % len(gbanks)]
                    kb.T(lambda e, gp=gp, tb=tb, o1=o1, o2g=o2g: e.matmul(gp[:, (tb % 4) * 128:(tb % 4 + 1) * 128], lhsT=o2g[:, tb, :], rhs=o1[:, tb, :], start=True, stop=True),
                         r=[o1, o2g], w=[gp])
                    if tb % 4 == 3:
                        tt = s0 + tb - 3
                        kb.A(lambda e, gp=gp, tt=tt: e.activation(out=GT[:, tt:tt + 4, :], in_=gp[:, :].rearrange("p (a b) -> p a b", b=128), func=AF.Copy), r=[gp], w=[GT])

        def chunk_loop(g, nxt):
            xT = xT2[g % 2]

            def emit_u(ch):
                uvb = UVb[ch % NB]
                kb.dma("sp" if ch % 2 == 0 else "pool", uvb[:, :], UVs[ch], r=[UVs], w=[uvb])
                ub = Alias(uvb, uvb[:, 0:1024])
                sp_ = stp[ch % 2]
                for k in range(8):
                    kb.T(lambda e, k=k, ub=ub, sp_=sp_: e.matmul(sp_[:, 0:TG], lhsT=ub[:, k * 128:(k + 1) * 128], rhs=xT[:, k, :], start=(k == 0), stop=(k == 7)),
                         r=[ub, xT], w=[sp_])
                a_, w_ = Aact[ch % 2], Wtb[ch % 3]
                kb.A(lambda e, a_=a_, sp_=sp_: e.activation(out=a_[:, :], in_=sp_[:, 0:TG], func=AF.Gelu_apprx_tanh), r=[sp_], w=[a_])
                kb.V(lambda e, a_=a_, w_=w_, ch=ch: e.tensor_tensor(out=w_[:, :], in0=a_[:, :], in1=GT[:, :, ch], op=ALU.mult), r=[a_, GT], w=[w_])

            def emit_v(ch):
                uvb = UVb[ch % NB]
                vb2 = Alias(uvb, uvb[:, 1024:2048])
                w_ = Wtb[ch % 3]
                for ti in range(NT):
                    for hf in range(2):
                        kb.T(lambda e, ti=ti, hf=hf, w_=w_, vb2=vb2, ch=ch: e.matmul(acc[ti][hf][:, 0:512], lhsT=w_[:, ti * 128:(ti + 1) * 128], rhs=vb2[:, hf * 512:(hf + 1) * 512],
                                                                                  start=(ch == 0), stop=(ch == NCH - 1)), r=[w_, vb2], w=[acc[ti][hf]])

            for ch in range(NCH):
                emit_u(ch)
                if ch >= 1:
                    emit_v(ch - 1)
                if nxt is not None and ch >= 2:
                    next(nxt, None)
            emit_v(NCH - 1)
            if nxt is not None:
                for _ in nxt:
                    pass

        def epilogue(g):
            t0 = g * TG
            for ti in range(NT):
                tok0 = t0 + ti * 128
                kb.dma("sp", xt[:, :], Y1[tok0:tok0 + 128, :], r=[Y1], w=[xt])
                kb.dma("pool", pt[:, :], p_ap[tok0:tok0 + 128, :], w=[pt])
                resid_ln_store(kb, xt, acc[ti], g_bc, b_bc, ybuf, tmp, None, None)
                kb.A(lambda e: e.activation(out=y2bf[:, :], in_=ybuf[:, :], func=AF.Copy), r=[ybuf], w=[y2bf])
                transpose_to(kb, c, y2bf, 8, mpsb, y2T, lambda c0, nn: y2T[:, c0:c0 + nn, :])
                kb.A(lambda e: e.activation(out=pbf[:, :], in_=pt[:, :], func=AF.Copy), r=[pt], w=[pbf])
                transpose_to(kb, c, pbf, 2, mpsb, pT, lambda c0, nn: pT[:, c0:c0 + nn, :])
                for hf in range(2):
                    hs = slice(hf * 512, (hf + 1) * 512)
                    for k in range(8):
                        kb.T(lambda e, k=k, hs=hs: e.matmul(mps[:, :], lhsT=y2T[:, k, :], rhs=wg[:, k, hs], start=(k == 0), stop=(k == 7)), r=[y2T, wg], w=[mps])
                    kb.A(lambda e, hs=hs: e.activation(out=sg[:, hs], in_=mps[:, :], func=AF.Sigmoid), r=[mps], w=[sg])
                    for k in range(2):
                        kb.T(lambda e, k=k, hs=hs: e.matmul(mps[:, :], lhsT=pT[:, k, :], rhs=wp[:, k, hs], start=(k == 0), stop=(k == 1)), r=[pT, wp], w=[mps])
                    kb.V(lambda e, hs=hs: e.tensor_tensor(out=ob[:, hs], in0=sg[:, hs], in1=mps[:, :], op=ALU.mult), r=[sg, mps], w=[ob])
                kb.V(lambda e: e.tensor_tensor(out=ob[:, :], in0=ob[:, :], in1=ybuf[:, :], op=ALU.add), r=[ob, ybuf], w=[ob])
                kb.dma("sp", OUT[tok0:tok0 + 128, :], ob[:, :], r=[ob], w=[OUT])

        for _ in stage1(0):
            pass
        for g in range(NG):
            nxt = stage1(g + 1) if g + 1 < NG else None
            stage2(g)
            chunk_loop(g, nxt)
            epilogue(g)


def load_cols(kb, c, dst, dst_ap_fn, src_rows_ap, R, stage, pst, nblk=1, blk_stride=0):
    for b in range(nblk):
        kb.dma("sp", stage[0:R, 0:128], src_rows_ap(b), w=[stage])
        kb.T(lambda e: e.transpose(out=pst[:, 0:R], in_=stage[0:R, 0:128], identity=c.cf[0:R, 0:R]), r=[stage, c.cf], w=[pst])
        kb.V(lambda e, b=b: e.tensor_copy(out=dst_ap_fn(b), in_=pst[:, 0:R]), r=[pst], w=[dst])


def conf_phase(kb, c, T, S, XIN, Y1, W):
    GS = 512
    NG = T // GS
    GPS = S // GS
    KW = 31
    with kb.scope():
        w1 = kb.sb("w1", [128, 8, 2048], BF16)
        w2 = kb.sb("w2", [128, 8, 1024], BF16)
        b1 = kb.sb("b1", [128, 16], F32)
        wdw = kb.sb("wdw", [128, 8, KW], F32)
        vecs = kb.sb("vecs", [128, 3, 8], F32)
        with kb.scope():
            stage = [kb.sb("stg", [128, 2048], F32) for _ in range(2)]
            load_w_bf(kb, w1, lambda k, c0, cw: w1[:, k, c0:c0 + cw], W["conv_w_pw1"], 8, 2048, stage)
            load_w_bf(kb, w2, lambda k, c0, cw: w2[:, k, c0:c0 + cw], W["conv_w_pw2"], 8, 1024, stage)
            pst = kb.ps("pst", [128, 512], F32)
            load_cols(kb, c, b1, lambda b: b1[:, :], lambda b: W["conv_b_pw1"].rearrange("(c p) -> c p", p=128), 16, stage[0], pst)
            load_cols(kb, c, wdw, lambda b: wdw[:, b, :], lambda b: W["conv_w_dw"][:, b * 128:(b + 1) * 128], KW, stage[1], pst, nblk=8)
            for i, nm in enumerate(("conv_b_dw", "conv_ln_g", "conv_ln_b")):
                load_cols(kb, c, vecs, lambda b, i=i: vecs[:, i, :], lambda b, nm=nm: W[nm].rearrange("(c p) -> c p", p=128), 8, stage[i % 2], pst)
        g_bc = bcast_row(kb, "lnm_g", W["ln_mix_g"], 1024)
        b_bc = bcast_row(kb, "lnm_b", W["ln_mix_b"], 1024, q="pool")
        xt = [kb.sb("xt", [128, 1024], F32) for _ in range(4)]
        xbf = kb.sb("xbf", [128, 1024], BF16)
        xT = kb.sb("xT", [128, 8, GS], BF16)
        gluH = kb.sb("gluH", [128, 8, KW - 1 + GS], F32)
        hc = kb.sb("hc", [128, 8, GS], F32)
        hsq = kb.sb("hsq", [128, 8, GS], F32)
        zT = kb.sb("zT", [128, 8, GS], BF16)
        sgb = [kb.sb("sgb", [128, GS], F32) for _ in range(2)]
        mean = kb.sb("mean", [128, GS], F32)
        msq = kb.sb("msq", [128, GS], F32)
        rstd = kb.sb("rstd2", [128, GS], F32)
        tn = [kb.sb("tn", [128, GS], F32) for _ in range(2)]
        ybuf = kb.sb("ybuf", [128, 1024], F32)
        tmp = ln_tmp(kb)
        pa = kb.ps("pa", [128, 512], F32)
        pg = kb.ps("pg", [128, 512], F32)
        s1 = kb.ps("s1", [128, 512], F32)
        s2 = kb.ps("s2", [128, 512], F32)
        po = [kb.ps("po", [128, 512], F32) for _ in range(2)]
        ptr = kb.ps("ptr", [128, 1024], BF16)
        H = KW - 1
        for g in range(NG):
            t0 = g * GS
            for ti in range(4):
                kb.dma("sp" if ti % 2 == 0 else "pool", xt[ti][:, :], XIN[t0 + ti * 128:t0 + (ti + 1) * 128, :], r=[XIN], w=[xt[ti]])
                kb.A(lambda e, ti=ti: e.activation(out=xbf[:, :], in_=xt[ti][:, :], func=AF.Copy), r=[xt[ti]], w=[xbf])
                transpose_to(kb, c, xbf, 8, ptr, xT, lambda c0, nn, ti=ti: xT[:, c0:c0 + nn, ti * 128:(ti + 1) * 128])
            if g % GPS == 0:
                kb.G(lambda e: e.memset(gluH[:, :, 0:H], 0.0), w=[gluH])
            for cc in range(8):
                for k in range(8):
                    kb.T(lambda e, k=k, cc=cc: e.matmul(pa[:, :], lhsT=w1[:, k, cc * 128:(cc + 1) * 128], rhs=xT[:, k, :], start=(k == 0), stop=(k == 7)), r=[w1, xT], w=[pa])
                for k in range(8):
                    kb.T(lambda e, k=k, cc=cc: e.matmul(pg[:, :], lhsT=w1[:, k, 1024 + cc * 128:1024 + (cc + 1) * 128], rhs=xT[:, k, :], start=(k == 0), stop=(k == 7)), r=[w1, xT], w=[pg])
                sg_ = sgb[cc % 2]
                kb.A(lambda e, sg_=sg_, cc=cc: e.activation(out=sg_[:, :], in_=pg[:, :], func=AF.Sigmoid, bias=b1[:, 8 + cc:9 + cc]), r=[pg, b1], w=[sg_])
                kb.V(lambda e, sg_=sg_, cc=cc: e.scalar_tensor_tensor(out=gluH[:, cc, H:H + GS], in0=pa[:, :], scalar=b1[:, cc:cc + 1], in1=sg_[:, :], op0=ALU.add, op1=ALU.mult),
                     r=[pa, b1, sg_], w=[gluH])
                kb.V(lambda e, cc=cc: e.tensor_scalar(out=hc[:, cc, :], in0=gluH[:, cc, H:H + GS], scalar1=wdw[:, cc, H:H + 1], scalar2=vecs[:, 0, cc:cc + 1], op0=ALU.mult, op1=ALU.add),
                     r=[gluH, wdw, vecs], w=[hc])
                for k in range(H):
                    kb.V(lambda e, cc=cc, k=k: e.scalar_tensor_tensor(out=hc[:, cc, :], in0=gluH[:, cc, k:k + GS], scalar=wdw[:, cc, k:k + 1], in1=hc[:, cc, :], op0=ALU.mult, op1=ALU.add),
                         r=[gluH, wdw, hc], w=[hc])
            kb.G(lambda e: e.tensor_copy(out=gluH[:, :, 0:H], in_=gluH[:, :, GS:GS + H]), r=[gluH], w=[gluH])
            kb.A(lambda e: e.activation(out=hsq[:, :, :], in_=hc[:, :, :], func=AF.Square), r=[hc], w=[hsq])
            for cc in range(8):
                kb.T(lambda e, cc=cc: e.matmul(s1[:, :], lhsT=c.ones(), rhs=hc[:, cc, :], start=(cc == 0), stop=(cc == 7)), r=[c.cf, hc], w=[s1])
            for cc in range(8):
                kb.T(lambda e, cc=cc: e.matmul(s2[:, :], lhsT=c.ones(), rhs=hsq[:, cc, :], start=(cc == 0), stop=(cc == 7)), r=[c.cf, hsq], w=[s2])
            kb.V(lambda e: e.tensor_scalar(out=mean[:, :], in0=s1[:, :], scalar1=1.0 / 1024, scalar2=None, op0=ALU.mult), r=[s1], w=[mean])
            kb.V(lambda e: e.tensor_tensor(out=msq[:, :], in0=mean[:, :], in1=mean[:, :], op=ALU.mult), r=[mean], w=[msq])
            kb.V(lambda e: e.scalar_tensor_tensor(out=msq[:, :], in0=s2[:, :], scalar=1.0 / 1024, in1=msq[:, :], op0=ALU.mult, op1=ALU.subtract), r=[s2, msq], w=[msq])
            kb.A(lambda e: e.activation(out=rstd[:, :], in_=msq[:, :], func=AF.Ln, bias=CONST.eps[:, 0:1]), r=[msq, CONST.eps], w=[rstd])
            kb.A(lambda e: e.activation(out=rstd[:, :], in_=rstd[:, :], func=AF.Exp, scale=-0.5), r=[rstd], w=[rstd])
            for cc in range(8):
                t_ = tn[cc % 2]
                kb.G(lambda e, cc=cc, t_=t_: e.tensor_tensor(out=t_[:, :], in0=hc[:, cc, :], in1=mean[:, :], op=ALU.subtract), r=[hc, mean], w=[t_])
                kb.V(lambda e, t_=t_: e.tensor_tensor(out=t_[:, :], in0=t_[:, :], in1=rstd[:, :], op=ALU.mult), r=[t_, rstd], w=[t_])
                kb.V(lambda e, cc=cc, t_=t_: e.tensor_scalar(out=t_[:, :], in0=t_[:, :], scalar1=vecs[:, 1, cc:cc + 1], scalar2=vecs[:, 2, cc:cc + 1], op0=ALU.mult, op1=ALU.add), r=[t_, vecs], w=[t_])
                kb.A(lambda e, cc=cc, t_=t_: e.activation(out=zT[:, cc, :], in_=t_[:, :], func=AF.Silu), r=[t_], w=[zT])
            for ti in range(4):
                for hf in range(2):
                    for cc in range(8):
                        kb.T(lambda e, ti=ti, hf=hf, cc=cc: e.matmul(po[hf][:, :], lhsT=zT[:, cc, ti * 128:(ti + 1) * 128], rhs=w2[:, cc, hf * 512:(hf + 1) * 512], start=(cc == 0), stop=(cc == 7)),
                             r=[zT, w2], w=[po[hf]])
                resid_ln_store(kb, xt[ti], po, g_bc, b_bc, ybuf, tmp, Y1, Y1[t0 + ti * 128:t0 + (ti + 1) * 128, :], q="sp" if ti % 2 == 0 else "pool")


C2W = NRELW + 72


def host_c2():
    c2 = np.zeros((128, C2W), np.float32)
    c2[:, 0:NRELW] = (np.arange(NRELW) - 2304)[None, :]
    for own in range(9):
        c2[:, NRELW + own * 8:NRELW + own * 8 + 8] = np.where(np.arange(8) < own, 0.0, NEG)[None, :]
    return c2


def moba_phase(kb, c, T, S, XIN, Y1, W, c2dram):
    NSEQ = T // S
    NQ = S // 128
    NBLK = S // 256
    GS = 512
    with kb.scope():
        qT = kb.sb("qT_all", [128, 8, S], BF16)
        kT = kb.sb("kT_all", [128, 8, S], BF16)
        va = kb.sb("v_all", [128, NQ, 1024], BF16)
        kmf = kb.sb("kmf", [128, 8, 8], F32)
        kmT = kb.sb("kmT", [128, 8, 8], BF16)
        for sq in range(NSEQ):
            base = sq * S
            with kb.scope():
                wqkv = kb.sb("wqkv", [128, 8, 3072], BF16)
                stage = [kb.sb("stg", [128, 1024], F32) for _ in range(2)]
                load_w_bf(kb, wqkv, lambda k, c0, cw: wqkv[:, k, c0:c0 + cw], W["moba_w_qkv"], 8, 3072, stage)
                xt = [kb.sb("xt", [128, 1024], F32) for _ in range(2)]
                xbf = kb.sb("xbf", [128, 1024], BF16)
                xT = kb.sb("xT", [128, 8, GS], BF16)
                pp = [kb.ps("pp", [128, 512], F32) for _ in range(2)]
                ptr = kb.ps("ptr", [128, 1024], BF16)
                n = 0
                for g in range(S // GS):
                    t0 = base + g * GS
                    for ti in range(4):
                        x_ = xt[ti % 2]
                        kb.dma("sp" if ti % 2 == 0 else "pool", x_[:, :], XIN[t0 + ti * 128:t0 + (ti + 1) * 128, :], r=[XIN], w=[x_])
                        kb.A(lambda e, x_=x_: e.activation(out=xbf[:, :], in_=x_[:, :], func=AF.Copy), r=[x_], w=[xbf])
                        transpose_to(kb, c, xbf, 8, ptr, xT, lambda c0, nn, ti=ti: xT[:, c0:c0 + nn, ti * 128:(ti + 1) * 128])
                    gsl = slice(g * GS, (g + 1) * GS)
                    for pr in range(16):
                        p_ = pp[n % 2]
                        n += 1
                        for k in range(8):
                            kb.T(lambda e, k=k, pr=pr, p_=p_: e.matmul(p_[:, :], lhsT=wqkv[:, k, pr * 128:(pr + 1) * 128], rhs=xT[:, k, :], start=(k == 0), stop=(k == 7)), r=[wqkv, xT], w=[p_])
                        if pr < 8:
                            kb.A(lambda e, pr=pr, p_=p_: e.activation(out=qT[:, pr, gsl], in_=p_[:, :], func=AF.Copy, scale=0.125), r=[p_], w=[qT])
                        else:
                            kb.V(lambda e, pr=pr, p_=p_: e.tensor_copy(out=kT[:, pr - 8, gsl], in_=p_[:, :]), r=[p_], w=[kT])
                    for ti in range(4):
                        for hf in range(2):
                            p_ = pp[n % 2]
                            n += 1
                            for k in range(8):
                                kb.T(lambda e, k=k, ti=ti, hf=hf, p_=p_: e.matmul(p_[:, :], lhsT=xT[:, k, ti * 128:(ti + 1) * 128], rhs=wqkv[:, k, 2048 + hf * 512:2048 + (hf + 1) * 512], start=(k == 0), stop=(k == 7)),
                                     r=[wqkv, xT], w=[p_])
                            if hf == 0:
                                kb.A(lambda e, ti=ti, g=g, p_=p_: e.activation(out=va[:, g * 4 + ti, 0:512], in_=p_[:, :], func=AF.Copy), r=[p_], w=[va])
                            else:
                                kb.V(lambda e, ti=ti, g=g, p_=p_: e.tensor_copy(out=va[:, g * 4 + ti, 512:1024], in_=p_[:, :]), r=[p_], w=[va])
                kb.V(lambda e: e.tensor_reduce(out=kmf[:, :, 0:NBLK], in_=kT[:, :, :].rearrange("p a (b j) -> p a b j", j=256), axis=AX.X, op=ALU.add), r=[kT], w=[kmf])
                kb.A(lambda e: e.activation(out=kmT[:, :, 0:NBLK], in_=kmf[:, :, 0:NBLK], func=AF.Copy, scale=1.0 / 256), r=[kmf], w=[kmT])
            with kb.scope():
                wo = kb.sb("wo", [128, 8, 1024], BF16)
                with kb.scope():
                    stage = [kb.sb("stg", [128, 1024], F32) for _ in range(2)]
                    load_w_bf(kb, wo, lambda k, c0, cw: wo[:, k, c0:c0 + cw], W["moba_w_out"], 8, 1024, stage)
                c2 = kb.sb("c2", [128, C2W], F32)
                kb.dma("sp", c2[:, :], c2dram[:, :], r=[c2dram], w=[c2])
                g_bc = bcast_row(kb, "lnm_g", W["ln_mix_g"], 1024)
                b_bc = bcast_row(kb, "lnm_b", W["ln_mix_b"], 1024, q="pool")
                xt = kb.sb("xt", [128, 1024], F32)
                L2 = [kb.sb("L", [128, S], F32) for _ in range(2)]
                Pb2 = [kb.sb("Pb", [128, S], BF16) for _ in range(2)]
                PT2 = [kb.sb("PT", [128, NQ, 128], BF16) for _ in range(2)]
                gm = kb.sb("gm", [128, 16, 8], F32)
                m8 = kb.sb("m8", [128, 16, 8], F32)
                selb = kb.sb("selb", [128, 16, 8], F32)
                rmax2 = [kb.sb("rmax", [128, 1], F32) for _ in range(2)]
                rsum2 = [kb.sb("rsum", [128, 1], F32) for _ in range(2)]
                attn = kb.sb("attn", [128, 1024], BF16)
                attnT = kb.sb("attnT", [128, 8, 128], BF16)
                ybuf = kb.sb("ybuf", [128, 1024], F32)
                tmp = ln_tmp(kb)
                pl = [kb.ps("pl", [128, 512], F32) for _ in range(4)]
                ptp = kb.ps("ptp", [128, 1024], BF16)
                pv = kb.ps("pv", [128, 512], F32)
                po = [kb.ps("po", [128, 512], F32) for _ in range(2)]
                for qi in range(NQ):
                    q0 = qi * 128
                    own = qi // 2
                    nk = q0 + 128
                    qs = slice(q0, q0 + 128)
                    kb.dma("pool", xt[:, :], XIN[base + q0:base + q0 + 128, :], r=[XIN], w=[xt])
                    gated = own >= 4
                    if gated:
                        pgt = po[1]
                        for h in range(16):
                            pr, r0 = h // 2, (h % 2) * 64
                            kb.T(lambda e, h=h, pr=pr, r0=r0: e.matmul(pgt[:, h * 8:(h + 1) * 8], lhsT=qT[r0:r0 + 64, pr, qs], rhs=kmT[r0:r0 + 64, pr, 0:8], start=True, stop=True), r=[qT, kmT], w=[pgt])
                        kb.V(lambda e: e.tensor_tensor(out=gm[:, :, :], in0=pgt[:, 0:128].rearrange("p (h n) -> p h n", n=8),
                                                       in1=c2[:, NRELW + own * 8:NRELW + own * 8 + 8].unsqueeze(1).broadcast_to([128, 16, 8]), op=ALU.add), r=[pgt, c2], w=[gm])
                        for h in range(16):
                            kb.V(lambda e, h=h: e.max(out=m8[:, h, :], in_=gm[:, h, :]), r=[gm], w=[m8])
                        kb.V(lambda e: e.tensor_tensor(out=selb[:, :, :], in0=gm[:, :, :], in1=m8[:, :, 2:3].broadcast_to([128, 16, 8]), op=ALU.is_ge), r=[gm, m8], w=[selb])
                        kb.V(lambda e: e.tensor_scalar(out=selb[:, :, :], in0=selb[:, :, :], scalar1=1.0, scalar2=1.0e30, op0=ALU.subtract, op1=ALU.mult), r=[selb], w=[selb])
                    for h in range(16):
                        pr, r0 = h // 2, (h % 2) * 64
                        slope = 2.0 ** (-(h + 1) / 2.0)
                        off = 2177 - q0
                        L, Pb, PT, rmax, rsum = L2[h % 2], Pb2[h % 2], PT2[h % 2], rmax2[h % 2], rsum2[h % 2]
                        for j0 in range((nk + 511) // 512):
                            c0, c1 = j0 * 512, min(nk, (j0 + 1) * 512)
                            j = (j0 + 2 * (h % 2)) % 4 if nk <= 1024 else j0
                            kb.T(lambda e, j=j, c0=c0, c1=c1, pr=pr, r0=r0: e.matmul(pl[j][:, 0:c1 - c0], lhsT=qT[r0:r0 + 64, pr, qs], rhs=kT[r0:r0 + 64, pr, c0:c1], start=True, stop=True), r=[qT, kT], w=[pl[j]])
                            kb.V(lambda e, j=j, c0=c0, c1=c1: e.scalar_tensor_tensor(out=L[:, c0:c1], in0=c2[:, off + c0:off + c1], scalar=slope, in1=pl[j][:, 0:c1 - c0], op0=ALU.mult, op1=ALU.add),
                                 r=[c2, pl[j]], w=[L])
                        if gated:
                            kb.V(lambda e, h=h: e.tensor_tensor(out=L[:, 0:own * 256].rearrange("p (b j) -> p b j", j=256), in0=L[:, 0:own * 256].rearrange("p (b j) -> p b j", j=256),
                                                                in1=selb[:, h, 0:own].unsqueeze(2).broadcast_to([128, own, 256]), op=ALU.add), r=[L, selb], w=[L])
                        kb.V(lambda e: e.tensor_tensor(out=L[:, nk - 128:nk], in0=L[:, nk - 128:nk], in1=c.tri_q(), op=ALU.add), r=[L, c.cf], w=[L])
                        kb.V(lambda e: e.tensor_reduce(out=rmax[:, :], in_=L[:, 0:nk], axis=AX.X, op=ALU.max), r=[L], w=[rmax])
                        kb.V(lambda e: e.tensor_scalar(out=rmax[:, :], in0=rmax[:, :], scalar1=-1.0, scalar2=None, op0=ALU.mult), r=[rmax], w=[rmax])
                        kb.V(lambda e: e.memset(rsum[:, :], 0.0), w=[rsum])
                        kb.A(lambda e: e.activation(out=Pb[:, 0:nk], in_=L[:, 0:nk], func=AF.Exp, bias=rmax[:, 0:1], accum_out=rsum[:, 0:1]), r=[L, rmax, rsum], w=[Pb, rsum])
                        nj = nk // 128
                        for j in range(nj):
                            kb.T(lambda e, j=j: e.transpose(out=ptp[:, (j % 8) * 128:(j % 8 + 1) * 128], in_=Pb[:, j * 128:(j + 1) * 128], identity=c.identb[:, :]), r=[Pb, c.identb], w=[ptp])
                            if j % 8 == 7 or j == nj - 1:
                                j0 = (j // 8) * 8
                                nn = j - j0 + 1
                                o = PT[:, j0:j0 + nn, :]
                                i_ = ptp[:, 0:nn * 128].rearrange("p (a b) -> p a b", b=128)
                                if (j // 8) % 2 == 0:
                                    kb.V(lambda e, o=o, i_=i_: e.tensor_copy(out=o, in_=i_), r=[ptp], w=[PT])
                                else:
                                    kb.A(lambda e, o=o, i_=i_: e.activation(out=o, in_=i_, func=AF.Copy), r=[ptp], w=[PT])
                        for j in range(nj):
                            kb.T(lambda e, j=j, h=h: e.matmul(pv[:, 0:64], lhsT=PT[:, j, :], rhs=va[:, j, h * 64:(h + 1) * 64], start=(j == 0), stop=(j == nj - 1)), r=[PT, va], w=[pv])
                        kb.V(lambda e: e.reciprocal(out=rsum[:, :], in_=rsum[:, :]), r=[rsum], w=[rsum])
                        kb.V(lambda e, h=h: e.tensor_scalar(out=attn[:, h * 64:(h + 1) * 64], in0=pv[:, 0:64], scalar1=rsum[:, 0:1], scalar2=None, op0=ALU.mult), r=[pv, rsum], w=[attn])
                    transpose_to(kb, c, attn, 8, ptp, attnT, lambda c0, nn: attnT[:, c0:c0 + nn, :])
                    for hf in range(2):
                        for k in range(8):
                            kb.T(lambda e, k=k, hf=hf: e.matmul(po[hf][:, :], lhsT=attnT[:, k, :], rhs=wo[:, k, hf * 512:(hf + 1) * 512], start=(k == 0), stop=(k == 7)), r=[attnT, wo], w=[po[hf]])
                    resid_ln_store(kb, xt, po, g_bc, b_bc, ybuf, tmp, Y1, Y1[base + q0:base + q0 + 128, :])


def ssd_phase_a(kb, c, T, S, XIN, W, XS, BTM, BCT, ZS, DT):
    GS = 512
    NG = T // GS
    GPS = S // GS
    with kb.scope():
        win = kb.sb("win", [128, 8, 5152], BF16)
        cw = kb.sb("cw", [128, 24, 4], F32)
        cb = kb.sb("cb", [128, 24], F32)
        with kb.scope():
            stage = [kb.sb("stg", [128, 2048], F32) for _ in range(2)]
            load_w_bf(kb, win, lambda k, c0, cw_: win[:, k, c0:c0 + cw_], W["ssd_w_in"], 8, 5152, stage)
            pst = kb.ps("pst", [128, 512], F32)
            load_cols(kb, c, cw, lambda b: cw[:, b, :], lambda b: W["ssd_conv_w"][:, b * 128:(b + 1) * 128], 4, stage[0], pst, nblk=24)
            load_cols(kb, c, cb, lambda b: cb[:, :], lambda b: W["ssd_conv_b"].rearrange("(c p) -> c p", p=128), 24, stage[1], pst)
        dtb = bcast_row(kb, "dtb", W["ssd_dt_bias"], 32)
        one1 = kb.sb("one1", [128, 1], F32)
        kb.V(lambda e: e.memset(one1[:, :], 1.0), w=[one1])
        xt = [kb.sb("xt", [128, 1024], F32) for _ in range(2)]
        xbf = kb.sb("xbf", [128, 1024], BF16)
        xT = kb.sb("xT", [128, 8, GS], BF16)
        rawH = [kb.sb("rawH", [128, 3 + GS], F32) for _ in range(2)]
        hal = kb.sb("hal", [128, 24, 3], F32)
        cacc = [kb.sb("cacc", [128, GS], F32) for _ in range(2)]
        xbcT = kb.sb("xbcT", [128, 24, GS], BF16)
        xs_sb = [kb.sb("xs_sb", [128, 2048], BF16) for _ in range(2)]
        b_sb = [kb.sb("b_sb", [128, 512], BF16) for _ in range(2)]
        zs_sb = [kb.sb("zs_sb", [128, 2048], BF16) for _ in range(2)]
        dtr = kb.sb("dtr", [128, 32], F32)
        dab = kb.sb("dab", [128, 32], F32)
        dmx = kb.sb("dmx", [128, 32], F32)
        dt_sb = [kb.sb("dt_sb", [128, 32], F32) for _ in range(2)]
        pa = [kb.ps("pa", [128, 512], F32) for _ in range(2)]
        pz = [kb.ps("pz", [128, 512], F32) for _ in range(2)]
        pd = kb.ps("pd", [128, 512], F32)
        ptr = [kb.ps("ptr", [128, 1024], BF16) for _ in range(2)]
        n = 0
        for g in range(NG):
            t0 = g * GS
            for ti in range(4):
                x_ = xt[ti % 2]
                kb.dma("sp" if ti % 2 == 0 else "pool", x_[:, :], XIN[t0 + ti * 128:t0 + (ti + 1) * 128, :], r=[XIN], w=[x_])
                kb.A(lambda e, x_=x_: e.activation(out=xbf[:, :], in_=x_[:, :], func=AF.Copy), r=[x_], w=[xbf])
                transpose_to(kb, c, xbf, 8, ptr[0], xT, lambda c0, nn, ti=ti: xT[:, c0:c0 + nn, ti * 128:(ti + 1) * 128])
            if g % GPS == 0:
                kb.G(lambda e: e.memset(hal[:, :, :], 0.0), w=[hal])
            for fc in range(24):
                p_, rh, ac = pa[fc % 2], rawH[fc % 2], cacc[fc % 2]
                col0 = 2048 + fc * 128
                for k in range(8):
                    kb.T(lambda e, k=k, col0=col0, p_=p_: e.matmul(p_[:, :], lhsT=win[:, k, col0:col0 + 128], rhs=xT[:, k, :], start=(k == 0), stop=(k == 7)), r=[win, xT], w=[p_])
                kb.A(lambda e, p_=p_, rh=rh: e.activation(out=rh[:, 3:3 + GS], in_=p_[:, :], func=AF.Copy), r=[p_], w=[rh])
                kb.G(lambda e, rh=rh, fc=fc: e.tensor_copy(out=rh[:, 0:3], in_=hal[:, fc, :]), r=[hal], w=[rh])
                kb.V(lambda e, rh=rh, ac=ac, fc=fc: e.tensor_scalar(out=ac[:, :], in0=rh[:, 3:3 + GS], scalar1=cw[:, fc, 3:4], scalar2=cb[:, fc:fc + 1], op0=ALU.mult, op1=ALU.add), r=[rh, cw, cb], w=[ac])
                for k in range(3):
                    kb.V(lambda e, rh=rh, ac=ac, fc=fc, k=k: e.scalar_tensor_tensor(out=ac[:, :], in0=rh[:, k:k + GS], scalar=cw[:, fc, k:k + 1], in1=ac[:, :], op0=ALU.mult, op1=ALU.add), r=[rh, cw, ac], w=[ac])
                kb.G(lambda e, rh=rh, fc=fc: e.tensor_copy(out=hal[:, fc, :], in_=rh[:, GS:GS + 3]), r=[rh], w=[hal])
                kb.A(lambda e, ac=ac, fc=fc: e.activation(out=xbcT[:, fc, :], in_=ac[:, :], func=AF.Silu), r=[ac], w=[xbcT])
            for j in range(8):
                kb.dma("sp" if j % 2 == 0 else "pool", BCT[j][:, t0:t0 + GS], xbcT[:, 16 + j, :], r=[xbcT], w=[BCT])
            for ti in range(4):
                tsl = slice(ti * 128, (ti + 1) * 128)
                rows = slice(t0 + ti * 128, t0 + (ti + 1) * 128)
                xs_, b_, zs_, dt_ = xs_sb[ti % 2], b_sb[ti % 2], zs_sb[ti % 2], dt_sb[ti % 2]
                for half in range(2):
                    pt_ = ptr[half]
                    for j in range(8):
                        kb.T(lambda e, j=j, half=half, pt_=pt_: e.transpose(out=pt_[:, j * 128:(j + 1) * 128], in_=xbcT[:, half * 8 + j, tsl], identity=c.identb[:, :]), r=[xbcT, c.identb], w=[pt_])
                    if half == 0:
                        kb.V(lambda e, pt_=pt_, xs_=xs_: e.tensor_copy(out=xs_[:, 0:1024], in_=pt_[:, :]), r=[pt_], w=[xs_])
                    else:
                        kb.A(lambda e, pt_=pt_, xs_=xs_: e.activation(out=xs_[:, 1024:2048], in_=pt_[:, :], func=AF.Copy), r=[pt_], w=[xs_])
                kb.dma("sp", XS[rows, :], xs_[:, :], r=[xs_], w=[XS])
                for j in range(4):
                    kb.T(lambda e, j=j: e.transpose(out=ptr[0][:, j * 128:(j + 1) * 128], in_=xbcT[:, 16 + j, tsl], identity=c.identb[:, :]), r=[xbcT, c.identb], w=[ptr[0]])
                kb.V(lambda e, b_=b_: e.tensor_copy(out=b_[:, :], in_=ptr[0][:, 0:512]), r=[ptr[0]], w=[b_])
                kb.dma("pool", BTM[rows, :], b_[:, :], r=[b_], w=[BTM])
                for sl in range(4):
                    p_ = pz[n % 2]
                    n += 1
                    for k in range(8):
                        kb.T(lambda e, k=k, sl=sl, p_=p_: e.matmul(p_[:, :], lhsT=xT[:, k, tsl], rhs=win[:, k, sl * 512:(sl + 1) * 512], start=(k == 0), stop=(k == 7)), r=[xT, win], w=[p_])
                    kb.A(lambda e, sl=sl, p_=p_, zs_=zs_: e.activation(out=zs_[:, sl * 512:(sl + 1) * 512], in_=p_[:, :], func=AF.Silu), r=[p_], w=[zs_])
                kb.dma("sp", ZS[rows, :], zs_[:, :], r=[zs_], w=[ZS])
                for k in range(8):
                    kb.T(lambda e, k=k: e.matmul(pd[:, 0:32], lhsT=xT[:, k, tsl], rhs=win[:, k, 5120:5152], start=(k == 0), stop=(k == 7)), r=[xT, win], w=[pd])
                kb.V(lambda e: e.tensor_tensor(out=dtr[:, :], in0=pd[:, 0:32], in1=dtb[:, :], op=ALU.add), r=[pd, dtb], w=[dtr])
                kb.A(lambda e: e.activation(out=dab[:, :], in_=dtr[:, :], func=AF.Abs), r=[dtr], w=[dab])
                kb.A(lambda e: e.activation(out=dab[:, :], in_=dab[:, :], func=AF.Exp, scale=-1.0), r=[dab], w=[dab])
                kb.A(lambda e: e.activation(out=dab[:, :], in_=dab[:, :], func=AF.Ln, bias=one1[:, 0:1]), r=[dab, one1], w=[dab])
                kb.V(lambda e: e.tensor_single_scalar(out=dmx[:, :], in_=dtr[:, :], scalar=0.0, op=ALU.max), r=[dtr], w=[dmx])
                kb.V(lambda e, dt_=dt_: e.tensor_tensor(out=dt_[:, :], in0=dmx[:, :], in1=dab[:, :], op=ALU.add), r=[dmx, dab], w=[dt_])
                kb.dma("pool", DT[rows, :], dt_[:, :], r=[dt_], w=[DT])


def ssd_phase_b(kb, c, T, S, XIN, Y1, W, XS, BTM, BCT, ZS, DT):
    NC_ = T // 128
    CPS = S // 128
    with kb.scope():
        wout = kb.sb("wout", [128, 16, 1024], BF16)
        with kb.scope():
            stage = [kb.sb("stg", [128, 1024], F32) for _ in range(2)]
            load_w_bf(kb, wout, lambda k, c0, cw_: wout[:, k, c0:c0 + cw_], W["ssd_w_out"], 16, 1024, stage)
        g_bc = bcast_row(kb, "lnm_g", W["ln_mix_g"], 1024)
        b_bc = bcast_row(kb, "lnm_b", W["ln_mix_b"], 1024, q="pool")
        ng_bc = bcast_row(kb, "ng_bc", W["ssd_norm_g"], 2048)
        aneg = bcast_row(kb, "aneg", W["ssd_a_log"], 32, q="pool")
        kb.A(lambda e: e.activation(out=aneg[:, :], in_=aneg[:, :], func=AF.Exp), r=[aneg], w=[aneg])
        kb.V(lambda e: e.tensor_scalar(out=aneg[:, :], in0=aneg[:, :], scalar1=-1.0, scalar2=None, op0=ALU.mult), r=[aneg], w=[aneg])
        dsk = bcast_row(kb, "dsk", W["ssd_d"], 32)
        xt = kb.sb("xt", [128, 1024], F32)
        xs = kb.sb("xs", [128, 2048], BF16)
        bt = kb.sb("bt", [128, 512], BF16)
        zs = kb.sb("zs", [128, 2048], BF16)
        dt = kb.sb("dt", [128, 32], F32)
        bct = kb.sb("bct", [128, 8, 128], BF16)
        dtA = kb.sb("dtA", [128, 32], F32)
        acs = kb.sb("acs", [128, 64], F32)
        ea = kb.sb("ea", [128, 32], F32)
        dte = kb.sb("dte", [128, 32], F32)
        cd = kb.sb("cd", [128, 32], F32)
        xdt = kb.sb("xdt", [128, 2048], BF16)
        xe = kb.sb("xe", [128, 2048], BF16)
        Mh = kb.sb("Mh", [128, 32, 128], F32)
        cbt = kb.sb("cbt", [128, 4, 128], F32)
        Dm = [kb.sb("Dm", [128, 4, 128], F32) for _ in range(2)]
        Wd = kb.sb("Wd", [128, 32, 128], BF16)
        yoff = kb.sb("yoff", [128, 2048], F32)
        y = kb.sb("y", [128, 2048], F32)
        t2 = kb.sb("t2", [128, 2048], F32)
        ss = kb.sb("ss", [128, 4], F32)
        gnb = kb.sb("gnb", [128, 2048], BF16)
        gnT = kb.sb("gnT", [128, 16, 128], BF16)
        H = kb.sb("H", [128, 2048], F32)
        Hbf = kb.sb("Hbf", [128, 2048], BF16)
        ybuf = kb.sb("ybuf", [128, 1024], F32)
        tmp = ln_tmp(kb)
        py = [kb.ps("py", [128, 512], F32) for _ in range(4)]
        pd = [kb.ps("pd", [128, 512], F32) for _ in range(2)]
        pm = kb.ps("pm", [128, 512], F32)
        ptr = kb.ps("ptr", [128, 1024], BF16)
        v3 = lambda ap: ap.rearrange("p (h d) -> p h d", d=64)
        for ci in range(NC_):
            rows = slice(ci * 128, (ci + 1) * 128)
            kb.dma("sp", xs[:, :], XS[rows, :], r=[XS], w=[xs])
            kb.dma("pool", zs[:, :], ZS[rows, :], r=[ZS], w=[zs])
            kb.dma("sp", bt[:, :], BTM[rows, :], r=[BTM], w=[bt])
            kb.dma("pool", dt[:, :], DT[rows, :], r=[DT], w=[dt])
            kb.dma("sp", bct[:, :, :], BCT.t.rearrange("j p t -> p j t")[:, :, rows], r=[BCT], w=[bct])
            kb.dma("pool", xt[:, :], XIN[rows, :], r=[XIN], w=[xt])
            if ci % CPS == 0:
                kb.G(lambda e: e.memset(H[:, :], 0.0), w=[H])
                kb.G(lambda e: e.memset(Hbf[:, :], 0.0), w=[Hbf])
            kb.V(lambda e: e.tensor_tensor(out=dtA[:, :], in0=dt[:, :], in1=aneg[:, :], op=ALU.mult), r=[dt, aneg], w=[dtA])
            kb.T(lambda e: e.matmul(pm[:, 0:32], lhsT=c.triu(), rhs=dtA[:, :], start=True, stop=True), r=[c.cf, dtA], w=[pm])
            kb.T(lambda e: e.matmul(pm[:, 32:64], lhsT=c.ones(), rhs=dtA[:, :], start=True, stop=True), r=[c.cf, dtA], w=[pm])
            kb.V(lambda e: e.tensor_copy(out=acs[:, :], in_=pm[:, 0:64]), r=[pm], w=[acs])
            kb.A(lambda e: e.activation(out=ea[:, :], in_=acs[:, 0:32], func=AF.Exp), r=[acs], w=[ea])
            kb.V(lambda e: e.tensor_tensor(out=dte[:, :], in0=acs[:, 32:64], in1=acs[:, 0:32], op=ALU.subtract), r=[acs], w=[dte])
            kb.A(lambda e: e.activation(out=dte[:, :], in_=dte[:, :], func=AF.Exp), r=[dte], w=[dte])
            kb.V(lambda e: e.tensor_tensor(out=dte[:, :], in0=dte[:, :], in1=dt[:, :], op=ALU.mult), r=[dte, dt], w=[dte])
            kb.A(lambda e: e.activation(out=cd[:, :], in_=acs[:, 32:64], func=AF.Exp), r=[acs], w=[cd])
            kb.V(lambda e: e.tensor_tensor(out=v3(xdt[:, :]), in0=v3(xs[:, :]), in1=dt[:, :].unsqueeze(2).broadcast_to([128, 32, 64]), op=ALU.mult), r=[xs, dt], w=[xdt])
            kb.G(lambda e: e.tensor_tensor(out=v3(xe[:, :]), in0=v3(xs[:, :]), in1=dte[:, :].unsqueeze(2).broadcast_to([128, 32, 64]), op=ALU.mult), r=[xs, dte], w=[xe])
            kb.V(lambda e: e.tensor_tensor(out=Mh[:, :, :], in0=c.triu().unsqueeze(1).broadcast_to([128, 32, 128]), in1=dtA[:, :].unsqueeze(2).broadcast_to([128, 32, 128]), op=ALU.mult), r=[c.cf, dtA], w=[Mh])
            for g in range(4):
                kb.T(lambda e, g=g: e.matmul(pm[:, g * 128:(g + 1) * 128], lhsT=bct[:, g, :], rhs=bct[:, 4 + g, :], start=True, stop=True), r=[bct], w=[pm])
            kb.V(lambda e: e.tensor_copy(out=cbt[:, :, :], in_=pm[:, :].rearrange("p (a b) -> p a b", b=128)), r=[pm], w=[cbt])
            for g in range(4):
                gs = slice(g * 512, (g + 1) * 512)
                kb.T(lambda e, g=g, gs=gs: e.matmul(py[g][:, :], lhsT=bct[:, 4 + g, :], rhs=Hbf[:, gs], start=True, stop=True), r=[bct, Hbf], w=[py[g]])
                kb.V(lambda e, g=g, gs=gs: e.tensor_tensor(out=v3(yoff[:, gs]), in0=v3(py[g][:, :]), in1=ea[:, g * 8:(g + 1) * 8].unsqueeze(2).broadcast_to([128, 8, 64]), op=ALU.mult), r=[py[g], ea], w=[yoff])
            for hb in range(8):
                p_, d_ = pd[hb % 2], Dm[hb % 2]
                for i in range(4):
                    h = hb * 4 + i
                    kb.T(lambda e, i=i, h=h, p_=p_: e.matmul(p_[:, i * 128:(i + 1) * 128], lhsT=c.ones(), rhs=Mh[:, h, :], start=True, stop=False), r=[c.cf, Mh], w=[p_])
                    kb.T(lambda e, i=i, h=h, p_=p_: e.matmul(p_[:, i * 128:(i + 1) * 128], lhsT=Mh[:, h, :], rhs=c.negones(), start=False, stop=True), r=[c.cf, Mh], w=[p_])
                kb.V(lambda e, p_=p_, d_=d_: e.tensor_tensor(out=d_[:, :, :], in0=p_[:, :].rearrange("p (a b) -> p a b", b=128), in1=c.negmask().unsqueeze(1).broadcast_to([128, 4, 128]), op=ALU.add), r=[p_, c.cf], w=[d_])
                kb.A(lambda e, d_=d_: e.activation(out=d_[:, :, :], in_=d_[:, :, :], func=AF.Exp), r=[d_], w=[d_])
                kb.G(lambda e, d_=d_, hb=hb: e.tensor_tensor(out=Wd[:, hb * 4:(hb + 1) * 4, :], in0=d_[:, :, :], in1=cbt[:, hb // 2, :].unsqueeze(1).broadcast_to([128, 4, 128]), op=ALU.mult), r=[d_, cbt], w=[Wd])
            for h in range(32):
                g = h // 8
                kb.T(lambda e, h=h, g=g: e.matmul(py[g][:, (h % 8) * 64:(h % 8 + 1) * 64], lhsT=Wd[:, h, :], rhs=xdt[:, h * 64:(h + 1) * 64], start=True, stop=True), r=[Wd, xdt], w=[py[g]])
            for g in range(4):
                gs = slice(g * 512, (g + 1) * 512)
                kb.V(lambda e, g=g, gs=gs: e.tensor_tensor(out=y[:, gs], in0=py[g][:, :], in1=yoff[:, gs], op=ALU.add), r=[py[g], yoff], w=[y])
            kb.G(lambda e: e.tensor_tensor(out=v3(t2[:, :]), in0=v3(xs[:, :]), in1=dsk[:, :].unsqueeze(2).broadcast_to([128, 32, 64]), op=ALU.mult), r=[xs, dsk], w=[t2])
            kb.V(lambda e: e.tensor_tensor(out=y[:, :], in0=y[:, :], in1=t2[:, :], op=ALU.add), r=[y, t2], w=[y])
            kb.V(lambda e: e.tensor_tensor(out=y[:, :], in0=y[:, :], in1=zs[:, :], op=ALU.mult), r=[y, zs], w=[y])
            kb.V(lambda e: e.memset(ss[:, :], 0.0), w=[ss])
            for g in range(4):
                gs = slice(g * 512, (g + 1) * 512)
                kb.A(lambda e, g=g, gs=gs: e.activation(out=t2[:, gs], in_=y[:, gs], func=AF.Square, accum_out=ss[:, g:g + 1]), r=[y, ss], w=[t2, ss])
            kb.A(lambda e: e.activation(out=ss[:, :], in_=ss[:, :], func=AF.Ln, scale=1.0 / 512, bias=CONST.eps[:, 0:1]), r=[ss, CONST.eps], w=[ss])
            kb.A(lambda e: e.activation(out=ss[:, :], in_=ss[:, :], func=AF.Exp, scale=-0.5), r=[ss], w=[ss])
            kb.V(lambda e: e.tensor_tensor(out=y[:, :].rearrange("p (g d) -> p g d", d=512), in0=y[:, :].rearrange("p (g d) -> p g d", d=512), in1=ss[:, :].unsqueeze(2).broadcast_to([128, 4, 512]), op=ALU.mult), r=[y, ss], w=[y])
            kb.G(lambda e: e.tensor_tensor(out=gnb[:, :], in0=y[:, :], in1=ng_bc[:, :], op=ALU.mult), r=[y, ng_bc], w=[gnb])
            transpose_to(kb, c, gnb, 16, ptr, gnT, lambda c0, nn: gnT[:, c0:c0 + nn, :])
            po = pd
            for hf in range(2):
                for k in range(16):
                    kb.T(lambda e, k=k, hf=hf: e.matmul(po[hf][:, :], lhsT=gnT[:, k, :], rhs=wout[:, k, hf * 512:(hf + 1) * 512], start=(k == 0), stop=(k == 15)), r=[gnT, wout], w=[po[hf]])
            resid_ln_store(kb, xt, po, g_bc, b_bc, ybuf, tmp, Y1, Y1[rows, :])
            for g in range(4):
                gs = slice(g * 512, (g + 1) * 512)
                kb.T(lambda e, g=g, gs=gs: e.matmul(py[g][:, :], lhsT=bt[:, g * 128:(g + 1) * 128], rhs=xe[:, gs], start=True, stop=True), r=[bt, xe], w=[py[g]])
            kb.V(lambda e: e.tensor_tensor(out=v3(H[:, :]), in0=v3(H[:, :]), in1=cd[:, :].unsqueeze(2).broadcast_to([128, 32, 64]), op=ALU.mult), r=[H, cd], w=[H])
            for g in range(4):
                gs = slice(g * 512, (g + 1) * 512)
                kb.V(lambda e, g=g, gs=gs: e.tensor_tensor(out=H[:, gs], in0=H[:, gs], in1=py[g][:, :], op=ALU.add), r=[H, py[g]], w=[H])
            kb.A(lambda e: e.activation(out=Hbf[:, :], in_=H[:, :], func=AF.Copy), r=[H], w=[Hbf])


W_SHAPES = {
    "ssd_w_in": (2, 1024, 5152), "ssd_conv_w": (2, 4, 3072), "ssd_conv_b": (2, 3072), "ssd_dt_bias": (2, 32),
    "ssd_a_log": (2, 32), "ssd_d": (2, 32), "ssd_norm_g": (2, 2048), "ssd_w_out": (2, 2048, 1024),
    "moba_w_qkv": (1, 1024, 3072), "moba_w_out": (1, 1024, 1024),
    "conv_w_pw1": (1, 1024, 2048), "conv_b_pw1": (1, 2048), "conv_w_dw": (1, 31, 1024), "conv_b_dw": (1, 1024),
    "conv_ln_g": (1, 1024), "conv_ln_b": (1, 1024), "conv_w_pw2": (1, 1024, 1024),
    "peer_w_q": (4, 1024, 2048), "peer_sub_keys": (4, 8, 2, 128, 128), "peer_u": (4, 16384, 1024), "peer_v": (4, 16384, 1024),
    "ln_mix_g": (4, 1024), "ln_mix_b": (4, 1024), "ln_ffn_g": (4, 1024), "ln_ffn_b": (4, 1024),
    "ple_w_gate": (4, 1024, 1024), "ple_w_proj": (4, 256, 1024),
}
DEPTH = 4
PER_LAYER = ("peer_w_q", "peer_sub_keys", "peer_u", "peer_v", "ln_mix_g", "ln_mix_b", "ln_ffn_g", "ln_ffn_b", "ple_w_gate", "ple_w_proj")


def build_full(T, S, depth=DEPTH, TG=256):
    nc = bass.Bass("TRN2", target_bir_lowering=False)
    kb = KB(nc)
    with nc.allow_low_precision("bf16 matmul operands with fp32 accumulation"):
        cd = kb.dram("consts", [128, CW], F32, kind="ExternalInput")
        c2d = kb.dram("c2", [128, C2W], F32, kind="ExternalInput")
        X = kb.dram("x", [T, 1024], F32, kind="ExternalInput")
        P = kb.dram("p", [DEPTH, T, 256], F32, kind="ExternalInput")
        OUT = kb.dram("out", [T, 1024], F32, kind="ExternalOutput")
        Wd = {k: kb.dram(k, list(shp), F32, kind="ExternalInput").t for k, shp in W_SHAPES.items()}
        XA = [kb.dram("xa%d" % i, [T, 1024], F32) for i in range(2)]
        Y1 = kb.dram("y1", [T, 1024], F32)
        UVs = kb.dram("UVs", [128, 128, 2048], BF16)
        XTd = kb.dram("XTd", [8, 128, T], BF16)
        QTd = kb.dram("QTd", [16, 128, T], BF16)
        XS = kb.dram("XS", [T, 2048], BF16)
        BTM = kb.dram("BTM", [T, 512], BF16)
        BCT = kb.dram("BCT", [8, 128, T], BF16)
        ZS = kb.dram("ZS", [T, 2048], BF16)
        DT = kb.dram("DT", [T, 32], F32)
        c = load_consts(kb, cd)
        xin = X
        for i in range(depth):
            kind, j = i % 3, i // 3
            W = {}
            for k in W_SHAPES:
                if k in PER_LAYER:
                    W[k] = Wd[k][i]
                elif k.startswith(("ssd_", "moba_", "conv_")):
                    n = W_SHAPES[k][0]
                    W[k] = Wd[k][min(j, n - 1)]
            if kind == 0:
                ssd_phase_a(kb, c, T, S, xin, W, XS, BTM, BCT, ZS, DT)
                ssd_phase_b(kb, c, T, S, xin, Y1, W, XS, BTM, BCT, ZS, DT)
            elif kind == 1:
                moba_phase(kb, c, T, S, xin, Y1, W, c2d)
            else:
                conf_phase(kb, c, T, S, xin, Y1, W)
            peer_prepass(kb, c, W["peer_u"], W["peer_v"], UVs)
            xout = OUT if i == depth - 1 else XA[i % 2]
            peer_q_phase(kb, c, T, Y1, W, XTd, QTd)
            peer_phase(kb, c, T, Y1, xout, P.t[i], W, UVs, XTd, QTd, TG=TG)
            xin = xout
        kb.barrier()
    return nc, kb


_CACHE = {}


def kernel(**inputs):
    NCORE = 8
    B, S = inputs["x"].shape[0], inputs["x"].shape[1]
    per = B // NCORE
    T = per * S
    key = (T, S)
    if key not in _CACHE:
        _CACHE[key] = build_full(T, S)[0]
    nc = _CACHE[key]
    consts, c2 = host_consts(), host_c2()
    shared = {k: np.ascontiguousarray(np.asarray(inputs[k], dtype=np.float32)) for k in W_SHAPES}
    x = np.asarray(inputs["x"], dtype=np.float32)
    p = np.asarray(inputs["p"], dtype=np.float32)
    in_maps = []
    for ci in range(NCORE):
        m = dict(shared)
        m["consts"] = consts
        m["c2"] = c2
        m["x"] = np.ascontiguousarray(x[ci * per:(ci + 1) * per].reshape(T, 1024))
        m["p"] = np.ascontiguousarray(p[:, ci * per:(ci + 1) * per].reshape(DEPTH, T, 256))
        in_maps.append(m)
    res = run_bass_kernel_spmd(nc, in_maps, core_ids=list(range(NCORE)))
    outs = [np.asarray(r["out"]).reshape(per, S, 1024) for r in res.results]
    return np.concatenate(outs, axis=0).astype(np.float32)
```

```python
import numpy as np
from contextlib import ExitStack, contextmanager
import concourse.bass as bass
import concourse.mybir as mybir
from concourse.bass_utils import run_bass_kernel_spmd

F32 = mybir.dt.float32
BF16 = mybir.dt.bfloat16
I32 = mybir.dt.int32
U32 = mybir.dt.uint32
AF = mybir.ActivationFunctionType
ALU = mybir.AluOpType
AX = mybir.AxisListType

D = 1024
ALPHA = 8.0 ** 0.25
EPS = 1e-5
NEG = -1.0e30
KD = 8


class Buf:
    __slots__ = ("t", "w", "r", "name")

    def __init__(self, t, name=""):
        self.t = t
        self.w = None
        self.r = {}
        self.name = name

    def __getitem__(self, idx):
        return self.t[idx]


class Alias:
    def __init__(self, parent, t):
        self.__dict__["parent"] = parent
        self.__dict__["t"] = t

    def __getitem__(self, idx):
        return self.t[idx]

    def __getattr__(self, k):
        return getattr(self.__dict__["parent"], k)

    def __setattr__(self, k, v):
        setattr(self.__dict__["parent"], k, v)


class KB:
    def __init__(self, nc):
        self.nc = nc
        self.root = ExitStack()
        self.stacks = [self.root]
        self.eng = {}
        for name, h in (("pe", nc.tensor), ("act", nc.scalar), ("dve", nc.vector), ("pool", nc.gpsimd), ("sp", nc.sync)):
            sem = self.root.enter_context(nc.semaphore("s_" + name))
            self.eng[name] = dict(h=h, sem=sem, cnt=0, seen={}, name=name)
        self.dq = {}
        for q in ("sp", "pool", "act"):
            sems = [self.root.enter_context(nc.semaphore("d_%s%d" % (q, i))) for i in range(KD)]
            self.dq[q] = dict(sems=sems, n=0, cnt=[0] * KD)
        self.uid = 0
        self.ninst = 0

    @contextmanager
    def scope(self):
        st = ExitStack()
        self.stacks.append(st)
        try:
            yield
        finally:
            self.barrier()
            self.stacks.pop()
            st.close()

    def _nm(self, name):
        self.uid += 1
        return "%s_%d" % (name, self.uid)

    def sb(self, name, shape, dt):
        t = self.stacks[-1].enter_context(self.nc.sbuf_tensor(self._nm(name), list(shape), dt))
        return Buf(t, name)

    def ps(self, name, shape, dt):
        t = self.stacks[-1].enter_context(self.nc.psum_tensor(self._nm(name), list(shape), dt))
        return Buf(t, name)

    def dram(self, name, shape, dt, kind="Internal"):
        t = self.nc.dram_tensor(name, list(shape), dt, kind=kind)
        return Buf(t.ap(), name)

    def _wait(self, e, deps):
        best = {}
        for sem, val in deps:
            k = id(sem)
            if k not in best or best[k][1] < val:
                best[k] = (sem, val)
        for k, (sem, val) in best.items():
            if e["seen"].get(k, 0) >= val:
                continue
            e["h"].wait_ge(sem, val)
            e["seen"][k] = val

    def _deps(self, e, r, w, skip_self, is_dma=False):
        deps = []
        me = None if is_dma else id(e["sem"])
        for b in r:
            if b.w is not None:
                deps.append(b.w)
        for b in w:
            if b.w is not None:
                deps.append(b.w)
            for k, tok in b.r.items():
                deps.append(tok)
        if skip_self:
            deps = [d for d in deps if id(d[0]) != me]
        return deps

    def _mark(self, tok, r, w):
        for b in w:
            b.w = tok
            b.r = {}
        k = id(tok[0])
        wroots = [getattr(x, "parent", x) for x in w]
        for b in r:
            if not any(getattr(b, "parent", b) is x for x in wroots):
                b.r[k] = tok

    def op(self, en, fn, r=(), w=()):
        e = self.eng[en]
        self._wait(e, self._deps(e, r, w, en == "pe"))
        inst = fn(e["h"])
        e["cnt"] += 1
        self.ninst += 1
        inst.then_inc(e["sem"], 1)
        tok = (e["sem"], e["cnt"])
        self._mark(tok, r, w)
        return tok

    def V(self, fn, r=(), w=()):
        return self.op("dve", fn, r, w)

    def A(self, fn, r=(), w=()):
        return self.op("act", fn, r, w)

    def G(self, fn, r=(), w=()):
        return self.op("pool", fn, r, w)

    def T(self, fn, r=(), w=()):
        return self.op("pe", fn, r, w)

    def dma(self, q, out, in_, r=(), w=()):
        e = self.eng[q]
        d = self.dq[q]
        i = d["n"] % KD
        d["n"] += 1
        sem = d["sems"][i]
        deps = self._deps(e, r, w, False, True)
        if d["cnt"][i] > 0:
            deps.append((sem, d["cnt"][i] * 16))
        self._wait(e, deps)
        e["h"].dma_start(out=out, in_=in_).then_inc(sem, 16)
        self.ninst += 1
        d["cnt"][i] += 1
        tok = (sem, d["cnt"][i] * 16)
        self._mark(tok, r, w)
        return tok

    def barrier(self):
        toks = []
        for e in self.eng.values():
            if e["cnt"] > 0:
                toks.append((e["sem"], e["cnt"]))
        for d in self.dq.values():
            for i in range(KD):
                if d["cnt"][i] > 0:
                    toks.append((d["sems"][i], d["cnt"][i] * 16))
        for e in self.eng.values():
            self._wait(e, toks)


class Consts:
    pass


CONST = None


def load_consts(kb, cdram):
    c = Consts()
    cf = kb.sb("cf", [128, CW], F32)
    kb.dma("sp", cf[:, :], cdram[:, :], r=[cdram], w=[cf])
    c.cf = cf
    c.identf = lambda: cf[:, 0:128]
    c.ones = lambda: cf[:, 128:256]
    c.triu = lambda: cf[:, 256:384]
    c.negmask = lambda: cf[:, 384:512]
    c.tri_q = lambda: cf[:, 512:640]
    c.iota128 = lambda: cf[:, 640:768]
    c.negones = lambda: cf[:, 768:896]
    ib = kb.sb("identb", [128, 128], BF16)
    kb.V(lambda e: e.tensor_copy(out=ib[:, :], in_=cf[:, 0:128]), r=[cf], w=[ib])
    c.identb = ib
    io = kb.sb("iotab", [128, 128], BF16)
    kb.V(lambda e: e.tensor_copy(out=io[:, :], in_=cf[:, 640:768]), r=[cf], w=[io])
    c.iotab = io
    c.eps = kb.sb("epsc", [128, 1], F32)
    kb.V(lambda e: e.memset(c.eps[:, :], EPS), w=[c.eps])
    global CONST
    CONST = c
    return c


CW = 896
NRELW = 2432


def host_consts():
    c = np.zeros((128, CW), np.float32)
    i = np.arange(128)
    c[:, 0:128] = np.eye(128, dtype=np.float32)
    c[:, 128:256] = 1.0
    c[:, 256:384] = (i[:, None] <= i[None, :]).astype(np.float32)
    c[:, 384:512] = np.where(i[:, None] <= i[None, :], 0.0, NEG)
    c[:, 512:640] = np.where(i[None, :] <= i[:, None], 0.0, NEG)
    c[:, 640:768] = i[None, :].astype(np.float32)
    c[:, 768:896] = -1.0
    return c


def host_nrel():
    return np.ascontiguousarray(np.broadcast_to((np.arange(NRELW) - 2304)[None, :].astype(np.float32), (128, NRELW)))


def bcast_row(kb, name, src_ap, n, q="sp", rbuf=None):
    t = kb.sb(name, [128, n], F32)
    kb.dma(q, t[:, :], src_ap.partition_broadcast(128), r=[rbuf] if rbuf else [], w=[t])
    return t


def load_w_bf(kb, dst, dst_ap_fn, src_ap, rows_k, cols, stage, cast_engs=("act", "dve")):
    step = stage[0].t.shape[1]
    n = 0
    for k in range(rows_k):
        for c0 in range(0, cols, step):
            cw = min(step, cols - c0)
            st = stage[n % len(stage)]
            kb.dma("sp" if n % 2 == 0 else "pool", st[:, 0:cw], src_ap[k * 128:(k + 1) * 128, c0:c0 + cw], w=[st])
            en = cast_engs[n % len(cast_engs)]
            if en == "act":
                kb.A(lambda e, st=st, k=k, c0=c0, cw=cw: e.activation(out=dst_ap_fn(k, c0, cw), in_=st[:, 0:cw], func=AF.Copy), r=[st], w=[dst])
            else:
                kb.op(en, lambda e, st=st, k=k, c0=c0, cw=cw: e.tensor_copy(out=dst_ap_fn(k, c0, cw), in_=st[:, 0:cw]), r=[st], w=[dst])
            n += 1


def transpose_to(kb, c, src_bf, ncol_chunks, pst, dst, dst_ap, evac="dve"):
    for c0 in range(0, ncol_chunks, 8):
        nn = min(8, ncol_chunks - c0)
        for j in range(nn):
            kb.T(lambda e, j=j, c0=c0: e.transpose(out=pst[:, j * 128:(j + 1) * 128], in_=src_bf[:, (c0 + j) * 128:(c0 + j + 1) * 128], identity=c.identb[:, :]),
                 r=[src_bf, c.identb], w=[pst])
        o = dst_ap(c0, nn)
        i = pst[:, 0:nn * 128].rearrange("p (a b) -> p a b", b=128)
        if evac == "act":
            kb.A(lambda e, o=o, i=i: e.activation(out=o, in_=i, func=AF.Copy), r=[pst], w=[dst])
        else:
            kb.V(lambda e, o=o, i=i: e.tensor_copy(out=o, in_=i), r=[pst], w=[dst])


def layer_norm(kb, y, g_bc, b_bc, out, stats, mv, rstd):
    kb.V(lambda e: e.bn_stats(out=stats[:, 0:6], in_=y[:, 0:512]), r=[y], w=[stats])
    kb.V(lambda e: e.bn_stats(out=stats[:, 6:12], in_=y[:, 512:1024]), r=[y], w=[stats])
    kb.V(lambda e: e.bn_aggr(out=mv[:, 0:2], in_=stats[:, 0:12]), r=[stats], w=[mv])
    kb.A(lambda e: e.activation(out=rstd[:, 0:1], in_=mv[:, 1:2], func=AF.Ln, bias=CONST.eps[:, 0:1]), r=[mv, CONST.eps], w=[rstd])
    kb.A(lambda e: e.activation(out=rstd[:, 0:1], in_=rstd[:, 0:1], func=AF.Exp, scale=-0.5), r=[rstd], w=[rstd])
    kb.V(lambda e: e.tensor_scalar(out=out[:, :], in0=y[:, :], scalar1=mv[:, 0:1], scalar2=rstd[:, 0:1], op0=ALU.subtract, op1=ALU.mult), r=[y, mv, rstd], w=[out])
    kb.V(lambda e: e.tensor_tensor(out=out[:, :], in0=out[:, :], in1=g_bc[:, :], op=ALU.mult), r=[out, g_bc], w=[out])
    kb.V(lambda e: e.tensor_tensor(out=out[:, :], in0=out[:, :], in1=b_bc[:, :], op=ALU.add), r=[out, b_bc], w=[out])


def resid_ln_store(kb, xt, mixps, g_bc, b_bc, ybuf, tmp, dst_dram, dst_ap, q="sp"):
    for h in range(2):
        kb.V(lambda e, h=h: e.scalar_tensor_tensor(out=ybuf[:, h * 512:(h + 1) * 512], in0=xt[:, h * 512:(h + 1) * 512], scalar=ALPHA,
                                                  in1=mixps[h][:, 0:512], op0=ALU.mult, op1=ALU.add), r=[xt, mixps[h]], w=[ybuf])
    layer_norm(kb, ybuf, g_bc, b_bc, ybuf, tmp["stats"], tmp["mv"], tmp["rstd"])
    if dst_ap is not None:
        kb.dma(q, dst_ap, ybuf[:, :], r=[ybuf], w=[dst_dram])


def ln_tmp(kb):
    return dict(stats=kb.sb("stats", [128, 12], F32), mv=kb.sb("mv", [128, 2], F32), rstd=kb.sb("rstd", [128, 1], F32))


GELU_MODE = "af"


def peer_prepass(kb, c, u_ap, v_ap, UVs, NCH=128):
    with kb.scope():
        st = [kb.sb("pp_st", [128, 1024], F32) for _ in range(4)]
        ub = [kb.sb("pp_ub", [128, 1024], BF16) for _ in range(2)]
        ut = [kb.sb("pp_ut", [128, 1024], BF16) for _ in range(2)]
        vb = [kb.sb("pp_vb", [128, 1024], BF16) for _ in range(2)]
        pst = [kb.ps("pp_ps", [128, 1024], BF16) for _ in range(2)]
        for ch in range(NCH):
            su, sv = st[(2 * ch) % 4], st[(2 * ch + 1) % 4]
            kb.dma("sp", su[:, :], u_ap[ch * 128:(ch + 1) * 128, :], w=[su])
            kb.dma("pool", sv[:, :], v_ap[ch * 128:(ch + 1) * 128, :], w=[sv])
            b = ub[ch % 2]
            kb.A(lambda e, b=b, su=su: e.activation(out=b[:, :], in_=su[:, :], func=AF.Copy), r=[su], w=[b])
            p = pst[ch % 2]
            for k in range(8):
                kb.T(lambda e, k=k, b=b, p=p: e.transpose(out=p[:, k * 128:(k + 1) * 128], in_=b[:, k * 128:(k + 1) * 128], identity=c.identb[:, :]),
                     r=[b, c.identb], w=[p])
            t = ut[ch % 2]
            kb.V(lambda e, t=t, p=p: e.tensor_copy(out=t[:, :], in_=p[:, :]), r=[p], w=[t])
            kb.dma("sp", UVs[ch][:, 0:1024], t[:, :], r=[t], w=[UVs])
            vv = vb[ch % 2]
            if ch % 2 == 0:
                kb.V(lambda e, vv=vv, sv=sv: e.tensor_copy(out=vv[:, :], in_=sv[:, :]), r=[sv], w=[vv])
            else:
                kb.A(lambda e, vv=vv, sv=sv: e.activation(out=vv[:, :], in_=sv[:, :], func=AF.Copy), r=[sv], w=[vv])
            kb.dma("pool", UVs[ch][:, 1024:2048], vv[:, :], r=[vv], w=[UVs])


def peer_q_phase(kb, c, T, Y1, W, XTd, QTd):
    GS = 512
    with kb.scope():
        wq = kb.sb("wq", [128, 8, 2048], BF16)
        with kb.scope():
            stage = [kb.sb("stg", [128, 2048], F32) for _ in range(2)]
            load_w_bf(kb, wq, lambda k, c0, cw: wq[:, k, c0:c0 + cw], W["peer_w_q"], 8, 2048, stage)
        xt = [kb.sb("xt", [128, 1024], F32) for _ in range(2)]
        xbf = [kb.sb("xbf", [128, 1024], BF16) for _ in range(2)]
        xT = [kb.sb("xT5", [128, 8, GS], BF16) for _ in range(2)]
        qT = [kb.sb("qT5", [128, 16, GS], BF16) for _ in range(2)]
        pq = [kb.ps("pq", [128, 512], F32) for _ in range(4)]
        ptr = [kb.ps("ptr", [128, 1024], BF16) for _ in range(2)]
        XTv = XTd.t.rearrange("k p t -> p k t")
        QTv = QTd.t.rearrange("k p t -> p k t")
        n = 0
        for b in range(T // GS):
            t0 = b * GS
            x5, q5 = xT[b % 2], qT[b % 2]
            for ti in range(4):
                x_, xb_ = xt[ti % 2], xbf[ti % 2]
                kb.dma("sp" if ti % 2 == 0 else "pool", x_[:, :], Y1[t0 + ti * 128:t0 + (ti + 1) * 128, :], r=[Y1], w=[x_])
                kb.A(lambda e, x_=x_, xb_=xb_: e.activation(out=xb_[:, :], in_=x_[:, :], func=AF.Copy), r=[x_], w=[xb_])
                transpose_to(kb, c, xb_, 8, ptr[ti % 2], x5, lambda c0, nn, ti=ti, x5=x5: x5[:, c0:c0 + nn, ti * 128:(ti + 1) * 128])
            kb.dma("sp", XTv[:, :, t0:t0 + GS], x5[:, :, :], r=[x5], w=[XTd])
            for hc in range(16):
                p_ = pq[n % 4]
                n += 1
                for k in range(8):
                    kb.T(lambda e, hc=hc, k=k, p_=p_, x5=x5: e.matmul(p_[:, :], lhsT=wq[:, k, hc * 128:(hc + 1) * 128], rhs=x5[:, k, :], start=(k == 0), stop=(k == 7)), r=[wq, x5], w=[p_])
                if hc % 2 == 0:
                    kb.A(lambda e, hc=hc, p_=p_, q5=q5: e.activation(out=q5[:, hc, :], in_=p_[:, :], func=AF.Copy), r=[p_], w=[q5])
                else:
                    kb.V(lambda e, hc=hc, p_=p_, q5=q5: e.tensor_copy(out=q5[:, hc, :], in_=p_[:, :]), r=[p_], w=[q5])
            kb.dma("pool", QTv[:, :, t0:t0 + GS], q5[:, :, :], r=[q5], w=[QTd])


def peer_phase(kb, c, T, Y1, OUT, p_ap, W, UVs, XTd, QTd, TG=256, NCH=128):
    NT = TG // 128
    NG = T // TG
    TB = 8
    with kb.scope():
        wg = kb.sb("wg", [128, 8, 1024], BF16)
        wp = kb.sb("wp", [128, 2, 1024], BF16)
        skT = kb.sb("skT", [128, 16, 128], BF16)
        with kb.scope():
            stage = [kb.sb("stg", [128, 2048], F32) for _ in range(2)]
            load_w_bf(kb, wg, lambda k, c0, cw: wg[:, k, c0:c0 + cw], W["ple_w_gate"], 8, 1024, stage)
            load_w_bf(kb, wp, lambda k, c0, cw: wp[:, k, c0:c0 + cw], W["ple_w_proj"], 2, 1024, stage)
            pskt = kb.ps("pskt", [128, 512], F32)
            sk = W["peer_sub_keys"].rearrange("h c n d -> (h c) n d")
            for hc in range(16):
                st = stage[hc % 2]
                kb.dma("sp", st[:, 0:128], sk[hc], w=[st])
                kb.T(lambda e, st=st: e.transpose(out=pskt[:, 0:128], in_=st[:, 0:128], identity=c.identf()), r=[st, c.cf], w=[pskt])
                kb.V(lambda e, hc=hc: e.tensor_copy(out=skT[:, hc, :], in_=pskt[:, 0:128]), r=[pskt], w=[skT])
        g_bc = bcast_row(kb, "lnf_g", W["ln_ffn_g"], 1024)
        b_bc = bcast_row(kb, "lnf_b", W["ln_ffn_b"], 1024, q="pool")
        iotaC = kb.sb("iotaC", [128, TB, 128], BF16)
        kb.V(lambda e: e.tensor_copy(out=iotaC[:, :, :], in_=c.iota128().unsqueeze(1).broadcast_to([128, TB, 128])), r=[c.cf], w=[iotaC])
        iota16 = c.iota128()[:, 0:16]

        xT2 = [kb.sb("xT", [128, 8, TG], BF16) for _ in range(2)]
        qT2 = [kb.sb("qT", [128, 16, TG], BF16) for _ in range(2)]
        T32 = [kb.sb("T3", [128, 3, TG], F32) for _ in range(2)]
        xt = kb.sb("xt", [128, 1024], F32)
        S = kb.sb("S", [128, 16, 128], F32)
        S2 = kb.sb("S2", [128, 256], F32)
        V16 = kb.sb("V16", [128, 16, 16], F32)
        I16u = kb.sb("I16u", [128, 16, 16], U32)
        I16f = kb.sb("I16f", [128, 16, 16], F32)
        cand = kb.sb("cand", [128, 8, 256], F32)
        B16 = kb.sb("B16", [128, 8, 16], F32)
        P16u = kb.sb("P16u", [128, 8, 16], U32)
        ABu = kb.sb("ABu", [128, 2, 128], U32)
        ABf = kb.sb("ABf", [128, 2, 128], F32)
        eq = kb.sb("eq", [128, 8, 16, 16], F32)
        J = kb.sb("J", [128, 3, 128], F32)
        e16 = kb.sb("e16", [128, 8, 16], F32)
        ssum = kb.sb("ssum", [128, 8], F32)
        OH1 = [kb.sb("OH1", [128, TB, 128], BF16) for _ in range(2)]
        OH2g = [kb.sb("OH2g", [128, TB, 128], BF16) for _ in range(2)]
        GT = kb.sb("GT", [128, TG, 128], BF16)
        NB = 4
        UVb = [kb.sb("UVb", [128, 2048], BF16) for _ in range(NB)]
        Aact = [kb.sb("Aact", [128, TG], F32) for _ in range(2)]
        Wtb = [kb.sb("Wtb", [128, TG], BF16) for _ in range(3)]
        ybuf = kb.sb("ybuf", [128, 1024], F32)
        y2bf = kb.sb("y2bf", [128, 1024], BF16)
        y2T = kb.sb("y2T", [128, 8, 128], BF16)
        pt = kb.sb("pt", [128, 256], F32)
        pbf = kb.sb("pbf", [128, 256], BF16)
        pT = kb.sb("pT", [128, 2, 128], BF16)
        Sflat = S[:, :, :].rearrange("p a b -> p (a b)")
        sg = Alias(S, Sflat[:, 0:1024])
        ob = Alias(S, Sflat[:, 1024:2048])
        tmp = ln_tmp(kb)
        acc = [[kb.ps("acc", [128, 512], F32) for _ in range(2)] for _ in range(NT)]
        stp = [kb.ps("stp", [128, 512], F32) for _ in range(2)]
        mps = kb.ps("mps", [128, 512], F32)
        mpsb = kb.ps("mpsb", [128, 1024], BF16)
        if NT == 1:
            gps2 = kb.ps("gps", [128, 512], F32)
        gbanks = [mps] + [b for pair in acc for b in pair] if NT > 1 else [mps, gps2] + [b for pair in acc for b in pair]
        XTv = XTd.t.rearrange("k p t -> p k t")
        QTv = QTd.t.rearrange("k p t -> p k t")

        def stage1(g):
            t0 = g * TG
            xT, qT, T3 = xT2[g % 2], qT2[g % 2], T32[g % 2]
            kb.dma("sp", xT[:, :, :], XTv[:, :, t0:t0 + TG], r=[XTd], w=[xT])
            kb.dma("pool", qT[:, :, :], QTv[:, :, t0:t0 + TG], r=[QTd], w=[qT])
            yield
            for ti in range(NT):
                tsl = slice(ti * 128, (ti + 1) * 128)
                for h4 in range(4):
                    for j in range(4):
                        hc = h4 * 4 + j
                        kb.T(lambda e, hc=hc, j=j: e.matmul(mps[:, j * 128:(j + 1) * 128], lhsT=qT[:, hc, tsl], rhs=skT[:, hc, :], start=True, stop=True),
                             r=[qT, skT], w=[mps])
                    kb.V(lambda e, h4=h4: e.tensor_copy(out=S[:, h4 * 4:(h4 + 1) * 4, :], in_=mps[:, :].rearrange("p (a b) -> p a b", b=128)), r=[mps], w=[S])
                    yield
                for hc in range(16):
                    kb.V(lambda e, hc=hc: e.max(out=V16[:, hc, 0:8], in_=S[:, hc, :]), r=[S], w=[V16])
                    kb.V(lambda e, hc=hc: e.max_index(out=I16u[:, hc, 0:8], in_max=V16[:, hc, 0:8], in_values=S[:, hc, :]), r=[S, V16], w=[I16u])
                    kb.V(lambda e, hc=hc: e.match_replace(out=S2[:, 0:128], in_to_replace=V16[:, hc, 0:8], in_values=S[:, hc, :], imm_value=NEG), r=[S, V16], w=[S2])
                    yield
                    kb.V(lambda e, hc=hc: e.max(out=V16[:, hc, 8:16], in_=S2[:, 0:128]), r=[S2], w=[V16])
                    kb.V(lambda e, hc=hc: e.max_index(out=I16u[:, hc, 8:16], in_max=V16[:, hc, 8:16], in_values=S2[:, 0:128]), r=[S2, V16], w=[I16u])
                    yield
                kb.V(lambda e: e.tensor_copy(out=I16f[:, :, :], in_=I16u[:, :, :]), r=[I16u], w=[I16f])
                V4 = V16[:, :, :].rearrange("p (h c) k -> p h c k", c=2)
                I4 = I16f[:, :, :].rearrange("p (h c) k -> p h c k", c=2)
                cand4 = cand[:, :, :].rearrange("p h (a b) -> p h a b", b=16)
                kb.V(lambda e: e.tensor_tensor(out=cand4, in0=V4[:, :, 0, :].unsqueeze(3).broadcast_to([128, 8, 16, 16]),
                                               in1=V4[:, :, 1, :].unsqueeze(2).broadcast_to([128, 8, 16, 16]), op=ALU.add), r=[V16], w=[cand])
                yield
                for h in range(8):
                    kb.V(lambda e, h=h: e.max(out=B16[:, h, 0:8], in_=cand[:, h, :]), r=[cand], w=[B16])
                    kb.V(lambda e, h=h: e.max_index(out=P16u[:, h, 0:8], in_max=B16[:, h, 0:8], in_values=cand[:, h, :]), r=[cand, B16], w=[P16u])
                    kb.V(lambda e, h=h: e.match_replace(out=S2[:, :], in_to_replace=B16[:, h, 0:8], in_values=cand[:, h, :], imm_value=NEG), r=[cand, B16], w=[S2])
                    yield
                    kb.V(lambda e, h=h: e.max(out=B16[:, h, 8:16], in_=S2[:, :]), r=[S2], w=[B16])
                    kb.V(lambda e, h=h: e.max_index(out=P16u[:, h, 8:16], in_max=B16[:, h, 8:16], in_values=S2[:, :]), r=[S2, B16], w=[P16u])
                    yield
                Pfl = P16u[:, :, :].rearrange("p h k -> p (h k)")
                kb.V(lambda e: e.tensor_single_scalar(out=ABu[:, 0, :], in_=Pfl, scalar=4, op=ALU.logical_shift_right), r=[P16u], w=[ABu])
                kb.V(lambda e: e.tensor_single_scalar(out=ABu[:, 1, :], in_=Pfl, scalar=15, op=ALU.bitwise_and), r=[P16u], w=[ABu])
                kb.V(lambda e: e.tensor_copy(out=ABf[:, :, :], in_=ABu[:, :, :]), r=[ABu], w=[ABf])
                yield
                for ci in range(2):
                    ab4 = ABf[:, ci, :].rearrange("p (h k) -> p h k", k=16).unsqueeze(3).broadcast_to([128, 8, 16, 16])
                    kb.V(lambda e, ab4=ab4: e.tensor_tensor(out=eq[:, :, :, :], in0=ab4, in1=iota16.unsqueeze(1).unsqueeze(1).broadcast_to([128, 8, 16, 16]), op=ALU.is_equal),
                         r=[ABf, c.cf], w=[eq])
                    yield
                    kb.V(lambda e, ci=ci: e.tensor_tensor(out=eq[:, :, :, :], in0=eq[:, :, :, :], in1=I4[:, :, ci, :].unsqueeze(2).broadcast_to([128, 8, 16, 16]), op=ALU.mult),
                         r=[eq, I16f], w=[eq])
                    yield
                    kb.V(lambda e, ci=ci: e.tensor_reduce(out=J[:, ci, :].rearrange("p (h k) -> p h k", k=16), in_=eq[:, :, :, :], axis=AX.X, op=ALU.add), r=[eq], w=[J])
                    yield
                kb.V(lambda e: e.tensor_tensor(out=e16[:, :, :], in0=B16[:, :, :], in1=B16[:, :, 0:1].broadcast_to([128, 8, 16]), op=ALU.subtract), r=[B16], w=[e16])
                kb.A(lambda e: e.activation(out=e16[:, :, :], in_=e16[:, :, :], func=AF.Exp), r=[e16], w=[e16])
                kb.V(lambda e: e.tensor_reduce(out=ssum[:, :], in_=e16[:, :, :], axis=AX.X, op=ALU.add), r=[e16], w=[ssum])
                yield
                kb.V(lambda e: e.reciprocal(out=ssum[:, :], in_=ssum[:, :]), r=[ssum], w=[ssum])
                kb.V(lambda e: e.tensor_tensor(out=J[:, 2, :].rearrange("p (h k) -> p h k", k=16), in0=e16[:, :, :], in1=ssum[:, :].unsqueeze(2).broadcast_to([128, 8, 16]), op=ALU.mult),
                     r=[e16, ssum], w=[J])
                yield
                for q3 in range(3):
                    kb.T(lambda e, q3=q3: e.transpose(out=mps[:, q3 * 128:(q3 + 1) * 128], in_=J[:, q3, :], identity=c.identf()), r=[J, c.cf], w=[mps])
                kb.V(lambda e: e.tensor_copy(out=T3[:, :, tsl], in_=mps[:, 0:384].rearrange("p (a b) -> p a b", b=128)), r=[mps], w=[T3])
                yield

        def stage2(g):
            T3f = T32[g % 2]
            nsb = 0
            for s0 in range(0, TG, TB):
                o1, o2g = OH1[nsb % 2], OH2g[nsb % 2]
                nsb += 1
                for tb in range(TB):
                    t_ = s0 + tb
                    kb.V(lambda e, o1=o1, tb=tb, t_=t_: e.tensor_scalar(out=o1[:, tb, :], in0=c.iotab[:, :], scalar1=T3f[:, 0, t_:t_ + 1], scalar2=None, op0=ALU.is_equal),
                         r=[c.iotab, T3f], w=[o1])
                    kb.V(lambda e, o2g=o2g, tb=tb, t_=t_: e.tensor_scalar(out=o2g[:, tb, :], in0=c.iotab[:, :], scalar1=T3f[:, 1, t_:t_ + 1], scalar2=T3f[:, 2, t_:t_ + 1], op0=ALU.is_equal, op1=ALU.mult),
                         r=[c.iotab, T3f], w=[o2g])
                for tb in range(TB):
                    gp = gbanks[((s0 + tb) // 4) % len(gbanks)]
                    kb.T(lambda e, gp=gp, tb=tb, o1=o1, o2g=o2g: e.matmul(gp[:, (tb % 4) * 128:(tb % 4 + 1) * 128], lhsT=o2g[:, tb, :], rhs=o1[:, tb, :], start=True, stop=True),
                         r=[o1, o2g], w=[gp])
                    if tb % 4 == 3:
                        tt = s0 + tb - 3
                        kb.A(lambda e, gp=gp, tt=tt: e.activation(out=GT[:, tt:tt + 4, :], in_=gp[:, :].rearrange("p (a b) -> p a b", b=128), func=AF.Copy), r=[gp], w=[GT])

        def chunk_loop(g, nxt):
            xT = xT2[g % 2]

            def emit_u(ch):
                uvb = UVb[ch % NB]
                kb.dma("sp", uvb[:, :], UVs[ch], r=[UVs], w=[uvb])
                ub = Alias(uvb, uvb[:, 0:1024])
                sp_ = stp[ch % 2]
                for k in range(8):
                    kb.T(lambda e, k=k, ub=ub, sp_=sp_: e.matmul(sp_[:, 0:TG], lhsT=ub[:, k * 128:(k + 1) * 128], rhs=xT[:, k, :], start=(k == 0), stop=(k == 7)),
                         r=[ub, xT], w=[sp_])
                a_, w_ = Aact[ch % 2], Wtb[ch % 3]
                kb.A(lambda e, a_=a_, sp_=sp_: e.activation(out=a_[:, :], in_=sp_[:, 0:TG], func=AF.Gelu_apprx_tanh), r=[sp_], w=[a_])
                kb.V(lambda e, a_=a_, w_=w_, ch=ch: e.tensor_tensor(out=w_[:, :], in0=a_[:, :], in1=GT[:, :, ch], op=ALU.mult), r=[a_, GT], w=[w_])

            def emit_v(ch):
                uvb = UVb[ch % NB]
                vb2 = Alias(uvb, uvb[:, 1024:2048])
                w_ = Wtb[ch % 3]
                for ti in range(NT):
                    for hf in range(2):
                        kb.T(lambda e, ti=ti, hf=hf, w_=w_, vb2=vb2, ch=ch: e.matmul(acc[ti][hf][:, 0:512], lhsT=w_[:, ti * 128:(ti + 1) * 128], rhs=vb2[:, hf * 512:(hf + 1) * 512],
                                                                                  start=(ch == 0), stop=(ch == NCH - 1)), r=[w_, vb2], w=[acc[ti][hf]])

            for ch in range(NCH):
                emit_u(ch)
                if ch >= 1:
                    emit_v(ch - 1)
                if nxt is not None and ch >= 2:
                    next(nxt, None)
            emit_v(NCH - 1)
            if nxt is not None:
                for _ in nxt:
                    pass

        def epilogue(g):
            t0 = g * TG
            for ti in range(NT):
                tok0 = t0 + ti * 128
                kb.dma("sp", xt[:, :], Y1[tok0:tok0 + 128, :], r=[Y1], w=[xt])
                kb.dma("pool", pt[:, :], p_ap[tok0:tok0 + 128, :], w=[pt])
                resid_ln_store(kb, xt, acc[ti], g_bc, b_bc, ybuf, tmp, None, None)
                kb.A(lambda e: e.activation(out=y2bf[:, :], in_=ybuf[:, :], func=AF.Copy), r=[ybuf], w=[y2bf])
                transpose_to(kb, c, y2bf, 8, mpsb, y2T, lambda c0, nn: y2T[:, c0:c0 + nn, :])
                kb.A(lambda e: e.activation(out=pbf[:, :], in_=pt[:, :], func=AF.Copy), r=[pt], w=[pbf])
                transpose_to(kb, c, pbf, 2, mpsb, pT, lambda c0, nn: pT[:, c0:c0 + nn, :])
                for hf in range(2):
                    hs = slice(hf * 512, (hf + 1) * 512)
                    for k in range(8):
                        kb.T(lambda e, k=k, hs=hs: e.matmul(mps[:, :], lhsT=y2T[:, k, :], rhs=wg[:, k, hs], start=(k == 0), stop=(k == 7)), r=[y2T, wg], w=[mps])
                    kb.A(lambda e, hs=hs: e.activation(out=sg[:, hs], in_=mps[:, :], func=AF.Sigmoid), r=[mps], w=[sg])
                    for k in range(2):
                        kb.T(lambda e, k=k, hs=hs: e.matmul(mps[:, :], lhsT=pT[:, k, :], rhs=wp[:, k, hs], start=(k == 0), stop=(k == 1)), r=[pT, wp], w=[mps])
                    kb.V(lambda e, hs=hs: e.tensor_tensor(out=ob[:, hs], in0=sg[:, hs], in1=mps[:, :], op=ALU.mult), r=[sg, mps], w=[ob])
                kb.V(lambda e: e.tensor_tensor(out=ob[:, :], in0=ob[:, :], in1=ybuf[:, :], op=ALU.add), r=[ob, ybuf], w=[ob])
                kb.dma("sp", OUT[tok0:tok0 + 128, :], ob[:, :], r=[ob], w=[OUT])

        for _ in stage1(0):
            pass
        for g in range(NG):
            nxt = stage1(g + 1) if g + 1 < NG else None
            stage2(g)
            chunk_loop(g, nxt)
            epilogue(g)


def load_cols(kb, c, dst, dst_ap_fn, src_rows_ap, R, stage, pst, nblk=1, blk_stride=0):
    for b in range(nblk):
        kb.dma("sp", stage[0:R, 0:128], src_rows_ap(b), w=[stage])
        kb.T(lambda e: e.transpose(out=pst[:, 0:R], in_=stage[0:R, 0:128], identity=c.cf[0:R, 0:R]), r=[stage, c.cf], w=[pst])
        kb.V(lambda e, b=b: e.tensor_copy(out=dst_ap_fn(b), in_=pst[:, 0:R]), r=[pst], w=[dst])


def conf_phase(kb, c, T, S, XIN, Y1, W):
    GS = 512
    NG = T // GS
    GPS = S // GS
    KW = 31
    with kb.scope():
        w1 = kb.sb("w1", [128, 8, 2048], BF16)
        w2 = kb.sb("w2", [128, 8, 1024], BF16)
        b1 = kb.sb("b1", [128, 16], F32)
        wdw = kb.sb("wdw", [128, 8, KW], F32)
        vecs = kb.sb("vecs", [128, 3, 8], F32)
        with kb.scope():
            stage = [kb.sb("stg", [128, 2048], F32) for _ in range(2)]
            load_w_bf(kb, w1, lambda k, c0, cw: w1[:, k, c0:c0 + cw], W["conv_w_pw1"], 8, 2048, stage)
            load_w_bf(kb, w2, lambda k, c0, cw: w2[:, k, c0:c0 + cw], W["conv_w_pw2"], 8, 1024, stage)
            pst = kb.ps("pst", [128, 512], F32)
            load_cols(kb, c, b1, lambda b: b1[:, :], lambda b: W["conv_b_pw1"].rearrange("(c p) -> c p", p=128), 16, stage[0], pst)
            load_cols(kb, c, wdw, lambda b: wdw[:, b, :], lambda b: W["conv_w_dw"][:, b * 128:(b + 1) * 128], KW, stage[1], pst, nblk=8)
            for i, nm in enumerate(("conv_b_dw", "conv_ln_g", "conv_ln_b")):
                load_cols(kb, c, vecs, lambda b, i=i: vecs[:, i, :], lambda b, nm=nm: W[nm].rearrange("(c p) -> c p", p=128), 8, stage[i % 2], pst)
        g_bc = bcast_row(kb, "lnm_g", W["ln_mix_g"], 1024)
        b_bc = bcast_row(kb, "lnm_b", W["ln_mix_b"], 1024, q="pool")
        xt = [kb.sb("xt", [128, 1024], F32) for _ in range(4)]
        xbf = kb.sb("xbf", [128, 1024], BF16)
        xT = kb.sb("xT", [128, 8, GS], BF16)
        gluH = kb.sb("gluH", [128, 8, KW - 1 + GS], F32)
        hc = kb.sb("hc", [128, 8, GS], F32)
        hsq = kb.sb("hsq", [128, 8, GS], F32)
        zT = kb.sb("zT", [128, 8, GS], BF16)
        sgb = [kb.sb("sgb", [128, GS], F32) for _ in range(2)]
        mean = kb.sb("mean", [128, GS], F32)
        msq = kb.sb("msq", [128, GS], F32)
        rstd = kb.sb("rstd2", [128, GS], F32)
        tn = [kb.sb("tn", [128, GS], F32) for _ in range(2)]
        ybuf = kb.sb("ybuf", [128, 1024], F32)
        tmp = ln_tmp(kb)
        pa = kb.ps("pa", [128, 512], F32)
        pg = kb.ps("pg", [128, 512], F32)
        s1 = kb.ps("s1", [128, 512], F32)
        s2 = kb.ps("s2", [128, 512], F32)
        po = [kb.ps("po", [128, 512], F32) for _ in range(2)]
        ptr = kb.ps("ptr", [128, 1024], BF16)
        H = KW - 1
        for g in range(NG):
            t0 = g * GS
            for ti in range(4):
                kb.dma("sp" if ti % 2 == 0 else "pool", xt[ti][:, :], XIN[t0 + ti * 128:t0 + (ti + 1) * 128, :], r=[XIN], w=[xt[ti]])
                kb.A(lambda e, ti=ti: e.activation(out=xbf[:, :], in_=xt[ti][:, :], func=AF.Copy), r=[xt[ti]], w=[xbf])
                transpose_to(kb, c, xbf, 8, ptr, xT, lambda c0, nn, ti=ti: xT[:, c0:c0 + nn, ti * 128:(ti + 1) * 128])
            if g % GPS == 0:
                kb.G(lambda e: e.memset(gluH[:, :, 0:H], 0.0), w=[gluH])
            for cc in range(8):
                for k in range(8):
                    kb.T(lambda e, k=k, cc=cc: e.matmul(pa[:, :], lhsT=w1[:, k, cc * 128:(cc + 1) * 128], rhs=xT[:, k, :], start=(k == 0), stop=(k == 7)), r=[w1, xT], w=[pa])
                for k in range(8):
                    kb.T(lambda e, k=k, cc=cc: e.matmul(pg[:, :], lhsT=w1[:, k, 1024 + cc * 128:1024 + (cc + 1) * 128], rhs=xT[:, k, :], start=(k == 0), stop=(k == 7)), r=[w1, xT], w=[pg])
                sg_ = sgb[cc % 2]
                kb.A(lambda e, sg_=sg_, cc=cc: e.activation(out=sg_[:, :], in_=pg[:, :], func=AF.Sigmoid, bias=b1[:, 8 + cc:9 + cc]), r=[pg, b1], w=[sg_])
                kb.V(lambda e, sg_=sg_, cc=cc: e.scalar_tensor_tensor(out=gluH[:, cc, H:H + GS], in0=pa[:, :], scalar=b1[:, cc:cc + 1], in1=sg_[:, :], op0=ALU.add, op1=ALU.mult),
                     r=[pa, b1, sg_], w=[gluH])
                kb.V(lambda e, cc=cc: e.tensor_scalar(out=hc[:, cc, :], in0=gluH[:, cc, H:H + GS], scalar1=wdw[:, cc, H:H + 1], scalar2=vecs[:, 0, cc:cc + 1], op0=ALU.mult, op1=ALU.add),
                     r=[gluH, wdw, vecs], w=[hc])
                for k in range(H):
                    kb.V(lambda e, cc=cc, k=k: e.scalar_tensor_tensor(out=hc[:, cc, :], in0=gluH[:, cc, k:k + GS], scalar=wdw[:, cc, k:k + 1], in1=hc[:, cc, :], op0=ALU.mult, op1=ALU.add),
                         r=[gluH, wdw, hc], w=[hc])
            kb.G(lambda e: e.tensor_copy(out=gluH[:, :, 0:H], in_=gluH[:, :, GS:GS + H]), r=[gluH], w=[gluH])
            kb.A(lambda e: e.activation(out=hsq[:, :, :], in_=hc[:, :, :], func=AF.Square), r=[hc], w=[hsq])
            for cc in range(8):
                kb.T(lambda e, cc=cc: e.matmul(s1[:, :], lhsT=c.ones(), rhs=hc[:, cc, :], start=(cc == 0), stop=(cc == 7)), r=[c.cf, hc], w=[s1])
            for cc in range(8):
                kb.T(lambda e, cc=cc: e.matmul(s2[:, :], lhsT=c.ones(), rhs=hsq[:, cc, :], start=(cc == 0), stop=(cc == 7)), r=[c.cf, hsq], w=[s2])
            kb.V(lambda e: e.tensor_scalar(out=mean[:, :], in0=s1[:, :], scalar1=1.0 / 1024, scalar2=None, op0=ALU.mult), r=[s1], w=[mean])
            kb.V(lambda e: e.tensor_tensor(out=msq[:, :], in0=mean[:, :], in1=mean[:, :], op=ALU.mult), r=[mean], w=[msq])
            kb.V(lambda e: e.scalar_tensor_tensor(out=msq[:, :], in0=s2[:, :], scalar=1.0 / 1024, in1=msq[:, :], op0=ALU.mult, op1=ALU.subtract), r=[s2, msq], w=[msq])
            kb.A(lambda e: e.activation(out=rstd[:, :], in_=msq[:, :], func=AF.Ln, bias=CONST.eps[:, 0:1]), r=[msq, CONST.eps], w=[rstd])
            kb.A(lambda e: e.activation(out=rstd[:, :], in_=rstd[:, :], func=AF.Exp, scale=-0.5), r=[rstd], w=[rstd])
            for cc in range(8):
                t_ = tn[cc % 2]
                kb.G(lambda e, cc=cc, t_=t_: e.tensor_tensor(out=t_[:, :], in0=hc[:, cc, :], in1=mean[:, :], op=ALU.subtract), r=[hc, mean], w=[t_])
                kb.V(lambda e, t_=t_: e.tensor_tensor(out=t_[:, :], in0=t_[:, :], in1=rstd[:, :], op=ALU.mult), r=[t_, rstd], w=[t_])
                kb.V(lambda e, cc=cc, t_=t_: e.tensor_scalar(out=t_[:, :], in0=t_[:, :], scalar1=vecs[:, 1, cc:cc + 1], scalar2=vecs[:, 2, cc:cc + 1], op0=ALU.mult, op1=ALU.add), r=[t_, vecs], w=[t_])
                kb.A(lambda e, cc=cc, t_=t_: e.activation(out=zT[:, cc, :], in_=t_[:, :], func=AF.Silu), r=[t_], w=[zT])
            for ti in range(4):
                for hf in range(2):
                    for cc in range(8):
                        kb.T(lambda e, ti=ti, hf=hf, cc=cc: e.matmul(po[hf][:, :], lhsT=zT[:, cc, ti * 128:(ti + 1) * 128], rhs=w2[:, cc, hf * 512:(hf + 1) * 512], start=(cc == 0), stop=(cc == 7)),
                             r=[zT, w2], w=[po[hf]])
                resid_ln_store(kb, xt[ti], po, g_bc, b_bc, ybuf, tmp, Y1, Y1[t0 + ti * 128:t0 + (ti + 1) * 128, :], q="sp" if ti % 2 == 0 else "pool")


C2W = NRELW + 72


def host_c2():
    c2 = np.zeros((128, C2W), np.float32)
    c2[:, 0:NRELW] = (np.arange(NRELW) - 2304)[None, :]
    for own in range(9):
        c2[:, NRELW + own * 8:NRELW + own * 8 + 8] = np.where(np.arange(8) < own, 0.0, NEG)[None, :]
    return c2


def moba_phase(kb, c, T, S, XIN, Y1, W, c2dram):
    NSEQ = T // S
    NQ = S // 128
    NBLK = S // 256
    GS = 512
    with kb.scope():
        qT = kb.sb("qT_all", [128, 8, S], BF16)
        kT = kb.sb("kT_all", [128, 8, S], BF16)
        va = kb.sb("v_all", [128, NQ, 1024], BF16)
        kmf = kb.sb("kmf", [128, 8, 8], F32)
        kmT = kb.sb("kmT", [128, 8, 8], BF16)
        for sq in range(NSEQ):
            base = sq * S
            with kb.scope():
                wqkv = kb.sb("wqkv", [128, 8, 3072], BF16)
                stage = [kb.sb("stg", [128, 1024], F32) for _ in range(2)]
                load_w_bf(kb, wqkv, lambda k, c0, cw: wqkv[:, k, c0:c0 + cw], W["moba_w_qkv"], 8, 3072, stage)
                xt = [kb.sb("xt", [128, 1024], F32) for _ in range(2)]
                xbf = kb.sb("xbf", [128, 1024], BF16)
                xT = kb.sb("xT", [128, 8, GS], BF16)
                pp = [kb.ps("pp", [128, 512], F32) for _ in range(2)]
                ptr = kb.ps("ptr", [128, 1024], BF16)
                n = 0
                for g in range(S // GS):
                    t0 = base + g * GS
                    for ti in range(4):
                        x_ = xt[ti % 2]
                        kb.dma("sp" if ti % 2 == 0 else "pool", x_[:, :], XIN[t0 + ti * 128:t0 + (ti + 1) * 128, :], r=[XIN], w=[x_])
                        kb.A(lambda e, x_=x_: e.activation(out=xbf[:, :], in_=x_[:, :], func=AF.Copy), r=[x_], w=[xbf])
                        transpose_to(kb, c, xbf, 8, ptr, xT, lambda c0, nn, ti=ti: xT[:, c0:c0 + nn, ti * 128:(ti + 1) * 128])
                    gsl = slice(g * GS, (g + 1) * GS)
                    for pr in range(16):
                        p_ = pp[n % 2]
                        n += 1
                        for k in range(8):
                            kb.T(lambda e, k=k, pr=pr, p_=p_: e.matmul(p_[:, :], lhsT=wqkv[:, k, pr * 128:(pr + 1) * 128], rhs=xT[:, k, :], start=(k == 0), stop=(k == 7)), r=[wqkv, xT], w=[p_])
                        if pr < 8:
                            kb.A(lambda e, pr=pr, p_=p_: e.activation(out=qT[:, pr, gsl], in_=p_[:, :], func=AF.Copy, scale=0.125), r=[p_], w=[qT])
                        else:
                            kb.V(lambda e, pr=pr, p_=p_: e.tensor_copy(out=kT[:, pr - 8, gsl], in_=p_[:, :]), r=[p_], w=[kT])
                    for ti in range(4):
                        for hf in range(2):
                            p_ = pp[n % 2]
                            n += 1
                            for k in range(8):
                                kb.T(lambda e, k=k, ti=ti, hf=hf, p_=p_: e.matmul(p_[:, :], lhsT=xT[:, k, ti * 128:(ti + 1) * 128], rhs=wqkv[:, k, 2048 + hf * 512:2048 + (hf + 1) * 512], start=(k == 0), stop=(k == 7)),
                                     r=[wqkv, xT], w=[p_])
                            if hf == 0:
                                kb.A(lambda e, ti=ti, g=g, p_=p_: e.activation(out=va[:, g * 4 + ti, 0:512], in_=p_[:, :], func=AF.Copy), r=[p_], w=[va])
                            else:
                                kb.V(lambda e, ti=ti, g=g, p_=p_: e.tensor_copy(out=va[:, g * 4 + ti, 512:1024], in_=p_[:, :]), r=[p_], w=[va])
                kb.V(lambda e: e.tensor_reduce(out=kmf[:, :, 0:NBLK], in_=kT[:, :, :].rearrange("p a (b j) -> p a b j", j=256), axis=AX.X, op=ALU.add), r=[kT], w=[kmf])
                kb.A(lambda e: e.activation(out=kmT[:, :, 0:NBLK], in_=kmf[:, :, 0:NBLK], func=AF.Copy, scale=1.0 / 256), r=[kmf], w=[kmT])
            with kb.scope():
                wo = kb.sb("wo", [128, 8, 1024], BF16)
                with kb.scope():
                    stage = [kb.sb("stg", [128, 1024], F32) for _ in range(2)]
                    load_w_bf(kb, wo, lambda k, c0, cw: wo[:, k, c0:c0 + cw], W["moba_w_out"], 8, 1024, stage)
                c2 = kb.sb("c2", [128, C2W], F32)
                kb.dma("sp", c2[:, :], c2dram[:, :], r=[c2dram], w=[c2])
                g_bc = bcast_row(kb, "lnm_g", W["ln_mix_g"], 1024)
                b_bc = bcast_row(kb, "lnm_b", W["ln_mix_b"], 1024, q="pool")
                xt = kb.sb("xt", [128, 1024], F32)
                L2 = [kb.sb("L", [128, S], F32) for _ in range(2)]
                Pb2 = [kb.sb("Pb", [128, S], BF16) for _ in range(2)]
                PT2 = [kb.sb("PT", [128, NQ, 128], BF16) for _ in range(2)]
                gm = kb.sb("gm", [128, 16, 8], F32)
                m8 = kb.sb("m8", [128, 16, 8], F32)
                selb = kb.sb("selb", [128, 16, 8], F32)
                rmax2 = [kb.sb("rmax", [128, 1], F32) for _ in range(2)]
                rsum2 = [kb.sb("rsum", [128, 1], F32) for _ in range(2)]
                attn = kb.sb("attn", [128, 1024], BF16)
                attnT = kb.sb("attnT", [128, 8, 128], BF16)
                ybuf = kb.sb("ybuf", [128, 1024], F32)
                tmp = ln_tmp(kb)
                pl = [kb.ps("pl", [128, 512], F32) for _ in range(4)]
                ptp = kb.ps("ptp", [128, 1024], BF16)
                pv = kb.ps("pv", [128, 512], F32)
                po = [kb.ps("po", [128, 512], F32) for _ in range(2)]
                for qi in range(NQ):
                    q0 = qi * 128
                    own = qi // 2
                    nk = q0 + 128
                    qs = slice(q0, q0 + 128)
                    kb.dma("pool", xt[:, :], XIN[base + q0:base + q0 + 128, :], r=[XIN], w=[xt])
                    gated = own >= 4
                    if gated:
                        pgt = po[1]
                        for h in range(16):
                            pr, r0 = h // 2, (h % 2) * 64
                            kb.T(lambda e, h=h, pr=pr, r0=r0: e.matmul(pgt[:, h * 8:(h + 1) * 8], lhsT=qT[r0:r0 + 64, pr, qs], rhs=kmT[r0:r0 + 64, pr, 0:8], start=True, stop=True), r=[qT, kmT], w=[pgt])
                        kb.V(lambda e: e.tensor_tensor(out=gm[:, :, :], in0=pgt[:, 0:128].rearrange("p (h n) -> p h n", n=8),
                                                       in1=c2[:, NRELW + own * 8:NRELW + own * 8 + 8].unsqueeze(1).broadcast_to([128, 16, 8]), op=ALU.add), r=[pgt, c2], w=[gm])
                        for h in range(16):
                            kb.V(lambda e, h=h: e.max(out=m8[:, h, :], in_=gm[:, h, :]), r=[gm], w=[m8])
                        kb.V(lambda e: e.tensor_tensor(out=selb[:, :, :], in0=gm[:, :, :], in1=m8[:, :, 2:3].broadcast_to([128, 16, 8]), op=ALU.is_ge), r=[gm, m8], w=[selb])
                        kb.V(lambda e: e.tensor_scalar(out=selb[:, :, :], in0=selb[:, :, :], scalar1=1.0, scalar2=1.0e30, op0=ALU.subtract, op1=ALU.mult), r=[selb], w=[selb])
                    for h in range(16):
                        pr, r0 = h // 2, (h % 2) * 64
                        slope = 2.0 ** (-(h + 1) / 2.0)
                        off = 2177 - q0
                        L, Pb, PT, rmax, rsum = L2[h % 2], Pb2[h % 2], PT2[h % 2], rmax2[h % 2], rsum2[h % 2]
                        for j0 in range((nk + 511) // 512):
                            c0, c1 = j0 * 512, min(nk, (j0 + 1) * 512)
                            j = (j0 + 2 * (h % 2)) % 4 if nk <= 1024 else j0
                            kb.T(lambda e, j=j, c0=c0, c1=c1, pr=pr, r0=r0: e.matmul(pl[j][:, 0:c1 - c0], lhsT=qT[r0:r0 + 64, pr, qs], rhs=kT[r0:r0 + 64, pr, c0:c1], start=True, stop=True), r=[qT, kT], w=[pl[j]])
                            kb.V(lambda e, j=j, c0=c0, c1=c1: e.scalar_tensor_tensor(out=L[:, c0:c1], in0=c2[:, off + c0:off + c1], scalar=slope, in1=pl[j][:, 0:c1 - c0], op0=ALU.mult, op1=ALU.add),
                                 r=[c2, pl[j]], w=[L])
                        if gated:
                            kb.V(lambda e, h=h: e.tensor_tensor(out=L[:, 0:own * 256].rearrange("p (b j) -> p b j", j=256), in0=L[:, 0:own * 256].rearrange("p (b j) -> p b j", j=256),
                                                                in1=selb[:, h, 0:own].unsqueeze(2).broadcast_to([128, own, 256]), op=ALU.add), r=[L, selb], w=[L])
                        kb.V(lambda e: e.tensor_tensor(out=L[:, nk - 128:nk], in0=L[:, nk - 128:nk], in1=c.tri_q(), op=ALU.add), r=[L, c.cf], w=[L])
                        kb.V(lambda e: e.tensor_reduce(out=rmax[:, :], in_=L[:, 0:nk], axis=AX.X, op=ALU.max), r=[L], w=[rmax])
                        kb.V(lambda e: e.tensor_scalar(out=rmax[:, :], in0=rmax[:, :], scalar1=-1.0, scalar2=None, op0=ALU.mult), r=[rmax], w=[rmax])
                        kb.V(lambda e: e.memset(rsum[:, :], 0.0), w=[rsum])
                        kb.A(lambda e: e.activation(out=Pb[:, 0:nk], in_=L[:, 0:nk], func=AF.Exp, bias=rmax[:, 0:1], accum_out=rsum[:, 0:1]), r=[L, rmax, rsum], w=[Pb, rsum])
                        nj = nk // 128
                        for j in range(nj):
                            kb.T(lambda e, j=j: e.transpose(out=ptp[:, (j % 8) * 128:(j % 8 + 1) * 128], in_=Pb[:, j * 128:(j + 1) * 128], identity=c.identb[:, :]), r=[Pb, c.identb], w=[ptp])
                            if j % 8 == 7 or j == nj - 1:
                                j0 = (j // 8) * 8
                                nn = j - j0 + 1
                                o = PT[:, j0:j0 + nn, :]
                                i_ = ptp[:, 0:nn * 128].rearrange("p (a b) -> p a b", b=128)
                                if (j // 8) % 2 == 0:
                                    kb.V(lambda e, o=o, i_=i_: e.tensor_copy(out=o, in_=i_), r=[ptp], w=[PT])
                                else:
                                    kb.A(lambda e, o=o, i_=i_: e.activation(out=o, in_=i_, func=AF.Copy), r=[ptp], w=[PT])
                        for j in range(nj):
                            kb.T(lambda e, j=j, h=h: e.matmul(pv[:, 0:64], lhsT=PT[:, j, :], rhs=va[:, j, h * 64:(h + 1) * 64], start=(j == 0), stop=(j == nj - 1)), r=[PT, va], w=[pv])
                        kb.V(lambda e: e.reciprocal(out=rsum[:, :], in_=rsum[:, :]), r=[rsum], w=[rsum])
                        kb.V(lambda e, h=h: e.tensor_scalar(out=attn[:, h * 64:(h + 1) * 64], in0=pv[:, 0:64], scalar1=rsum[:, 0:1], scalar2=None, op0=ALU.mult), r=[pv, rsum], w=[attn])
                    transpose_to(kb, c, attn, 8, ptp, attnT, lambda c0, nn: attnT[:, c0:c0 + nn, :])
                    for hf in range(2):
                        for k in range(8):
                            kb.T(lambda e, k=k, hf=hf: e.matmul(po[hf][:, :], lhsT=attnT[:, k, :], rhs=wo[:, k, hf * 512:(hf + 1) * 512], start=(k == 0), stop=(k == 7)), r=[attnT, wo], w=[po[hf]])
                    resid_ln_store(kb, xt, po, g_bc, b_bc, ybuf, tmp, Y1, Y1[base + q0:base + q0 + 128, :])


def ssd_phase_a(kb, c, T, S, XIN, W, XS, BTM, BCT, ZS, DT):
    GS = 512
    NG = T // GS
    GPS = S // GS
    with kb.scope():
        win = kb.sb("win", [128, 8, 5152], BF16)
        cw = kb.sb("cw", [128, 24, 4], F32)
        cb = kb.sb("cb", [128, 24], F32)
        with kb.scope():
            stage = [kb.sb("stg", [128, 2048], F32) for _ in range(2)]
            load_w_bf(kb, win, lambda k, c0, cw_: win[:, k, c0:c0 + cw_], W["ssd_w_in"], 8, 5152, stage)
            pst = kb.ps("pst", [128, 512], F32)
            load_cols(kb, c, cw, lambda b: cw[:, b, :], lambda b: W["ssd_conv_w"][:, b * 128:(b + 1) * 128], 4, stage[0], pst, nblk=24)
            load_cols(kb, c, cb, lambda b: cb[:, :], lambda b: W["ssd_conv_b"].rearrange("(c p) -> c p", p=128), 24, stage[1], pst)
        dtb = bcast_row(kb, "dtb", W["ssd_dt_bias"], 32)
        one1 = kb.sb("one1", [128, 1], F32)
        kb.V(lambda e: e.memset(one1[:, :], 1.0), w=[one1])
        xt = [kb.sb("xt", [128, 1024], F32) for _ in range(2)]
        xbf = kb.sb("xbf", [128, 1024], BF16)
        xT = kb.sb("xT", [128, 8, GS], BF16)
        rawH = [kb.sb("rawH", [128, 3 + GS], F32) for _ in range(2)]
        hal = kb.sb("hal", [128, 24, 3], F32)
        cacc = [kb.sb("cacc", [128, GS], F32) for _ in range(2)]
        xbcT = kb.sb("xbcT", [128, 24, GS], BF16)
        xs_sb = [kb.sb("xs_sb", [128, 2048], BF16) for _ in range(2)]
        b_sb = [kb.sb("b_sb", [128, 512], BF16) for _ in range(2)]
        zs_sb = [kb.sb("zs_sb", [128, 2048], BF16) for _ in range(2)]
        dtr = kb.sb("dtr", [128, 32], F32)
        dab = kb.sb("dab", [128, 32], F32)
        dmx = kb.sb("dmx", [128, 32], F32)
        dt_sb = [kb.sb("dt_sb", [128, 32], F32) for _ in range(2)]
        pa = [kb.ps("pa", [128, 512], F32) for _ in range(2)]
        pz = [kb.ps("pz", [128, 512], F32) for _ in range(2)]
        pd = kb.ps("pd", [128, 512], F32)
        ptr = [kb.ps("ptr", [128, 1024], BF16) for _ in range(2)]
        n = 0
        for g in range(NG):
            t0 = g * GS
            for ti in range(4):
                x_ = xt[ti % 2]
                kb.dma("sp" if ti % 2 == 0 else "pool", x_[:, :], XIN[t0 + ti * 128:t0 + (ti + 1) * 128, :], r=[XIN], w=[x_])
                kb.A(lambda e, x_=x_: e.activation(out=xbf[:, :], in_=x_[:, :], func=AF.Copy), r=[x_], w=[xbf])
                transpose_to(kb, c, xbf, 8, ptr[0], xT, lambda c0, nn, ti=ti: xT[:, c0:c0 + nn, ti * 128:(ti + 1) * 128])
            if g % GPS == 0:
                kb.G(lambda e: e.memset(hal[:, :, :], 0.0), w=[hal])
            for fc in range(24):
                p_, rh, ac = pa[fc % 2], rawH[fc % 2], cacc[fc % 2]
                col0 = 2048 + fc * 128
                for k in range(8):
                    kb.T(lambda e, k=k, col0=col0, p_=p_: e.matmul(p_[:, :], lhsT=win[:, k, col0:col0 + 128], rhs=xT[:, k, :], start=(k == 0), stop=(k == 7)), r=[win, xT], w=[p_])
                kb.A(lambda e, p_=p_, rh=rh: e.activation(out=rh[:, 3:3 + GS], in_=p_[:, :], func=AF.Copy), r=[p_], w=[rh])
                kb.G(lambda e, rh=rh, fc=fc: e.tensor_copy(out=rh[:, 0:3], in_=hal[:, fc, :]), r=[hal], w=[rh])
                kb.V(lambda e, rh=rh, ac=ac, fc=fc: e.tensor_scalar(out=ac[:, :], in0=rh[:, 3:3 + GS], scalar1=cw[:, fc, 3:4], scalar2=cb[:, fc:fc + 1], op0=ALU.mult, op1=ALU.add), r=[rh, cw, cb], w=[ac])
                for k in range(3):
                    kb.V(lambda e, rh=rh, ac=ac, fc=fc, k=k: e.scalar_tensor_tensor(out=ac[:, :], in0=rh[:, k:k + GS], scalar=cw[:, fc, k:k + 1], in1=ac[:, :], op0=ALU.mult, op1=ALU.add), r=[rh, cw, ac], w=[ac])
                kb.G(lambda e, rh=rh, fc=fc: e.tensor_copy(out=hal[:, fc, :], in_=rh[:, GS:GS + 3]), r=[rh], w=[hal])
                kb.A(lambda e, ac=ac, fc=fc: e.activation(out=xbcT[:, fc, :], in_=ac[:, :], func=AF.Silu), r=[ac], w=[xbcT])
            for j in range(8):
                kb.dma("sp" if j % 2 == 0 else "pool", BCT[j][:, t0:t0 + GS], xbcT[:, 16 + j, :], r=[xbcT], w=[BCT])
            for ti in range(4):
                tsl = slice(ti * 128, (ti + 1) * 128)
                rows = slice(t0 + ti * 128, t0 + (ti + 1) * 128)
                xs_, b_, zs_, dt_ = xs_sb[ti % 2], b_sb[ti % 2], zs_sb[ti % 2], dt_sb[ti % 2]
                for half in range(2):
                    pt_ = ptr[half]
                    for j in range(8):
                        kb.T(lambda e, j=j, half=half, pt_=pt_: e.transpose(out=pt_[:, j * 128:(j + 1) * 128], in_=xbcT[:, half * 8 + j, tsl], identity=c.identb[:, :]), r=[xbcT, c.identb], w=[pt_])
                    if half == 0:
                        kb.V(lambda e, pt_=pt_, xs_=xs_: e.tensor_copy(out=xs_[:, 0:1024], in_=pt_[:, :]), r=[pt_], w=[xs_])
                    else:
                        kb.A(lambda e, pt_=pt_, xs_=xs_: e.activation(out=xs_[:, 1024:2048], in_=pt_[:, :], func=AF.Copy), r=[pt_], w=[xs_])
                kb.dma("sp", XS[rows, :], xs_[:, :], r=[xs_], w=[XS])
                for j in range(4):
                    kb.T(lambda e, j=j: e.transpose(out=ptr[0][:, j * 128:(j + 1) * 128], in_=xbcT[:, 16 + j, tsl], identity=c.identb[:, :]), r=[xbcT, c.identb], w=[ptr[0]])
                kb.V(lambda e, b_=b_: e.tensor_copy(out=b_[:, :], in_=ptr[0][:, 0:512]), r=[ptr[0]], w=[b_])
                kb.dma("pool", BTM[rows, :], b_[:, :], r=[b_], w=[BTM])
                for sl in range(4):
                    p_ = pz[n % 2]
                    n += 1
                    for k in range(8):
                        kb.T(lambda e, k=k, sl=sl, p_=p_: e.matmul(p_[:, :], lhsT=xT[:, k, tsl], rhs=win[:, k, sl * 512:(sl + 1) * 512], start=(k == 0), stop=(k == 7)), r=[xT, win], w=[p_])
                    kb.A(lambda e, sl=sl, p_=p_, zs_=zs_: e.activation(out=zs_[:, sl * 512:(sl + 1) * 512], in_=p_[:, :], func=AF.Silu), r=[p_], w=[zs_])
                kb.dma("sp", ZS[rows, :], zs_[:, :], r=[zs_], w=[ZS])
                for k in range(8):
                    kb.T(lambda e, k=k: e.matmul(pd[:, 0:32], lhsT=xT[:, k, tsl], rhs=win[:, k, 5120:5152], start=(k == 0), stop=(k == 7)), r=[xT, win], w=[pd])
                kb.V(lambda e: e.tensor_tensor(out=dtr[:, :], in0=pd[:, 0:32], in1=dtb[:, :], op=ALU.add), r=[pd, dtb], w=[dtr])
                kb.A(lambda e: e.activation(out=dab[:, :], in_=dtr[:, :], func=AF.Abs), r=[dtr], w=[dab])
                kb.A(lambda e: e.activation(out=dab[:, :], in_=dab[:, :], func=AF.Exp, scale=-1.0), r=[dab], w=[dab])
                kb.A(lambda e: e.activation(out=dab[:, :], in_=dab[:, :], func=AF.Ln, bias=one1[:, 0:1]), r=[dab, one1], w=[dab])
                kb.V(lambda e: e.tensor_single_scalar(out=dmx[:, :], in_=dtr[:, :], scalar=0.0, op=ALU.max), r=[dtr], w=[dmx])
                kb.V(lambda e, dt_=dt_: e.tensor_tensor(out=dt_[:, :], in0=dmx[:, :], in1=dab[:, :], op=ALU.add), r=[dmx, dab], w=[dt_])
                kb.dma("pool", DT[rows, :], dt_[:, :], r=[dt_], w=[DT])


def ssd_phase_b(kb, c, T, S, XIN, Y1, W, XS, BTM, BCT, ZS, DT):
    NC_ = T // 128
    CPS = S // 128
    with kb.scope():
        wout = kb.sb("wout", [128, 16, 1024], BF16)
        with kb.scope():
            stage = [kb.sb("stg", [128, 1024], F32) for _ in range(2)]
            load_w_bf(kb, wout, lambda k, c0, cw_: wout[:, k, c0:c0 + cw_], W["ssd_w_out"], 16, 1024, stage)
        g_bc = bcast_row(kb, "lnm_g", W["ln_mix_g"], 1024)
        b_bc = bcast_row(kb, "lnm_b", W["ln_mix_b"], 1024, q="pool")
        ng_bc = bcast_row(kb, "ng_bc", W["ssd_norm_g"], 2048)
        aneg = bcast_row(kb, "aneg", W["ssd_a_log"], 32, q="pool")
        kb.A(lambda e: e.activation(out=aneg[:, :], in_=aneg[:, :], func=AF.Exp), r=[aneg], w=[aneg])
        kb.V(lambda e: e.tensor_scalar(out=aneg[:, :], in0=aneg[:, :], scalar1=-1.0, scalar2=None, op0=ALU.mult), r=[aneg], w=[aneg])
        dsk = bcast_row(kb, "dsk", W["ssd_d"], 32)
        xt = kb.sb("xt", [128, 1024], F32)
        xs = kb.sb("xs", [128, 2048], BF16)
        bt = kb.sb("bt", [128, 512], BF16)
        zs = kb.sb("zs", [128, 2048], BF16)
        dt = kb.sb("dt", [128, 32], F32)
        bct = kb.sb("bct", [128, 8, 128], BF16)
        dtA = kb.sb("dtA", [128, 32], F32)
        acs = kb.sb("acs", [128, 64], F32)
        ea = kb.sb("ea", [128, 32], F32)
        dte = kb.sb("dte", [128, 32], F32)
        cd = kb.sb("cd", [128, 32], F32)
        xdt = kb.sb("xdt", [128, 2048], BF16)
        xe = kb.sb("xe", [128, 2048], BF16)
        Mh = kb.sb("Mh", [128, 32, 128], F32)
        cbt = kb.sb("cbt", [128, 4, 128], F32)
        Dm = [kb.sb("Dm", [128, 4, 128], F32) for _ in range(2)]
        Wd = kb.sb("Wd", [128, 32, 128], BF16)
        yoff = kb.sb("yoff", [128, 2048], F32)
        y = kb.sb("y", [128, 2048], F32)
        t2 = kb.sb("t2", [128, 2048], F32)
        ss = kb.sb("ss", [128, 4], F32)
        gnb = kb.sb("gnb", [128, 2048], BF16)
        gnT = kb.sb("gnT", [128, 16, 128], BF16)
        H = kb.sb("H", [128, 2048], F32)
        Hbf = kb.sb("Hbf", [128, 2048], BF16)
        ybuf = kb.sb("ybuf", [128, 1024], F32)
        tmp = ln_tmp(kb)
        py = [kb.ps("py", [128, 512], F32) for _ in range(4)]
        pd = [kb.ps("pd", [128, 512], F32) for _ in range(2)]
        pm = kb.ps("pm", [128, 512], F32)
        ptr = kb.ps("ptr", [128, 1024], BF16)
        v3 = lambda ap: ap.rearrange("p (h d) -> p h d", d=64)
        for ci in range(NC_):
            rows = slice(ci * 128, (ci + 1) * 128)
            kb.dma("sp", xs[:, :], XS[rows, :], r=[XS], w=[xs])
            kb.dma("pool", zs[:, :], ZS[rows, :], r=[ZS], w=[zs])
            kb.dma("sp", bt[:, :], BTM[rows, :], r=[BTM], w=[bt])
            kb.dma("pool", dt[:, :], DT[rows, :], r=[DT], w=[dt])
            kb.dma("sp", bct[:, :, :], BCT.t.rearrange("j p t -> p j t")[:, :, rows], r=[BCT], w=[bct])
            kb.dma("pool", xt[:, :], XIN[rows, :], r=[XIN], w=[xt])
            if ci % CPS == 0:
                kb.G(lambda e: e.memset(H[:, :], 0.0), w=[H])
                kb.G(lambda e: e.memset(Hbf[:, :], 0.0), w=[Hbf])
            kb.V(lambda e: e.tensor_tensor(out=dtA[:, :], in0=dt[:, :], in1=aneg[:, :], op=ALU.mult), r=[dt, aneg], w=[dtA])
            kb.T(lambda e: e.matmul(pm[:, 0:32], lhsT=c.triu(), rhs=dtA[:, :], start=True, stop=True), r=[c.cf, dtA], w=[pm])
            kb.T(lambda e: e.matmul(pm[:, 32:64], lhsT=c.ones(), rhs=dtA[:, :], start=True, stop=True), r=[c.cf, dtA], w=[pm])
            kb.V(lambda e: e.tensor_copy(out=acs[:, :], in_=pm[:, 0:64]), r=[pm], w=[acs])
            kb.A(lambda e: e.activation(out=ea[:, :], in_=acs[:, 0:32], func=AF.Exp), r=[acs], w=[ea])
            kb.V(lambda e: e.tensor_tensor(out=dte[:, :], in0=acs[:, 32:64], in1=acs[:, 0:32], op=ALU.subtract), r=[acs], w=[dte])
            kb.A(lambda e: e.activation(out=dte[:, :], in_=dte[:, :], func=AF.Exp), r=[dte], w=[dte])
            kb.V(lambda e: e.tensor_tensor(out=dte[:, :], in0=dte[:, :], in1=dt[:, :], op=ALU.mult), r=[dte, dt], w=[dte])
            kb.A(lambda e: e.activation(out=cd[:, :], in_=acs[:, 32:64], func=AF.Exp), r=[acs], w=[cd])
            kb.V(lambda e: e.tensor_tensor(out=v3(xdt[:, :]), in0=v3(xs[:, :]), in1=dt[:, :].unsqueeze(2).broadcast_to([128, 32, 64]), op=ALU.mult), r=[xs, dt], w=[xdt])
            kb.G(lambda e: e.tensor_tensor(out=v3(xe[:, :]), in0=v3(xs[:, :]), in1=dte[:, :].unsqueeze(2).broadcast_to([128, 32, 64]), op=ALU.mult), r=[xs, dte], w=[xe])
            kb.V(lambda e: e.tensor_tensor(out=Mh[:, :, :], in0=c.triu().unsqueeze(1).broadcast_to([128, 32, 128]), in1=dtA[:, :].unsqueeze(2).broadcast_to([128, 32, 128]), op=ALU.mult), r=[c.cf, dtA], w=[Mh])
            for g in range(4):
                kb.T(lambda e, g=g: e.matmul(pm[:, g * 128:(g + 1) * 128], lhsT=bct[:, g, :], rhs=bct[:, 4 + g, :], start=True, stop=True), r=[bct], w=[pm])
            kb.V(lambda e: e.tensor_copy(out=cbt[:, :, :], in_=pm[:, :].rearrange("p (a b) -> p a b", b=128)), r=[pm], w=[cbt])
            for g in range(4):
                gs = slice(g * 512, (g + 1) * 512)
                kb.T(lambda e, g=g, gs=gs: e.matmul(py[g][:, :], lhsT=bct[:, 4 + g, :], rhs=Hbf[:, gs], start=True, stop=True), r=[bct, Hbf], w=[py[g]])
                kb.V(lambda e, g=g, gs=gs: e.tensor_tensor(out=v3(yoff[:, gs]), in0=v3(py[g][:, :]), in1=ea[:, g * 8:(g + 1) * 8].unsqueeze(2).broadcast_to([128, 8, 64]), op=ALU.mult), r=[py[g], ea], w=[yoff])
            for hb in range(8):
                p_, d_ = pd[hb % 2], Dm[hb % 2]
                for i in range(4):
                    h = hb * 4 + i
                    kb.T(lambda e, i=i, h=h, p_=p_: e.matmul(p_[:, i * 128:(i + 1) * 128], lhsT=c.ones(), rhs=Mh[:, h, :], start=True, stop=False), r=[c.cf, Mh], w=[p_])
                    kb.T(lambda e, i=i, h=h, p_=p_: e.matmul(p_[:, i * 128:(i + 1) * 128], lhsT=Mh[:, h, :], rhs=c.negones(), start=False, stop=True), r=[c.cf, Mh], w=[p_])
                kb.V(lambda e, p_=p_, d_=d_: e.tensor_tensor(out=d_[:, :, :], in0=p_[:, :].rearrange("p (a b) -> p a b", b=128), in1=c.negmask().unsqueeze(1).broadcast_to([128, 4, 128]), op=ALU.add), r=[p_, c.cf], w=[d_])
                kb.A(lambda e, d_=d_: e.activation(out=d_[:, :, :], in_=d_[:, :, :], func=AF.Exp), r=[d_], w=[d_])
                kb.G(lambda e, d_=d_, hb=hb: e.tensor_tensor(out=Wd[:, hb * 4:(hb + 1) * 4, :], in0=d_[:, :, :], in1=cbt[:, hb // 2, :].unsqueeze(1).broadcast_to([128, 4, 128]), op=ALU.mult), r=[d_, cbt], w=[Wd])
            for h in range(32):
                g = h // 8
                kb.T(lambda e, h=h, g=g: e.matmul(py[g][:, (h % 8) * 64:(h % 8 + 1) * 64], lhsT=Wd[:, h, :], rhs=xdt[:, h * 64:(h + 1) * 64], start=True, stop=True), r=[Wd, xdt], w=[py[g]])
            for g in range(4):
                gs = slice(g * 512, (g + 1) * 512)
                kb.V(lambda e, g=g, gs=gs: e.tensor_tensor(out=y[:, gs], in0=py[g][:, :], in1=yoff[:, gs], op=ALU.add), r=[py[g], yoff], w=[y])
            kb.G(lambda e: e.tensor_tensor(out=v3(t2[:, :]), in0=v3(xs[:, :]), in1=dsk[:, :].unsqueeze(2).broadcast_to([128, 32, 64]), op=ALU.mult), r=[xs, dsk], w=[t2])
            kb.V(lambda e: e.tensor_tensor(out=y[:, :], in0=y[:, :], in1=t2[:, :], op=ALU.add), r=[y, t2], w=[y])
            kb.V(lambda e: e.tensor_tensor(out=y[:, :], in0=y[:, :], in1=zs[:, :], op=ALU.mult), r=[y, zs], w=[y])
            kb.V(lambda e: e.memset(ss[:, :], 0.0), w=[ss])
            for g in range(4):
                gs = slice(g * 512, (g + 1) * 512)
                kb.A(lambda e, g=g, gs=gs: e.activation(out=t2[:, gs], in_=y[:, gs], func=AF.Square, accum_out=ss[:, g:g + 1]), r=[y, ss], w=[t2, ss])
            kb.A(lambda e: e.activation(out=ss[:, :], in_=ss[:, :], func=AF.Ln, scale=1.0 / 512, bias=CONST.eps[:, 0:1]), r=[ss, CONST.eps], w=[ss])
            kb.A(lambda e: e.activation(out=ss[:, :], in_=ss[:, :], func=AF.Exp, scale=-0.5), r=[ss], w=[ss])
            kb.V(lambda e: e.tensor_tensor(out=y[:, :].rearrange("p (g d) -> p g d", d=512), in0=y[:, :].rearrange("p (g d) -> p g d", d=512), in1=ss[:, :].unsqueeze(2).broadcast_to([128, 4, 512]), op=ALU.mult), r=[y, ss], w=[y])
            kb.G(lambda e: e.tensor_tensor(out=gnb[:, :], in0=y[:, :], in1=ng_bc[:, :], op=ALU.mult), r=[y, ng_bc], w=[gnb])
            transpose_to(kb, c, gnb, 16, ptr, gnT, lambda c0, nn: gnT[:, c0:c0 + nn, :])
            po = pd
            for hf in range(2):
                for k in range(16):
                    kb.T(lambda e, k=k, hf=hf: e.matmul(po[hf][:, :], lhsT=gnT[:, k, :], rhs=wout[:, k, hf * 512:(hf + 1) * 512], start=(k == 0), stop=(k == 15)), r=[gnT, wout], w=[po[hf]])
            resid_ln_store(kb, xt, po, g_bc, b_bc, ybuf, tmp, Y1, Y1[rows, :])
            for g in range(4):
                gs = slice(g * 512, (g + 1) * 512)
                kb.T(lambda e, g=g, gs=gs: e.matmul(py[g][:, :], lhsT=bt[:, g * 128:(g + 1) * 128], rhs=xe[:, gs], start=True, stop=True), r=[bt, xe], w=[py[g]])
            kb.V(lambda e: e.tensor_tensor(out=v3(H[:, :]), in0=v3(H[:, :]), in1=cd[:, :].unsqueeze(2).broadcast_to([128, 32, 64]), op=ALU.mult), r=[H, cd], w=[H])
            for g in range(4):
                gs = slice(g * 512, (g + 1) * 512)
                kb.V(lambda e, g=g, gs=gs: e.tensor_tensor(out=H[:, gs], in0=H[:, gs], in1=py[g][:, :], op=ALU.add), r=[H, py[g]], w=[H])
            kb.A(lambda e: e.activation(out=Hbf[:, :], in_=H[:, :], func=AF.Copy), r=[H], w=[Hbf])


W_SHAPES = {
    "ssd_w_in": (2, 1024, 5152), "ssd_conv_w": (2, 4, 3072), "ssd_conv_b": (2, 3072), "ssd_dt_bias": (2, 32),
    "ssd_a_log": (2, 32), "ssd_d": (2, 32), "ssd_norm_g": (2, 2048), "ssd_w_out": (2, 2048, 1024),
    "moba_w_qkv": (1, 1024, 3072), "moba_w_out": (1, 1024, 1024),
    "conv_w_pw1": (1, 1024, 2048), "conv_b_pw1": (1, 2048), "conv_w_dw": (1, 31, 1024), "conv_b_dw": (1, 1024),
    "conv_ln_g": (1, 1024), "conv_ln_b": (1, 1024), "conv_w_pw2": (1, 1024, 1024),
    "peer_w_q": (4, 1024, 2048), "peer_sub_keys": (4, 8, 2, 128, 128), "peer_u": (4, 16384, 1024), "peer_v": (4, 16384, 1024),
    "ln_mix_g": (4, 1024), "ln_mix_b": (4, 1024), "ln_ffn_g": (4, 1024), "ln_ffn_b": (4, 1024),
    "ple_w_gate": (4, 1024, 1024), "ple_w_proj": (4, 256, 1024),
}
DEPTH = 4
PER_LAYER = ("peer_w_q", "peer_sub_keys", "peer_u", "peer_v", "ln_mix_g", "ln_mix_b", "ln_ffn_g", "ln_ffn_b", "ple_w_gate", "ple_w_proj")


def build_full(T, S, depth=DEPTH, TG=256):
    nc = bass.Bass("TRN2", target_bir_lowering=False)
    kb = KB(nc)
    with nc.allow_low_precision("bf16 matmul operands with fp32 accumulation"):
        cd = kb.dram("consts", [128, CW], F32, kind="ExternalInput")
        c2d = kb.dram("c2", [128, C2W], F32, kind="ExternalInput")
        X = kb.dram("x", [T, 1024], F32, kind="ExternalInput")
        P = kb.dram("p", [DEPTH, T, 256], F32, kind="ExternalInput")
        OUT = kb.dram("out", [T, 1024], F32, kind="ExternalOutput")
        Wd = {k: kb.dram(k, list(shp), F32, kind="ExternalInput").t for k, shp in W_SHAPES.items()}
        XA = [kb.dram("xa%d" % i, [T, 1024], F32) for i in range(2)]
        Y1 = kb.dram("y1", [T, 1024], F32)
        UVs = kb.dram("UVs", [128, 128, 2048], BF16)
        XTd = kb.dram("XTd", [8, 128, T], BF16)
        QTd = kb.dram("QTd", [16, 128, T], BF16)
        XS = kb.dram("XS", [T, 2048], BF16)
        BTM = kb.dram("BTM", [T, 512], BF16)
        BCT = kb.dram("BCT", [8, 128, T], BF16)
        ZS = kb.dram("ZS", [T, 2048], BF16)
        DT = kb.dram("DT", [T, 32], F32)
        c = load_consts(kb, cd)
        xin = X
        for i in range(depth):
            kind, j = i % 3, i // 3
            W = {}
            for k in W_SHAPES:
                if k in PER_LAYER:
                    W[k] = Wd[k][i]
                elif k.startswith(("ssd_", "moba_", "conv_")):
                    n = W_SHAPES[k][0]
                    W[k] = Wd[k][min(j, n - 1)]
            if kind == 0:
                ssd_phase_a(kb, c, T, S, xin, W, XS, BTM, BCT, ZS, DT)
                ssd_phase_b(kb, c, T, S, xin, Y1, W, XS, BTM, BCT, ZS, DT)
            elif kind == 1:
                moba_phase(kb, c, T, S, xin, Y1, W, c2d)
            else:
                conf_phase(kb, c, T, S, xin, Y1, W)
            peer_prepass(kb, c, W["peer_u"], W["peer_v"], UVs)
            xout = OUT if i == depth - 1 else XA[i % 2]
            peer_q_phase(kb, c, T, Y1, W, XTd, QTd)
            peer_phase(kb, c, T, Y1, xout, P.t[i], W, UVs, XTd, QTd, TG=TG)
            xin = xout
        kb.barrier()
    return nc, kb


_CACHE = {}


def kernel(**inputs):
    NCORE = 8
    B, S = inputs["x"].shape[0], inputs["x"].shape[1]
    per = B // NCORE
    T = per * S
    key = (T, S)
    if key not in _CACHE:
        _CACHE[key] = build_full(T, S)[0]
    nc = _CACHE[key]
    consts, c2 = host_consts(), host_c2()
    shared = {k: np.ascontiguousarray(np.asarray(inputs[k], dtype=np.float32)) for k in W_SHAPES}
    x = np.asarray(inputs["x"], dtype=np.float32)
    p = np.asarray(inputs["p"], dtype=np.float32)
    in_maps = []
    for ci in range(NCORE):
        m = dict(shared)
        m["consts"] = consts
        m["c2"] = c2
        m["x"] = np.ascontiguousarray(x[ci * per:(ci + 1) * per].reshape(T, 1024))
        m["p"] = np.ascontiguousarray(p[:, ci * per:(ci + 1) * per].reshape(DEPTH, T, 256))
        in_maps.append(m)
    res = run_bass_kernel_spmd(nc, in_maps, core_ids=list(range(NCORE)))
    outs = [np.asarray(r["out"]).reshape(per, S, 1024) for r in res.results]
    return np.concatenate(outs, axis=0).astype(np.float32)
```
